# Optimizing a Trainium2 kernel written in Bass

```python
import math
import jax
import jax.numpy as jnp
from jax import lax
import numpy as np

D_MODEL = 2048
BATCH = 4
SEQ = 4096
DEPTH = 2

GRID_W = 64
CTX_LEN = 256
N_MOD = 6
N_BRANCH = 3
BRANCH_W = D_MODEL // 2
ML_HEADS = 4
ML_DV = BRANCH_W // ML_HEADS
ML_DQK = ML_DV // 2
ML_CHUNK = 64
MLA_HEADS = 8
MLA_Q_LORA = 512
MLA_KV_LORA = 512
MLA_NOPE = 128
MLA_ROPE = 64
MLA_DV = BRANCH_W // MLA_HEADS
MLA_DQK = MLA_NOPE + MLA_ROPE
ATTN_BLOCK = 128
ROPE_THETA = 10000.0
S5_WIDTH = BRANCH_W
S5_GROUP = 16
S5_GROUPS = S5_WIDTH // S5_GROUP
S5_STATE = 64
D_FF = 4 * D_MODEL
EPS = 1e-6
NEG_BIG = -1e30
IN_SIZES = (ML_HEADS * ML_DQK, ML_HEADS * ML_DQK, ML_HEADS * ML_DV, ML_HEADS * ML_DV, 4 * ML_HEADS, MLA_Q_LORA, MLA_KV_LORA, MLA_ROPE, S5_WIDTH, N_BRANCH * D_MODEL)
D_IN = sum(IN_SIZES)

kernel_name = 'hybrid_mlstm_mla_s5_dit_trunk'


def rms_norm(x, g):
    xf = x.astype(jnp.float32)
    y = xf * lax.rsqrt(jnp.mean(xf * xf, axis=-1, keepdims=True) + EPS)
    return (y * g.astype(jnp.float32)).astype(x.dtype)


def modulate(h, shift, scale):
    return h * (1.0 + scale) + shift


def split_cols(z):
    bounds = np.cumsum(IN_SIZES)[:-1].tolist()
    return jnp.split(z, bounds, axis=-1)


def flip_time(a, direction, axis):
    return jnp.flip(a, axis=axis) if direction == 1 else a


def axial_rope_tables(n_tokens):
    rows = n_tokens // GRID_W
    row = jnp.repeat(jnp.arange(rows, dtype=jnp.float32), GRID_W)
    col = jnp.tile(jnp.arange(GRID_W, dtype=jnp.float32), rows)
    n_freq = MLA_ROPE // 4
    inv_freq = ROPE_THETA ** (-jnp.arange(n_freq, dtype=jnp.float32) / n_freq)
    ang_r = row[:, None] * inv_freq
    ang_c = col[:, None] * inv_freq
    return (jnp.cos(ang_r), jnp.sin(ang_r), jnp.cos(ang_c), jnp.sin(ang_c))


def rotate_half_pairs(x, cos, sin):
    m = x.shape[-1] // 2
    x1, x2 = x[..., :m], x[..., m:]
    c, s = cos[:, None, :], sin[:, None, :]
    return jnp.concatenate([x1 * c - x2 * s, x2 * c + x1 * s], axis=-1)


def axial_rope(x, rope):
    cos_r, sin_r, cos_c, sin_c = rope
    xf = x.astype(jnp.float32)
    half = MLA_ROPE // 2
    out = jnp.concatenate([rotate_half_pairs(xf[..., :half], cos_r, sin_r), rotate_half_pairs(xf[..., half:], cos_c, sin_c)], axis=-1)
    return out.astype(x.dtype)


def mlstm_prep(q, k, v, gates, gate_b):
    b, t, _ = q.shape

    def heads(a, dh):
        return a.reshape(b, t, ML_HEADS, dh).transpose(0, 2, 1, 3).astype(jnp.float32)

    qh = heads(q, ML_DQK) * (ML_DQK ** -0.5)
    kh = heads(k, ML_DQK)
    vh = heads(v, ML_DV)
    g = (gates.reshape(b, t, 4, ML_HEADS).astype(jnp.float32) + gate_b.astype(jnp.float32)).transpose(2, 0, 3, 1)
    log_gates = ((g[0], jax.nn.log_sigmoid(g[1])), (g[2], jax.nn.log_sigmoid(g[3])))
    return qh, kh, vh, log_gates


def mlstm_chunkwise(q, k, v, log_i, log_f, state):
    b, h, t, _ = q.shape
    nc = t // ML_CHUNK

    def chunks(a):
        return jnp.moveaxis(a.reshape(a.shape[:2] + (nc, ML_CHUNK) + a.shape[3:]), 2, 0)

    causal = jnp.tril(jnp.ones((ML_CHUNK, ML_CHUNK), dtype=bool))

    def step(carry, inp):
        c_mat, n_vec, m = carry
        qc, kc, vc, lic, lfc = inp
        cum_f = jnp.cumsum(lfc, axis=-1)
        log_w = jnp.where(causal, cum_f[..., :, None] - cum_f[..., None, :] + lic[..., None, :], NEG_BIG)
        log_inter = cum_f + m[..., None]
        m_t = jnp.maximum(log_inter, jnp.max(log_w, axis=-1))
        w_inter = jnp.exp(log_inter - m_t)
        s = jnp.einsum('bhtd,bhsd->bhts', qc, kc) * jnp.exp(log_w - m_t[..., None])
        num = w_inter[..., None] * jnp.einsum('bhtd,bhdv->bhtv', qc, c_mat) + jnp.einsum('bhts,bhsv->bhtv', s, vc)
        den = w_inter * jnp.einsum('bhtd,bhd->bht', qc, n_vec) + jnp.sum(s, axis=-1)
        h_out = num / jnp.maximum(jnp.abs(den), jnp.exp(-m_t))[..., None]
        log_to_end = cum_f[..., -1:] - cum_f + lic
        m_new = jnp.maximum(cum_f[..., -1] + m, jnp.max(log_to_end, axis=-1))
        decay = jnp.exp(cum_f[..., -1] + m - m_new)
        w_end = jnp.exp(log_to_end - m_new[..., None])
        c_mat = decay[..., None, None] * c_mat + jnp.einsum('bhs,bhsd,bhsv->bhdv', w_end, kc, vc)
        n_vec = decay[..., None] * n_vec + jnp.einsum('bhs,bhsd->bhd', w_end, kc)
        return (c_mat, n_vec, m_new), h_out

    state, hs = lax.scan(step, state, (chunks(q), chunks(k), chunks(v), chunks(log_i), chunks(log_f)))
    hs = jnp.moveaxis(hs, 0, 2).reshape(b, h, t, v.shape[-1])
    return hs, state


def mlstm_bidir(ctx_in, lat_in):
    qc, kc, vc, gc = ctx_in
    qx, kx, vx, gx = lat_in
    b = qx.shape[0]
    h_ctx = jnp.zeros(qc.shape[:3] + (ML_DV,), jnp.float32)
    h_lat = jnp.zeros(qx.shape[:3] + (ML_DV,), jnp.float32)
    for d in range(2):
        state0 = (jnp.zeros((b, ML_HEADS, ML_DQK, ML_DV), jnp.float32), jnp.zeros((b, ML_HEADS, ML_DQK), jnp.float32), jnp.zeros((b, ML_HEADS), jnp.float32))
        hc, state_ctx = mlstm_chunkwise(flip_time(qc, d, 2), flip_time(kc, d, 2), flip_time(vc, d, 2), flip_time(gc[d][0], d, 2), flip_time(gc[d][1], d, 2), state0)
        hx, _ = mlstm_chunkwise(flip_time(qx, d, 2), flip_time(kx, d, 2), flip_time(vx, d, 2), flip_time(gx[d][0], d, 2), flip_time(gx[d][1], d, 2), state_ctx)
        h_ctx = h_ctx + flip_time(hc, d, 2)
        h_lat = h_lat + flip_time(hx, d, 2)
    return h_ctx, h_lat


def mlstm_out(h, o, norm_g):
    b, _, t, _ = h.shape
    hn = rms_norm(h.transpose(0, 2, 1, 3), norm_g).reshape(b, t, ML_HEADS * ML_DV)
    return (hn * jax.nn.sigmoid(o.astype(jnp.float32))).astype(o.dtype)


def mla_qkv(q_a, kv_a, k_pe, qa_g, kva_g, w_uq, w_ukv, qn_g, kn_g, rope):
    b, t, _ = q_a.shape
    q = (rms_norm(q_a, qa_g) @ w_uq).reshape(b, t, MLA_HEADS, MLA_DQK)
    kv = (rms_norm(kv_a, kva_g) @ w_ukv).reshape(b, t, MLA_HEADS, MLA_NOPE + MLA_DV)
    k_pe = jnp.broadcast_to(k_pe[:, :, None, :], (b, t, MLA_HEADS, MLA_ROPE))
    k = jnp.concatenate([kv[..., :MLA_NOPE], k_pe], axis=-1)
    v = kv[..., MLA_NOPE:]
    q = rms_norm(q, qn_g)
    k = rms_norm(k, kn_g)
    if rope is not None:
        q = jnp.concatenate([q[..., :MLA_NOPE], axial_rope(q[..., MLA_NOPE:], rope)], axis=-1)
        k = jnp.concatenate([k[..., :MLA_NOPE], axial_rope(k[..., MLA_NOPE:], rope)], axis=-1)
    return q, k, v


def attend(q, k, v):
    s = jnp.einsum('bqhd,bkhd->bhqk', q, k).astype(jnp.float32) * (MLA_DQK ** -0.5)
    p = jax.nn.softmax(s, axis=-1).astype(v.dtype)
    return jnp.einsum('bhqk,bkhd->bqhd', p, v)


def latent_attention(q, k_ctx, v_ctx, k_lat, v_lat):
    k = jnp.concatenate([k_ctx, k_lat], axis=1)
    v = jnp.concatenate([v_ctx, v_lat], axis=1)
    b, t, h, d = q.shape
    nb = t // ATTN_BLOCK
    qb = jnp.moveaxis(q.reshape(b, nb, ATTN_BLOCK, h, d), 1, 0)
    out = lax.map(lambda qi: attend(qi, k, v), qb)
    return jnp.moveaxis(out, 0, 1).reshape(b, t, MLA_HEADS * MLA_DV)


def s5_discretize(a_re, a_im, log_dt, b_re, b_im):
    lam = lax.complex(jnp.minimum(a_re.astype(jnp.float32), -1e-4), a_im.astype(jnp.float32))
    dt = jnp.exp(log_dt.astype(jnp.float32))[:, None]
    lam_bar = jnp.exp(lam * dt)
    b_bar = ((lam_bar - 1.0) / lam)[..., None] * lax.complex(b_re.astype(jnp.float32), b_im.astype(jnp.float32))
    return lam_bar, b_bar


def ssm_combine(e1, e2):
    a1, b1 = e1
    a2, b2 = e2
    return a2 * a1, a2 * b1 + b2


def s5_states(u, lam_bar, b_bar, x0):
    bu = jnp.einsum('gnc,btgc->btgn', b_bar, u)
    bu = bu.at[:, 0].add(lam_bar * x0)
    a = jnp.broadcast_to(lam_bar, bu.shape)
    _, xs = lax.associative_scan(ssm_combine, (a, bu), axis=1)
    return xs


def s5_readout(xs, c_mat):
    return jnp.einsum('gcn,btgn->btgc', c_mat, xs).real


def s5_mixer(u_c, u_x, a_re, a_im, log_dt, b_re, b_im, c_re, c_im, d_skip, w_glu, b_glu, with_ctx_out):
    def grouped(u):
        return u.astype(jnp.float32).reshape(u.shape[0], u.shape[1], S5_GROUPS, S5_GROUP)

    uc, ux = grouped(u_c), grouped(u_x)
    dg = d_skip.astype(jnp.float32).reshape(S5_GROUPS, S5_GROUP)
    yx = dg * ux
    yc = dg * uc if with_ctx_out else None
    for d in range(2):
        lam_bar, b_bar = s5_discretize(a_re[d], a_im[d], log_dt[d], b_re[d], b_im[d])
        c_mat = lax.complex(c_re[d].astype(jnp.float32), c_im[d].astype(jnp.float32))
        x0 = jnp.zeros((uc.shape[0], S5_GROUPS, S5_STATE), jnp.complex64)
        xs_c = s5_states(flip_time(uc, d, 1), lam_bar, b_bar, x0)
        xs_x = s5_states(flip_time(ux, d, 1), lam_bar, b_bar, xs_c[:, -1])
        yx = yx + flip_time(s5_readout(xs_x, c_mat), d, 1)
        if with_ctx_out:
            yc = yc + flip_time(s5_readout(xs_c, c_mat), d, 1)

    def glu(y):
        g = jax.nn.gelu(y.reshape(y.shape[0], y.shape[1], S5_WIDTH)).astype(u_x.dtype)
        return g * jax.nn.sigmoid(g @ w_glu + b_glu)

    return glu(yx), (glu(yc) if with_ctx_out else None)


def merge_branches(branches, gate_pre, w_branch, w_out):
    stacked = jnp.stack(branches, axis=-2)
    b, t = stacked.shape[0], stacked.shape[1]
    gates = jax.nn.sigmoid(gate_pre.reshape(b, t, N_BRANCH, D_MODEL).astype(jnp.float32)).astype(stacked.dtype)
    proj = jnp.einsum('btrc,rcd->btrd', stacked, w_branch)
    return jnp.sum(gates * proj, axis=-2) @ w_out


def sq_relu_mlp(h, w1, w2):
    return jnp.square(jax.nn.relu(h @ w1)) @ w2


def token_mixers(hx, hc, w_in, b_in, ml_gate_b, ml_norm_g, mla_qa_g, mla_kva_g, mla_w_uq, mla_w_ukv, mla_qn_g, mla_kn_g, s5_a_re, s5_a_im, s5_log_dt, s5_b_re, s5_b_im, s5_c_re, s5_c_im, s5_d, s5_w_glu, s5_b_glu, w_branch, w_out, rope, with_ctx_out):
    zx = split_cols(hx @ w_in + b_in)
    zc = split_cols(hc @ w_in + b_in)
    ml_c = mlstm_prep(zc[0], zc[1], zc[2], zc[4], ml_gate_b)
    ml_x = mlstm_prep(zx[0], zx[1], zx[2], zx[4], ml_gate_b)
    h_a_c, h_a_x = mlstm_bidir(ml_c, ml_x)
    a_x = mlstm_out(h_a_x, zx[3], ml_norm_g)
    q_c, k_c, v_c = mla_qkv(zc[5], zc[6], zc[7], mla_qa_g, mla_kva_g, mla_w_uq, mla_w_ukv, mla_qn_g, mla_kn_g, None)
    q_x, k_x, v_x = mla_qkv(zx[5], zx[6], zx[7], mla_qa_g, mla_kva_g, mla_w_uq, mla_w_ukv, mla_qn_g, mla_kn_g, rope)
    b_x = latent_attention(q_x, k_c, v_c, k_x, v_x)
    c_x, c_c = s5_mixer(zc[8], zx[8], s5_a_re, s5_a_im, s5_log_dt, s5_b_re, s5_b_im, s5_c_re, s5_c_im, s5_d, s5_w_glu, s5_b_glu, with_ctx_out)
    out_x = merge_branches((a_x, b_x, c_x), zx[9], w_branch, w_out)
    if not with_ctx_out:
        return out_x, None
    a_c = mlstm_out(h_a_c, zc[3], ml_norm_g)
    b_c = attend(q_c, k_c, v_c).reshape(q_c.shape[0], q_c.shape[1], MLA_HEADS * MLA_DV)
    out_c = merge_branches((a_c, b_c, c_c), zc[9], w_branch, w_out)
    return out_x, out_c


def setup_inputs(seed: int = 0) -> dict:
    key = jax.random.key(seed)
    ks = jax.random.split(key, 32)
    f32 = jnp.float32

    def nrm(k, shape, scale):
        return scale * jax.random.normal(k, shape, f32)

    L, D, H = DEPTH, D_MODEL, ML_HEADS
    G, N, GC = S5_GROUPS, S5_STATE, S5_GROUP
    f_bias = jnp.linspace(3.0, 6.0, H, dtype=f32)
    gate_base = jnp.stack([jnp.zeros((H,), f32), f_bias, jnp.zeros((H,), f32), f_bias])
    return {
        'x': nrm(ks[0], (BATCH, SEQ, D), 1.0),
        'c': nrm(ks[1], (BATCH, D), 1.0),
        'ctx': nrm(ks[2], (BATCH, CTX_LEN, D), 1.0),
        'c_ctx': nrm(ks[3], (D,), 1.0),
        'w_mod': nrm(ks[4], (L, D, N_MOD * D), 0.5 * D ** -0.5),
        'b_mod': nrm(ks[5], (L, N_MOD * D), 0.01),
        'norm_g': 1.0 + nrm(ks[6], (L, 2, D), 0.02),
        'w_in': nrm(ks[7], (L, D, D_IN), D ** -0.5),
        'b_in': nrm(ks[8], (L, D_IN), 0.01),
        'ml_gate_b': gate_base[None] + nrm(ks[9], (L, 4, H), 0.1),
        'ml_norm_g': 1.0 + nrm(ks[10], (L, H, ML_DV), 0.02),
        'mla_qa_g': 1.0 + nrm(ks[11], (L, MLA_Q_LORA), 0.02),
        'mla_kva_g': 1.0 + nrm(ks[12], (L, MLA_KV_LORA), 0.02),
        'mla_w_uq': nrm(ks[13], (L, MLA_Q_LORA, MLA_HEADS * MLA_DQK), MLA_Q_LORA ** -0.5),
        'mla_w_ukv': nrm(ks[14], (L, MLA_KV_LORA, MLA_HEADS * (MLA_NOPE + MLA_DV)), MLA_KV_LORA ** -0.5),
        'mla_qn_g': 1.0 + nrm(ks[15], (L, MLA_DQK), 0.02),
        'mla_kn_g': 1.0 + nrm(ks[16], (L, MLA_DQK), 0.02),
        's5_a_re': -0.5 + nrm(ks[17], (L, 2, G, N), 0.01),
        's5_a_im': math.pi * jnp.arange(N, dtype=f32) + nrm(ks[18], (L, 2, G, N), 0.01),
        's5_log_dt': jax.random.uniform(ks[19], (L, 2, G), f32, math.log(1e-3), math.log(1e-1)),
        's5_b_re': nrm(ks[20], (L, 2, G, N, GC), (2.0 * GC) ** -0.5),
        's5_b_im': nrm(ks[21], (L, 2, G, N, GC), (2.0 * GC) ** -0.5),
        's5_c_re': nrm(ks[22], (L, 2, G, GC, N), N ** -0.5),
        's5_c_im': nrm(ks[23], (L, 2, G, GC, N), N ** -0.5),
        's5_d': nrm(ks[24], (L, S5_WIDTH), 1.0),
        's5_w_glu': nrm(ks[25], (L, S5_WIDTH, S5_WIDTH), S5_WIDTH ** -0.5),
        's5_b_glu': nrm(ks[26], (L, S5_WIDTH), 0.01),
        'w_branch': nrm(ks[27], (L, N_BRANCH, BRANCH_W, D), BRANCH_W ** -0.5),
        'w_out': nrm(ks[28], (L, D, D), D ** -0.5),
        'w_ff1': nrm(ks[29], (L, D, D_FF), D ** -0.5),
        'w_ff2': nrm(ks[30], (L, D_FF, D), D_FF ** -0.5),
    }


def reference(x, c, ctx, c_ctx, w_mod, b_mod, norm_g, w_in, b_in, ml_gate_b, ml_norm_g, mla_qa_g, mla_kva_g, mla_w_uq, mla_w_ukv, mla_qn_g, mla_kn_g, s5_a_re, s5_a_im, s5_log_dt, s5_b_re, s5_b_im, s5_c_re, s5_c_im, s5_d, s5_w_glu, s5_b_glu, w_branch, w_out, w_ff1, w_ff2):
    batch = x.shape[0]
    rope = axial_rope_tables(x.shape[1])
    sc = jax.nn.silu(c)
    scc = jax.nn.silu(c_ctx)
    for l in range(DEPTH):
        with_ctx_out = l < DEPTH - 1
        mx = (sc @ w_mod[l] + b_mod[l]).reshape(batch, N_MOD, D_MODEL).transpose(1, 0, 2)[:, :, None, :]
        mc = (scc @ w_mod[l] + b_mod[l]).reshape(N_MOD, 1, 1, D_MODEL)
        hx = modulate(rms_norm(x, norm_g[l, 0]), mx[0], mx[1])
        hc = modulate(rms_norm(ctx, norm_g[l, 0]), mc[0], mc[1])
        out_x, out_c = token_mixers(hx, hc, w_in[l], b_in[l], ml_gate_b[l], ml_norm_g[l], mla_qa_g[l], mla_kva_g[l], mla_w_uq[l], mla_w_ukv[l], mla_qn_g[l], mla_kn_g[l], s5_a_re[l], s5_a_im[l], s5_log_dt[l], s5_b_re[l], s5_b_im[l], s5_c_re[l], s5_c_im[l], s5_d[l], s5_w_glu[l], s5_b_glu[l], w_branch[l], w_out[l], rope, with_ctx_out)
        x = x + mx[2] * out_x
        hx = modulate(rms_norm(x, norm_g[l, 1]), mx[3], mx[4])
        x = x + mx[5] * sq_relu_mlp(hx, w_ff1[l], w_ff2[l])
        if with_ctx_out:
            ctx = ctx + mc[2] * out_c
            hc = modulate(rms_norm(ctx, norm_g[l, 1]), mc[3], mc[4])
            ctx = ctx + mc[5] * sq_relu_mlp(hc, w_ff1[l], w_ff2[l])
    return x
```

```python
import contextlib
import math
import numpy as np
import concourse.bass as bass
import concourse.mybir as mybir
from concourse.bass_utils import run_bass_kernel_spmd

F32 = mybir.dt.float32
BF16 = mybir.dt.bfloat16
AF = mybir.ActivationFunctionType
ALU = mybir.AluOpType
AX = mybir.AxisListType

ENGS = ("pe", "act", "dve", "pool", "sp")
EPOCH = 30000

D = 2048
KC = 16
DEPTH = 2
DIN = 11344
OFF_Q, OFF_K, OFF_V, OFF_O, OFF_G, OFF_QA, OFF_KVA, OFF_KPE, OFF_U, OFF_BG = 0, 512, 1024, 2048, 3072, 3088, 3600, 4112, 4176, 5200
EPS = 1e-6
SL = 128


class T:
    def __init__(self, h, name=""):
        self.h = h
        self.name = name
        self.w = {}
        self.r = {}

    def __getitem__(self, k):
        return self.h[k]

    def ap(self):
        return self.h.ap()


class Prog:
    def __init__(self, nc, n_dma_sems=8):
        self.nc = nc
        self.es = contextlib.ExitStack()
        self.q = {e: [] for e in ENGS}
        self.tick = {e: 0 for e in ENGS}
        self.epoch = {e: 0 for e in ENGS}
        self.sems = {}
        self.seen = {e: {} for e in ENGS}
        self.dma_slots = {}
        self.n_dma_sems = n_dma_sems
        self.dma_i = {e: 0 for e in ENGS}
        self.nsem = 0
        self.ninstr = 0
        self.scopes = []

    def sem(self, key):
        if key not in self.sems:
            self.sems[key] = self.es.enter_context(self.nc.semaphore("s%d" % self.nsem))
            self.nsem += 1
        return self.sems[key]

    def _stack(self):
        return self.scopes[-1] if self.scopes else self.es

    def sbuf(self, name, shape, dtype):
        self.uid = getattr(self, "uid", 0) + 1
        name = "%s_u%d" % (name, self.uid)
        return T(self._stack().enter_context(self.nc.sbuf_tensor(name, list(shape), dtype)), name)

    def psum(self, name, shape, dtype=F32):
        return T(self.es.enter_context(self.nc.psum_tensor(name, list(shape), dtype)), name)

    def dram(self, name, shape, dtype, kind=None):
        if kind is None:
            h = self.nc.dram_tensor(name, list(shape), dtype)
        else:
            h = self.nc.dram_tensor(name, list(shape), dtype, kind=kind)
        return T(h, name)

    @contextlib.contextmanager
    def scope(self):
        st = contextlib.ExitStack()
        self.scopes.append(st)
        try:
            yield
        finally:
            self.barrier()
            self.flush()
            self.scopes.pop()
            st.close()

    def _waits(self, e, reads, writes, pw=()):
        deps = {}
        for b in reads:
            for k, v in b.w.items():
                if deps.get(k, 0) < v:
                    deps[k] = v
        for b in writes:
            for d in (b.w, b.r):
                for k, v in d.items():
                    if deps.get(k, 0) < v:
                        deps[k] = v
        for b in pw:
            for k, v in b.r.items():
                if deps.get(k, 0) < v:
                    deps[k] = v
        out = []
        for k, v in deps.items():
            if k[0] == e and k[1] == "p" and e == "pe":
                continue
            if self.seen[e].get(k, 0) >= v:
                continue
            self.seen[e][k] = v
            out.append((k, v))
        return out

    def op(self, e, fn, reads=(), writes=(), pw=()):
        waits = self._waits(e, reads, writes, pw)
        if self.tick[e] >= EPOCH:
            self.epoch[e] += 1
            self.tick[e] = 0
        key = (e, "p", self.epoch[e])
        self.tick[e] += 1
        val = self.tick[e]
        s = self.sem(key)
        wl = [(self.sem(k), v) for k, v in waits]
        self.q[e].append((wl, fn, s, 1))
        for b in reads:
            b.r[key] = val
        for b in writes:
            b.w[key] = val
        for b in pw:
            b.w[key] = val
        self.ninstr += 1

    def dma(self, e, out_ap, in_ap, reads=(), writes=(), **kw):
        i = self.dma_i[e]
        self.dma_i[e] += 1
        slot = (e, "d", i % self.n_dma_sems)
        cnt = self.dma_slots.get(slot, 0)
        waits = self._waits(e, reads, writes)
        if cnt > 0 and self.seen[e].get(slot, 0) < cnt:
            self.seen[e][slot] = cnt
            waits.append((slot, cnt))
        cnt += 16
        self.dma_slots[slot] = cnt
        s = self.sem(slot)
        wl = [(self.sem(k), v) for k, v in waits]

        def fn(eng, out_ap=out_ap, in_ap=in_ap, kw=kw):
            return eng.dma_start(out=out_ap, in_=in_ap, **kw)

        self.q[e].append((wl, fn, s, 16))
        for b in reads:
            b.r[slot] = cnt
        for b in writes:
            b.w[slot] = cnt
        self.ninstr += 1

    def barrier(self):
        targets = {}
        for e in ENGS:
            if self.tick[e] > 0:
                targets[(e, "p", self.epoch[e])] = self.tick[e]
        for slot, cnt in self.dma_slots.items():
            targets[slot] = cnt
        for e in ENGS:
            wl = []
            for k, v in targets.items():
                if k[0] == e and k[1] == "p" and e == "pe":
                    continue
                if self.seen[e].get(k, 0) >= v:
                    continue
                self.seen[e][k] = v
                wl.append((self.sem(k), v))
            if wl:
                self.q[e].append((wl, None, None, 0))

    def flush(self):
        nc = self.nc
        q = self.q
        if not any(q[e] for e in ENGS):
            return
        with nc.Block() as block:

            def run(eng, lst):
                for wl, fn, s, inc in lst:
                    for ws, v in wl:
                        eng.wait_ge(ws, v)
                    if fn is not None:
                        fn(eng).then_inc(s, inc)

            @block.tensor
            def _(eng):
                run(eng, q["pe"])

            @block.scalar
            def _(eng):
                run(eng, q["act"])

            @block.vector
            def _(eng):
                run(eng, q["dve"])

            @block.gpsimd
            def _(eng):
                run(eng, q["pool"])

            @block.sync
            def _(eng):
                run(eng, q["sp"])

        self.q = {e: [] for e in ENGS}

    def finish(self):
        self.barrier()
        self.flush()
        self.es.close()


def bc(ap, shape):
    return ap.to_broadcast(list(shape))


class Builder:
    def __init__(self, TL, TC, depth=DEPTH, debug=(), stop_after=None):
        self.TL, self.TC, self.TT = TL, TC, TL + TC
        self.depth = depth
        self.debug = set(debug)
        self.stop_after = stop_after
        assert TC % 128 == 0 and TC <= 512 and TL % 512 == 0
        self.groups = [(0, TC)] + [(TC + 512 * i, 512) for i in range(TL // 512)]
        self.NT = self.TT // 128
        self.nc = bass.Bass("TRN2", target_bir_lowering=False)
        self.P = Prog(self.nc)
        self.rr = {}

    def ext_in(self, name, shape, dt=F32):
        return self.P.dram(name, shape, dt, kind="ExternalInput")

    def scratch(self, name, shape, dt):
        kind = "ExternalOutput" if name in self.debug else None
        return self.P.dram(name, shape, dt, kind=kind)

    def rot(self, key, lst):
        i = self.rr.get(key, 0)
        self.rr[key] = i + 1
        return lst[i % len(lst)]

    def act(self, out, in_, func, reads, writes, pw=(), **kw):
        self.P.op("act", lambda e: e.activation(out=out, in_=in_, func=func, **kw), reads=reads, writes=writes, pw=pw)

    def tt(self, eng, out, in0, in1, op, reads, writes, pw=()):
        self.P.op(eng, lambda e: e.tensor_tensor(out=out, in0=in0, in1=in1, op=op), reads=reads, writes=writes, pw=pw)

    def ts(self, eng, out, in0, s1, op0, reads, writes, s2=None, op1=None, pw=()):
        if op1 is None:
            self.P.op(eng, lambda e: e.tensor_scalar(out=out, in0=in0, scalar1=s1, scalar2=None, op0=op0), reads=reads, writes=writes, pw=pw)
        else:
            self.P.op(eng, lambda e: e.tensor_scalar(out=out, in0=in0, scalar1=s1, scalar2=s2, op0=op0, op1=op1), reads=reads, writes=writes, pw=pw)

    def stt(self, out, in0, scalar, in1, op0, op1, reads, writes, pw=()):
        self.P.op("dve", lambda e: e.scalar_tensor_tensor(out=out, in0=in0, scalar=scalar, in1=in1, op0=op0, op1=op1), reads=reads, writes=writes, pw=pw)

    def mm(self, out, lhsT, rhs, start, stop, reads, writes):
        self.P.op("pe", lambda e: e.matmul(out=out, lhsT=lhsT, rhs=rhs, start=start, stop=stop), reads=reads, writes=writes)

    def tr(self, out, in_, ident, reads, writes):
        self.P.op("pe", lambda e: e.transpose(out=out, in_=in_, identity=ident), reads=reads, writes=writes)

    def copy(self, eng, out, in_, reads, writes, pw=()):
        if eng == "act":
            self.act(out, in_, AF.Copy, reads, writes, pw=pw)
        else:
            self.P.op(eng, lambda e: e.tensor_copy(out=out, in_=in_), reads=reads, writes=writes, pw=pw)

    def rsqrt(self, out, in_, scale, reads, writes):
        self.act(out, in_, AF.Sqrt, reads, writes, scale=scale, bias=self.eps_t[0:out.shape[0], 0:1])
        self.P.op("dve", lambda e: e.reciprocal(out=out, in_=out), reads=writes, writes=writes)

    def declare(self):
        P, TT, TL, TC, L = self.P, self.TT, self.TL, self.TC, self.depth
        I = self.ext_in
        self.x_in = I("x", [TL, D])
        self.ctx_in = I("ctx", [TC, D])
        self.cvec = I("cvec", [2, D])
        self.w_mod = I("w_mod", [L, D, 6 * D])
        self.b_mod = I("b_mod", [L, 6 * D])
        self.norm_g = I("norm_g", [L, 2, D])
        self.w_in = I("w_in", [L, D, DIN])
        self.b_in = I("b_in", [L, DIN])
        self.ml_gate_b = I("ml_gate_b", [L, 16])
        self.ml_norm_g = I("ml_norm_g", [L, 1024])
        self.qa_g = I("mla_qa_g", [L, 512])
        self.kva_g = I("mla_kva_g", [L, 512])
        self.w_uq = I("mla_w_uq", [L, 512, 1536])
        self.w_ukv = I("mla_w_ukv", [L, 512, 2048])
        self.qn_g = I("mla_qn_g", [L, 192])
        self.kn_g = I("mla_kn_g", [L, 192])
        self.s5_lam = I("s5_lam", [L, 2, 3, 128, 32])
        self.s5_b = I("s5_b", [L, 2, 2, 128, 8, 2, 128])
        self.s5_c = I("s5_c", [L, 2, 2, 128, 32, 32])
        self.s5_d = I("s5_d", [L, 1024])
        self.w_glu = I("s5_w_glu", [L, 1024, 1024])
        self.b_glu = I("s5_b_glu", [L, 1024])
        self.w_branch = I("w_branch", [L, 3, 1024, D])
        self.w_out = I("w_out", [L, D, D])
        self.w_ff1 = I("w_ff1", [L, D, 4 * D])
        self.w_ff2 = I("w_ff2", [L, 4 * D, D])
        self.cst = I("cst", [128, 640])
        self.rope = I("rope", [TT, 128])
        self.y = P.dram("y", [TL, D], F32, kind="ExternalOutput")

        S = self.scratch
        self.xT = [S("xT0", [D, TT], F32), S("xT1", [D, TT], F32)]
        self.qT = S("qT", [4, 128, TT], BF16)
        self.kT = S("kT", [4, 128, TT], BF16)
        self.ktok = S("ktok", [TT, 512], BF16)
        self.vtok = S("vtok", [TT, 1024], BF16)
        self.otok = S("otok", [TT, 1024], BF16)
        self.gtok = S("gtok", [TT, 16], F32)
        self.kpe = S("kpe", [TT, 64], F32)
        self.qaT = S("qaT", [512, TT], BF16)
        self.kvaT = S("kvaT", [512, TT], BF16)
        self.rsq = S("rsq", [TT, 2], F32)
        self.uT = S("uT", [1024, TT], F32)
        self.uTr = S("uTr", [1024, TT], BF16)
        self.gT = S("gT", [6144, TT], BF16)
        self.hml = S("hml", [2, TT, 1024], F32)
        self.qhT = S("qhT", [8, 192, TT], BF16)
        self.khT = S("khT", [8, 192, TT], BF16)
        self.vh = S("vh", [TT, 1024], BF16)
        self.ytok = S("ytok", [2, TT, 1024], F32)
        self.brT = S("brT", [3, 1024, TT], BF16)

        self.cst_t = P.sbuf("cst_t", [128, 640], F32)
        self.cstb = P.sbuf("cstb", [128, 640], BF16)
        self.eps_t = P.sbuf("eps_t", [128, 1], F32)
        self.modT = P.sbuf("modT", [128, 6, 16, 2], F32)
        self.A1 = P.sbuf("A1", [128, 2, 16, 2], F32)
        self.ps = [P.psum("ps%d" % i, [128, 512], F32) for i in range(8)]
        P.dma("sp", self.cst_t[:], self.cst[:], reads=[self.cst], writes=[self.cst_t])
        P.op("dve", lambda e: e.tensor_copy(out=self.cstb[:], in_=self.cst_t[:]), reads=[self.cst_t], writes=[self.cstb])
        P.op("dve", lambda e: e.memset(self.eps_t[:], EPS), writes=[self.eps_t])
        c = self.cst_t
        self.ident_f, self.J_f = c[:, 0:128], c[:, 128:256]
        self.triL_f, self.triU_f = c[0:64, 256:320], c[0:64, 320:384]
        self.ones_f = c[:, 384:512]
        cb = self.cstb
        self.ident_b, self.J_b, self.ones_b = cb[:, 0:128], cb[:, 128:256], cb[:, 384:512]

    def seq_rows(self, s0, n):
        if s0 < self.TC:
            return self.ctx_in, self.ctx_in[s0:s0 + n, :]
        return self.x_in, self.x_in[s0 - self.TC:s0 - self.TC + n, :]

    def phase_T0(self):
        P = self.P
        with P.scope():
            xt = [P.sbuf("t0x%d" % i, [128, D], F32) for i in range(2)]
            xo = [P.sbuf("t0o%d" % i, [128, KC, 128], F32) for i in range(2)]
            dst = self.xT[0].ap().rearrange("(c p) t -> p c t", p=128)
            for i in range(self.NT):
                src_t, src = self.seq_rows(i * 128, 128)
                a, o = xt[i % 2], xo[i % 2]
                P.dma("sp", a[:], src, reads=[src_t], writes=[a])
                for b4 in range(4):
                    ps = self.ps[(i * 4 + b4) % 8]
                    for j in range(4):
                        c = b4 * 4 + j
                        self.tr(ps[:, j * 128:(j + 1) * 128], a[:, c * 128:(c + 1) * 128], self.ident_f, [a, self.cst_t], [ps])
                    self.copy("act" if b4 % 2 == 0 else "dve", o[:, b4 * 4:(b4 + 1) * 4, :], ps[:].rearrange("p (j t) -> p j t", j=4), [ps], [], pw=[o])
                P.dma("sp", dst[:, :, i * 128:(i + 1) * 128], o[:], reads=[o], writes=[self.xT[0]])

    def phase_A(self, l):
        P = self.P
        with P.scope():
            cv = P.sbuf("a_cv", [128, 2, KC], F32)
            scT = P.sbuf("a_sc", [128, KC, 2], BF16)
            bm = P.sbuf("a_bm", [128, 6, KC], F32)
            ng = P.sbuf("a_ng", [128, 2, KC], F32)
            wt = [P.sbuf("a_w%d" % i, [128, KC, 512], BF16) for i in range(3)]
            P.dma("sp", cv[:], self.cvec.ap().rearrange("r (c p) -> p r c", p=128), reads=[self.cvec], writes=[cv], allow_slow_non_contiguous=True)
            P.dma("sp", bm[:], self.b_mod.ap()[l].rearrange("(i c p) -> p i c", p=128, c=KC), reads=[self.b_mod], writes=[bm], allow_slow_non_contiguous=True)
            P.dma("sp", ng[:], self.norm_g.ap()[l].rearrange("i (c p) -> p i c", p=128), reads=[self.norm_g], writes=[ng], allow_slow_non_contiguous=True)
            self.act(scT[:].rearrange("p c r -> p r c"), cv[:], AF.Silu, [cv], [scT])
            wv = self.w_mod.ap()[l].rearrange("(kc p) n -> p kc n", p=128)
            for idx in range(6):
                for jj in range(4):
                    w = self.rot("a_w", wt)
                    c0 = idx * D + jj * 512
                    P.dma("pool", w[:], wv[:, :, c0:c0 + 512], reads=[self.w_mod], writes=[w])
                    ps = self.rot("a_ps", self.ps[0:4])
                    for j in range(4):
                        for kc in range(KC):
                            self.mm(ps[:, 2 * j:2 * j + 2], w[:, kc, j * 128:(j + 1) * 128], scT[:, kc, :], kc == 0, kc == KC - 1, [w, scT], [ps])
                    self.tt("dve", self.modT[:, idx, 4 * jj:4 * jj + 4, :], ps[:, 0:8].rearrange("p (j r) -> p j r", r=2),
                            bc(bm[:, idx, 4 * jj:4 * jj + 4].unsqueeze(2), [128, 4, 2]), ALU.add, [ps, bm], [], pw=[self.modT])
            for n, idx in ((0, 1), (1, 4)):
                self.ts("dve", self.A1[:, n], self.modT[:, idx], 1.0, ALU.add, [self.modT], [], pw=[self.A1])
                self.tt("dve", self.A1[:, n], self.A1[:, n], bc(ng[:, n, :].unsqueeze(2), [128, KC, 2]), ALU.mult, [self.A1, ng], [self.A1])

    def norm_group(self, xg, hT, ng, n, col, sqb, rbc, tmp, ps):
        P = self.P
        shift = self.modT[:, 0 if n == 0 else 3]
        for c in range(KC):
            s = sqb[c % len(sqb)]
            self.act(s[:, 0:ng], xg[:, c, 0:ng], AF.Square, [xg], [s])
            self.mm(ps[:, 0:ng], self.ones_b, s[:, 0:ng], c == 0, c == KC - 1, [s, self.cstb], [ps])
        self.act(rbc[:, 0:ng], ps[:, 0:ng], AF.Sqrt, [ps, self.eps_t], [rbc], scale=1.0 / D, bias=self.eps_t[:, 0:1])
        P.op("dve", lambda e: e.reciprocal(out=rbc[:, 0:ng], in_=rbc[:, 0:ng]), reads=[rbc], writes=[rbc])
        for c in range(KC):
            t = tmp[c % len(tmp)]
            self.tt("dve", t[:, 0:ng], xg[:, c, 0:ng], rbc[:, 0:ng], ALU.mult, [xg, rbc], [t])
            self.act(hT[:, c, 0:ng], t[:, 0:ng], AF.Identity, [t, self.A1, self.modT], [], pw=[hT],
                     scale=self.A1[:, n, c, col:col + 1], bias=shift[:, c, col:col + 1])

    def phase_B(self, l):
        P, TT = self.P, self.TT
        xTl = self.xT[l]
        with P.scope():
            xg = P.sbuf("b_xg", [128, KC, 512], F32)
            hT = P.sbuf("b_hT", [128, KC, 512], BF16)
            sqb = [P.sbuf("b_sq%d" % i, [128, 512], BF16) for i in range(3)]
            tmp = [P.sbuf("b_tmp%d" % i, [128, 512], F32) for i in range(3)]
            rbc = P.sbuf("b_rbc", [128, 512], F32)
            wt = [P.sbuf("b_w%d" % i, [128, KC, 512], BF16) for i in range(3)]
            wsm = P.sbuf("b_wsm", [128, KC, 80], BF16)
            ob = [P.sbuf("b_ob%d" % i, [128, 4, 512], BF16) for i in range(2)]
            of = [P.sbuf("b_of%d" % i, [128, 4, 512], F32) for i in range(2)]
            ot = [P.sbuf("b_ot%d" % i, [128, 512], BF16) for i in range(3)]
            otf = [P.sbuf("b_otf%d" % i, [128, 512], F32) for i in range(2)]
            sqs = [P.sbuf("b_sqs%d" % i, [128, 512], BF16) for i in range(2)]
            binT = P.sbuf("b_binT", [128, 89], F32)
            bq = P.sbuf("b_bq", [128, 4], F32)
            btok = P.sbuf("b_btok", [128, 3664], F32)
            gb = P.sbuf("b_gb", [128, 16], F32)
            rs = P.sbuf("b_rs", [128, 4, 2], F32)
            gl = P.sbuf("b_gl", [128, 80], F32)
            e1 = P.sbuf("b_e1", [128, 8], F32)
            ur = P.sbuf("b_ur", [128, 4, 8, 128], BF16)
            gfm = P.sbuf("b_gfm", [128, 8], F32)
            bgs = P.sbuf("b_bgs", [128, 8], F32)
            P.dma("sp", gfm[:, 0:4], self.qa_g.ap()[l].rearrange("(c p) -> p c", p=128), reads=[self.qa_g], writes=[gfm], allow_slow_non_contiguous=True)
            P.dma("sp", gfm[:, 4:8], self.kva_g.ap()[l].rearrange("(c p) -> p c", p=128), reads=[self.kva_g], writes=[gfm], allow_slow_non_contiguous=True)
            bi = self.b_in.ap()[l]
            fm = [("q", OFF_Q, 512), ("k", OFF_K, 512), ("qa", OFF_QA, 512), ("kva", OFF_KVA, 512), ("u", OFF_U, 1024), ("bg", OFF_BG, 6144)]
            fmc = {}
            c0 = 0
            for name, off, n in fm:
                fmc[name] = c0
                P.dma("sp", binT[:, c0:c0 + n // 128], bi[off:off + n].rearrange("(c p) -> p c", p=128), reads=[self.b_in], writes=[binT], allow_slow_non_contiguous=True)
                c0 += n // 128
            self.ts("dve", bq[:], binT[:, 0:4], 128.0 ** -0.5, ALU.mult, [binT], [bq])
            self.tt("dve", bgs[:], binT[:, fmc["qa"]:fmc["qa"] + 8], gfm[:], ALU.mult, [binT, gfm], [bgs])
            tmo = {"k": 0, "v": 512, "o": 1536, "u": 2560, "g": 3584, "kpe": 3600}
            for name, off, n in (("k", OFF_K, 512), ("v", OFF_V, 1024), ("o", OFF_O, 1024), ("u", OFF_U, 1024), ("g", OFF_G, 16), ("kpe", OFF_KPE, 64)):
                P.dma("sp", btok[:, tmo[name]:tmo[name] + n], bass.AP(self.b_in.h, l * DIN + off, [[0, 128], [1, n]]), reads=[self.b_in], writes=[btok])
            P.dma("sp", gb[:], bass.AP(self.ml_gate_b.h, l * 16, [[0, 128], [1, 16]]), reads=[self.ml_gate_b], writes=[gb])
            self.tt("dve", btok[:, 3584:3600], btok[:, 3584:3600], gb[:], ALU.add, [btok, gb], [btok])
            wv = self.w_in.ap()[l].rearrange("(kc p) n -> p kc n", p=128)
            P.dma("pool", wsm[:, :, 0:16], wv[:, :, OFF_G:OFF_G + 16], reads=[self.w_in], writes=[wsm])
            P.dma("pool", wsm[:, :, 16:80], wv[:, :, OFF_KPE:OFF_KPE + 64], reads=[self.w_in], writes=[wsm])

            for (s0, ng) in self.groups:
                col = 1 if s0 < self.TC else 0
                ntt = ng // 128
                P.dma("sp", xg[:, :, 0:ng], xTl.ap().rearrange("(c p) t -> p c t", p=128)[:, :, s0:s0 + ng], reads=[xTl], writes=[xg])
                self.norm_group(xg, hT, ng, 0, col, sqb, rbc, tmp, self.ps[7])

                def fm_tile(w, wc, name, ci, dst, dt_f32=False):
                    o = self.rot("b_of", of) if dt_f32 else self.rot("b_ob", ob)
                    for j in range(4):
                        ps = self.rot("b_ps", self.ps[0:5])
                        for kc in range(KC):
                            self.mm(ps[:, 0:ng], w[:, kc, wc + j * 128:wc + (j + 1) * 128], hT[:, kc, 0:ng], kc == 0, kc == KC - 1, [w, hT], [ps])
                        bcol = binT[:, fmc[name] + ci + j:fmc[name] + ci + j + 1]
                        if name == "q":
                            self.act(o[:, j, 0:ng], ps[:, 0:ng], AF.Identity, [ps, bq], [], pw=[o], scale=128.0 ** -0.5, bias=bq[:, ci + j:ci + j + 1])
                        elif name == "bg":
                            self.act(o[:, j, 0:ng], ps[:, 0:ng], AF.Sigmoid, [ps, binT], [], pw=[o], bias=bcol)
                        elif name in ("qa", "kva"):
                            which = 0 if name == "qa" else 1
                            self.act(o[:, j, 0:ng], ps[:, 0:ng], AF.Identity, [ps, gfm, bgs], [], pw=[o],
                                     scale=gfm[:, 4 * which + j:4 * which + j + 1], bias=bgs[:, 4 * which + j:4 * which + j + 1])
                            sq = self.rot("b_sqs", sqs)
                            self.act(sq[:, 0:ng], ps[:, 0:ng], AF.Square, [ps, binT], [sq], bias=bcol)
                            which = 0 if name == "qa" else 1
                            for t in range(ntt):
                                self.mm(self.ps[6][:, 8 * which + t:8 * which + t + 1], sq[:, t * 128:(t + 1) * 128], self.ones_b[:, 0:1],
                                        (which == 0 and j == 0 and t == 0), j == 3, [sq, self.cstb], [self.ps[6]])
                        else:
                            eng = "act" if j % 2 == 0 else "dve"
                            if eng == "act":
                                self.act(o[:, j, 0:ng], ps[:, 0:ng], AF.Identity, [ps, binT], [], pw=[o], bias=bcol)
                            else:
                                self.ts("dve", o[:, j, 0:ng], ps[:, 0:ng], bcol, ALU.add, [ps, binT], [], pw=[o])
                    P.dma("sp", dst, o[:, :, 0:ng], reads=[o], writes=[dst_t[0]])

                def tm_tile(w, wc, n, name, bo):
                    for t in range(ntt):
                        ps = self.rot("b_ps", self.ps[0:5])
                        for kc in range(KC):
                            self.mm(ps[:, 0:n], hT[:, kc, t * 128:(t + 1) * 128], w[:, kc, wc:wc + n], kc == 0, kc == KC - 1, [w, hT], [ps])
                        r0 = s0 + t * 128
                        if name in ("k", "v"):
                            o = self.rot("b_ot", ot)
                            self.tt("dve", o[:, 0:n], ps[:, 0:n], btok[:, bo:bo + n], ALU.add, [ps, btok], [o])
                            dstT = self.ktok if name == "k" else self.vtok
                            dcol = 0 if name == "k" else tm_c[0]
                            P.dma("sp", dstT.ap()[r0:r0 + 128, dcol:dcol + n], o[:, 0:n], reads=[o], writes=[dstT])
                        elif name == "o":
                            f = self.rot("b_otf", otf)
                            o = self.rot("b_ot", ot)
                            self.tt("dve", f[:, 0:n], ps[:, 0:n], btok[:, bo:bo + n], ALU.add, [ps, btok], [f])
                            self.act(o[:, 0:n], f[:, 0:n], AF.Sigmoid, [f], [o])
                            P.dma("sp", self.otok.ap()[r0:r0 + 128, tm_c[0]:tm_c[0] + n], o[:, 0:n], reads=[o], writes=[self.otok])
                        elif name == "u":
                            o = self.rot("b_ot", ot)
                            self.tt("dve", o[:, 0:n], ps[:, 0:n], btok[:, bo:bo + n], ALU.add, [ps, btok], [o])
                            pst = self.ps[5]
                            psb = pst[:].bitcast(BF16)
                            for j in range(4):
                                self.tr(psb[:, j * 128:(j + 1) * 128], o[:, j * 128:(j + 1) * 128], self.J_b, [o, self.cstb], [pst])
                            ch0 = tm_c[0] // 128
                            self.copy("act", ur[:, t, ch0:ch0 + 4, :], psb[:, 0:512].rearrange("p (j t) -> p j t", j=4), [pst], [], pw=[ur])
                            if ch0 == 4:
                                i0 = self.mirror(r0)
                                P.dma("sp", self.uTr.ap().rearrange("(c p) t -> p c t", p=128)[:, :, i0:i0 + 128], ur[:, t], reads=[ur], writes=[self.uTr])
                        elif name == "gk":
                            self.tt("dve", gl[:], ps[:, 0:80], btok[:, 3584:3664], ALU.add, [ps, btok], [gl])
                            g4 = gl[:, 0:16].rearrange("p (d t h) -> p d t h", d=2, t=2)
                            self.act(e1[:].rearrange("p (d h) -> p d h", d=2), g4[:, :, 1, :], AF.Exp, [gl], [e1], scale=-1.0)
                            self.act(e1[:], e1[:], AF.Ln, [e1], [e1], bias=1.0)
                            self.ts("dve", g4[:, :, 1, :], e1[:].rearrange("p (d h) -> p d h", d=2), -1.0, ALU.mult, [e1], [gl])
                            P.dma("sp", self.gtok.ap()[r0:r0 + 128, :], gl[:, 0:16], reads=[gl], writes=[self.gtok])
                            P.dma("sp", self.kpe.ap()[r0:r0 + 128, :], gl[:, 16:80], reads=[gl], writes=[self.kpe])

                def load_w(off):
                    w = self.rot("b_w", wt)
                    P.dma("pool", w[:], wv[:, :, off:off + 512], reads=[self.w_in], writes=[w])
                    return w

                def fmdst(Tt, row0, f32=False):
                    return Tt.ap().rearrange("(j p) t -> p j t", p=128)[:, row0 // 128:row0 // 128 + 4, s0:s0 + ng]

                dst_t = [None]
                tm_c = [0]
                w = load_w(OFF_Q)
                dst_t[0] = self.qT
                fm_tile(w, 0, "q", 0, self.qT.ap().rearrange("h p t -> p h t")[:, :, s0:s0 + ng])
                w = load_w(OFF_K)
                dst_t[0] = self.kT
                fm_tile(w, 0, "k", 0, self.kT.ap().rearrange("h p t -> p h t")[:, :, s0:s0 + ng])
                tm_tile(w, 0, 512, "k", tmo["k"])
                for h2 in range(2):
                    w = load_w(OFF_V + 512 * h2)
                    tm_c[0] = 512 * h2
                    tm_tile(w, 0, 512, "v", tmo["v"] + 512 * h2)
                for h2 in range(2):
                    w = load_w(OFF_O + 512 * h2)
                    tm_c[0] = 512 * h2
                    tm_tile(w, 0, 512, "o", tmo["o"] + 512 * h2)
                tm_tile(wsm, 0, 80, "gk", 0)
                for name, off, Tt in (("qa", OFF_QA, self.qaT), ("kva", OFF_KVA, self.kvaT)):
                    w = load_w(off)
                    dst_t[0] = Tt
                    fm_tile(w, 0, name, 0, fmdst(Tt, 0))
                for which in range(2):
                    self.act(rs[:, 0:ntt, which], self.ps[6][:, 8 * which:8 * which + ntt], AF.Sqrt, [self.ps[6], self.eps_t], [], pw=[rs], scale=1.0 / 512, bias=self.eps_t[:, 0:1])
                P.op("dve", lambda e: e.reciprocal(out=rs[:, 0:ntt, :], in_=rs[:, 0:ntt, :]), reads=[rs], writes=[rs])
                P.dma("sp", self.rsq.ap()[s0:s0 + ng, :].rearrange("(t p) c -> p t c", p=128), rs[:, 0:ntt, :], reads=[rs], writes=[self.rsq])
                for h2 in range(2):
                    w = load_w(OFF_U + 512 * h2)
                    dst_t[0] = self.uT
                    fm_tile(w, 0, "u", 4 * h2, fmdst(self.uT, 512 * h2), dt_f32=True)
                    tm_c[0] = 512 * h2
                    tm_tile(w, 0, 512, "u", tmo["u"] + 512 * h2)
                for b12 in range(12):
                    w = load_w(OFF_BG + 512 * b12)
                    dst_t[0] = self.gT
                    fm_tile(w, 0, "bg", 4 * b12, fmdst(self.gT, 512 * b12))

    def mirror(self, r0, n=128):
        TC, TL = self.TC, self.TL
        if r0 < TC:
            return TC - n - r0
        return TC + (TL - n - (r0 - TC))


def make_consts(TL, TC):
    cst = np.zeros((128, 640), np.float32)
    cst[:, 0:128] = np.eye(128)
    cst[:, 128:256] = np.eye(128)[::-1]
    k = np.arange(64)
    cst[0:64, 256:320] = (k[:, None] <= k[None, :])
    cst[0:64, 320:384] = (k[:, None] >= k[None, :])
    cst[:, 384:512] = 1.0
    TT = TL + TC
    rope = np.zeros((TT, 128), np.float32)
    rope[:, 0:64] = 1.0
    t = np.arange(TL)
    row = (t // 64).astype(np.float32)
    col = (t % 64).astype(np.float32)
    inv = (np.float32(10000.0) ** (-np.arange(16, dtype=np.float32) / np.float32(16))).astype(np.float32)
    ar = (row[:, None] * inv).astype(np.float32)
    ac = (col[:, None] * inv).astype(np.float32)
    rope[TC:, 0:64] = np.concatenate([np.cos(ar), np.cos(ar), np.cos(ac), np.cos(ac)], axis=1)
    rope[TC:, 64:128] = np.concatenate([-np.sin(ar), np.sin(ar), -np.sin(ac), np.sin(ac)], axis=1)
    return cst, rope


def prep_shared(inp, L):
    f = lambda a: np.ascontiguousarray(np.asarray(a, dtype=np.float32))
    sh = {}
    for k in ("w_mod", "b_mod", "norm_g", "w_in", "b_in", "mla_qa_g", "mla_kva_g", "mla_w_uq", "mla_w_ukv", "mla_qn_g", "mla_kn_g",
              "s5_d", "s5_w_glu", "s5_b_glu", "w_branch", "w_out", "w_ff1", "w_ff2"):
        sh[k] = f(inp[k])[:L]
    sh["ml_gate_b"] = f(inp["ml_gate_b"])[:L].reshape(L, 16)
    sh["ml_norm_g"] = f(inp["ml_norm_g"])[:L].reshape(L, 1024)
    lam = np.zeros((L, 2, 3, 128, 32), np.float32)
    sb = np.zeros((L, 2, 2, 128, 8, 2, 128), np.float32)
    sc = np.zeros((L, 2, 2, 128, 32, 32), np.float32)
    a_re, a_im, ldt = f(inp["s5_a_re"]), f(inp["s5_a_im"]), f(inp["s5_log_dt"])
    bs = (f(inp["s5_b_re"]), f(inp["s5_b_im"]))
    cs = (f(inp["s5_c_re"]), f(inp["s5_c_im"]))

    def lay(a):
        return a.reshape(32, 2, 64).transpose(1, 2, 0).reshape(128, 32)

    for l in range(L):
        for d in range(2):
            lam[l, d, 0] = lay(a_re[l, d])
            lam[l, d, 1] = lay(a_im[l, d])
            lam[l, d, 2] = lay(np.repeat(ldt[l, d][:, None], 64, axis=1))
            for ri in range(2):
                b = bs[ri][l, d]
                c = cs[ri][l, d]
                for g in range(64):
                    q, pp, g2 = g // 8, (g % 8) // 2, g % 2
                    sb[l, d, ri, pp * 32 + g2 * 16:pp * 32 + g2 * 16 + 16, q, 0, g2 * 64:(g2 + 1) * 64] = b[g].T
                    if pp == 3:
                        sb[l, d, ri, pp * 32 + g2 * 16:pp * 32 + g2 * 16 + 16, q, 1, g2 * 64:(g2 + 1) * 64] = b[g].T
                    sc[l, d, ri, g2 * 64:(g2 + 1) * 64, g // 2, g2 * 16:(g2 + 1) * 16] = c[g].T
    sh["s5_lam"], sh["s5_b"], sh["s5_c"] = lam, sb, sc
    return sh


def prep_core(inp, b, TL, TC):
    f = lambda a: np.ascontiguousarray(np.asarray(a, dtype=np.float32))
    return {
        "x": f(inp["x"][b]),
        "ctx": f(inp["ctx"][b]),
        "cvec": f(np.stack([np.asarray(inp["c"][b]), np.asarray(inp["c_ctx"])])),
    }


PHASES = ("A", "B", "C1", "C2", "C3", "DE")


def build_program(TL, TC, depth=DEPTH, debug=(), stop_after=None):
    B = Builder(TL, TC, depth, debug, stop_after)
    B.declare()
    B.phase_T0()
    done = False
    for l in range(depth):
        for name in PHASES:
            getattr(B, "phase_" + name)(l)
            if stop_after == (name, l):
                done = True
                break
        if done:
            break
    B.P.finish()
    return B


def phase_C1(self, l):
    P, TC, TL, TT = self.P, self.TC, self.TL, self.TT
    with P.scope():
        QTg = [P.sbuf("m_q%d" % d, [128, 4, 512], BF16) for d in range(2)]
        KTg = [P.sbuf("m_k%d" % d, [128, 4, 512], BF16) for d in range(2)]
        Ktg = [P.sbuf("m_kt%d" % d, [64, 8, 512], BF16) for d in range(2)]
        Vg = [P.sbuf("m_v%d" % d, [64, 8, 1024], BF16) for d in range(2)]
        Gg = [P.sbuf("m_g%d" % d, [64, 8, 16], F32) for d in range(2)]
        Cf = [P.sbuf("m_cf%d" % d, [128, 4, 257], F32) for d in range(2)]
        Cb = [P.sbuf("m_cb%d" % d, [128, 4, 257], BF16) for d in range(2)]
        sm = [[P.sbuf("m_s%d_%d" % (d, i), [128, 40], F32) for i in range(2)] for d in range(2)]
        smb = [[P.sbuf("m_sb%d_%d" % (d, i), [64, 8], BF16) for i in range(2)] for d in range(2)]
        Vs = [[P.sbuf("m_vs%d_%d" % (d, i), [64, 4, 256], BF16) for i in range(2)] for d in range(2)]
        VF = [[P.sbuf("m_vf%d_%d" % (d, i), [64, 4, 256], BF16) for i in range(2)] for d in range(2)]
        Sm = [[P.sbuf("m_sm%d_%d" % (d, i), [64, 4, 64], BF16) for i in range(2)] for d in range(2)]
        hst = [[P.sbuf("m_h%d_%d" % (d, i), [64, 4, 256], F32) for i in range(2)] for d in range(2)]
        for d in range(2):
            P.op("dve", lambda e, d=d: e.memset(Cf[d][:], 0.0), writes=[Cf[d]])
            P.op("pool", lambda e, d=d: e.memset(Cb[d][:], 0.0), writes=[Cb[d]])
        order = []
        for d in range(2):
            o = []
            glist = self.groups if d == 0 else [self.groups[0]] + self.groups[1:][::-1]
            for gi, (s0, ng) in enumerate(glist):
                cis = list(range(ng // 64))
                if d == 1:
                    cis = cis[::-1]
                for ci in cis:
                    o.append((s0, ng, ci))
            order.append(o)
        cur = [None, None]
        psm = [self.ps[0], self.ps[5]]
        pacc = [(self.ps[1], self.ps[2]), (self.ps[6], self.ps[7])]
        pdc = (self.ps[3], self.ps[4])
        for step in range(len(order[0])):
            for d in range(2):
                s0, ng, ci = order[d][step]
                nch = ng // 64
                if cur[d] != s0:
                    cur[d] = s0
                    P.dma("sp", QTg[d][:, :, 0:ng], self.qT.ap().rearrange("h p t -> p h t")[:, :, s0:s0 + ng], reads=[self.qT], writes=[QTg[d]])
                    P.dma("sp", KTg[d][:, :, 0:ng], self.kT.ap().rearrange("h p t -> p h t")[:, :, s0:s0 + ng], reads=[self.kT], writes=[KTg[d]])
                    P.dma("sp", Ktg[d][:, 0:nch, :], self.ktok.ap()[s0:s0 + ng, :].rearrange("(c p) n -> p c n", p=64), reads=[self.ktok], writes=[Ktg[d]])
                    P.dma("sp", Vg[d][:, 0:nch, :], self.vtok.ap()[s0:s0 + ng, :].rearrange("(c p) n -> p c n", p=64), reads=[self.vtok], writes=[Vg[d]])
                    P.dma("sp", Gg[d][:, 0:nch, :], self.gtok.ap()[s0:s0 + ng, :].rearrange("(c p) n -> p c n", p=64), reads=[self.gtok], writes=[Gg[d]])
                c0 = ci * 64
                t0 = s0 + c0
                par = step % 2
                S_, SB_, Vs_, VF_, Sm_, H_ = sm[d][par], smb[d][par], Vs[d][par], VF[d][par], Sm[d][par], hst[d][par]
                pA, (pa0, pa1), (pd0, pd1) = psm[d], pacc[d], pdc
                tri = self.triL_f if d == 0 else self.triU_f
                li = Gg[d][:, ci, d * 8:d * 8 + 4]
                lf = Gg[d][:, ci, d * 8 + 4:d * 8 + 8]
                cst = self.cst_t
                self.mm(pA[0:64, 0:4], tri, lf, True, True, [cst, Gg[d]], [pA])
                self.mm(pA[:, 8:12], self.ones_f[0:64, :], lf, True, True, [cst, Gg[d]], [pA])
                dif, esc, ecn, eF, escF, absd, rden = (S_[0:64, 0:4], S_[0:64, 4:8], S_[0:64, 8:12], S_[:, 12:16], S_[0:64, 16:20], S_[0:64, 20:24], S_[0:64, 24:28])
                self.tt("dve", dif, li, pA[0:64, 0:4], ALU.subtract, [Gg[d], pA], [S_])
                self.act(esc, dif, AF.Exp, [S_], [S_])
                self.act(ecn, pA[0:64, 0:4], AF.Exp, [pA], [S_], scale=-1.0)
                self.act(eF, pA[:, 8:12], AF.Exp, [pA], [S_])
                self.tt("dve", escF, esc, eF[0:64, :], ALU.mult, [S_], [S_])
                self.copy("act", SB_[:, 0:4], esc, [S_], [SB_])
                self.copy("act", SB_[:, 4:8], escF, [S_], [SB_])
                vv = Vg[d][:, ci, :].rearrange("p (h v) -> p h v", h=4)
                self.tt("dve", Vs_[:], vv, bc(esc.unsqueeze(2), [64, 4, 256]), ALU.mult, [Vg[d], S_], [Vs_])
                self.tt("pool", VF_[:], vv, bc(escF.unsqueeze(2), [64, 4, 256]), ALU.mult, [Vg[d], S_], [VF_])
                for h in range(4):
                    self.mm(pA[0:64, 256 + h * 64:256 + (h + 1) * 64], KTg[d][:, h, c0:c0 + 64], QTg[d][:, h, c0:c0 + 64], True, True, [KTg[d], QTg[d]], [pA])
                self.tt("dve", Sm_[:], pA[0:64, 256:512].rearrange("p (h t) -> p h t", h=4), bc(tri.unsqueeze(1), [64, 4, 64]), ALU.mult, [pA, cst], [Sm_])
                for h in range(4):
                    bank = pa0 if h < 2 else pa1
                    off = (h % 2) * 256
                    self.mm(bank[0:64, off:off + 256], QTg[d][:, h, c0:c0 + 64], Cb[d][:, h, 0:256], True, False, [QTg[d], Cb[d]], [bank])
                    self.mm(bank[0:64, off:off + 256], Sm_[:, h, :], Vs_[:, h, :], False, True, [Sm_, Vs_], [bank])
                for h in range(4):
                    self.mm(pA[0:64, 16 + h:17 + h], QTg[d][:, h, c0:c0 + 64], Cb[d][:, h, 256:257], True, False, [QTg[d], Cb[d]], [pA])
                    self.mm(pA[0:64, 16 + h:17 + h], Sm_[:, h, :], SB_[:, h:h + 1], False, True, [Sm_, SB_], [pA])
                for h in range(4):
                    bank = pd0 if h < 2 else pd1
                    off = (h % 2) * 256
                    self.mm(bank[:, off:off + 256], Ktg[d][:, ci, h * 128:(h + 1) * 128], VF_[:, h, :], True, True, [Ktg[d], VF_], [bank])
                    self.mm(pA[:, 24 + h:25 + h], Ktg[d][:, ci, h * 128:(h + 1) * 128], SB_[:, 4 + h:5 + h], True, True, [Ktg[d], SB_], [pA])
                self.act(absd, pA[0:64, 16:20], AF.Abs, [pA], [S_])
                self.tt("dve", rden, absd, ecn, ALU.max, [S_], [S_])
                P.op("dve", lambda e, rden=rden: e.reciprocal(out=rden, in_=rden), reads=[S_], writes=[S_])
                for b2, bank in enumerate((pa0, pa1)):
                    self.tt("dve", H_[:, 2 * b2:2 * b2 + 2, :], bank[0:64, :].rearrange("p (h v) -> p h v", h=2),
                            bc(rden[:, 2 * b2:2 * b2 + 2].unsqueeze(2), [64, 2, 256]), ALU.mult, [bank, S_], [], pw=[H_])
                P.dma("sp", self.hml.ap()[d, t0:t0 + 64, :], H_[:].rearrange("p h v -> p (h v)"), reads=[H_], writes=[self.hml])
                for h in range(4):
                    bank = pd0 if h < 2 else pd1
                    off = (h % 2) * 256
                    self.stt(Cf[d][:, h, 0:256], Cf[d][:, h, 0:256], eF[:, h:h + 1], bank[:, off:off + 256], ALU.mult, ALU.add, [Cf[d], S_, bank], [Cf[d]])
                    self.stt(Cf[d][:, h, 256:257], Cf[d][:, h, 256:257], eF[:, h:h + 1], pA[:, 24 + h:25 + h], ALU.mult, ALU.add, [Cf[d], S_, pA], [Cf[d]])
                self.copy("act", Cb[d][:], Cf[d][:], [Cf[d]], [Cb[d]])

    with P.scope():
        gnb = P.sbuf("mo_gn", [128, 1024], F32)
        P.dma("sp", gnb[:], bass.AP(self.ml_norm_g.h, l * 1024, [[0, 128], [1, 1024]]), reads=[self.ml_norm_g], writes=[gnb])
        h0 = [P.sbuf("mo_h0_%d" % i, [128, 1024], F32) for i in range(2)]
        h1 = [P.sbuf("mo_h1_%d" % i, [128, 1024], F32) for i in range(2)]
        og = [P.sbuf("mo_og_%d" % i, [128, 1024], BF16) for i in range(2)]
        sq = P.sbuf("mo_sq", [128, 1024], F32)
        ss = [P.sbuf("mo_ss%d" % i, [128, 4], F32) for i in range(2)]
        ab = [P.sbuf("mo_ab%d" % i, [128, 1024], BF16) for i in range(2)]
        aT = [P.sbuf("mo_aT%d" % i, [128, 8, 512], BF16) for i in range(2)]
        for gi, (s0, ng) in enumerate(self.groups):
            A_ = aT[gi % 2]
            for t in range(ng // 128):
                r0 = s0 + t * 128
                i = self.rr.get("m_o", 0)
                self.rr["m_o"] = i + 1
                a0, a1, o_, s_, b_ = h0[i % 2], h1[i % 2], og[i % 2], ss[i % 2], ab[i % 2]
                P.dma("sp", a0[:], self.hml.ap()[0, r0:r0 + 128, :], reads=[self.hml], writes=[a0])
                P.dma("sp", a1[:], self.hml.ap()[1, r0:r0 + 128, :], reads=[self.hml], writes=[a1])
                P.dma("sp", o_[:], self.otok.ap()[r0:r0 + 128, :], reads=[self.otok], writes=[o_])
                self.tt("dve", a0[:], a0[:], a1[:], ALU.add, [a0, a1], [a0])
                self.act(sq[:], a0[:], AF.Square, [a0], [sq])
                P.op("dve", lambda e, s_=s_: e.tensor_reduce(out=s_[:], in_=sq[:].rearrange("p (h v) -> p h v", h=4), axis=AX.X, op=ALU.add), reads=[sq], writes=[s_])
                self.act(s_[:], s_[:], AF.Sqrt, [s_, self.eps_t], [s_], scale=1.0 / 256, bias=self.eps_t[:, 0:1])
                P.op("dve", lambda e, s_=s_: e.reciprocal(out=s_[:], in_=s_[:]), reads=[s_], writes=[s_])
                a3 = a0[:].rearrange("p (h v) -> p h v", h=4)
                self.tt("dve", a3, a3, bc(s_[:].unsqueeze(2), [128, 4, 256]), ALU.mult, [a0, s_], [a0])
                self.tt("pool", a0[:], a0[:], gnb[:], ALU.mult, [a0, gnb], [a0])
                self.tt("dve", b_[:], a0[:], o_[:], ALU.mult, [a0, o_], [b_])
                pst = self.rot("m_pst", self.ps[0:2])
                psb = pst[:].bitcast(BF16)
                for j in range(8):
                    self.tr(psb[:, j * 128:(j + 1) * 128], b_[:, j * 128:(j + 1) * 128], self.ident_b, [b_, self.cstb], [pst])
                self.copy("act", A_[:, :, t * 128:(t + 1) * 128], psb[:, 0:1024].rearrange("p (j t) -> p j t", j=8), [pst], [], pw=[A_])
            P.dma("sp", self.brT.ap()[0].rearrange("(c p) t -> p c t", p=128)[:, :, s0:s0 + ng], A_[:, :, 0:ng], reads=[A_], writes=[self.brT])


Builder.phase_C1 = phase_C1


def rope_ops(self, eng, src, dst, tmp1, tmp2, rt, nh, reads, writes):
    C = bc(rt[:, 0:64].unsqueeze(1), [128, nh, 64])
    S5 = rt[:, 64:128].rearrange("p (b f i) -> p b f i", b=2, f=2)
    s5 = src.rearrange("p h (b f i) -> p h b f i", b=2, f=2)
    t5 = tmp2.rearrange("p h (b f i) -> p h b f i", b=2, f=2)
    self.tt(eng, tmp1, src, C, ALU.mult, reads, writes)
    for f in range(2):
        self.tt(eng, t5[:, :, :, f, :], s5[:, :, :, 1 - f, :], bc(S5[:, :, f, :].unsqueeze(1), [128, nh, 2, 16]), ALU.mult, reads, writes)
    self.tt(eng, dst, tmp1, tmp2, ALU.add, reads, writes)


def phase_C2(self, l):
    P, TC, TL, TT, NT = self.P, self.TC, self.TL, self.TT, self.NT
    with P.scope():
        wq = P.sbuf("c_wq", [128, 4, 1536], BF16)
        wkv = P.sbuf("c_wkv", [128, 4, 2048], BF16)
        gq = P.sbuf("c_gq", [128, 192], F32)
        gk = P.sbuf("c_gk", [128, 192], F32)
        P.dma("pool", wq[:], self.w_uq.ap()[l].rearrange("(kc p) n -> p kc n", p=128), reads=[self.w_uq], writes=[wq])
        P.dma("pool", wkv[:], self.w_ukv.ap()[l].rearrange("(kc p) n -> p kc n", p=128), reads=[self.w_ukv], writes=[wkv])
        P.dma("sp", gq[:], bass.AP(self.qn_g.h, l * 192, [[0, 128], [1, 192]]), reads=[self.qn_g], writes=[gq])
        P.dma("sp", gk[:], bass.AP(self.kn_g.h, l * 192, [[0, 128], [1, 192]]), reads=[self.kn_g], writes=[gk])
        qaTg = P.sbuf("c_qa", [128, 4, 512], BF16)
        kvaTg = P.sbuf("c_kva", [128, 4, 512], BF16)
        rsg = P.sbuf("c_rs", [128, 4, 2], F32)
        rpt = [P.sbuf("c_rp%d" % i, [128, 128], F32) for i in range(2)]
        kpt = [P.sbuf("c_kp%d" % i, [128, 64], F32) for i in range(2)]
        qf = P.sbuf("c_qf", [128, 2048], F32)
        sqv = P.sbuf("c_sq", [128, 2048], F32)
        fin = [P.sbuf("c_fin%d" % i, [128, 8, 192], BF16) for i in range(2)]
        vfin = [P.sbuf("c_vf%d" % i, [128, 8, 128], BF16) for i in range(2)]
        rp = P.sbuf("c_rpp", [128, 8, 64], F32)
        t1 = P.sbuf("c_t1", [128, 8, 64], F32)
        t2 = P.sbuf("c_t2", [128, 8, 64], F32)
        st = [P.sbuf("c_st%d" % i, [128, 12], F32) for i in range(2)]
        kk = P.sbuf("c_kk", [128, 4, 64], F32)
        stn = [P.sbuf("c_stn%d" % i, [128, 8, 512], BF16) for i in range(2)]
        str_ = [P.sbuf("c_str%d" % i, [64, 8, 512], BF16) for i in range(2)]
        for (s0, ng) in self.groups:
            ntt = ng // 128
            P.dma("sp", qaTg[:, :, 0:ng], self.qaT.ap().rearrange("(c p) t -> p c t", p=128)[:, :, s0:s0 + ng], reads=[self.qaT], writes=[qaTg])
            P.dma("sp", kvaTg[:, :, 0:ng], self.kvaT.ap().rearrange("(c p) t -> p c t", p=128)[:, :, s0:s0 + ng], reads=[self.kvaT], writes=[kvaTg])
            P.dma("sp", rsg[:, 0:ntt, :], self.rsq.ap()[s0:s0 + ng, :].rearrange("(t p) c -> p t c", p=128), reads=[self.rsq], writes=[rsg])
            qn_, qr_ = stn[0], str_[0]
            kn_, kr_ = stn[1], str_[1]
            for t in range(ntt):
                r0 = s0 + t * 128
                i = self.rr.get("c_i", 0)
                self.rr["c_i"] = i + 1
                rt, kp, S_ = rpt[i % 2], kpt[i % 2], st[i % 2]
                P.dma("sp", rt[:], self.rope.ap()[r0:r0 + 128, :], reads=[self.rope], writes=[rt])
                P.dma("sp", kp[:], self.kpe.ap()[r0:r0 + 128, :], reads=[self.kpe], writes=[kp])
                for cg in range(3):
                    ps = self.rot("c_ps", self.ps[0:6])
                    for kc in range(4):
                        self.mm(ps[:, 0:512], qaTg[:, kc, t * 128:(t + 1) * 128], wq[:, kc, cg * 512:(cg + 1) * 512], kc == 0, kc == 3, [qaTg, wq], [ps])
                    self.act(qf[:, cg * 512:(cg + 1) * 512], ps[:, 0:512], AF.Identity, [ps, rsg], [], pw=[qf], scale=rsg[:, t, 0:1])
                    self.act(sqv[:, cg * 512:(cg + 1) * 512], ps[:, 0:512], AF.Square, [ps, rsg], [], pw=[sqv], scale=rsg[:, t, 0:1])
                q3 = qf[:, 0:1536].rearrange("p (h e) -> p h e", h=8)
                s3 = sqv[:, 0:1536].rearrange("p (h e) -> p h e", h=8)
                rq = S_[:, 0:8]
                P.op("dve", lambda e, rq=rq, s3=s3: e.tensor_reduce(out=rq, in_=s3, axis=AX.X, op=ALU.add), reads=[sqv], writes=[S_])
                self.act(rq, rq, AF.Sqrt, [S_, self.eps_t], [S_], scale=1.0 / 192, bias=self.eps_t[:, 0:1])
                P.op("dve", lambda e, rq=rq: e.reciprocal(out=rq, in_=rq), reads=[S_], writes=[S_])
                self.tt("dve", s3, q3, bc(rq.unsqueeze(2), [128, 8, 192]), ALU.mult, [qf, S_], [sqv])
                F_ = fin[0]
                self.tt("dve", F_[:, :, 0:128], s3[:, :, 0:128], bc(gq[:, 0:128].unsqueeze(1), [128, 8, 128]), ALU.mult, [sqv, gq], [F_])
                self.tt("pool", rp[:], s3[:, :, 128:192], bc(gq[:, 128:192].unsqueeze(1), [128, 8, 64]), ALU.mult, [sqv, gq], [rp])
                rope_ops(self, "dve", rp[:], F_[:, :, 128:192], t1[:], t2[:], rt, 8, [rp, rt, t1, t2], [t1, t2, F_])
                pn, pr = self.ps[6], self.ps[7]
                pnb, prb = pn[:].bitcast(BF16), pr[:].bitcast(BF16)
                for h in range(8):
                    self.tr(pnb[:, h * 128:(h + 1) * 128], F_[:, h, 0:128], self.ident_b, [F_, self.cstb], [pn])
                    self.tr(prb[0:64, h * 128:(h + 1) * 128], F_[:, h, 128:192], self.ident_b, [F_, self.cstb], [pr])
                self.copy("act", qn_[:, :, t * 128:(t + 1) * 128], pnb[:, 0:1024].rearrange("p (h t) -> p h t", h=8), [pn], [], pw=[qn_])
                self.copy("act", qr_[:, :, t * 128:(t + 1) * 128], prb[0:64, 0:1024].rearrange("p (h t) -> p h t", h=8), [pr], [], pw=[qr_])
                for cg in range(4):
                    ps = self.rot("c_ps", self.ps[0:6])
                    for kc in range(4):
                        self.mm(ps[:, 0:512], kvaTg[:, kc, t * 128:(t + 1) * 128], wkv[:, kc, cg * 512:(cg + 1) * 512], kc == 0, kc == 3, [kvaTg, wkv], [ps])
                    self.act(qf[:, cg * 512:(cg + 1) * 512], ps[:, 0:512], AF.Identity, [ps, rsg], [], pw=[qf], scale=rsg[:, t, 1:2])
                k3 = qf[:].rearrange("p (h e) -> p h e", h=8)
                s3k = sqv[:, 0:1024].rearrange("p (h e) -> p h e", h=8)
                self.act(s3k, k3[:, :, 0:128], AF.Square, [qf], [sqv])
                rk = S_[:, 0:8]
                sp_ = S_[:, 8:9]
                P.op("dve", lambda e, rk=rk, s3k=s3k: e.tensor_reduce(out=rk, in_=s3k, axis=AX.X, op=ALU.add), reads=[sqv], writes=[S_])
                self.act(kk[:, 0, :], kp[:], AF.Square, [kp], [kk, S_], accum_out=sp_)
                self.ts("dve", rk, rk, sp_, ALU.add, [S_], [S_])
                self.act(rk, rk, AF.Sqrt, [S_, self.eps_t], [S_], scale=1.0 / 192, bias=self.eps_t[:, 0:1])
                P.op("dve", lambda e, rk=rk: e.reciprocal(out=rk, in_=rk), reads=[S_], writes=[S_])
                Fk = fin[1]
                self.tt("dve", s3k, k3[:, :, 0:128], bc(rk.unsqueeze(2), [128, 8, 128]), ALU.mult, [qf, S_], [sqv])
                self.tt("dve", Fk[:, :, 0:128], s3k, bc(gk[:, 0:128].unsqueeze(1), [128, 8, 128]), ALU.mult, [sqv, gk], [Fk])
                self.tt("pool", kk[:, 1, :], kp[:], gk[:, 128:192], ALU.mult, [kp, gk], [kk])
                rope_ops(self, "pool", kk[:, 1:2, :], kk[:, 0:1, :], kk[:, 2:3, :], kk[:, 3:4, :], rt, 1, [kk, rt], [kk])
                self.tt("dve", Fk[:, :, 128:192], bc(kk[:, 0:1, :], [128, 8, 64]), bc(rk.unsqueeze(2), [128, 8, 64]), ALU.mult, [kk, S_], [Fk])
                V_ = vfin[i % 2]
                self.copy("act", V_[:], k3[:, :, 128:256], [qf], [V_])
                P.dma("sp", self.vh.ap()[r0:r0 + 128, :], V_[:].rearrange("p h e -> p (h e)"), reads=[V_], writes=[self.vh])
                pn, pr = self.ps[6], self.ps[7]
                for h in range(8):
                    self.tr(pnb[:, h * 128:(h + 1) * 128], Fk[:, h, 0:128], self.ident_b, [Fk, self.cstb], [pn])
                    self.tr(prb[0:64, h * 128:(h + 1) * 128], Fk[:, h, 128:192], self.ident_b, [Fk, self.cstb], [pr])
                self.copy("act", kn_[:, :, t * 128:(t + 1) * 128], pnb[:, 0:1024].rearrange("p (h t) -> p h t", h=8), [pn], [], pw=[kn_])
                self.copy("act", kr_[:, :, t * 128:(t + 1) * 128], prb[0:64, 0:1024].rearrange("p (h t) -> p h t", h=8), [pr], [], pw=[kr_])
            for Tt, n_, r_ in ((self.qhT, qn_, qr_), (self.khT, kn_, kr_)):
                P.dma("sp", Tt.ap()[:, 0:128, s0:s0 + ng].rearrange("h d t -> d h t"), n_[:, :, 0:ng], reads=[n_], writes=[Tt])
                P.dma("sp", Tt.ap()[:, 128:192, s0:s0 + ng].rearrange("h d t -> d h t"), r_[:, :, 0:ng], reads=[r_], writes=[Tt])
    if "qk_only" in self.debug:
        return
    with P.scope():
        Kn = [P.sbuf("c_Kn%d" % i, [128, TT], BF16) for i in range(2)]
        Kr = [P.sbuf("c_Kr%d" % i, [64, TT], BF16) for i in range(2)]
        Vh = [P.sbuf("c_Vh%d" % i, [128, NT, 132], BF16) for i in range(2)]
        Qn = [P.sbuf("c_Qn%d" % i, [128, 512], BF16) for i in range(2)]
        Qr = [P.sbuf("c_Qr%d" % i, [64, 512], BF16) for i in range(2)]
        pT = [P.sbuf("c_pT%d" % i, [128, 512], BF16) for i in range(3)]
        ob = [P.sbuf("c_ob%d" % i, [128, 4, 128], BF16) for i in range(2)]
        rd = [P.sbuf("c_rd%d" % i, [128, 4], F32) for i in range(2)]
        bst = [P.sbuf("c_bst%d" % i, [128, 512], BF16) for i in range(2)]
        for i in range(2):
            P.op("pool", lambda e, i=i: e.memset(Vh[i][:], 1.0), writes=[Vh[i]])
        sc = 192.0 ** -0.5
        for h in range(8):
            K1, K2, V_ = Kn[h % 2], Kr[h % 2], Vh[h % 2]
            P.dma("sp", K1[:], self.khT.ap()[h, 0:128, :], reads=[self.khT], writes=[K1])
            P.dma("sp", K2[:], self.khT.ap()[h, 128:192, :], reads=[self.khT], writes=[K2])
            P.dma("sp", V_[:, :, 0:128], self.vh.ap()[:, h * 128:(h + 1) * 128].rearrange("(kt p) e -> p kt e", p=128), reads=[self.vh], writes=[V_])
            for (q0, nq) in self.groups:
                i = self.rr.get("c_q", 0)
                self.rr["c_q"] = i + 1
                Q1, Q2, O_, R_, B_ = Qn[i % 2], Qr[i % 2], ob[i % 2], rd[i % 2], bst[i % 2]
                P.dma("sp", Q1[:, 0:nq], self.qhT.ap()[h, 0:128, q0:q0 + nq], reads=[self.qhT], writes=[Q1])
                P.dma("sp", Q2[:, 0:nq], self.qhT.ap()[h, 128:192, q0:q0 + nq], reads=[self.qhT], writes=[Q2])
                nqt = nq // 128
                pO = self.ps[0:4]
                nkt = TC // 128 if q0 < TC else NT
                for kt in range(nkt):
                    pS = self.rot("c_pS", self.ps[4:8])
                    self.mm(pS[:, 0:nq], K1[:, kt * 128:(kt + 1) * 128], Q1[:, 0:nq], True, False, [K1, Q1], [pS])
                    self.mm(pS[:, 0:nq], K2[:, kt * 128:(kt + 1) * 128], Q2[:, 0:nq], False, True, [K2, Q2], [pS])
                    p_ = self.rot("c_pT", pT)
                    self.act(p_[:, 0:nq], pS[:, 0:nq], AF.Exp, [pS], [p_], scale=sc)
                    for j in range(nqt):
                        self.mm(pO[j][:, 0:129], p_[:, j * 128:(j + 1) * 128], V_[:, kt, 0:129], kt == 0, kt == nkt - 1, [p_, V_], [pO[j]])
                pS = self.rot("c_pS", self.ps[4:8])
                psb = pS[:].bitcast(BF16)
                for j in range(nqt):
                    P.op("dve", lambda e, j=j, R_=R_: e.reciprocal(out=R_[:, j:j + 1], in_=pO[j][:, 128:129]), reads=[pO[j]], writes=[R_])
                    self.act(O_[:, j, :], pO[j][:, 0:128], AF.Identity, [pO[j], R_], [], pw=[O_], scale=R_[:, j:j + 1])
                    self.tr(psb[:, j * 128:(j + 1) * 128], O_[:, j, :], self.ident_b, [O_, self.cstb], [pS])
                self.copy("dve", B_[:, 0:nq], psb[:, 0:nq], [pS], [B_])
                P.dma("sp", self.brT.ap()[1, h * 128:(h + 1) * 128, q0:q0 + nq], B_[:, 0:nq], reads=[B_], writes=[self.brT])


Builder.phase_C2 = phase_C2


def phase_C3(self, l):
    P, TC, TL, TT = self.P, self.TC, self.TL, self.TT
    TWO_PI = 2.0 * math.pi
    for d in range(2):
        with P.scope():
            lam = P.sbuf("s_lam", [128, 3, 32], F32)
            sc = P.sbuf("s_sc", [128, 24, 32], F32)
            Er = P.sbuf("s_Er", [128, 32, SL], F32)
            Ei = P.sbuf("s_Ei", [128, 32, SL], F32)
            Tr = P.sbuf("s_Tr", [128, 32, SL], F32)
            Ti = P.sbuf("s_Ti", [128, 32, SL], F32)
            tmpE = P.sbuf("s_tmpE", [128, 32, SL], F32)
            Bw = P.sbuf("s_Bw", [128, 2, 8, 2, 128], BF16)
            Cw = P.sbuf("s_Cw", [128, 2, 32, 32], BF16)
            Cn = P.sbuf("s_Cn", [128, 2, 32, 32], BF16)
            xst = P.sbuf("s_xst", [128, 32, 2], F32)
            P.dma("sp", lam[:], self.s5_lam.ap()[l, d].rearrange("i p q -> p i q"), reads=[self.s5_lam], writes=[lam])
            P.dma("pool", Bw[:], self.s5_b.ap()[l, d].rearrange("r p q v n -> p r q v n"), reads=[self.s5_b], writes=[Bw])
            P.dma("pool", Cw[:], self.s5_c.ap()[l, d].rearrange("r p q n -> p r q n"), reads=[self.s5_c], writes=[Cw])
            self.ts("dve", Cn[:], Cw[:], -1.0, ALU.mult, [Cw], [Cn])
            P.op("dve", lambda e: e.memset(xst[:], 0.0), writes=[xst])
            S = lambda i: sc[:, i, :]
            rd = [sc]
            ar, aim, dt, r1, th, k_, c1, s1, lbr, lbi, fr, fi = (S(i) for i in range(12))
            u0, u1, u2, u3 = S(12), S(13), S(14), S(15)
            self.ts("dve", ar, lam[:, 0, :], -1e-4, ALU.min, [lam], [sc])
            self.copy("dve", aim, lam[:, 1, :], [lam], [sc])
            self.act(dt, lam[:, 2, :], AF.Exp, [lam], [sc])
            self.tt("dve", u0, ar, dt, ALU.mult, rd, rd)
            self.act(r1, u0, AF.Exp, rd, rd)
            self.tt("dve", th, aim, dt, ALU.mult, rd, rd)
            self.ts("dve", u0, th, 1.0 / TWO_PI, ALU.mult, rd, rd)
            self.ts("dve", u1, u0, 12582912.0, ALU.add, rd, rd)
            self.ts("dve", k_, u1, -12582912.0, ALU.add, rd, rd)
            self.stt(u2, k_, -TWO_PI, th, ALU.mult, ALU.add, rd, rd)
            self.ts("dve", u2, u2, math.pi, ALU.min, rd, rd, s2=-math.pi, op1=ALU.max)
            self.act(s1, u2, AF.Sin, rd, rd)
            self.act(u3, u2, AF.Abs, rd, rd)
            self.ts("dve", u3, u3, -1.0, ALU.mult, rd, rd, s2=math.pi / 2, op1=ALU.add)
            self.act(c1, u3, AF.Sin, rd, rd)
            self.tt("dve", lbr, r1, c1, ALU.mult, rd, rd)
            self.tt("dve", lbi, r1, s1, ALU.mult, rd, rd)
            x_, y_ = S(16), S(17)
            self.ts("dve", x_, lbr, -1.0, ALU.add, rd, rd)
            self.copy("dve", y_, lbi, rd, rd)
            a_, b_ = ar, aim
            n1, n2, den = S(18), S(19), S(20)
            self.tt("dve", n1, x_, a_, ALU.mult, rd, rd)
            self.tt("dve", u0, y_, b_, ALU.mult, rd, rd)
            self.tt("dve", n1, n1, u0, ALU.add, rd, rd)
            self.tt("dve", n2, y_, a_, ALU.mult, rd, rd)
            self.tt("dve", u0, x_, b_, ALU.mult, rd, rd)
            self.tt("dve", n2, n2, u0, ALU.subtract, rd, rd)
            self.tt("dve", den, a_, a_, ALU.mult, rd, rd)
            self.tt("dve", u0, b_, b_, ALU.mult, rd, rd)
            self.tt("dve", den, den, u0, ALU.add, rd, rd)
            P.op("dve", lambda e, den=den: e.reciprocal(out=den, in_=den), reads=rd, writes=rd)
            self.tt("dve", fr, n1, den, ALU.mult, rd, rd)
            self.tt("dve", fi, n2, den, ALU.mult, rd, rd)
            self.copy("dve", Er[:, :, 0], c1, rd, [Er])
            self.copy("dve", Ei[:, :, 0], s1, rd, [Ei])
            n = 1
            while n < SL:
                pr_ = bc(Er[:, :, n - 1:n], [128, 32, n])
                pi_ = bc(Ei[:, :, n - 1:n], [128, 32, n])
                sr, si = Er[:, :, 0:n], Ei[:, :, 0:n]
                dr, di = Er[:, :, n:2 * n], Ei[:, :, n:2 * n]
                tm = tmpE[:, :, 0:n]
                self.tt("dve", dr, sr, pr_, ALU.mult, [Er], [Er])
                self.tt("dve", tm, si, pi_, ALU.mult, [Ei], [tmpE])
                self.tt("dve", dr, dr, tm, ALU.subtract, [Er, tmpE], [Er])
                self.tt("dve", di, sr, pi_, ALU.mult, [Er, Ei], [Ei])
                self.tt("dve", tm, si, pr_, ALU.mult, [Er, Ei], [tmpE])
                self.tt("dve", di, di, tm, ALU.add, [Ei, tmpE], [Ei])
                n *= 2
            frb = bc(fr.unsqueeze(2), [128, 32, SL])
            fib = bc(fi.unsqueeze(2), [128, 32, SL])
            self.tt("dve", Tr[:], Er[:], frb, ALU.mult, [Er, sc], [Tr])
            self.tt("dve", tmpE[:], Ei[:], fib, ALU.mult, [Ei, sc], [tmpE])
            self.tt("dve", Tr[:], Tr[:], tmpE[:], ALU.add, [Tr, tmpE], [Tr])
            self.tt("dve", Ti[:], Er[:], fib, ALU.mult, [Er, sc], [Ti])
            self.tt("dve", tmpE[:], Ei[:], frb, ALU.mult, [Ei, sc], [tmpE])
            self.tt("dve", Ti[:], Ti[:], tmpE[:], ALU.subtract, [Ti, tmpE], [Ti])

            ut = [P.sbuf("s_ut%d" % i, [128, 512], BF16) for i in range(2)]
            mt = [P.sbuf("s_m%d" % i, [128, 512], F32) for i in range(4)]
            bre = [P.sbuf("s_bre%d" % i, [128, 512], F32) for i in range(2)]
            bim = [P.sbuf("s_bim%d" % i, [128, 512], F32) for i in range(2)]
            wre = [P.sbuf("s_wre%d" % i, [128, 512], F32) for i in range(2)]
            wim = [P.sbuf("s_wim%d" % i, [128, 512], F32) for i in range(2)]
            pr4 = [[P.sbuf("s_p%d_%d" % (k, i), [128, 512], BF16) for k in range(4)] for i in range(2)]
            tsm = [P.sbuf("s_ts%d" % i, [128, 2], F32) for i in range(2)]
            yev = [P.sbuf("s_yev%d" % i, [128, 4, 128], F32) for i in range(2)]
            first = True
            src = self.uT if d == 0 else self.uTr
            for (i0, ng) in self.groups:
                nchk = ng // SL
                ntt = ng // 128
                for q in range(8):
                    U_ = self.rot("s_ut", ut)
                    P.dma("pool" if d == 0 else "sp", U_[:, 0:ng], src.ap()[q * 128:(q + 1) * 128, i0:i0 + ng], reads=[src], writes=[U_])
                    pY = self.rot("s_pY", self.ps[6:8])
                    Y_ = self.rot("s_yev", yev)
                    for pp in range(4):
                        p = 4 * q + pp
                        ii = self.rr.get("s_i", 0)
                        self.rr["s_i"] = ii + 1
                        pR = self.rot("s_pR", self.ps[0:6])
                        pI = self.rot("s_pR", self.ps[0:6])
                        rows = slice(32 * pp, 32 * pp + 32) if pp < 3 else slice(64, 128)
                        vv = 0 if pp < 3 else 1
                        self.mm(pR[:, 0:ng], Bw[rows, 0, q, vv, :], U_[rows, 0:ng], True, True, [Bw, U_], [pR])
                        self.mm(pI[:, 0:ng], Bw[rows, 1, q, vv, :], U_[rows, 0:ng], True, True, [Bw, U_], [pI])
                        v3 = lambda ap: ap[:, 0:ng].rearrange("p (c j) -> p c j", j=SL)
                        Trb = bc(Tr[:, p, :].unsqueeze(1), [128, nchk, SL])
                        Tib = bc(Ti[:, p, :].unsqueeze(1), [128, nchk, SL])
                        Erb = bc(Er[:, p, :].unsqueeze(1), [128, nchk, SL])
                        Eib = bc(Ei[:, p, :].unsqueeze(1), [128, nchk, SL])
                        m1, m2, m3, m4 = mt
                        BR, BI, WR, WI = bre[ii % 2], bim[ii % 2], wre[ii % 2], wim[ii % 2]
                        self.tt("dve", v3(m1), v3(pR), Trb, ALU.mult, [pR, Tr], [m1])
                        self.tt("dve", v3(m2), v3(pI), Tib, ALU.mult, [pI, Ti], [m2])
                        self.tt("dve", v3(m3), v3(pI), Trb, ALU.mult, [pI, Tr], [m3])
                        self.tt("dve", v3(m4), v3(pR), Tib, ALU.mult, [pR, Ti], [m4])
                        self.tt("pool", BR[:, 0:ng], m1[:, 0:ng], m2[:, 0:ng], ALU.subtract, [m1, m2], [BR])
                        self.tt("pool", BI[:, 0:ng], m3[:, 0:ng], m4[:, 0:ng], ALU.add, [m3, m4], [BI])
                        r1p = r1[:, p:p + 1]
                        ErL, EiL = Er[:, p, SL - 1:SL], Ei[:, p, SL - 1:SL]
                        for c in range(nchk):
                            a0 = c * SL
                            if not (first and c == 0):
                                self.stt(BR[:, a0:a0 + 1], xst[:, p, 0:1], r1p, BR[:, a0:a0 + 1], ALU.mult, ALU.add, [xst, sc, BR], [BR])
                                self.stt(BI[:, a0:a0 + 1], xst[:, p, 1:2], r1p, BI[:, a0:a0 + 1], ALU.mult, ALU.add, [xst, sc, BI], [BI])
                            P.op("dve", lambda e, WR=WR, BR=BR, a0=a0, r1p=r1p: e.tensor_tensor_scan(out=WR[:, a0:a0 + SL], data0=bc(r1p, [128, SL]), data1=BR[:, a0:a0 + SL], initial=0.0, op0=ALU.mult, op1=ALU.add), reads=[BR, sc], writes=[WR])
                            P.op("dve", lambda e, WI=WI, BI=BI, a0=a0, r1p=r1p: e.tensor_tensor_scan(out=WI[:, a0:a0 + SL], data0=bc(r1p, [128, SL]), data1=BI[:, a0:a0 + SL], initial=0.0, op0=ALU.mult, op1=ALU.add), reads=[BI, sc], writes=[WI])
                            T_ = tsm[c % 2]
                            la = a0 + SL - 1
                            self.ts("dve", T_[:, 0:1], WI[:, la:la + 1], EiL, ALU.mult, [WI, Ei], [T_])
                            self.stt(xst[:, p, 0:1], WR[:, la:la + 1], ErL, T_[:, 0:1], ALU.mult, ALU.subtract, [WR, Er, T_], [xst])
                            self.ts("dve", T_[:, 1:2], WR[:, la:la + 1], EiL, ALU.mult, [WR, Ei], [T_])
                            self.stt(xst[:, p, 1:2], WI[:, la:la + 1], ErL, T_[:, 1:2], ALU.mult, ALU.add, [WI, Er, T_], [xst])
                        PR = pr4[ii % 2]
                        self.tt("pool", v3(PR[0]), v3(WR), Erb, ALU.mult, [WR, Er], [PR[0]])
                        self.tt("pool", v3(PR[1]), v3(WI), Eib, ALU.mult, [WI, Ei], [PR[1]])
                        self.tt("pool", v3(PR[2]), v3(WR), Eib, ALU.mult, [WR, Ei], [PR[2]])
                        self.tt("pool", v3(PR[3]), v3(WI), Erb, ALU.mult, [WI, Er], [PR[3]])
                        for t in range(ntt):
                            o_ = pY[:, t * 128 + pp * 32:t * 128 + pp * 32 + 32]
                            tsl = slice(t * 128, (t + 1) * 128)
                            self.mm(o_, PR[0][:, tsl], Cw[:, 0, p, :], True, False, [PR[0], Cw], [pY])
                            self.mm(o_, PR[1][:, tsl], Cn[:, 0, p, :], False, False, [PR[1], Cn], [pY])
                            self.mm(o_, PR[2][:, tsl], Cn[:, 1, p, :], False, False, [PR[2], Cn], [pY])
                            self.mm(o_, PR[3][:, tsl], Cn[:, 1, p, :], False, True, [PR[3], Cn], [pY])
                    self.copy("act", Y_[:, 0:ntt, :], pY[:, 0:ng].rearrange("p (t c) -> p t c", c=128), [pY], [Y_])
                    P.dma("sp", self.ytok.ap()[d, i0:i0 + ng, q * 128:(q + 1) * 128].rearrange("(t p) c -> p t c", p=128), Y_[:, 0:ntt, :], reads=[Y_], writes=[self.ytok])
                first = False
    with P.scope():
        wg = P.sbuf("g_wg", [128, 8, 1024], BF16)
        dsk = P.sbuf("g_dsk", [128, 8], F32)
        bgl = P.sbuf("g_bgl", [128, 8], F32)
        P.dma("pool", wg[:], self.w_glu.ap()[l].rearrange("(kc p) n -> p kc n", p=128), reads=[self.w_glu], writes=[wg])
        P.dma("sp", dsk[:], self.s5_d.ap()[l].rearrange("(c p) -> p c", p=128), reads=[self.s5_d], writes=[dsk], allow_slow_non_contiguous=True)
        P.dma("sp", bgl[:], self.b_glu.ap()[l].rearrange("(c p) -> p c", p=128), reads=[self.b_glu], writes=[bgl], allow_slow_non_contiguous=True)
        y0 = [P.sbuf("g_y0_%d" % i, [128, 1024], F32) for i in range(2)]
        y1 = [P.sbuf("g_y1_%d" % i, [128, 1024], F32) for i in range(2)]
        uTt = P.sbuf("g_uT", [128, 8, 512], F32)
        yT = P.sbuf("g_yT", [128, 8, 512], F32)
        t_a = P.sbuf("g_ta", [128, 8, 512], F32)
        gb = P.sbuf("g_gb", [128, 8, 512], BF16)
        sg = [P.sbuf("g_sg%d" % i, [128, 512], F32) for i in range(2)]
        cT = P.sbuf("g_cT", [128, 8, 512], BF16)
        for (s0, ng) in self.groups:
            P.dma("sp", uTt[:, :, 0:ng], self.uT.ap().rearrange("(c p) t -> p c t", p=128)[:, :, s0:s0 + ng], reads=[self.uT], writes=[uTt])
            for t in range(ng // 128):
                r0 = s0 + t * 128
                a0, a1 = self.rot("g_y0", y0), self.rot("g_y1", y1)
                P.dma("sp", a0[:], self.ytok.ap()[0, r0:r0 + 128, :], reads=[self.ytok], writes=[a0])
                m0 = self.mirror(r0)
                P.dma("sp", a1[:], self.ytok.ap()[1, m0:m0 + 128, :], reads=[self.ytok], writes=[a1])
                for half in range(2):
                    ps = self.rot("g_ps", self.ps[0:4])
                    for qq in range(4):
                        q = half * 4 + qq
                        o_ = ps[:, qq * 128:(qq + 1) * 128]
                        self.mm(o_, a0[:, q * 128:(q + 1) * 128], self.ident_f, True, False, [a0, self.cst_t], [ps])
                        self.mm(o_, a1[:, q * 128:(q + 1) * 128], self.J_f, False, True, [a1, self.cst_t], [ps])
                    for qq in range(4):
                        q = half * 4 + qq
                        self.stt(yT[:, q, t * 128:(t + 1) * 128], uTt[:, q, t * 128:(t + 1) * 128], dsk[:, q:q + 1], ps[:, qq * 128:(qq + 1) * 128], ALU.mult, ALU.add, [uTt, dsk, ps], [], pw=[yT])
            Y = yT[:, :, 0:ng]
            A = t_a[:, :, 0:ng]
            self.act(A, Y, AF.Square, [yT], [t_a])
            self.ts("dve", A, A, 0.044715, ALU.mult, [t_a], [t_a], s2=1.0, op1=ALU.add)
            self.tt("pool", A, A, Y, ALU.mult, [t_a, yT], [t_a])
            self.act(A, A, AF.Sigmoid, [t_a], [t_a], scale=2.0 * math.sqrt(2.0 / math.pi))
            self.tt("dve", Y, Y, A, ALU.mult, [yT, t_a], [yT])
            self.copy("act", gb[:, :, 0:ng], Y, [yT], [gb])
            for jc in range(8):
                ps = self.rot("g_ps", self.ps[0:4])
                for kc in range(8):
                    self.mm(ps[:, 0:ng], wg[:, kc, jc * 128:(jc + 1) * 128], gb[:, kc, 0:ng], kc == 0, kc == 7, [wg, gb], [ps])
                s_ = self.rot("g_sg", sg)
                self.act(s_[:, 0:ng], ps[:, 0:ng], AF.Sigmoid, [ps, bgl], [s_], bias=bgl[:, jc:jc + 1])
                self.tt("dve", cT[:, jc, 0:ng], yT[:, jc, 0:ng], s_[:, 0:ng], ALU.mult, [yT, s_], [], pw=[cT])
            P.dma("sp", self.brT.ap()[2].rearrange("(c p) t -> p c t", p=128)[:, :, s0:s0 + ng], cT[:, :, 0:ng], reads=[cT], writes=[self.brT])


Builder.phase_C3 = phase_C3


def phase_DE(self, l):
    P, TC, TL, TT = self.P, self.TC, self.TL, self.TT
    last = (l == self.depth - 1)
    with P.scope():
        X = P.sbuf("e_X", [128, KC, 512], F32)
        M = P.sbuf("e_M", [128, KC, 512], BF16)
        H = P.sbuf("e_H", [128, 32, 512], BF16)
        wt = [P.sbuf("e_w%d" % i, [128, KC, 512], BF16) for i in range(3)]
        gt = [P.sbuf("e_g%d" % i, [128, 3, 512], BF16) for i in range(2)]
        ma = [P.sbuf("e_ma%d" % i, [128, 512], F32) for i in range(2)]
        mb = [P.sbuf("e_mb%d" % i, [128, 512], F32) for i in range(2)]
        sqb = [P.sbuf("e_sq%d" % i, [128, 512], BF16) for i in range(3)]
        tmp = [P.sbuf("e_tmp%d" % i, [128, 512], F32) for i in range(3)]
        rbc = P.sbuf("e_rbc", [128, 512], F32)
        rl = [P.sbuf("e_rl%d" % i, [128, 512], F32) for i in range(3)]
        yt = [P.sbuf("e_yt%d" % i, [128, D], F32) for i in range(2)] if last else None
        wbr = self.w_branch.ap()[l]
        wov = self.w_out.ap()[l].rearrange("(kc p) n -> p kc n", p=128)
        w1v = self.w_ff1.ap()[l].rearrange("(kc p) n -> p kc n", p=128)
        w2v = self.w_ff2.ap()[l].rearrange("(f p) n -> p f n", p=128)
        gate1, gate2 = self.modT[:, 2], self.modT[:, 5]
        xin = self.xT[l]
        groups = [g for g in self.groups if not (last and g[0] < TC)]
        for (s0, ng) in groups:
            col = 1 if s0 < TC else 0
            P.dma("sp", X[:, :, 0:ng], xin.ap().rearrange("(c p) t -> p c t", p=128)[:, :, s0:s0 + ng], reads=[xin], writes=[X])
            for r in range(3):
                P.dma("sp", H[:, r * 8:(r + 1) * 8, 0:ng], self.brT.ap()[r].rearrange("(c p) t -> p c t", p=128)[:, :, s0:s0 + ng], reads=[self.brT], writes=[H])
            for jb in range(4):
                ws = []
                for r in range(3):
                    w = self.rot("e_w", wt)
                    P.dma("pool", w[:, 0:8, :], wbr[r].rearrange("(kc p) n -> p kc n", p=128)[:, :, jb * 512:(jb + 1) * 512], reads=[self.w_branch], writes=[w])
                    ws.append(w)
                for jj in range(4):
                    j = jb * 4 + jj
                    G_ = self.rot("e_g", gt)
                    P.dma("sp", G_[:, :, 0:ng], self.gT.ap().rearrange("(r c p) t -> p r c t", r=3, p=128)[:, :, j, s0:s0 + ng], reads=[self.gT], writes=[G_])
                    A_, B_ = self.rot("e_ma", ma), self.rot("e_mb", mb)
                    for r in range(3):
                        ps = self.rot("e_ps", self.ps[0:6])
                        for kc in range(8):
                            self.mm(ps[:, 0:ng], ws[r][:, kc, jj * 128:(jj + 1) * 128], H[:, r * 8 + kc, 0:ng], kc == 0, kc == 7, [ws[r], H], [ps])
                        if r == 0:
                            self.tt("dve", A_[:, 0:ng], ps[:, 0:ng], G_[:, 0, 0:ng], ALU.mult, [ps, G_], [A_])
                        elif r == 1:
                            self.tt("dve", B_[:, 0:ng], ps[:, 0:ng], G_[:, 1, 0:ng], ALU.mult, [ps, G_], [B_])
                            self.tt("pool", A_[:, 0:ng], A_[:, 0:ng], B_[:, 0:ng], ALU.add, [A_, B_], [A_])
                        else:
                            self.tt("dve", B_[:, 0:ng], ps[:, 0:ng], G_[:, 2, 0:ng], ALU.mult, [ps, G_, A_], [B_])
                            self.tt("pool", M[:, j, 0:ng], A_[:, 0:ng], B_[:, 0:ng], ALU.add, [A_, B_], [], pw=[M])
            for jb in range(4):
                w = self.rot("e_w", wt)
                P.dma("pool", w[:], wov[:, :, jb * 512:(jb + 1) * 512], reads=[self.w_out], writes=[w])
                for jj in range(4):
                    j = jb * 4 + jj
                    ps = self.rot("e_ps", self.ps[0:6])
                    for kc in range(KC):
                        self.mm(ps[:, 0:ng], w[:, kc, jj * 128:(jj + 1) * 128], M[:, kc, 0:ng], kc == 0, kc == KC - 1, [w, M], [ps])
                    self.stt(X[:, j, 0:ng], ps[:, 0:ng], gate1[:, j, col:col + 1], X[:, j, 0:ng], ALU.mult, ALU.add, [ps, self.modT, X], [X])
            self.norm_group(X, M, ng, 1, col, sqb, rbc, tmp, self.ps[7])
            for half in range(2):
                for fb in range(8):
                    w = self.rot("e_w", wt)
                    c0 = half * 4096 + fb * 512
                    P.dma("pool", w[:], w1v[:, :, c0:c0 + 512], reads=[self.w_ff1], writes=[w])
                    for jj in range(4):
                        f = fb * 4 + jj
                        ps = self.rot("e_ps", self.ps[0:6])
                        for kc in range(KC):
                            self.mm(ps[:, 0:ng], w[:, kc, jj * 128:(jj + 1) * 128], M[:, kc, 0:ng], kc == 0, kc == KC - 1, [w, M], [ps])
                        R_ = self.rot("e_rl", rl)
                        self.act(R_[:, 0:ng], ps[:, 0:ng], AF.Relu, [ps], [R_])
                        self.tt("pool" if f % 2 == 0 else "dve", H[:, f, 0:ng], R_[:, 0:ng], R_[:, 0:ng], ALU.mult, [R_], [], pw=[H])
                for jb in range(4):
                    wa = self.rot("e_w", wt)
                    wb_ = self.rot("e_w", wt)
                    f0 = half * 32
                    P.dma("pool", wa[:], w2v[:, f0:f0 + 16, jb * 512:(jb + 1) * 512], reads=[self.w_ff2], writes=[wa])
                    P.dma("pool", wb_[:], w2v[:, f0 + 16:f0 + 32, jb * 512:(jb + 1) * 512], reads=[self.w_ff2], writes=[wb_])
                    for jj in range(4):
                        j = jb * 4 + jj
                        ps = self.rot("e_ps", self.ps[0:6])
                        for f in range(32):
                            ww = wa if f < 16 else wb_
                            self.mm(ps[:, 0:ng], ww[:, f % 16, jj * 128:(jj + 1) * 128], H[:, f, 0:ng], f == 0, f == 31, [ww, H], [ps])
                        self.stt(X[:, j, 0:ng], ps[:, 0:ng], gate2[:, j, col:col + 1], X[:, j, 0:ng], ALU.mult, ALU.add, [ps, self.modT, X], [X])
            if not last:
                P.dma("sp", self.xT[l + 1].ap().rearrange("(c p) t -> p c t", p=128)[:, :, s0:s0 + ng], X[:, :, 0:ng], reads=[X], writes=[self.xT[l + 1]])
            else:
                for t in range(ng // 128):
                    Y_ = self.rot("e_yt", yt)
                    for b4 in range(4):
                        ps = self.rot("e_ps", self.ps[0:6])
                        for jj in range(4):
                            c = b4 * 4 + jj
                            self.tr(ps[:, jj * 128:(jj + 1) * 128], X[:, c, t * 128:(t + 1) * 128], self.ident_f, [X, self.cst_t], [ps])
                        self.copy("act" if b4 % 2 == 0 else "dve", Y_[:, b4 * 512:(b4 + 1) * 512], ps[:, 0:512], [ps], [], pw=[Y_])
                    r0 = s0 - TC + t * 128
                    P.dma("sp", self.y.ap()[r0:r0 + 128, :], Y_[:], reads=[Y_], writes=[self.y])


Builder.phase_DE = phase_DE


_CACHE = {}


def kernel(**inputs):
    TL, TC = 4096, 256
    if "prog" not in _CACHE:
        _CACHE["prog"] = build_program(TL, TC, depth=DEPTH)
    B = _CACHE["prog"]
    sh = prep_shared(inputs, DEPTH)
    cst, rope = make_consts(TL, TC)
    maps = []
    for core in range(8):
        b = core % 4
        m = dict(sh)
        m.update(prep_core(inputs, b, TL, TC))
        m["cst"] = cst
        m["rope"] = rope
        maps.append(m)
    res = run_bass_kernel_spmd(B.nc, maps, core_ids=list(range(8)))
    out = np.stack([np.asarray(res.results[b]["y"], dtype=np.float32) for b in range(4)], axis=0)
    return out
```

```python
import contextlib
import math
import numpy as np
import concourse.bass as bass
import concourse.mybir as mybir
from concourse.bass_utils import run_bass_kernel_spmd

F32 = mybir.dt.float32
BF16 = mybir.dt.bfloat16
AF = mybir.ActivationFunctionType
ALU = mybir.AluOpType
AX = mybir.AxisListType

ENGS = ("pe", "act", "dve", "pool", "sp")
EPOCH = 30000

D = 2048
KC = 16
DEPTH = 2
DIN = 11344
OFF_Q, OFF_K, OFF_V, OFF_O, OFF_G, OFF_QA, OFF_KVA, OFF_KPE, OFF_U, OFF_BG = 0, 512, 1024, 2048, 3072, 3088, 3600, 4112, 4176, 5200
EPS = 1e-6
SL = 128


class T:
    def __init__(self, h, name=""):
        self.h = h
        self.name = name
        self.w = {}
        self.r = {}

    def __getitem__(self, k):
        return self.h[k]

    def ap(self):
        return self.h.ap()


class Prog:
    def __init__(self, nc, n_dma_sems=8):
        self.nc = nc
        self.es = contextlib.ExitStack()
        self.q = {e: [] for e in ENGS}
        self.tick = {e: 0 for e in ENGS}
        self.epoch = {e: 0 for e in ENGS}
        self.sems = {}
        self.seen = {e: {} for e in ENGS}
        self.dma_slots = {}
        self.n_dma_sems = n_dma_sems
        self.dma_i = {e: 0 for e in ENGS}
        self.nsem = 0
        self.ninstr = 0
        self.scopes = []

    def sem(self, key):
        if key not in self.sems:
            self.sems[key] = self.es.enter_context(self.nc.semaphore("s%d" % self.nsem))
            self.nsem += 1
        return self.sems[key]

    def _stack(self):
        return self.scopes[-1] if self.scopes else self.es

    def sbuf(self, name, shape, dtype):
        self.uid = getattr(self, "uid", 0) + 1
        name = "%s_u%d" % (name, self.uid)
        return T(self._stack().enter_context(self.nc.sbuf_tensor(name, list(shape), dtype)), name)

    def psum(self, name, shape, dtype=F32):
        return T(self.es.enter_context(self.nc.psum_tensor(name, list(shape), dtype)), name)

    def dram(self, name, shape, dtype, kind=None):
        if kind is None:
            h = self.nc.dram_tensor(name, list(shape), dtype)
        else:
            h = self.nc.dram_tensor(name, list(shape), dtype, kind=kind)
        return T(h, name)

    @contextlib.contextmanager
    def scope(self):
        st = contextlib.ExitStack()
        self.scopes.append(st)
        try:
            yield
        finally:
            self.barrier()
            self.flush()
            self.scopes.pop()
            st.close()

    def _waits(self, e, reads, writes, pw=()):
        deps = {}
        for b in reads:
            for k, v in b.w.items():
                if deps.get(k, 0) < v:
                    deps[k] = v
        for b in writes:
            for d in (b.w, b.r):
                for k, v in d.items():
                    if deps.get(k, 0) < v:
                        deps[k] = v
        for b in pw:
            for k, v in b.r.items():
                if deps.get(k, 0) < v:
                    deps[k] = v
        out = []
        for k, v in deps.items():
            if k[0] == e and k[1] == "p" and e == "pe":
                continue
            if self.seen[e].get(k, 0) >= v:
                continue
            self.seen[e][k] = v
            out.append((k, v))
        return out

    def op(self, e, fn, reads=(), writes=(), pw=()):
        waits = self._waits(e, reads, writes, pw)
        if self.tick[e] >= EPOCH:
            self.epoch[e] += 1
            self.tick[e] = 0
        key = (e, "p", self.epoch[e])
        self.tick[e] += 1
        val = self.tick[e]
        s = self.sem(key)
        wl = [(self.sem(k), v) for k, v in waits]
        self.q[e].append((wl, fn, s, 1))
        for b in reads:
            b.r[key] = val
        for b in writes:
            b.w[key] = val
        for b in pw:
            b.w[key] = val
        self.ninstr += 1

    def dma(self, e, out_ap, in_ap, reads=(), writes=(), **kw):
        i = self.dma_i[e]
        self.dma_i[e] += 1
        slot = (e, "d", i % self.n_dma_sems)
        cnt = self.dma_slots.get(slot, 0)
        waits = self._waits(e, reads, writes)
        if cnt > 0 and self.seen[e].get(slot, 0) < cnt:
            self.seen[e][slot] = cnt
            waits.append((slot, cnt))
        cnt += 16
        self.dma_slots[slot] = cnt
        s = self.sem(slot)
        wl = [(self.sem(k), v) for k, v in waits]

        def fn(eng, out_ap=out_ap, in_ap=in_ap, kw=kw):
            return eng.dma_start(out=out_ap, in_=in_ap, **kw)

        self.q[e].append((wl, fn, s, 16))
        for b in reads:
            b.r[slot] = cnt
        for b in writes:
            b.w[slot] = cnt
        self.ninstr += 1

    def barrier(self):
        targets = {}
        for e in ENGS:
            if self.tick[e] > 0:
                targets[(e, "p", self.epoch[e])] = self.tick[e]
        for slot, cnt in self.dma_slots.items():
            targets[slot] = cnt
        for e in ENGS:
            wl = []
            for k, v in targets.items():
                if k[0] == e and k[1] == "p" and e == "pe":
                    continue
                if self.seen[e].get(k, 0) >= v:
                    continue
                self.seen[e][k] = v
                wl.append((self.sem(k), v))
            if wl:
                self.q[e].append((wl, None, None, 0))

    def flush(self):
        nc = self.nc
        q = self.q
        if not any(q[e] for e in ENGS):
            return
        with nc.Block() as block:

            def run(eng, lst):
                for wl, fn, s, inc in lst:
                    for ws, v in wl:
                        eng.wait_ge(ws, v)
                    if fn is not None:
                        fn(eng).then_inc(s, inc)

            @block.tensor
            def _(eng):
                run(eng, q["pe"])

            @block.scalar
            def _(eng):
                run(eng, q["act"])

            @block.vector
            def _(eng):
                run(eng, q["dve"])

            @block.gpsimd
            def _(eng):
                run(eng, q["pool"])

            @block.sync
            def _(eng):
                run(eng, q["sp"])

        self.q = {e: [] for e in ENGS}

    def finish(self):
        self.barrier()
        self.flush()
        self.es.close()


def bc(ap, shape):
    return ap.to_broadcast(list(shape))


class Builder:
    def __init__(self, TL, TC, depth=DEPTH, debug=(), stop_after=None):
        self.TL, self.TC, self.TT = TL, TC, TL + TC
        self.depth = depth
        self.debug = set(debug)
        self.stop_after = stop_after
        assert TC % 128 == 0 and TC <= 512 and TL % 512 == 0
        self.groups = [(0, TC)] + [(TC + 512 * i, 512) for i in range(TL // 512)]
        self.NT = self.TT // 128
        self.nc = bass.Bass("TRN2", target_bir_lowering=False)
        self.P = Prog(self.nc)
        self.rr = {}

    def ext_in(self, name, shape, dt=F32):
        return self.P.dram(name, shape, dt, kind="ExternalInput")

    def scratch(self, name, shape, dt):
        kind = "ExternalOutput" if name in self.debug else None
        return self.P.dram(name, shape, dt, kind=kind)

    def rot(self, key, lst):
        i = self.rr.get(key, 0)
        self.rr[key] = i + 1
        return lst[i % len(lst)]

    def act(self, out, in_, func, reads, writes, pw=(), **kw):
        self.P.op("act", lambda e: e.activation(out=out, in_=in_, func=func, **kw), reads=reads, writes=writes, pw=pw)

    def tt(self, eng, out, in0, in1, op, reads, writes, pw=()):
        self.P.op(eng, lambda e: e.tensor_tensor(out=out, in0=in0, in1=in1, op=op), reads=reads, writes=writes, pw=pw)

    def ts(self, eng, out, in0, s1, op0, reads, writes, s2=None, op1=None, pw=()):
        if op1 is None:
            self.P.op(eng, lambda e: e.tensor_scalar(out=out, in0=in0, scalar1=s1, scalar2=None, op0=op0), reads=reads, writes=writes, pw=pw)
        else:
            self.P.op(eng, lambda e: e.tensor_scalar(out=out, in0=in0, scalar1=s1, scalar2=s2, op0=op0, op1=op1), reads=reads, writes=writes, pw=pw)

    def stt(self, out, in0, scalar, in1, op0, op1, reads, writes, pw=()):
        self.P.op("dve", lambda e: e.scalar_tensor_tensor(out=out, in0=in0, scalar=scalar, in1=in1, op0=op0, op1=op1), reads=reads, writes=writes, pw=pw)

    def mm(self, out, lhsT, rhs, start, stop, reads, writes):
        self.P.op("pe", lambda e: e.matmul(out=out, lhsT=lhsT, rhs=rhs, start=start, stop=stop), reads=reads, writes=writes)

    def tr(self, out, in_, ident, reads, writes):
        self.P.op("pe", lambda e: e.transpose(out=out, in_=in_, identity=ident), reads=reads, writes=writes)

    def copy(self, eng, out, in_, reads, writes, pw=()):
        if eng == "act":
            self.act(out, in_, AF.Copy, reads, writes, pw=pw)
        else:
            self.P.op(eng, lambda e: e.tensor_copy(out=out, in_=in_), reads=reads, writes=writes, pw=pw)

    def rsqrt(self, out, in_, scale, reads, writes):
        self.act(out, in_, AF.Sqrt, reads, writes, scale=scale, bias=self.eps_t[0:out.shape[0], 0:1])
        self.P.op("dve", lambda e: e.reciprocal(out=out, in_=out), reads=writes, writes=writes)

    def declare(self):
        P, TT, TL, TC, L = self.P, self.TT, self.TL, self.TC, self.depth
        I = self.ext_in
        self.x_in = I("x", [TL, D])
        self.ctx_in = I("ctx", [TC, D])
        self.cvec = I("cvec", [2, D])
        self.w_mod = I("w_mod", [L, D, 6 * D])
        self.b_mod = I("b_mod", [L, 6 * D])
        self.norm_g = I("norm_g", [L, 2, D])
        self.w_in = I("w_in", [L, D, DIN])
        self.b_in = I("b_in", [L, DIN])
        self.ml_gate_b = I("ml_gate_b", [L, 16])
        self.ml_norm_g = I("ml_norm_g", [L, 1024])
        self.qa_g = I("mla_qa_g", [L, 512])
        self.kva_g = I("mla_kva_g", [L, 512])
        self.w_uq = I("mla_w_uq", [L, 512, 1536])
        self.w_ukv = I("mla_w_ukv", [L, 512, 2048])
        self.qn_g = I("mla_qn_g", [L, 192])
        self.kn_g = I("mla_kn_g", [L, 192])
        self.s5_lam = I("s5_lam", [L, 2, 3, 128, 32])
        self.s5_b = I("s5_b", [L, 2, 2, 128, 8, 2, 128])
        self.s5_c = I("s5_c", [L, 2, 2, 128, 32, 32])
        self.s5_d = I("s5_d", [L, 1024])
        self.w_glu = I("s5_w_glu", [L, 1024, 1024])
        self.b_glu = I("s5_b_glu", [L, 1024])
        self.w_branch = I("w_branch", [L, 3, 1024, D])
        self.w_out = I("w_out", [L, D, D])
        self.w_ff1 = I("w_ff1", [L, D, 4 * D])
        self.w_ff2 = I("w_ff2", [L, 4 * D, D])
        self.cst = I("cst", [128, 640])
        self.rope = I("rope", [TT, 128])
        self.y = P.dram("y", [TL, D], F32, kind="ExternalOutput")

        S = self.scratch
        self.xT = [S("xT0", [D, TT], F32), S("xT1", [D, TT], F32)]
        self.qT = S("qT", [4, 128, TT], BF16)
        self.kT = S("kT", [4, 128, TT], BF16)
        self.ktok = S("ktok", [TT, 512], BF16)
        self.vtok = S("vtok", [TT, 1024], BF16)
        self.otok = S("otok", [TT, 1024], BF16)
        self.gtok = S("gtok", [TT, 16], F32)
        self.kpe = S("kpe", [TT, 64], F32)
        self.qaT = S("qaT", [512, TT], BF16)
        self.kvaT = S("kvaT", [512, TT], BF16)
        self.rsq = S("rsq", [TT, 2], F32)
        self.uT = S("uT", [1024, TT], F32)
        self.uTr = S("uTr", [1024, TT], BF16)
        self.gT = S("gT", [6144, TT], BF16)
        self.hml = S("hml", [2, TT, 1024], F32)
        self.qhT = S("qhT", [8, 192, TT], BF16)
        self.khT = S("khT", [8, 192, TT], BF16)
        self.vh = S("vh", [TT, 1024], BF16)
        self.ytok = S("ytok", [2, TT, 1024], F32)
        self.brT = S("brT", [3, 1024, TT], BF16)

        self.cst_t = P.sbuf("cst_t", [128, 640], F32)
        self.cstb = P.sbuf("cstb", [128, 640], BF16)
        self.eps_t = P.sbuf("eps_t", [128, 1], F32)
        self.modT = P.sbuf("modT", [128, 6, 16, 2], F32)
        self.A1 = P.sbuf("A1", [128, 2, 16, 2], F32)
        self.ps = [P.psum("ps%d" % i, [128, 512], F32) for i in range(8)]
        P.dma("sp", self.cst_t[:], self.cst[:], reads=[self.cst], writes=[self.cst_t])
        P.op("dve", lambda e: e.tensor_copy(out=self.cstb[:], in_=self.cst_t[:]), reads=[self.cst_t], writes=[self.cstb])
        P.op("dve", lambda e: e.memset(self.eps_t[:], EPS), writes=[self.eps_t])
        c = self.cst_t
        self.ident_f, self.J_f = c[:, 0:128], c[:, 128:256]
        self.triL_f, self.triU_f = c[0:64, 256:320], c[0:64, 320:384]
        self.ones_f = c[:, 384:512]
        cb = self.cstb
        self.ident_b, self.J_b, self.ones_b = cb[:, 0:128], cb[:, 128:256], cb[:, 384:512]

    def seq_rows(self, s0, n):
        if s0 < self.TC:
            return self.ctx_in, self.ctx_in[s0:s0 + n, :]
        return self.x_in, self.x_in[s0 - self.TC:s0 - self.TC + n, :]

    def phase_T0(self):
        P = self.P
        with P.scope():
            xt = [P.sbuf("t0x%d" % i, [128, D], F32) for i in range(2)]
            xo = [P.sbuf("t0o%d" % i, [128, KC, 128], F32) for i in range(2)]
            dst = self.xT[0].ap().rearrange("(c p) t -> p c t", p=128)
            for i in range(self.NT):
                src_t, src = self.seq_rows(i * 128, 128)
                a, o = xt[i % 2], xo[i % 2]
                P.dma("sp", a[:], src, reads=[src_t], writes=[a])
                for b4 in range(4):
                    ps = self.ps[(i * 4 + b4) % 8]
                    for j in range(4):
                        c = b4 * 4 + j
                        self.tr(ps[:, j * 128:(j + 1) * 128], a[:, c * 128:(c + 1) * 128], self.ident_f, [a, self.cst_t], [ps])
                    self.copy("act" if b4 % 2 == 0 else "dve", o[:, b4 * 4:(b4 + 1) * 4, :], ps[:].rearrange("p (j t) -> p j t", j=4), [ps], [], pw=[o])
                P.dma("act", dst[:, :, i * 128:(i + 1) * 128], o[:], reads=[o], writes=[self.xT[0]])

    def phase_A(self, l):
        P = self.P
        with P.scope():
            cv = P.sbuf("a_cv", [128, 2, KC], F32)
            scT = P.sbuf("a_sc", [128, KC, 2], BF16)
            bm = P.sbuf("a_bm", [128, 6, KC], F32)
            ng = P.sbuf("a_ng", [128, 2, KC], F32)
            wt = [P.sbuf("a_w%d" % i, [128, KC, 512], BF16) for i in range(3)]
            P.dma("sp", cv[:], self.cvec.ap().rearrange("r (c p) -> p r c", p=128), reads=[self.cvec], writes=[cv], allow_slow_non_contiguous=True)
            P.dma("sp", bm[:], self.b_mod.ap()[l].rearrange("(i c p) -> p i c", p=128, c=KC), reads=[self.b_mod], writes=[bm], allow_slow_non_contiguous=True)
            P.dma("sp", ng[:], self.norm_g.ap()[l].rearrange("i (c p) -> p i c", p=128), reads=[self.norm_g], writes=[ng], allow_slow_non_contiguous=True)
            self.act(scT[:].rearrange("p c r -> p r c"), cv[:], AF.Silu, [cv], [scT])
            wv = self.w_mod.ap()[l].rearrange("(kc p) n -> p kc n", p=128)
            for idx in range(6):
                for jj in range(4):
                    w = self.rot("a_w", wt)
                    c0 = idx * D + jj * 512
                    P.dma("pool", w[:], wv[:, :, c0:c0 + 512], reads=[self.w_mod], writes=[w])
                    ps = self.rot("a_ps", self.ps[0:4])
                    for j in range(4):
                        for kc in range(KC):
                            self.mm(ps[:, 2 * j:2 * j + 2], w[:, kc, j * 128:(j + 1) * 128], scT[:, kc, :], kc == 0, kc == KC - 1, [w, scT], [ps])
                    self.tt("dve", self.modT[:, idx, 4 * jj:4 * jj + 4, :], ps[:, 0:8].rearrange("p (j r) -> p j r", r=2),
                            bc(bm[:, idx, 4 * jj:4 * jj + 4].unsqueeze(2), [128, 4, 2]), ALU.add, [ps, bm], [], pw=[self.modT])
            for n, idx in ((0, 1), (1, 4)):
                self.ts("dve", self.A1[:, n], self.modT[:, idx], 1.0, ALU.add, [self.modT], [], pw=[self.A1])
                self.tt("dve", self.A1[:, n], self.A1[:, n], bc(ng[:, n, :].unsqueeze(2), [128, KC, 2]), ALU.mult, [self.A1, ng], [self.A1])

    def norm_group(self, xg, hT, ng, n, col, sqb, rbc, tmp, ps):
        P = self.P
        shift = self.modT[:, 0 if n == 0 else 3]
        for c in range(KC):
            s = sqb[c % len(sqb)]
            self.act(s[:, 0:ng], xg[:, c, 0:ng], AF.Square, [xg], [s])
            self.mm(ps[:, 0:ng], self.ones_b, s[:, 0:ng], c == 0, c == KC - 1, [s, self.cstb], [ps])
        self.act(rbc[:, 0:ng], ps[:, 0:ng], AF.Sqrt, [ps, self.eps_t], [rbc], scale=1.0 / D, bias=self.eps_t[:, 0:1])
        P.op("dve", lambda e: e.reciprocal(out=rbc[:, 0:ng], in_=rbc[:, 0:ng]), reads=[rbc], writes=[rbc])
        for c in range(KC):
            t = tmp[c % len(tmp)]
            self.tt("dve", t[:, 0:ng], xg[:, c, 0:ng], rbc[:, 0:ng], ALU.mult, [xg, rbc], [t])
            self.act(hT[:, c, 0:ng], t[:, 0:ng], AF.Identity, [t, self.A1, self.modT], [], pw=[hT],
                     scale=self.A1[:, n, c, col:col + 1], bias=shift[:, c, col:col + 1])

    def phase_B(self, l):
        P, TT = self.P, self.TT
        xTl = self.xT[l]
        with P.scope():
            xg = P.sbuf("b_xg", [128, KC, 512], F32)
            hT = P.sbuf("b_hT", [128, KC, 512], BF16)
            sqb = [P.sbuf("b_sq%d" % i, [128, 512], BF16) for i in range(3)]
            tmp = [P.sbuf("b_tmp%d" % i, [128, 512], F32) for i in range(3)]
            rbc = P.sbuf("b_rbc", [128, 512], F32)
            wt = [P.sbuf("b_w%d" % i, [128, KC, 512], BF16) for i in range(3)]
            wsm = P.sbuf("b_wsm", [128, KC, 80], BF16)
            ob = [P.sbuf("b_ob%d" % i, [128, 4, 512], BF16) for i in range(2)]
            of = [P.sbuf("b_of%d" % i, [128, 4, 512], F32) for i in range(2)]
            ot = [P.sbuf("b_ot%d" % i, [128, 512], BF16) for i in range(3)]
            otf = [P.sbuf("b_otf%d" % i, [128, 512], F32) for i in range(2)]
            sqs = [P.sbuf("b_sqs%d" % i, [128, 512], BF16) for i in range(2)]
            binT = P.sbuf("b_binT", [128, 89], F32)
            bq = P.sbuf("b_bq", [128, 4], F32)
            btok = P.sbuf("b_btok", [128, 3664], F32)
            gb = P.sbuf("b_gb", [128, 16], F32)
            rs = P.sbuf("b_rs", [128, 4, 2], F32)
            gl = P.sbuf("b_gl", [128, 80], F32)
            e1 = P.sbuf("b_e1", [128, 8], F32)
            ur = P.sbuf("b_ur", [128, 4, 8, 128], BF16)
            gfm = P.sbuf("b_gfm", [128, 8], F32)
            bgs = P.sbuf("b_bgs", [128, 8], F32)
            P.dma("sp", gfm[:, 0:4], self.qa_g.ap()[l].rearrange("(c p) -> p c", p=128), reads=[self.qa_g], writes=[gfm], allow_slow_non_contiguous=True)
            P.dma("sp", gfm[:, 4:8], self.kva_g.ap()[l].rearrange("(c p) -> p c", p=128), reads=[self.kva_g], writes=[gfm], allow_slow_non_contiguous=True)
            bi = self.b_in.ap()[l]
            fm = [("q", OFF_Q, 512), ("k", OFF_K, 512), ("qa", OFF_QA, 512), ("kva", OFF_KVA, 512), ("u", OFF_U, 1024), ("bg", OFF_BG, 6144)]
            fmc = {}
            c0 = 0
            for name, off, n in fm:
                fmc[name] = c0
                P.dma("sp", binT[:, c0:c0 + n // 128], bi[off:off + n].rearrange("(c p) -> p c", p=128), reads=[self.b_in], writes=[binT], allow_slow_non_contiguous=True)
                c0 += n // 128
            self.ts("dve", bq[:], binT[:, 0:4], 128.0 ** -0.5, ALU.mult, [binT], [bq])
            self.tt("dve", bgs[:], binT[:, fmc["qa"]:fmc["qa"] + 8], gfm[:], ALU.mult, [binT, gfm], [bgs])
            tmo = {"k": 0, "v": 512, "o": 1536, "u": 2560, "g": 3584, "kpe": 3600}
            for name, off, n in (("k", OFF_K, 512), ("v", OFF_V, 1024), ("o", OFF_O, 1024), ("u", OFF_U, 1024), ("g", OFF_G, 16), ("kpe", OFF_KPE, 64)):
                P.dma("sp", btok[:, tmo[name]:tmo[name] + n], bass.AP(self.b_in.h, l * DIN + off, [[0, 128], [1, n]]), reads=[self.b_in], writes=[btok])
            P.dma("sp", gb[:], bass.AP(self.ml_gate_b.h, l * 16, [[0, 128], [1, 16]]), reads=[self.ml_gate_b], writes=[gb])
            self.tt("dve", btok[:, 3584:3600], btok[:, 3584:3600], gb[:], ALU.add, [btok, gb], [btok])
            wv = self.w_in.ap()[l].rearrange("(kc p) n -> p kc n", p=128)
            P.dma("pool", wsm[:, :, 0:16], wv[:, :, OFF_G:OFF_G + 16], reads=[self.w_in], writes=[wsm])
            P.dma("pool", wsm[:, :, 16:80], wv[:, :, OFF_KPE:OFF_KPE + 64], reads=[self.w_in], writes=[wsm])

            for (s0, ng) in self.groups:
                col = 1 if s0 < self.TC else 0
                ntt = ng // 128
                P.dma("sp", xg[:, :, 0:ng], xTl.ap().rearrange("(c p) t -> p c t", p=128)[:, :, s0:s0 + ng], reads=[xTl], writes=[xg])
                self.norm_group(xg, hT, ng, 0, col, sqb, rbc, tmp, self.ps[7])

                def fm_tile(w, wc, name, ci, dst, dt_f32=False):
                    o = self.rot("b_of", of) if dt_f32 else self.rot("b_ob", ob)
                    for j in range(4):
                        ps = self.rot("b_ps", self.ps[0:5])
                        for kc in range(KC):
                            self.mm(ps[:, 0:ng], w[:, kc, wc + j * 128:wc + (j + 1) * 128], hT[:, kc, 0:ng], kc == 0, kc == KC - 1, [w, hT], [ps])
                        bcol = binT[:, fmc[name] + ci + j:fmc[name] + ci + j + 1]
                        if name == "q":
                            self.act(o[:, j, 0:ng], ps[:, 0:ng], AF.Identity, [ps, bq], [], pw=[o], scale=128.0 ** -0.5, bias=bq[:, ci + j:ci + j + 1])
                        elif name == "bg":
                            self.act(o[:, j, 0:ng], ps[:, 0:ng], AF.Sigmoid, [ps, binT], [], pw=[o], bias=bcol)
                        elif name in ("qa", "kva"):
                            which = 0 if name == "qa" else 1
                            self.act(o[:, j, 0:ng], ps[:, 0:ng], AF.Identity, [ps, gfm, bgs], [], pw=[o],
                                     scale=gfm[:, 4 * which + j:4 * which + j + 1], bias=bgs[:, 4 * which + j:4 * which + j + 1])
                            sq = self.rot("b_sqs", sqs)
                            self.act(sq[:, 0:ng], ps[:, 0:ng], AF.Square, [ps, binT], [sq], bias=bcol)
                            which = 0 if name == "qa" else 1
                            for t in range(ntt):
                                self.mm(self.ps[6][:, 8 * which + t:8 * which + t + 1], sq[:, t * 128:(t + 1) * 128], self.ones_b[:, 0:1],
                                        (which == 0 and j == 0 and t == 0), j == 3, [sq, self.cstb], [self.ps[6]])
                        else:
                            eng = "act" if j % 2 == 0 else "dve"
                            if eng == "act":
                                self.act(o[:, j, 0:ng], ps[:, 0:ng], AF.Identity, [ps, binT], [], pw=[o], bias=bcol)
                            else:
                                self.ts("dve", o[:, j, 0:ng], ps[:, 0:ng], bcol, ALU.add, [ps, binT], [], pw=[o])
                    P.dma("act", dst, o[:, :, 0:ng], reads=[o], writes=[dst_t[0]])

                def tm_tile(w, wc, n, name, bo):
                    for t in range(ntt):
                        ps = self.rot("b_ps", self.ps[0:5])
                        for kc in range(KC):
                            self.mm(ps[:, 0:n], hT[:, kc, t * 128:(t + 1) * 128], w[:, kc, wc:wc + n], kc == 0, kc == KC - 1, [w, hT], [ps])
                        r0 = s0 + t * 128
                        if name in ("k", "v"):
                            o = self.rot("b_ot", ot)
                            self.tt("dve", o[:, 0:n], ps[:, 0:n], btok[:, bo:bo + n], ALU.add, [ps, btok], [o])
                            dstT = self.ktok if name == "k" else self.vtok
                            dcol = 0 if name == "k" else tm_c[0]
                            P.dma("act", dstT.ap()[r0:r0 + 128, dcol:dcol + n], o[:, 0:n], reads=[o], writes=[dstT])
                        elif name == "o":
                            f = self.rot("b_otf", otf)
                            o = self.rot("b_ot", ot)
                            self.tt("dve", f[:, 0:n], ps[:, 0:n], btok[:, bo:bo + n], ALU.add, [ps, btok], [f])
                            self.act(o[:, 0:n], f[:, 0:n], AF.Sigmoid, [f], [o])
                            P.dma("act", self.otok.ap()[r0:r0 + 128, tm_c[0]:tm_c[0] + n], o[:, 0:n], reads=[o], writes=[self.otok])
                        elif name == "u":
                            o = self.rot("b_ot", ot)
                            self.tt("dve", o[:, 0:n], ps[:, 0:n], btok[:, bo:bo + n], ALU.add, [ps, btok], [o])
                            pst = self.ps[5]
                            psb = pst[:].bitcast(BF16)
                            for j in range(4):
                                self.tr(psb[:, j * 128:(j + 1) * 128], o[:, j * 128:(j + 1) * 128], self.J_b, [o, self.cstb], [pst])
                            ch0 = tm_c[0] // 128
                            self.copy("act", ur[:, t, ch0:ch0 + 4, :], psb[:, 0:512].rearrange("p (j t) -> p j t", j=4), [pst], [], pw=[ur])
                            if ch0 == 4:
                                i0 = self.mirror(r0)
                                P.dma("act", self.uTr.ap().rearrange("(c p) t -> p c t", p=128)[:, :, i0:i0 + 128], ur[:, t], reads=[ur], writes=[self.uTr])
                        elif name == "gk":
                            self.tt("dve", gl[:], ps[:, 0:80], btok[:, 3584:3664], ALU.add, [ps, btok], [gl])
                            g4 = gl[:, 0:16].rearrange("p (d t h) -> p d t h", d=2, t=2)
                            self.act(e1[:].rearrange("p (d h) -> p d h", d=2), g4[:, :, 1, :], AF.Exp, [gl], [e1], scale=-1.0)
                            self.act(e1[:], e1[:], AF.Ln, [e1], [e1], bias=1.0)
                            self.ts("dve", g4[:, :, 1, :], e1[:].rearrange("p (d h) -> p d h", d=2), -1.0, ALU.mult, [e1], [gl])
                            P.dma("act", self.gtok.ap()[r0:r0 + 128, :], gl[:, 0:16], reads=[gl], writes=[self.gtok])
                            P.dma("act", self.kpe.ap()[r0:r0 + 128, :], gl[:, 16:80], reads=[gl], writes=[self.kpe])

                def load_w(off):
                    w = self.rot("b_w", wt)
                    P.dma("pool", w[:], wv[:, :, off:off + 512], reads=[self.w_in], writes=[w])
                    return w

                def fmdst(Tt, row0, f32=False):
                    return Tt.ap().rearrange("(j p) t -> p j t", p=128)[:, row0 // 128:row0 // 128 + 4, s0:s0 + ng]

                dst_t = [None]
                tm_c = [0]
                w = load_w(OFF_Q)
                dst_t[0] = self.qT
                fm_tile(w, 0, "q", 0, self.qT.ap().rearrange("h p t -> p h t")[:, :, s0:s0 + ng])
                w = load_w(OFF_K)
                dst_t[0] = self.kT
                fm_tile(w, 0, "k", 0, self.kT.ap().rearrange("h p t -> p h t")[:, :, s0:s0 + ng])
                tm_tile(w, 0, 512, "k", tmo["k"])
                for h2 in range(2):
                    w = load_w(OFF_V + 512 * h2)
                    tm_c[0] = 512 * h2
                    tm_tile(w, 0, 512, "v", tmo["v"] + 512 * h2)
                for h2 in range(2):
                    w = load_w(OFF_O + 512 * h2)
                    tm_c[0] = 512 * h2
                    tm_tile(w, 0, 512, "o", tmo["o"] + 512 * h2)
                tm_tile(wsm, 0, 80, "gk", 0)
                for name, off, Tt in (("qa", OFF_QA, self.qaT), ("kva", OFF_KVA, self.kvaT)):
                    w = load_w(off)
                    dst_t[0] = Tt
                    fm_tile(w, 0, name, 0, fmdst(Tt, 0))
                for which in range(2):
                    self.act(rs[:, 0:ntt, which], self.ps[6][:, 8 * which:8 * which + ntt], AF.Sqrt, [self.ps[6], self.eps_t], [], pw=[rs], scale=1.0 / 512, bias=self.eps_t[:, 0:1])
                P.op("dve", lambda e: e.reciprocal(out=rs[:, 0:ntt, :], in_=rs[:, 0:ntt, :]), reads=[rs], writes=[rs])
                P.dma("act", self.rsq.ap()[s0:s0 + ng, :].rearrange("(t p) c -> p t c", p=128), rs[:, 0:ntt, :], reads=[rs], writes=[self.rsq])
                for h2 in range(2):
                    w = load_w(OFF_U + 512 * h2)
                    dst_t[0] = self.uT
                    fm_tile(w, 0, "u", 4 * h2, fmdst(self.uT, 512 * h2), dt_f32=True)
                    tm_c[0] = 512 * h2
                    tm_tile(w, 0, 512, "u", tmo["u"] + 512 * h2)
                for b12 in range(12):
                    w = load_w(OFF_BG + 512 * b12)
                    dst_t[0] = self.gT
                    fm_tile(w, 0, "bg", 4 * b12, fmdst(self.gT, 512 * b12))

    def mirror(self, r0, n=128):
        TC, TL = self.TC, self.TL
        if r0 < TC:
            return TC - n - r0
        return TC + (TL - n - (r0 - TC))


def make_consts(TL, TC):
    cst = np.zeros((128, 640), np.float32)
    cst[:, 0:128] = np.eye(128)
    cst[:, 128:256] = np.eye(128)[::-1]
    k = np.arange(64)
    cst[0:64, 256:320] = (k[:, None] <= k[None, :])
    cst[0:64, 320:384] = (k[:, None] >= k[None, :])
    cst[:, 384:512] = 1.0
    TT = TL + TC
    rope = np.zeros((TT, 128), np.float32)
    rope[:, 0:64] = 1.0
    t = np.arange(TL)
    row = (t // 64).astype(np.float32)
    col = (t % 64).astype(np.float32)
    inv = (np.float32(10000.0) ** (-np.arange(16, dtype=np.float32) / np.float32(16))).astype(np.float32)
    ar = (row[:, None] * inv).astype(np.float32)
    ac = (col[:, None] * inv).astype(np.float32)
    rope[TC:, 0:64] = np.concatenate([np.cos(ar), np.cos(ar), np.cos(ac), np.cos(ac)], axis=1)
    rope[TC:, 64:128] = np.concatenate([-np.sin(ar), np.sin(ar), -np.sin(ac), np.sin(ac)], axis=1)
    return cst, rope


def prep_shared(inp, L):
    f = lambda a: np.ascontiguousarray(np.asarray(a, dtype=np.float32))
    sh = {}
    for k in ("w_mod", "b_mod", "norm_g", "w_in", "b_in", "mla_qa_g", "mla_kva_g", "mla_w_uq", "mla_w_ukv", "mla_qn_g", "mla_kn_g",
              "s5_d", "s5_w_glu", "s5_b_glu", "w_branch", "w_out", "w_ff1", "w_ff2"):
        sh[k] = f(inp[k])[:L]
    sh["ml_gate_b"] = f(inp["ml_gate_b"])[:L].reshape(L, 16)
    sh["ml_norm_g"] = f(inp["ml_norm_g"])[:L].reshape(L, 1024)
    lam = np.zeros((L, 2, 3, 128, 32), np.float32)
    sb = np.zeros((L, 2, 2, 128, 8, 2, 128), np.float32)
    sc = np.zeros((L, 2, 2, 128, 32, 32), np.float32)
    a_re, a_im, ldt = f(inp["s5_a_re"]), f(inp["s5_a_im"]), f(inp["s5_log_dt"])
    bs = (f(inp["s5_b_re"]), f(inp["s5_b_im"]))
    cs = (f(inp["s5_c_re"]), f(inp["s5_c_im"]))

    def lay(a):
        return a.reshape(32, 2, 64).transpose(1, 2, 0).reshape(128, 32)

    for l in range(L):
        for d in range(2):
            lam[l, d, 0] = lay(a_re[l, d])
            lam[l, d, 1] = lay(a_im[l, d])
            lam[l, d, 2] = lay(np.repeat(ldt[l, d][:, None], 64, axis=1))
            for ri in range(2):
                b = bs[ri][l, d]
                c = cs[ri][l, d]
                for g in range(64):
                    q, pp, g2 = g // 8, (g % 8) // 2, g % 2
                    sb[l, d, ri, pp * 32 + g2 * 16:pp * 32 + g2 * 16 + 16, q, 0, g2 * 64:(g2 + 1) * 64] = b[g].T
                    if pp == 3:
                        sb[l, d, ri, pp * 32 + g2 * 16:pp * 32 + g2 * 16 + 16, q, 1, g2 * 64:(g2 + 1) * 64] = b[g].T
                    sc[l, d, ri, g2 * 64:(g2 + 1) * 64, g // 2, g2 * 16:(g2 + 1) * 16] = c[g].T
    sh["s5_lam"], sh["s5_b"], sh["s5_c"] = lam, sb, sc
    return sh


def prep_core(inp, b, TL, TC):
    f = lambda a: np.ascontiguousarray(np.asarray(a, dtype=np.float32))
    return {
        "x": f(inp["x"][b]),
        "ctx": f(inp["ctx"][b]),
        "cvec": f(np.stack([np.asarray(inp["c"][b]), np.asarray(inp["c_ctx"])])),
    }


PHASES = ("A", "B", "C1", "C2", "C3", "DE")


def build_program(TL, TC, depth=DEPTH, debug=(), stop_after=None):
    B = Builder(TL, TC, depth, debug, stop_after)
    B.declare()
    B.phase_T0()
    done = False
    for l in range(depth):
        for name in PHASES:
            getattr(B, "phase_" + name)(l)
            if stop_after == (name, l):
                done = True
                break
        if done:
            break
    B.P.finish()
    return B


def phase_C1(self, l):
    P, TC, TL, TT = self.P, self.TC, self.TL, self.TT
    with P.scope():
        QTg = [P.sbuf("m_q%d" % d, [128, 4, 512], BF16) for d in range(2)]
        KTg = [P.sbuf("m_k%d" % d, [128, 4, 512], BF16) for d in range(2)]
        Ktg = [P.sbuf("m_kt%d" % d, [64, 8, 512], BF16) for d in range(2)]
        Vg = [P.sbuf("m_v%d" % d, [64, 8, 1024], BF16) for d in range(2)]
        Gg = [P.sbuf("m_g%d" % d, [64, 8, 16], F32) for d in range(2)]
        Cf = [P.sbuf("m_cf%d" % d, [128, 4, 257], F32) for d in range(2)]
        Cb = [P.sbuf("m_cb%d" % d, [128, 4, 257], BF16) for d in range(2)]
        sm = [[P.sbuf("m_s%d_%d" % (d, i), [128, 40], F32) for i in range(2)] for d in range(2)]
        smb = [[P.sbuf("m_sb%d_%d" % (d, i), [64, 8], BF16) for i in range(2)] for d in range(2)]
        Vs = [[P.sbuf("m_vs%d_%d" % (d, i), [64, 4, 256], BF16) for i in range(2)] for d in range(2)]
        VF = [[P.sbuf("m_vf%d_%d" % (d, i), [64, 4, 256], BF16) for i in range(2)] for d in range(2)]
        Sm = [[P.sbuf("m_sm%d_%d" % (d, i), [64, 4, 64], BF16) for i in range(2)] for d in range(2)]
        hst = [[P.sbuf("m_h%d_%d" % (d, i), [64, 4, 256], F32) for i in range(2)] for d in range(2)]
        for d in range(2):
            P.op("dve", lambda e, d=d: e.memset(Cf[d][:], 0.0), writes=[Cf[d]])
            P.op("pool", lambda e, d=d: e.memset(Cb[d][:], 0.0), writes=[Cb[d]])
        order = []
        for d in range(2):
            o = []
            glist = self.groups if d == 0 else [self.groups[0]] + self.groups[1:][::-1]
            for gi, (s0, ng) in enumerate(glist):
                cis = list(range(ng // 64))
                if d == 1:
                    cis = cis[::-1]
                for ci in cis:
                    o.append((s0, ng, ci))
            order.append(o)
        cur = [None, None]
        psm = [self.ps[0], self.ps[5]]
        pacc = [(self.ps[1], self.ps[2]), (self.ps[6], self.ps[7])]
        pdc = (self.ps[3], self.ps[4])
        for step in range(len(order[0])):
            for d in range(2):
                s0, ng, ci = order[d][step]
                nch = ng // 64
                if cur[d] != s0:
                    cur[d] = s0
                    P.dma("sp", QTg[d][:, :, 0:ng], self.qT.ap().rearrange("h p t -> p h t")[:, :, s0:s0 + ng], reads=[self.qT], writes=[QTg[d]])
                    P.dma("sp", KTg[d][:, :, 0:ng], self.kT.ap().rearrange("h p t -> p h t")[:, :, s0:s0 + ng], reads=[self.kT], writes=[KTg[d]])
                    P.dma("sp", Ktg[d][:, 0:nch, :], self.ktok.ap()[s0:s0 + ng, :].rearrange("(c p) n -> p c n", p=64), reads=[self.ktok], writes=[Ktg[d]])
                    P.dma("sp", Vg[d][:, 0:nch, :], self.vtok.ap()[s0:s0 + ng, :].rearrange("(c p) n -> p c n", p=64), reads=[self.vtok], writes=[Vg[d]])
                    P.dma("sp", Gg[d][:, 0:nch, :], self.gtok.ap()[s0:s0 + ng, :].rearrange("(c p) n -> p c n", p=64), reads=[self.gtok], writes=[Gg[d]])
                c0 = ci * 64
                t0 = s0 + c0
                par = step % 2
                S_, SB_, Vs_, VF_, Sm_, H_ = sm[d][par], smb[d][par], Vs[d][par], VF[d][par], Sm[d][par], hst[d][par]
                pA, (pa0, pa1), (pd0, pd1) = psm[d], pacc[d], pdc
                tri = self.triL_f if d == 0 else self.triU_f
                li = Gg[d][:, ci, d * 8:d * 8 + 4]
                lf = Gg[d][:, ci, d * 8 + 4:d * 8 + 8]
                cst = self.cst_t
                self.mm(pA[0:64, 0:4], tri, lf, True, True, [cst, Gg[d]], [pA])
                self.mm(pA[:, 8:12], self.ones_f[0:64, :], lf, True, True, [cst, Gg[d]], [pA])
                dif, esc, ecn, eF, escF, absd, rden = (S_[0:64, 0:4], S_[0:64, 4:8], S_[0:64, 8:12], S_[:, 12:16], S_[0:64, 16:20], S_[0:64, 20:24], S_[0:64, 24:28])
                self.tt("dve", dif, li, pA[0:64, 0:4], ALU.subtract, [Gg[d], pA], [S_])
                self.act(esc, dif, AF.Exp, [S_], [S_])
                self.act(ecn, pA[0:64, 0:4], AF.Exp, [pA], [S_], scale=-1.0)
                self.act(eF, pA[:, 8:12], AF.Exp, [pA], [S_])
                self.tt("dve", escF, esc, eF[0:64, :], ALU.mult, [S_], [S_])
                self.copy("act", SB_[:, 0:4], esc, [S_], [SB_])
                self.copy("act", SB_[:, 4:8], escF, [S_], [SB_])
                vv = Vg[d][:, ci, :].rearrange("p (h v) -> p h v", h=4)
                self.tt("dve", Vs_[:], vv, bc(esc.unsqueeze(2), [64, 4, 256]), ALU.mult, [Vg[d], S_], [Vs_])
                self.tt("pool", VF_[:], vv, bc(escF.unsqueeze(2), [64, 4, 256]), ALU.mult, [Vg[d], S_], [VF_])
                for h in range(4):
                    self.mm(pA[0:64, 256 + h * 64:256 + (h + 1) * 64], KTg[d][:, h, c0:c0 + 64], QTg[d][:, h, c0:c0 + 64], True, True, [KTg[d], QTg[d]], [pA])
                self.tt("dve", Sm_[:], pA[0:64, 256:512].rearrange("p (h t) -> p h t", h=4), bc(tri.unsqueeze(1), [64, 4, 64]), ALU.mult, [pA, cst], [Sm_])
                for h in range(4):
                    bank = pa0 if h < 2 else pa1
                    off = (h % 2) * 256
                    self.mm(bank[0:64, off:off + 256], QTg[d][:, h, c0:c0 + 64], Cb[d][:, h, 0:256], True, False, [QTg[d], Cb[d]], [bank])
                    self.mm(bank[0:64, off:off + 256], Sm_[:, h, :], Vs_[:, h, :], False, True, [Sm_, Vs_], [bank])
                for h in range(4):
                    self.mm(pA[0:64, 16 + h:17 + h], QTg[d][:, h, c0:c0 + 64], Cb[d][:, h, 256:257], True, False, [QTg[d], Cb[d]], [pA])
                    self.mm(pA[0:64, 16 + h:17 + h], Sm_[:, h, :], SB_[:, h:h + 1], False, True, [Sm_, SB_], [pA])
                for h in range(4):
                    bank = pd0 if h < 2 else pd1
                    off = (h % 2) * 256
                    self.mm(bank[:, off:off + 256], Ktg[d][:, ci, h * 128:(h + 1) * 128], VF_[:, h, :], True, True, [Ktg[d], VF_], [bank])
                    self.mm(pA[:, 24 + h:25 + h], Ktg[d][:, ci, h * 128:(h + 1) * 128], SB_[:, 4 + h:5 + h], True, True, [Ktg[d], SB_], [pA])
                self.act(absd, pA[0:64, 16:20], AF.Abs, [pA], [S_])
                self.tt("dve", rden, absd, ecn, ALU.max, [S_], [S_])
                P.op("dve", lambda e, rden=rden: e.reciprocal(out=rden, in_=rden), reads=[S_], writes=[S_])
                for b2, bank in enumerate((pa0, pa1)):
                    self.tt("dve", H_[:, 2 * b2:2 * b2 + 2, :], bank[0:64, :].rearrange("p (h v) -> p h v", h=2),
                            bc(rden[:, 2 * b2:2 * b2 + 2].unsqueeze(2), [64, 2, 256]), ALU.mult, [bank, S_], [], pw=[H_])
                P.dma("act", self.hml.ap()[d, t0:t0 + 64, :], H_[:].rearrange("p h v -> p (h v)"), reads=[H_], writes=[self.hml])
                for h in range(4):
                    bank = pd0 if h < 2 else pd1
                    off = (h % 2) * 256
                    self.stt(Cf[d][:, h, 0:256], Cf[d][:, h, 0:256], eF[:, h:h + 1], bank[:, off:off + 256], ALU.mult, ALU.add, [Cf[d], S_, bank], [Cf[d]])
                    self.stt(Cf[d][:, h, 256:257], Cf[d][:, h, 256:257], eF[:, h:h + 1], pA[:, 24 + h:25 + h], ALU.mult, ALU.add, [Cf[d], S_, pA], [Cf[d]])
                self.copy("act", Cb[d][:], Cf[d][:], [Cf[d]], [Cb[d]])

    with P.scope():
        gnb = P.sbuf("mo_gn", [128, 1024], F32)
        P.dma("sp", gnb[:], bass.AP(self.ml_norm_g.h, l * 1024, [[0, 128], [1, 1024]]), reads=[self.ml_norm_g], writes=[gnb])
        h0 = [P.sbuf("mo_h0_%d" % i, [128, 1024], F32) for i in range(2)]
        h1 = [P.sbuf("mo_h1_%d" % i, [128, 1024], F32) for i in range(2)]
        og = [P.sbuf("mo_og_%d" % i, [128, 1024], BF16) for i in range(2)]
        sq = P.sbuf("mo_sq", [128, 1024], F32)
        ss = [P.sbuf("mo_ss%d" % i, [128, 4], F32) for i in range(2)]
        ab = [P.sbuf("mo_ab%d" % i, [128, 1024], BF16) for i in range(2)]
        aT = [P.sbuf("mo_aT%d" % i, [128, 8, 512], BF16) for i in range(2)]
        for gi, (s0, ng) in enumerate(self.groups):
            A_ = aT[gi % 2]
            for t in range(ng // 128):
                r0 = s0 + t * 128
                i = self.rr.get("m_o", 0)
                self.rr["m_o"] = i + 1
                a0, a1, o_, s_, b_ = h0[i % 2], h1[i % 2], og[i % 2], ss[i % 2], ab[i % 2]
                P.dma("sp", a0[:], self.hml.ap()[0, r0:r0 + 128, :], reads=[self.hml], writes=[a0])
                P.dma("sp", a1[:], self.hml.ap()[1, r0:r0 + 128, :], reads=[self.hml], writes=[a1])
                P.dma("sp", o_[:], self.otok.ap()[r0:r0 + 128, :], reads=[self.otok], writes=[o_])
                self.tt("dve", a0[:], a0[:], a1[:], ALU.add, [a0, a1], [a0])
                self.act(sq[:], a0[:], AF.Square, [a0], [sq])
                P.op("dve", lambda e, s_=s_: e.tensor_reduce(out=s_[:], in_=sq[:].rearrange("p (h v) -> p h v", h=4), axis=AX.X, op=ALU.add), reads=[sq], writes=[s_])
                self.act(s_[:], s_[:], AF.Sqrt, [s_, self.eps_t], [s_], scale=1.0 / 256, bias=self.eps_t[:, 0:1])
                P.op("dve", lambda e, s_=s_: e.reciprocal(out=s_[:], in_=s_[:]), reads=[s_], writes=[s_])
                a3 = a0[:].rearrange("p (h v) -> p h v", h=4)
                self.tt("dve", a3, a3, bc(s_[:].unsqueeze(2), [128, 4, 256]), ALU.mult, [a0, s_], [a0])
                self.tt("pool", a0[:], a0[:], gnb[:], ALU.mult, [a0, gnb], [a0])
                self.tt("dve", b_[:], a0[:], o_[:], ALU.mult, [a0, o_], [b_])
                pst = self.rot("m_pst", self.ps[0:2])
                psb = pst[:].bitcast(BF16)
                for j in range(8):
                    self.tr(psb[:, j * 128:(j + 1) * 128], b_[:, j * 128:(j + 1) * 128], self.ident_b, [b_, self.cstb], [pst])
                self.copy("act", A_[:, :, t * 128:(t + 1) * 128], psb[:, 0:1024].rearrange("p (j t) -> p j t", j=8), [pst], [], pw=[A_])
            P.dma("act", self.brT.ap()[0].rearrange("(c p) t -> p c t", p=128)[:, :, s0:s0 + ng], A_[:, :, 0:ng], reads=[A_], writes=[self.brT])


Builder.phase_C1 = phase_C1


def rope_ops(self, eng, src, dst, tmp1, tmp2, rt, nh, reads, writes):
    C = bc(rt[:, 0:64].unsqueeze(1), [128, nh, 64])
    S5 = rt[:, 64:128].rearrange("p (b f i) -> p b f i", b=2, f=2)
    s5 = src.rearrange("p h (b f i) -> p h b f i", b=2, f=2)
    t5 = tmp2.rearrange("p h (b f i) -> p h b f i", b=2, f=2)
    self.tt(eng, tmp1, src, C, ALU.mult, reads, writes)
    for f in range(2):
        self.tt(eng, t5[:, :, :, f, :], s5[:, :, :, 1 - f, :], bc(S5[:, :, f, :].unsqueeze(1), [128, nh, 2, 16]), ALU.mult, reads, writes)
    self.tt(eng, dst, tmp1, tmp2, ALU.add, reads, writes)


def phase_C2(self, l):
    P, TC, TL, TT, NT = self.P, self.TC, self.TL, self.TT, self.NT
    with P.scope():
        wq = P.sbuf("c_wq", [128, 4, 1536], BF16)
        wkv = P.sbuf("c_wkv", [128, 4, 2048], BF16)
        gq = P.sbuf("c_gq", [128, 192], F32)
        gk = P.sbuf("c_gk", [128, 192], F32)
        P.dma("pool", wq[:], self.w_uq.ap()[l].rearrange("(kc p) n -> p kc n", p=128), reads=[self.w_uq], writes=[wq])
        P.dma("pool", wkv[:], self.w_ukv.ap()[l].rearrange("(kc p) n -> p kc n", p=128), reads=[self.w_ukv], writes=[wkv])
        P.dma("sp", gq[:], bass.AP(self.qn_g.h, l * 192, [[0, 128], [1, 192]]), reads=[self.qn_g], writes=[gq])
        P.dma("sp", gk[:], bass.AP(self.kn_g.h, l * 192, [[0, 128], [1, 192]]), reads=[self.kn_g], writes=[gk])
        qaTg = P.sbuf("c_qa", [128, 4, 512], BF16)
        kvaTg = P.sbuf("c_kva", [128, 4, 512], BF16)
        rsg = P.sbuf("c_rs", [128, 4, 2], F32)
        rpt = [P.sbuf("c_rp%d" % i, [128, 128], F32) for i in range(2)]
        kpt = [P.sbuf("c_kp%d" % i, [128, 64], F32) for i in range(2)]
        qf = P.sbuf("c_qf", [128, 2048], F32)
        sqv = P.sbuf("c_sq", [128, 2048], F32)
        fin = [P.sbuf("c_fin%d" % i, [128, 8, 192], BF16) for i in range(2)]
        vfin = [P.sbuf("c_vf%d" % i, [128, 8, 128], BF16) for i in range(2)]
        rp = P.sbuf("c_rpp", [128, 8, 64], F32)
        t1 = P.sbuf("c_t1", [128, 8, 64], F32)
        t2 = P.sbuf("c_t2", [128, 8, 64], F32)
        st = [P.sbuf("c_st%d" % i, [128, 12], F32) for i in range(2)]
        kk = P.sbuf("c_kk", [128, 4, 64], F32)
        stn = [P.sbuf("c_stn%d" % i, [128, 8, 512], BF16) for i in range(2)]
        str_ = [P.sbuf("c_str%d" % i, [64, 8, 512], BF16) for i in range(2)]
        for (s0, ng) in self.groups:
            ntt = ng // 128
            P.dma("sp", qaTg[:, :, 0:ng], self.qaT.ap().rearrange("(c p) t -> p c t", p=128)[:, :, s0:s0 + ng], reads=[self.qaT], writes=[qaTg])
            P.dma("sp", kvaTg[:, :, 0:ng], self.kvaT.ap().rearrange("(c p) t -> p c t", p=128)[:, :, s0:s0 + ng], reads=[self.kvaT], writes=[kvaTg])
            P.dma("sp", rsg[:, 0:ntt, :], self.rsq.ap()[s0:s0 + ng, :].rearrange("(t p) c -> p t c", p=128), reads=[self.rsq], writes=[rsg])
            qn_, qr_ = stn[0], str_[0]
            kn_, kr_ = stn[1], str_[1]
            for t in range(ntt):
                r0 = s0 + t * 128
                i = self.rr.get("c_i", 0)
                self.rr["c_i"] = i + 1
                rt, kp, S_ = rpt[i % 2], kpt[i % 2], st[i % 2]
                P.dma("sp", rt[:], self.rope.ap()[r0:r0 + 128, :], reads=[self.rope], writes=[rt])
                P.dma("sp", kp[:], self.kpe.ap()[r0:r0 + 128, :], reads=[self.kpe], writes=[kp])
                for cg in range(3):
                    ps = self.rot("c_ps", self.ps[0:6])
                    for kc in range(4):
                        self.mm(ps[:, 0:512], qaTg[:, kc, t * 128:(t + 1) * 128], wq[:, kc, cg * 512:(cg + 1) * 512], kc == 0, kc == 3, [qaTg, wq], [ps])
                    self.act(qf[:, cg * 512:(cg + 1) * 512], ps[:, 0:512], AF.Identity, [ps, rsg], [], pw=[qf], scale=rsg[:, t, 0:1])
                    self.act(sqv[:, cg * 512:(cg + 1) * 512], ps[:, 0:512], AF.Square, [ps, rsg], [], pw=[sqv], scale=rsg[:, t, 0:1])
                q3 = qf[:, 0:1536].rearrange("p (h e) -> p h e", h=8)
                s3 = sqv[:, 0:1536].rearrange("p (h e) -> p h e", h=8)
                rq = S_[:, 0:8]
                P.op("dve", lambda e, rq=rq, s3=s3: e.tensor_reduce(out=rq, in_=s3, axis=AX.X, op=ALU.add), reads=[sqv], writes=[S_])
                self.act(rq, rq, AF.Sqrt, [S_, self.eps_t], [S_], scale=1.0 / 192, bias=self.eps_t[:, 0:1])
                P.op("dve", lambda e, rq=rq: e.reciprocal(out=rq, in_=rq), reads=[S_], writes=[S_])
                self.tt("dve", s3, q3, bc(rq.unsqueeze(2), [128, 8, 192]), ALU.mult, [qf, S_], [sqv])
                F_ = fin[0]
                self.tt("dve", F_[:, :, 0:128], s3[:, :, 0:128], bc(gq[:, 0:128].unsqueeze(1), [128, 8, 128]), ALU.mult, [sqv, gq], [F_])
                self.tt("pool", rp[:], s3[:, :, 128:192], bc(gq[:, 128:192].unsqueeze(1), [128, 8, 64]), ALU.mult, [sqv, gq], [rp])
                rope_ops(self, "dve", rp[:], F_[:, :, 128:192], t1[:], t2[:], rt, 8, [rp, rt, t1, t2], [t1, t2, F_])
                pn, pr = self.ps[6], self.ps[7]
                pnb, prb = pn[:].bitcast(BF16), pr[:].bitcast(BF16)
                for h in range(8):
                    self.tr(pnb[:, h * 128:(h + 1) * 128], F_[:, h, 0:128], self.ident_b, [F_, self.cstb], [pn])
                    self.tr(prb[0:64, h * 128:(h + 1) * 128], F_[:, h, 128:192], self.ident_b, [F_, self.cstb], [pr])
                self.copy("act", qn_[:, :, t * 128:(t + 1) * 128], pnb[:, 0:1024].rearrange("p (h t) -> p h t", h=8), [pn], [], pw=[qn_])
                self.copy("act", qr_[:, :, t * 128:(t + 1) * 128], prb[0:64, 0:1024].rearrange("p (h t) -> p h t", h=8), [pr], [], pw=[qr_])
                for cg in range(4):
                    ps = self.rot("c_ps", self.ps[0:6])
                    for kc in range(4):
                        self.mm(ps[:, 0:512], kvaTg[:, kc, t * 128:(t + 1) * 128], wkv[:, kc, cg * 512:(cg + 1) * 512], kc == 0, kc == 3, [kvaTg, wkv], [ps])
                    self.act(qf[:, cg * 512:(cg + 1) * 512], ps[:, 0:512], AF.Identity, [ps, rsg], [], pw=[qf], scale=rsg[:, t, 1:2])
                k3 = qf[:].rearrange("p (h e) -> p h e", h=8)
                s3k = sqv[:, 0:1024].rearrange("p (h e) -> p h e", h=8)
                self.act(s3k, k3[:, :, 0:128], AF.Square, [qf], [sqv])
                rk = S_[:, 0:8]
                sp_ = S_[:, 8:9]
                P.op("dve", lambda e, rk=rk, s3k=s3k: e.tensor_reduce(out=rk, in_=s3k, axis=AX.X, op=ALU.add), reads=[sqv], writes=[S_])
                self.act(kk[:, 0, :], kp[:], AF.Square, [kp], [kk, S_], accum_out=sp_)
                self.ts("dve", rk, rk, sp_, ALU.add, [S_], [S_])
                self.act(rk, rk, AF.Sqrt, [S_, self.eps_t], [S_], scale=1.0 / 192, bias=self.eps_t[:, 0:1])
                P.op("dve", lambda e, rk=rk: e.reciprocal(out=rk, in_=rk), reads=[S_], writes=[S_])
                Fk = fin[1]
                self.tt("dve", s3k, k3[:, :, 0:128], bc(rk.unsqueeze(2), [128, 8, 128]), ALU.mult, [qf, S_], [sqv])
                self.tt("dve", Fk[:, :, 0:128], s3k, bc(gk[:, 0:128].unsqueeze(1), [128, 8, 128]), ALU.mult, [sqv, gk], [Fk])
                self.tt("pool", kk[:, 1, :], kp[:], gk[:, 128:192], ALU.mult, [kp, gk], [kk])
                rope_ops(self, "pool", kk[:, 1:2, :], kk[:, 0:1, :], kk[:, 2:3, :], kk[:, 3:4, :], rt, 1, [kk, rt], [kk])
                self.tt("dve", Fk[:, :, 128:192], bc(kk[:, 0:1, :], [128, 8, 64]), bc(rk.unsqueeze(2), [128, 8, 64]), ALU.mult, [kk, S_], [Fk])
                V_ = vfin[i % 2]
                self.copy("act", V_[:], k3[:, :, 128:256], [qf], [V_])
                P.dma("act", self.vh.ap()[r0:r0 + 128, :], V_[:].rearrange("p h e -> p (h e)"), reads=[V_], writes=[self.vh])
                pn, pr = self.ps[6], self.ps[7]
                for h in range(8):
                    self.tr(pnb[:, h * 128:(h + 1) * 128], Fk[:, h, 0:128], self.ident_b, [Fk, self.cstb], [pn])
                    self.tr(prb[0:64, h * 128:(h + 1) * 128], Fk[:, h, 128:192], self.ident_b, [Fk, self.cstb], [pr])
                self.copy("act", kn_[:, :, t * 128:(t + 1) * 128], pnb[:, 0:1024].rearrange("p (h t) -> p h t", h=8), [pn], [], pw=[kn_])
                self.copy("act", kr_[:, :, t * 128:(t + 1) * 128], prb[0:64, 0:1024].rearrange("p (h t) -> p h t", h=8), [pr], [], pw=[kr_])
            for Tt, n_, r_ in ((self.qhT, qn_, qr_), (self.khT, kn_, kr_)):
                P.dma("act", Tt.ap()[:, 0:128, s0:s0 + ng].rearrange("h d t -> d h t"), n_[:, :, 0:ng], reads=[n_], writes=[Tt])
                P.dma("act", Tt.ap()[:, 128:192, s0:s0 + ng].rearrange("h d t -> d h t"), r_[:, :, 0:ng], reads=[r_], writes=[Tt])
    if "qk_only" in self.debug:
        return
    with P.scope():
        Kn = [P.sbuf("c_Kn%d" % i, [128, TT], BF16) for i in range(2)]
        Kr = [P.sbuf("c_Kr%d" % i, [64, TT], BF16) for i in range(2)]
        Vh = [P.sbuf("c_Vh%d" % i, [128, NT, 132], BF16) for i in range(2)]
        Qn = [P.sbuf("c_Qn%d" % i, [128, 512], BF16) for i in range(2)]
        Qr = [P.sbuf("c_Qr%d" % i, [64, 512], BF16) for i in range(2)]
        pT = [P.sbuf("c_pT%d" % i, [128, 512], BF16) for i in range(3)]
        ob = [P.sbuf("c_ob%d" % i, [128, 4, 128], BF16) for i in range(2)]
        rd = [P.sbuf("c_rd%d" % i, [128, 4], F32) for i in range(2)]
        bst = [P.sbuf("c_bst%d" % i, [128, 512], BF16) for i in range(2)]
        for i in range(2):
            P.op("pool", lambda e, i=i: e.memset(Vh[i][:], 1.0), writes=[Vh[i]])
        sc = 192.0 ** -0.5
        for h in range(8):
            K1, K2, V_ = Kn[h % 2], Kr[h % 2], Vh[h % 2]
            P.dma("sp", K1[:], self.khT.ap()[h, 0:128, :], reads=[self.khT], writes=[K1])
            P.dma("sp", K2[:], self.khT.ap()[h, 128:192, :], reads=[self.khT], writes=[K2])
            P.dma("sp", V_[:, :, 0:128], self.vh.ap()[:, h * 128:(h + 1) * 128].rearrange("(kt p) e -> p kt e", p=128), reads=[self.vh], writes=[V_])
            for (q0, nq) in self.groups:
                i = self.rr.get("c_q", 0)
                self.rr["c_q"] = i + 1
                Q1, Q2, O_, R_, B_ = Qn[i % 2], Qr[i % 2], ob[i % 2], rd[i % 2], bst[i % 2]
                P.dma("sp", Q1[:, 0:nq], self.qhT.ap()[h, 0:128, q0:q0 + nq], reads=[self.qhT], writes=[Q1])
                P.dma("sp", Q2[:, 0:nq], self.qhT.ap()[h, 128:192, q0:q0 + nq], reads=[self.qhT], writes=[Q2])
                nqt = nq // 128
                pO = self.ps[0:4]
                nkt = TC // 128 if q0 < TC else NT
                for kt in range(nkt):
                    pS = self.rot("c_pS", self.ps[4:8])
                    self.mm(pS[:, 0:nq], K1[:, kt * 128:(kt + 1) * 128], Q1[:, 0:nq], True, False, [K1, Q1], [pS])
                    self.mm(pS[:, 0:nq], K2[:, kt * 128:(kt + 1) * 128], Q2[:, 0:nq], False, True, [K2, Q2], [pS])
                    p_ = self.rot("c_pT", pT)
                    self.act(p_[:, 0:nq], pS[:, 0:nq], AF.Exp, [pS], [p_], scale=sc)
                    for j in range(nqt):
                        self.mm(pO[j][:, 0:129], p_[:, j * 128:(j + 1) * 128], V_[:, kt, 0:129], kt == 0, kt == nkt - 1, [p_, V_], [pO[j]])
                pS = self.rot("c_pS", self.ps[4:8])
                psb = pS[:].bitcast(BF16)
                for j in range(nqt):
                    P.op("dve", lambda e, j=j, R_=R_: e.reciprocal(out=R_[:, j:j + 1], in_=pO[j][:, 128:129]), reads=[pO[j]], writes=[R_])
                    self.act(O_[:, j, :], pO[j][:, 0:128], AF.Identity, [pO[j], R_], [], pw=[O_], scale=R_[:, j:j + 1])
                    self.tr(psb[:, j * 128:(j + 1) * 128], O_[:, j, :], self.ident_b, [O_, self.cstb], [pS])
                self.copy("dve", B_[:, 0:nq], psb[:, 0:nq], [pS], [B_])
                P.dma("act", self.brT.ap()[1, h * 128:(h + 1) * 128, q0:q0 + nq], B_[:, 0:nq], reads=[B_], writes=[self.brT])


Builder.phase_C2 = phase_C2


def phase_C3(self, l):
    P, TC, TL, TT = self.P, self.TC, self.TL, self.TT
    TWO_PI = 2.0 * math.pi
    for d in range(2):
        with P.scope():
            lam = P.sbuf("s_lam", [128, 3, 32], F32)
            sc = P.sbuf("s_sc", [128, 24, 32], F32)
            Er = P.sbuf("s_Er", [128, 32, SL], F32)
            Ei = P.sbuf("s_Ei", [128, 32, SL], F32)
            Tr = P.sbuf("s_Tr", [128, 32, SL], F32)
            Ti = P.sbuf("s_Ti", [128, 32, SL], F32)
            tmpE = P.sbuf("s_tmpE", [128, 32, SL], F32)
            Bw = P.sbuf("s_Bw", [128, 2, 8, 2, 128], BF16)
            Cw = P.sbuf("s_Cw", [128, 2, 32, 32], BF16)
            Cn = P.sbuf("s_Cn", [128, 2, 32, 32], BF16)
            xst = P.sbuf("s_xst", [128, 32, 2], F32)
            P.dma("sp", lam[:], self.s5_lam.ap()[l, d].rearrange("i p q -> p i q"), reads=[self.s5_lam], writes=[lam])
            P.dma("pool", Bw[:], self.s5_b.ap()[l, d].rearrange("r p q v n -> p r q v n"), reads=[self.s5_b], writes=[Bw])
            P.dma("pool", Cw[:], self.s5_c.ap()[l, d].rearrange("r p q n -> p r q n"), reads=[self.s5_c], writes=[Cw])
            self.ts("dve", Cn[:], Cw[:], -1.0, ALU.mult, [Cw], [Cn])
            P.op("dve", lambda e: e.memset(xst[:], 0.0), writes=[xst])
            S = lambda i: sc[:, i, :]
            rd = [sc]
            ar, aim, dt, r1, th, k_, c1, s1, lbr, lbi, fr, fi = (S(i) for i in range(12))
            u0, u1, u2, u3 = S(12), S(13), S(14), S(15)
            self.ts("dve", ar, lam[:, 0, :], -1e-4, ALU.min, [lam], [sc])
            self.copy("dve", aim, lam[:, 1, :], [lam], [sc])
            self.act(dt, lam[:, 2, :], AF.Exp, [lam], [sc])
            self.tt("dve", u0, ar, dt, ALU.mult, rd, rd)
            self.act(r1, u0, AF.Exp, rd, rd)
            self.tt("dve", th, aim, dt, ALU.mult, rd, rd)
            self.ts("dve", u0, th, 1.0 / TWO_PI, ALU.mult, rd, rd)
            self.ts("dve", u1, u0, 12582912.0, ALU.add, rd, rd)
            self.ts("dve", k_, u1, -12582912.0, ALU.add, rd, rd)
            self.stt(u2, k_, -TWO_PI, th, ALU.mult, ALU.add, rd, rd)
            self.ts("dve", u2, u2, math.pi, ALU.min, rd, rd, s2=-math.pi, op1=ALU.max)
            self.act(s1, u2, AF.Sin, rd, rd)
            self.act(u3, u2, AF.Abs, rd, rd)
            self.ts("dve", u3, u3, -1.0, ALU.mult, rd, rd, s2=math.pi / 2, op1=ALU.add)
            self.act(c1, u3, AF.Sin, rd, rd)
            self.tt("dve", lbr, r1, c1, ALU.mult, rd, rd)
            self.tt("dve", lbi, r1, s1, ALU.mult, rd, rd)
            x_, y_ = S(16), S(17)
            self.ts("dve", x_, lbr, -1.0, ALU.add, rd, rd)
            self.copy("dve", y_, lbi, rd, rd)
            a_, b_ = ar, aim
            n1, n2, den = S(18), S(19), S(20)
            self.tt("dve", n1, x_, a_, ALU.mult, rd, rd)
            self.tt("dve", u0, y_, b_, ALU.mult, rd, rd)
            self.tt("dve", n1, n1, u0, ALU.add, rd, rd)
            self.tt("dve", n2, y_, a_, ALU.mult, rd, rd)
            self.tt("dve", u0, x_, b_, ALU.mult, rd, rd)
            self.tt("dve", n2, n2, u0, ALU.subtract, rd, rd)
            self.tt("dve", den, a_, a_, ALU.mult, rd, rd)
            self.tt("dve", u0, b_, b_, ALU.mult, rd, rd)
            self.tt("dve", den, den, u0, ALU.add, rd, rd)
            P.op("dve", lambda e, den=den: e.reciprocal(out=den, in_=den), reads=rd, writes=rd)
            self.tt("dve", fr, n1, den, ALU.mult, rd, rd)
            self.tt("dve", fi, n2, den, ALU.mult, rd, rd)
            self.copy("dve", Er[:, :, 0], c1, rd, [Er])
            self.copy("dve", Ei[:, :, 0], s1, rd, [Ei])
            n = 1
            while n < SL:
                pr_ = bc(Er[:, :, n - 1:n], [128, 32, n])
                pi_ = bc(Ei[:, :, n - 1:n], [128, 32, n])
                sr, si = Er[:, :, 0:n], Ei[:, :, 0:n]
                dr, di = Er[:, :, n:2 * n], Ei[:, :, n:2 * n]
                tm = tmpE[:, :, 0:n]
                self.tt("dve", dr, sr, pr_, ALU.mult, [Er], [Er])
                self.tt("dve", tm, si, pi_, ALU.mult, [Ei], [tmpE])
                self.tt("dve", dr, dr, tm, ALU.subtract, [Er, tmpE], [Er])
                self.tt("dve", di, sr, pi_, ALU.mult, [Er, Ei], [Ei])
                self.tt("dve", tm, si, pr_, ALU.mult, [Er, Ei], [tmpE])
                self.tt("dve", di, di, tm, ALU.add, [Ei, tmpE], [Ei])
                n *= 2
            frb = bc(fr.unsqueeze(2), [128, 32, SL])
            fib = bc(fi.unsqueeze(2), [128, 32, SL])
            self.tt("dve", Tr[:], Er[:], frb, ALU.mult, [Er, sc], [Tr])
            self.tt("dve", tmpE[:], Ei[:], fib, ALU.mult, [Ei, sc], [tmpE])
            self.tt("dve", Tr[:], Tr[:], tmpE[:], ALU.add, [Tr, tmpE], [Tr])
            self.tt("dve", Ti[:], Er[:], fib, ALU.mult, [Er, sc], [Ti])
            self.tt("dve", tmpE[:], Ei[:], frb, ALU.mult, [Ei, sc], [tmpE])
            self.tt("dve", Ti[:], Ti[:], tmpE[:], ALU.subtract, [Ti, tmpE], [Ti])

            ut = [P.sbuf("s_ut%d" % i, [128, 512], BF16) for i in range(2)]
            ut32 = [P.sbuf("s_ut32_%d" % i, [128, 512], F32) for i in range(2)]
            NB = 4
            mt = [[P.sbuf("s_m%d_%d" % (k, i), [128, 512], F32) for k in range(4)] for i in range(2)]
            bre = [P.sbuf("s_bre%d" % i, [128, 512], F32) for i in range(NB)]
            bim = [P.sbuf("s_bim%d" % i, [128, 512], F32) for i in range(NB)]
            wre = [P.sbuf("s_wre%d" % i, [128, 512], F32) for i in range(NB)]
            wim = [P.sbuf("s_wim%d" % i, [128, 512], F32) for i in range(NB)]
            pr4 = [[P.sbuf("s_p%d_%d" % (k, i), [128, 512], BF16) for k in range(4)] for i in range(NB)]
            tsm = [P.sbuf("s_ts%d" % i, [128, 2], F32) for i in range(4)]
            yev = [P.sbuf("s_yev%d" % i, [128, 4, 128], F32) for i in range(2)]
            first = True
            src = self.uT if d == 0 else self.uTr
            for (i0, ng) in self.groups:
                nchk = ng // SL
                ntt = ng // 128
                v3 = lambda ap: ap[:, 0:ng].rearrange("p (c j) -> p c j", j=SL)
                for q in range(8):
                    U_ = self.rot("s_ut", ut)
                    if d == 0:
                        U32 = self.rot("s_ut32", ut32)
                        P.dma("sp", U32[:, 0:ng], src.ap()[q * 128:(q + 1) * 128, i0:i0 + ng], reads=[src], writes=[U32])
                        self.copy("act", U_[:, 0:ng], U32[:, 0:ng], [U32], [U_])
                    else:
                        P.dma("sp", U_[:, 0:ng], src.ap()[q * 128:(q + 1) * 128, i0:i0 + ng], reads=[src], writes=[U_])
                    pY = self.rot("s_pY", self.ps[6:8])
                    Y_ = self.rot("s_yev", yev)
                    for hp in range(2):
                        prs = []
                        for k2 in range(2):
                            pp = 2 * hp + k2
                            p = 4 * q + pp
                            ii = self.rr.get("s_i", 0)
                            self.rr["s_i"] = ii + 1
                            pR = self.rot("s_pR", self.ps[0:6])
                            pI = self.rot("s_pR", self.ps[0:6])
                            rows = slice(32 * pp, 32 * pp + 32) if pp < 3 else slice(64, 128)
                            vv = 0 if pp < 3 else 1
                            self.mm(pR[:, 0:ng], Bw[rows, 0, q, vv, :], U_[rows, 0:ng], True, True, [Bw, U_], [pR])
                            self.mm(pI[:, 0:ng], Bw[rows, 1, q, vv, :], U_[rows, 0:ng], True, True, [Bw, U_], [pI])
                            prs.append(dict(pp=pp, p=p, pR=pR, pI=pI, m=mt[k2], BR=bre[ii % NB], BI=bim[ii % NB], WR=wre[ii % NB], WI=wim[ii % NB],
                                            PR=pr4[ii % NB], T=tsm[ii % 4]))
                        for z in prs:
                            p = z["p"]
                            Trb = bc(Tr[:, p, :].unsqueeze(1), [128, nchk, SL])
                            Tib = bc(Ti[:, p, :].unsqueeze(1), [128, nchk, SL])
                            m1, m2, m3, m4 = z["m"]
                            self.tt("dve", v3(m1), v3(z["pR"]), Trb, ALU.mult, [z["pR"], Tr], [m1])
                            self.tt("dve", v3(m2), v3(z["pI"]), Tib, ALU.mult, [z["pI"], Ti], [m2])
                            self.tt("dve", v3(m3), v3(z["pI"]), Trb, ALU.mult, [z["pI"], Tr], [m3])
                            self.tt("dve", v3(m4), v3(z["pR"]), Tib, ALU.mult, [z["pR"], Ti], [m4])
                            self.tt("pool", z["BR"][:, 0:ng], m1[:, 0:ng], m2[:, 0:ng], ALU.subtract, [m1, m2], [z["BR"]])
                            self.tt("pool", z["BI"][:, 0:ng], m3[:, 0:ng], m4[:, 0:ng], ALU.add, [m3, m4], [z["BI"]])
                        for c in range(nchk):
                            a0 = c * SL
                            la = a0 + SL - 1
                            steps = []
                            for z in prs:
                                p, BR, BI, WR, WI, T_ = z["p"], z["BR"], z["BI"], z["WR"], z["WI"], z["T"]
                                r1p = r1[:, p:p + 1]
                                ErL, EiL = Er[:, p, SL - 1:SL], Ei[:, p, SL - 1:SL]
                                st_ = []
                                if not (first and c == 0):
                                    st_.append(lambda BR=BR, p=p, r1p=r1p: self.stt(BR[:, a0:a0 + 1], xst[:, p, 0:1], r1p, BR[:, a0:a0 + 1], ALU.mult, ALU.add, [xst, sc, BR], [BR]))
                                    st_.append(lambda BI=BI, p=p, r1p=r1p: self.stt(BI[:, a0:a0 + 1], xst[:, p, 1:2], r1p, BI[:, a0:a0 + 1], ALU.mult, ALU.add, [xst, sc, BI], [BI]))
                                else:
                                    st_.append(None)
                                    st_.append(None)
                                st_.append(lambda WR=WR, BR=BR, o_=WR[:, a0:a0 + SL], d0=bc(r1p, [128, SL]), d1=BR[:, a0:a0 + SL]: P.op("dve", lambda e: e.tensor_tensor_scan(out=o_, data0=d0, data1=d1, initial=0.0, op0=ALU.mult, op1=ALU.add), reads=[BR, sc], writes=[WR]))
                                st_.append(lambda WI=WI, BI=BI, o_=WI[:, a0:a0 + SL], d0=bc(r1p, [128, SL]), d1=BI[:, a0:a0 + SL]: P.op("dve", lambda e: e.tensor_tensor_scan(out=o_, data0=d0, data1=d1, initial=0.0, op0=ALU.mult, op1=ALU.add), reads=[BI, sc], writes=[WI]))
                                st_.append(lambda WI=WI, T_=T_, EiL=EiL: self.ts("dve", T_[:, 0:1], WI[:, la:la + 1], EiL, ALU.mult, [WI, Ei], [T_]))
                                st_.append(lambda WR=WR, T_=T_, EiL=EiL: self.ts("dve", T_[:, 1:2], WR[:, la:la + 1], EiL, ALU.mult, [WR, Ei], [T_]))
                                st_.append(lambda WR=WR, T_=T_, ErL=ErL, p=p: self.stt(xst[:, p, 0:1], WR[:, la:la + 1], ErL, T_[:, 0:1], ALU.mult, ALU.subtract, [WR, Er, T_], [xst]))
                                st_.append(lambda WI=WI, T_=T_, ErL=ErL, p=p: self.stt(xst[:, p, 1:2], WI[:, la:la + 1], ErL, T_[:, 1:2], ALU.mult, ALU.add, [WI, Er, T_], [xst]))
                                steps.append(st_)
                            for k in range(len(steps[0])):
                                for st_ in steps:
                                    if st_[k] is not None:
                                        st_[k]()
                        for z in prs:
                            p, pp, PR, WR, WI = z["p"], z["pp"], z["PR"], z["WR"], z["WI"]
                            Erb = bc(Er[:, p, :].unsqueeze(1), [128, nchk, SL])
                            Eib = bc(Ei[:, p, :].unsqueeze(1), [128, nchk, SL])
                            self.tt("pool", v3(PR[0]), v3(WR), Erb, ALU.mult, [WR, Er], [PR[0]])
                            self.tt("pool", v3(PR[1]), v3(WI), Eib, ALU.mult, [WI, Ei], [PR[1]])
                            self.tt("pool", v3(PR[2]), v3(WR), Eib, ALU.mult, [WR, Ei], [PR[2]])
                            self.tt("pool", v3(PR[3]), v3(WI), Erb, ALU.mult, [WI, Er], [PR[3]])
                            for t in range(ntt):
                                o_ = pY[:, t * 128 + pp * 32:t * 128 + pp * 32 + 32]
                                tsl = slice(t * 128, (t + 1) * 128)
                                self.mm(o_, PR[0][:, tsl], Cw[:, 0, p, :], True, False, [PR[0], Cw], [pY])
                                self.mm(o_, PR[1][:, tsl], Cn[:, 0, p, :], False, False, [PR[1], Cn], [pY])
                                self.mm(o_, PR[2][:, tsl], Cn[:, 1, p, :], False, False, [PR[2], Cn], [pY])
                                self.mm(o_, PR[3][:, tsl], Cn[:, 1, p, :], False, True, [PR[3], Cn], [pY])
                    self.copy("act", Y_[:, 0:ntt, :], pY[:, 0:ng].rearrange("p (t c) -> p t c", c=128), [pY], [Y_])
                    P.dma("act", self.ytok.ap()[d, i0:i0 + ng, q * 128:(q + 1) * 128].rearrange("(t p) c -> p t c", p=128), Y_[:, 0:ntt, :], reads=[Y_], writes=[self.ytok])
                first = False
    with P.scope():
        wg = P.sbuf("g_wg", [128, 8, 1024], BF16)
        dsk = P.sbuf("g_dsk", [128, 8], F32)
        bgl = P.sbuf("g_bgl", [128, 8], F32)
        P.dma("pool", wg[:], self.w_glu.ap()[l].rearrange("(kc p) n -> p kc n", p=128), reads=[self.w_glu], writes=[wg])
        P.dma("sp", dsk[:], self.s5_d.ap()[l].rearrange("(c p) -> p c", p=128), reads=[self.s5_d], writes=[dsk], allow_slow_non_contiguous=True)
        P.dma("sp", bgl[:], self.b_glu.ap()[l].rearrange("(c p) -> p c", p=128), reads=[self.b_glu], writes=[bgl], allow_slow_non_contiguous=True)
        y0 = [P.sbuf("g_y0_%d" % i, [128, 1024], F32) for i in range(2)]
        y1 = [P.sbuf("g_y1_%d" % i, [128, 1024], F32) for i in range(2)]
        uTt = P.sbuf("g_uT", [128, 8, 512], F32)
        yT = P.sbuf("g_yT", [128, 8, 512], F32)
        t_a = P.sbuf("g_ta", [128, 8, 512], F32)
        gb = P.sbuf("g_gb", [128, 8, 512], BF16)
        sg = [P.sbuf("g_sg%d" % i, [128, 512], F32) for i in range(2)]
        cT = P.sbuf("g_cT", [128, 8, 512], BF16)
        for (s0, ng) in self.groups:
            P.dma("sp", uTt[:, :, 0:ng], self.uT.ap().rearrange("(c p) t -> p c t", p=128)[:, :, s0:s0 + ng], reads=[self.uT], writes=[uTt])
            for t in range(ng // 128):
                r0 = s0 + t * 128
                a0, a1 = self.rot("g_y0", y0), self.rot("g_y1", y1)
                P.dma("sp", a0[:], self.ytok.ap()[0, r0:r0 + 128, :], reads=[self.ytok], writes=[a0])
                m0 = self.mirror(r0)
                P.dma("sp", a1[:], self.ytok.ap()[1, m0:m0 + 128, :], reads=[self.ytok], writes=[a1])
                for half in range(2):
                    ps = self.rot("g_ps", self.ps[0:4])
                    for qq in range(4):
                        q = half * 4 + qq
                        o_ = ps[:, qq * 128:(qq + 1) * 128]
                        self.mm(o_, a0[:, q * 128:(q + 1) * 128], self.ident_f, True, False, [a0, self.cst_t], [ps])
                        self.mm(o_, a1[:, q * 128:(q + 1) * 128], self.J_f, False, True, [a1, self.cst_t], [ps])
                    for qq in range(4):
                        q = half * 4 + qq
                        self.stt(yT[:, q, t * 128:(t + 1) * 128], uTt[:, q, t * 128:(t + 1) * 128], dsk[:, q:q + 1], ps[:, qq * 128:(qq + 1) * 128], ALU.mult, ALU.add, [uTt, dsk, ps], [], pw=[yT])
            Y = yT[:, :, 0:ng]
            A = t_a[:, :, 0:ng]
            self.act(A, Y, AF.Square, [yT], [t_a])
            self.ts("dve", A, A, 0.044715, ALU.mult, [t_a], [t_a], s2=1.0, op1=ALU.add)
            self.tt("pool", A, A, Y, ALU.mult, [t_a, yT], [t_a])
            self.act(A, A, AF.Sigmoid, [t_a], [t_a], scale=2.0 * math.sqrt(2.0 / math.pi))
            self.tt("dve", Y, Y, A, ALU.mult, [yT, t_a], [yT])
            self.copy("act", gb[:, :, 0:ng], Y, [yT], [gb])
            for jc in range(8):
                ps = self.rot("g_ps", self.ps[0:4])
                for kc in range(8):
                    self.mm(ps[:, 0:ng], wg[:, kc, jc * 128:(jc + 1) * 128], gb[:, kc, 0:ng], kc == 0, kc == 7, [wg, gb], [ps])
                s_ = self.rot("g_sg", sg)
                self.act(s_[:, 0:ng], ps[:, 0:ng], AF.Sigmoid, [ps, bgl], [s_], bias=bgl[:, jc:jc + 1])
                self.tt("dve", cT[:, jc, 0:ng], yT[:, jc, 0:ng], s_[:, 0:ng], ALU.mult, [yT, s_], [], pw=[cT])
            P.dma("act", self.brT.ap()[2].rearrange("(c p) t -> p c t", p=128)[:, :, s0:s0 + ng], cT[:, :, 0:ng], reads=[cT], writes=[self.brT])


Builder.phase_C3 = phase_C3


def phase_DE(self, l):
    P, TC, TL, TT = self.P, self.TC, self.TL, self.TT
    last = (l == self.depth - 1)
    with P.scope():
        X = P.sbuf("e_X", [128, KC, 512], F32)
        M = P.sbuf("e_M", [128, KC, 512], BF16)
        H = P.sbuf("e_H", [128, 32, 512], BF16)
        wt = [P.sbuf("e_w%d" % i, [128, KC, 512], BF16) for i in range(3)]
        gt = [P.sbuf("e_g%d" % i, [128, 3, 512], BF16) for i in range(2)]
        ma = [P.sbuf("e_ma%d" % i, [128, 512], F32) for i in range(2)]
        mb = [P.sbuf("e_mb%d" % i, [128, 512], F32) for i in range(2)]
        sqb = [P.sbuf("e_sq%d" % i, [128, 512], BF16) for i in range(3)]
        tmp = [P.sbuf("e_tmp%d" % i, [128, 512], F32) for i in range(3)]
        rbc = P.sbuf("e_rbc", [128, 512], F32)
        rl = [P.sbuf("e_rl%d" % i, [128, 512], F32) for i in range(3)]
        yt = [P.sbuf("e_yt%d" % i, [128, D], F32) for i in range(2)] if last else None
        wbr = self.w_branch.ap()[l]
        wov = self.w_out.ap()[l].rearrange("(kc p) n -> p kc n", p=128)
        w1v = self.w_ff1.ap()[l].rearrange("(kc p) n -> p kc n", p=128)
        w2v = self.w_ff2.ap()[l].rearrange("(f p) n -> p f n", p=128)
        gate1, gate2 = self.modT[:, 2], self.modT[:, 5]
        xin = self.xT[l]
        groups = [g for g in self.groups if not (last and g[0] < TC)]
        for (s0, ng) in groups:
            col = 1 if s0 < TC else 0
            P.dma("sp", X[:, :, 0:ng], xin.ap().rearrange("(c p) t -> p c t", p=128)[:, :, s0:s0 + ng], reads=[xin], writes=[X])
            for r in range(3):
                P.dma("sp", H[:, r * 8:(r + 1) * 8, 0:ng], self.brT.ap()[r].rearrange("(c p) t -> p c t", p=128)[:, :, s0:s0 + ng], reads=[self.brT], writes=[H])
            for jb in range(4):
                ws = []
                for r in range(3):
                    w = self.rot("e_w", wt)
                    P.dma("pool", w[:, 0:8, :], wbr[r].rearrange("(kc p) n -> p kc n", p=128)[:, :, jb * 512:(jb + 1) * 512], reads=[self.w_branch], writes=[w])
                    ws.append(w)
                for jj in range(4):
                    j = jb * 4 + jj
                    G_ = self.rot("e_g", gt)
                    P.dma("sp", G_[:, :, 0:ng], self.gT.ap().rearrange("(r c p) t -> p r c t", r=3, p=128)[:, :, j, s0:s0 + ng], reads=[self.gT], writes=[G_])
                    A_, B_ = self.rot("e_ma", ma), self.rot("e_mb", mb)
                    for r in range(3):
                        ps = self.rot("e_ps", self.ps[0:6])
                        for kc in range(8):
                            self.mm(ps[:, 0:ng], ws[r][:, kc, jj * 128:(jj + 1) * 128], H[:, r * 8 + kc, 0:ng], kc == 0, kc == 7, [ws[r], H], [ps])
                        if r == 0:
                            self.tt("dve", A_[:, 0:ng], ps[:, 0:ng], G_[:, 0, 0:ng], ALU.mult, [ps, G_], [A_])
                        elif r == 1:
                            self.tt("dve", B_[:, 0:ng], ps[:, 0:ng], G_[:, 1, 0:ng], ALU.mult, [ps, G_], [B_])
                            self.tt("dve", A_[:, 0:ng], A_[:, 0:ng], B_[:, 0:ng], ALU.add, [A_, B_], [A_])
                        else:
                            self.tt("dve", B_[:, 0:ng], ps[:, 0:ng], G_[:, 2, 0:ng], ALU.mult, [ps, G_, A_], [B_])
                            self.tt("dve", M[:, j, 0:ng], A_[:, 0:ng], B_[:, 0:ng], ALU.add, [A_, B_], [], pw=[M])
            for jb in range(4):
                w = self.rot("e_w", wt)
                P.dma("pool", w[:], wov[:, :, jb * 512:(jb + 1) * 512], reads=[self.w_out], writes=[w])
                for jj in range(4):
                    j = jb * 4 + jj
                    ps = self.rot("e_ps", self.ps[0:6])
                    for kc in range(KC):
                        self.mm(ps[:, 0:ng], w[:, kc, jj * 128:(jj + 1) * 128], M[:, kc, 0:ng], kc == 0, kc == KC - 1, [w, M], [ps])
                    self.stt(X[:, j, 0:ng], ps[:, 0:ng], gate1[:, j, col:col + 1], X[:, j, 0:ng], ALU.mult, ALU.add, [ps, self.modT, X], [X])
            self.norm_group(X, M, ng, 1, col, sqb, rbc, tmp, self.ps[7])
            for half in range(2):
                for fb in range(8):
                    w = self.rot("e_w", wt)
                    c0 = half * 4096 + fb * 512
                    P.dma("pool", w[:], w1v[:, :, c0:c0 + 512], reads=[self.w_ff1], writes=[w])
                    for jj in range(4):
                        f = fb * 4 + jj
                        ps = self.rot("e_ps", self.ps[0:6])
                        for kc in range(KC):
                            self.mm(ps[:, 0:ng], w[:, kc, jj * 128:(jj + 1) * 128], M[:, kc, 0:ng], kc == 0, kc == KC - 1, [w, M], [ps])
                        R_ = self.rot("e_rl", rl)
                        self.act(R_[:, 0:ng], ps[:, 0:ng], AF.Relu, [ps], [R_])
                        self.tt("dve", H[:, f, 0:ng], R_[:, 0:ng], R_[:, 0:ng], ALU.mult, [R_], [], pw=[H])
                for jb in range(4):
                    wa = self.rot("e_w", wt)
                    wb_ = self.rot("e_w", wt)
                    f0 = half * 32
                    P.dma("pool", wa[:], w2v[:, f0:f0 + 16, jb * 512:(jb + 1) * 512], reads=[self.w_ff2], writes=[wa])
                    P.dma("pool", wb_[:], w2v[:, f0 + 16:f0 + 32, jb * 512:(jb + 1) * 512], reads=[self.w_ff2], writes=[wb_])
                    for jj in range(4):
                        j = jb * 4 + jj
                        ps = self.rot("e_ps", self.ps[0:6])
                        for f in range(32):
                            ww = wa if f < 16 else wb_
                            self.mm(ps[:, 0:ng], ww[:, f % 16, jj * 128:(jj + 1) * 128], H[:, f, 0:ng], f == 0, f == 31, [ww, H], [ps])
                        self.stt(X[:, j, 0:ng], ps[:, 0:ng], gate2[:, j, col:col + 1], X[:, j, 0:ng], ALU.mult, ALU.add, [ps, self.modT, X], [X])
            if not last:
                P.dma("act", self.xT[l + 1].ap().rearrange("(c p) t -> p c t", p=128)[:, :, s0:s0 + ng], X[:, :, 0:ng], reads=[X], writes=[self.xT[l + 1]])
            else:
                for t in range(ng // 128):
                    Y_ = self.rot("e_yt", yt)
                    for b4 in range(4):
                        ps = self.rot("e_ps", self.ps[0:6])
                        for jj in range(4):
                            c = b4 * 4 + jj
                            self.tr(ps[:, jj * 128:(jj + 1) * 128], X[:, c, t * 128:(t + 1) * 128], self.ident_f, [X, self.cst_t], [ps])
                        self.copy("act" if b4 % 2 == 0 else "dve", Y_[:, b4 * 512:(b4 + 1) * 512], ps[:, 0:512], [ps], [], pw=[Y_])
                    r0 = s0 - TC + t * 128
                    P.dma("act", self.y.ap()[r0:r0 + 128, :], Y_[:], reads=[Y_], writes=[self.y])


Builder.phase_DE = phase_DE


_CACHE = {}


def kernel(**inputs):
    TL, TC = 4096, 256
    if "prog" not in _CACHE:
        _CACHE["prog"] = build_program(TL, TC, depth=DEPTH)
    B = _CACHE["prog"]
    sh = prep_shared(inputs, DEPTH)
    cst, rope = make_consts(TL, TC)
    maps = []
    for core in range(8):
        b = core % 4
        m = dict(sh)
        m.update(prep_core(inputs, b, TL, TC))
        m["cst"] = cst
        m["rope"] = rope
        maps.append(m)
    res = run_bass_kernel_spmd(B.nc, maps, core_ids=list(range(8)))
    out = np.stack([np.asarray(res.results[b]["y"], dtype=np.float32) for b in range(4)], axis=0)
    return out
```

```python
import contextlib
import math
import numpy as np
import concourse.bass as bass
import concourse.mybir as mybir
from concourse.bass_utils import run_bass_kernel_spmd

F32 = mybir.dt.float32
BF16 = mybir.dt.bfloat16
AF = mybir.ActivationFunctionType
ALU = mybir.AluOpType
AX = mybir.AxisListType

ENGS = ("pe", "act", "dve", "pool", "sp")
EPOCH = 30000

D = 2048
KC = 16
DEPTH = 2
DIN = 11344
OFF_Q, OFF_K, OFF_V, OFF_O, OFF_G, OFF_QA, OFF_KVA, OFF_KPE, OFF_U, OFF_BG = 0, 512, 1024, 2048, 3072, 3088, 3600, 4112, 4176, 5200
EPS = 1e-6
SL = 128


class T:
    def __init__(self, h, name=""):
        self.h = h
        self.name = name
        self.w = {}
        self.r = {}

    def __getitem__(self, k):
        return self.h[k]

    def ap(self):
        return self.h.ap()


class Prog:
    def __init__(self, nc, n_dma_sems=8):
        self.nc = nc
        self.es = contextlib.ExitStack()
        self.q = {e: [] for e in ENGS}
        self.tick = {e: 0 for e in ENGS}
        self.epoch = {e: 0 for e in ENGS}
        self.sems = {}
        self.seen = {e: {} for e in ENGS}
        self.dma_slots = {}
        self.n_dma_sems = n_dma_sems
        self.dma_i = {e: 0 for e in ENGS}
        self.nsem = 0
        self.ninstr = 0
        self.scopes = []

    def sem(self, key):
        if key not in self.sems:
            self.sems[key] = self.es.enter_context(self.nc.semaphore("s%d" % self.nsem))
            self.nsem += 1
        return self.sems[key]

    def _stack(self):
        return self.scopes[-1] if self.scopes else self.es

    def sbuf(self, name, shape, dtype):
        self.uid = getattr(self, "uid", 0) + 1
        name = "%s_u%d" % (name, self.uid)
        return T(self._stack().enter_context(self.nc.sbuf_tensor(name, list(shape), dtype)), name)

    def psum(self, name, shape, dtype=F32):
        return T(self.es.enter_context(self.nc.psum_tensor(name, list(shape), dtype)), name)

    def dram(self, name, shape, dtype, kind=None):
        if kind is None:
            h = self.nc.dram_tensor(name, list(shape), dtype)
        else:
            h = self.nc.dram_tensor(name, list(shape), dtype, kind=kind)
        return T(h, name)

    @contextlib.contextmanager
    def scope(self):
        st = contextlib.ExitStack()
        self.scopes.append(st)
        try:
            yield
        finally:
            self.barrier()
            self.flush()
            self.scopes.pop()
            st.close()

    def _waits(self, e, reads, writes, pw=()):
        deps = {}
        for b in reads:
            for k, v in b.w.items():
                if deps.get(k, 0) < v:
                    deps[k] = v
        for b in writes:
            for d in (b.w, b.r):
                for k, v in d.items():
                    if deps.get(k, 0) < v:
                        deps[k] = v
        for b in pw:
            for k, v in b.r.items():
                if deps.get(k, 0) < v:
                    deps[k] = v
        out = []
        for k, v in deps.items():
            if k[0] == e and k[1] == "p" and e == "pe":
                continue
            if self.seen[e].get(k, 0) >= v:
                continue
            self.seen[e][k] = v
            out.append((k, v))
        return out

    def op(self, e, fn, reads=(), writes=(), pw=()):
        waits = self._waits(e, reads, writes, pw)
        if self.tick[e] >= EPOCH:
            self.epoch[e] += 1
            self.tick[e] = 0
        key = (e, "p", self.epoch[e])
        self.tick[e] += 1
        val = self.tick[e]
        s = self.sem(key)
        wl = [(self.sem(k), v) for k, v in waits]
        self.q[e].append((wl, fn, s, 1))
        for b in reads:
            b.r[key] = val
        for b in writes:
            b.w[key] = val
        for b in pw:
            b.w[key] = val
        self.ninstr += 1

    def dma(self, e, out_ap, in_ap, reads=(), writes=(), **kw):
        i = self.dma_i[e]
        self.dma_i[e] += 1
        slot = (e, "d", i % self.n_dma_sems)
        cnt = self.dma_slots.get(slot, 0)
        waits = self._waits(e, reads, writes)
        if cnt > 0 and self.seen[e].get(slot, 0) < cnt:
            self.seen[e][slot] = cnt
            waits.append((slot, cnt))
        cnt += 16
        self.dma_slots[slot] = cnt
        s = self.sem(slot)
        wl = [(self.sem(k), v) for k, v in waits]

        def fn(eng, out_ap=out_ap, in_ap=in_ap, kw=kw):
            return eng.dma_start(out=out_ap, in_=in_ap, **kw)

        self.q[e].append((wl, fn, s, 16))
        for b in reads:
            b.r[slot] = cnt
        for b in writes:
            b.w[slot] = cnt
        self.ninstr += 1

    def barrier(self):
        targets = {}
        for e in ENGS:
            if self.tick[e] > 0:
                targets[(e, "p", self.epoch[e])] = self.tick[e]
        for slot, cnt in self.dma_slots.items():
            targets[slot] = cnt
        for e in ENGS:
            wl = []
            for k, v in targets.items():
                if k[0] == e and k[1] == "p" and e == "pe":
                    continue
                if self.seen[e].get(k, 0) >= v:
                    continue
                self.seen[e][k] = v
                wl.append((self.sem(k), v))
            if wl:
                self.q[e].append((wl, None, None, 0))

    def flush(self):
        nc = self.nc
        q = self.q
        if not any(q[e] for e in ENGS):
            return
        with nc.Block() as block:

            def run(eng, lst):
                for wl, fn, s, inc in lst:
                    for ws, v in wl:
                        eng.wait_ge(ws, v)
                    if fn is not None:
                        fn(eng).then_inc(s, inc)

            @block.tensor
            def _(eng):
                run(eng, q["pe"])

            @block.scalar
            def _(eng):
                run(eng, q["act"])

            @block.vector
            def _(eng):
                run(eng, q["dve"])

            @block.gpsimd
            def _(eng):
                run(eng, q["pool"])

            @block.sync
            def _(eng):
                run(eng, q["sp"])

        self.q = {e: [] for e in ENGS}

    def finish(self):
        self.barrier()
        self.flush()
        self.es.close()


def bc(ap, shape):
    return ap.to_broadcast(list(shape))


class Builder:
    def __init__(self, TL, TC, depth=DEPTH, debug=(), stop_after=None):
        self.TL, self.TC, self.TT = TL, TC, TL + TC
        self.depth = depth
        self.debug = set(debug)
        self.stop_after = stop_after
        assert TC % 128 == 0 and TC <= 512 and TL % 512 == 0
        self.groups = [(0, TC)] + [(TC + 512 * i, 512) for i in range(TL // 512)]
        self.NT = self.TT // 128
        self.nc = bass.Bass("TRN2", target_bir_lowering=False)
        self.P = Prog(self.nc)
        self.rr = {}

    def ext_in(self, name, shape, dt=F32):
        return self.P.dram(name, shape, dt, kind="ExternalInput")

    def scratch(self, name, shape, dt):
        kind = "ExternalOutput" if name in self.debug else None
        return self.P.dram(name, shape, dt, kind=kind)

    def rot(self, key, lst):
        i = self.rr.get(key, 0)
        self.rr[key] = i + 1
        return lst[i % len(lst)]

    def act(self, out, in_, func, reads, writes, pw=(), **kw):
        self.P.op("act", lambda e: e.activation(out=out, in_=in_, func=func, **kw), reads=reads, writes=writes, pw=pw)

    def tt(self, eng, out, in0, in1, op, reads, writes, pw=()):
        self.P.op(eng, lambda e: e.tensor_tensor(out=out, in0=in0, in1=in1, op=op), reads=reads, writes=writes, pw=pw)

    def ts(self, eng, out, in0, s1, op0, reads, writes, s2=None, op1=None, pw=()):
        if op1 is None:
            self.P.op(eng, lambda e: e.tensor_scalar(out=out, in0=in0, scalar1=s1, scalar2=None, op0=op0), reads=reads, writes=writes, pw=pw)
        else:
            self.P.op(eng, lambda e: e.tensor_scalar(out=out, in0=in0, scalar1=s1, scalar2=s2, op0=op0, op1=op1), reads=reads, writes=writes, pw=pw)

    def stt(self, out, in0, scalar, in1, op0, op1, reads, writes, pw=()):
        self.P.op("dve", lambda e: e.scalar_tensor_tensor(out=out, in0=in0, scalar=scalar, in1=in1, op0=op0, op1=op1), reads=reads, writes=writes, pw=pw)

    def mm(self, out, lhsT, rhs, start, stop, reads, writes):
        self.P.op("pe", lambda e: e.matmul(out=out, lhsT=lhsT, rhs=rhs, start=start, stop=stop), reads=reads, writes=writes)

    def tr(self, out, in_, ident, reads, writes):
        self.P.op("pe", lambda e: e.transpose(out=out, in_=in_, identity=ident), reads=reads, writes=writes)

    def copy(self, eng, out, in_, reads, writes, pw=()):
        if eng == "act":
            self.act(out, in_, AF.Copy, reads, writes, pw=pw)
        else:
            self.P.op(eng, lambda e: e.tensor_copy(out=out, in_=in_), reads=reads, writes=writes, pw=pw)

    def rsqrt(self, out, in_, scale, reads, writes):
        self.act(out, in_, AF.Sqrt, reads, writes, scale=scale, bias=self.eps_t[0:out.shape[0], 0:1])
        self.P.op("dve", lambda e: e.reciprocal(out=out, in_=out), reads=writes, writes=writes)

    def declare(self):
        P, TT, TL, TC, L = self.P, self.TT, self.TL, self.TC, self.depth
        I = self.ext_in
        self.x_in = I("x", [TL, D])
        self.ctx_in = I("ctx", [TC, D])
        self.cvec = I("cvec", [2, D])
        self.w_mod = I("w_mod", [L, D, 6 * D])
        self.b_mod = I("b_mod", [L, 6 * D])
        self.norm_g = I("norm_g", [L, 2, D])
        self.w_in = I("w_in", [L, D, DIN])
        self.b_in = I("b_in", [L, DIN])
        self.ml_gate_b = I("ml_gate_b", [L, 16])
        self.ml_norm_g = I("ml_norm_g", [L, 1024])
        self.qa_g = I("mla_qa_g", [L, 512])
        self.kva_g = I("mla_kva_g", [L, 512])
        self.w_uq = I("mla_w_uq", [L, 512, 1536])
        self.w_ukv = I("mla_w_ukv", [L, 512, 2048])
        self.qn_g = I("mla_qn_g", [L, 192])
        self.kn_g = I("mla_kn_g", [L, 192])
        self.s5_lam = I("s5_lam", [L, 2, 3, 128, 32])
        self.s5_b = I("s5_b", [L, 2, 2, 128, 8, 2, 128])
        self.s5_c = I("s5_c", [L, 2, 2, 128, 32, 32])
        self.s5_d = I("s5_d", [L, 1024])
        self.w_glu = I("s5_w_glu", [L, 1024, 1024])
        self.b_glu = I("s5_b_glu", [L, 1024])
        self.w_branch = I("w_branch", [L, 3, 1024, D])
        self.w_out = I("w_out", [L, D, D])
        self.w_ff1 = I("w_ff1", [L, D, 4 * D])
        self.w_ff2 = I("w_ff2", [L, 4 * D, D])
        self.cst = I("cst", [128, 640])
        self.rope = I("rope", [TT, 128])
        self.y = P.dram("y", [TL, D], F32, kind="ExternalOutput")

        S = self.scratch
        self.xT = [S("xT0", [D, TT], F32), S("xT1", [D, TT], F32)]
        self.qT = S("qT", [4, 128, TT], BF16)
        self.kT = S("kT", [4, 128, TT], BF16)
        self.ktok = S("ktok", [TT, 512], BF16)
        self.vtok = S("vtok", [TT, 1024], BF16)
        self.otok = S("otok", [TT, 1024], BF16)
        self.gtok = S("gtok", [TT, 16], F32)
        self.kpe = S("kpe", [TT, 64], F32)
        self.qaT = S("qaT", [512, TT], BF16)
        self.kvaT = S("kvaT", [512, TT], BF16)
        self.rsq = S("rsq", [TT, 2], F32)
        self.uT = S("uT", [1024, TT], F32)
        self.uTr = S("uTr", [1024, TT], BF16)
        self.gT = S("gT", [6144, TT], BF16)
        self.hml = S("hml", [2, TT, 1024], F32)
        self.qhT = S("qhT", [8, 192, TT], BF16)
        self.khT = S("khT", [8, 192, TT], BF16)
        self.vh = S("vh", [TT, 1024], BF16)
        self.ytok = S("ytok", [2, TT, 1024], F32)
        self.brT = S("brT", [3, 1024, TT], BF16)

        self.cst_t = P.sbuf("cst_t", [128, 640], F32)
        self.cstb = P.sbuf("cstb", [128, 640], BF16)
        self.eps_t = P.sbuf("eps_t", [128, 1], F32)
        self.modT = P.sbuf("modT", [128, 6, 16, 2], F32)
        self.A1 = P.sbuf("A1", [128, 2, 16, 2], F32)
        self.ps = [P.psum("ps%d" % i, [128, 512], F32) for i in range(8)]
        P.dma("sp", self.cst_t[:], self.cst[:], reads=[self.cst], writes=[self.cst_t])
        P.op("dve", lambda e: e.tensor_copy(out=self.cstb[:], in_=self.cst_t[:]), reads=[self.cst_t], writes=[self.cstb])
        P.op("dve", lambda e: e.memset(self.eps_t[:], EPS), writes=[self.eps_t])
        c = self.cst_t
        self.ident_f, self.J_f = c[:, 0:128], c[:, 128:256]
        self.triL_f, self.triU_f = c[0:64, 256:320], c[0:64, 320:384]
        self.ones_f = c[:, 384:512]
        cb = self.cstb
        self.ident_b, self.J_b, self.ones_b = cb[:, 0:128], cb[:, 128:256], cb[:, 384:512]

    def seq_rows(self, s0, n):
        if s0 < self.TC:
            return self.ctx_in, self.ctx_in[s0:s0 + n, :]
        return self.x_in, self.x_in[s0 - self.TC:s0 - self.TC + n, :]

    def phase_T0(self):
        P = self.P
        with P.scope():
            xt = [P.sbuf("t0x%d" % i, [128, D], F32) for i in range(2)]
            xo = [P.sbuf("t0o%d" % i, [128, KC, 128], F32) for i in range(2)]
            dst = self.xT[0].ap().rearrange("(c p) t -> p c t", p=128)
            for i in range(self.NT):
                src_t, src = self.seq_rows(i * 128, 128)
                a, o = xt[i % 2], xo[i % 2]
                P.dma("sp", a[:], src, reads=[src_t], writes=[a])
                for b4 in range(4):
                    ps = self.ps[(i * 4 + b4) % 8]
                    for j in range(4):
                        c = b4 * 4 + j
                        self.tr(ps[:, j * 128:(j + 1) * 128], a[:, c * 128:(c + 1) * 128], self.ident_f, [a, self.cst_t], [ps])
                    self.copy("act" if b4 % 2 == 0 else "dve", o[:, b4 * 4:(b4 + 1) * 4, :], ps[:].rearrange("p (j t) -> p j t", j=4), [ps], [], pw=[o])
                P.dma("act", dst[:, :, i * 128:(i + 1) * 128], o[:], reads=[o], writes=[self.xT[0]])

    def phase_A(self, l):
        P = self.P
        with P.scope():
            cv = P.sbuf("a_cv", [128, 2, KC], F32)
            scT = P.sbuf("a_sc", [128, KC, 2], BF16)
            bm = P.sbuf("a_bm", [128, 6, KC], F32)
            ng = P.sbuf("a_ng", [128, 2, KC], F32)
            wt = [P.sbuf("a_w%d" % i, [128, KC, 512], BF16) for i in range(3)]
            P.dma("sp", cv[:], self.cvec.ap().rearrange("r (c p) -> p r c", p=128), reads=[self.cvec], writes=[cv], allow_slow_non_contiguous=True)
            P.dma("sp", bm[:], self.b_mod.ap()[l].rearrange("(i c p) -> p i c", p=128, c=KC), reads=[self.b_mod], writes=[bm], allow_slow_non_contiguous=True)
            P.dma("sp", ng[:], self.norm_g.ap()[l].rearrange("i (c p) -> p i c", p=128), reads=[self.norm_g], writes=[ng], allow_slow_non_contiguous=True)
            self.act(scT[:].rearrange("p c r -> p r c"), cv[:], AF.Silu, [cv], [scT])
            wv = self.w_mod.ap()[l].rearrange("(kc p) n -> p kc n", p=128)
            for idx in range(6):
                for jj in range(4):
                    w = self.rot("a_w", wt)
                    c0 = idx * D + jj * 512
                    P.dma("pool", w[:], wv[:, :, c0:c0 + 512], reads=[self.w_mod], writes=[w])
                    ps = self.rot("a_ps", self.ps[0:4])
                    for j in range(4):
                        for kc in range(KC):
                            self.mm(ps[:, 2 * j:2 * j + 2], w[:, kc, j * 128:(j + 1) * 128], scT[:, kc, :], kc == 0, kc == KC - 1, [w, scT], [ps])
                    self.tt("dve", self.modT[:, idx, 4 * jj:4 * jj + 4, :], ps[:, 0:8].rearrange("p (j r) -> p j r", r=2),
                            bc(bm[:, idx, 4 * jj:4 * jj + 4].unsqueeze(2), [128, 4, 2]), ALU.add, [ps, bm], [], pw=[self.modT])
            for n, idx in ((0, 1), (1, 4)):
                self.ts("dve", self.A1[:, n], self.modT[:, idx], 1.0, ALU.add, [self.modT], [], pw=[self.A1])
                self.tt("dve", self.A1[:, n], self.A1[:, n], bc(ng[:, n, :].unsqueeze(2), [128, KC, 2]), ALU.mult, [self.A1, ng], [self.A1])

    def norm_group(self, xg, hT, ng, n, col, sqb, rbc, tmp, ps):
        P = self.P
        shift = self.modT[:, 0 if n == 0 else 3]
        for c in range(KC):
            s = sqb[c % len(sqb)]
            self.act(s[:, 0:ng], xg[:, c, 0:ng], AF.Square, [xg], [s])
            self.mm(ps[:, 0:ng], self.ones_b, s[:, 0:ng], c == 0, c == KC - 1, [s, self.cstb], [ps])
        self.act(rbc[:, 0:ng], ps[:, 0:ng], AF.Sqrt, [ps, self.eps_t], [rbc], scale=1.0 / D, bias=self.eps_t[:, 0:1])
        P.op("dve", lambda e: e.reciprocal(out=rbc[:, 0:ng], in_=rbc[:, 0:ng]), reads=[rbc], writes=[rbc])
        for c in range(KC):
            t = tmp[c % len(tmp)]
            self.tt("dve", t[:, 0:ng], xg[:, c, 0:ng], rbc[:, 0:ng], ALU.mult, [xg, rbc], [t])
            self.act(hT[:, c, 0:ng], t[:, 0:ng], AF.Identity, [t, self.A1, self.modT], [], pw=[hT],
                     scale=self.A1[:, n, c, col:col + 1], bias=shift[:, c, col:col + 1])

    def phase_B(self, l):
        P, TT = self.P, self.TT
        xTl = self.xT[l]
        with P.scope():
            xg = P.sbuf("b_xg", [128, KC, 512], F32)
            hT = P.sbuf("b_hT", [128, KC, 512], BF16)
            sqb = [P.sbuf("b_sq%d" % i, [128, 512], BF16) for i in range(3)]
            tmp = [P.sbuf("b_tmp%d" % i, [128, 512], F32) for i in range(3)]
            rbc = P.sbuf("b_rbc", [128, 512], F32)
            wt = [P.sbuf("b_w%d" % i, [128, KC, 512], BF16) for i in range(3)]
            wsm = P.sbuf("b_wsm", [128, KC, 80], BF16)
            ob = [P.sbuf("b_ob%d" % i, [128, 4, 512], BF16) for i in range(2)]
            of = [P.sbuf("b_of%d" % i, [128, 4, 512], F32) for i in range(2)]
            ot = [P.sbuf("b_ot%d" % i, [128, 512], BF16) for i in range(3)]
            otf = [P.sbuf("b_otf%d" % i, [128, 512], F32) for i in range(2)]
            sqs = [P.sbuf("b_sqs%d" % i, [128, 512], BF16) for i in range(2)]
            binT = P.sbuf("b_binT", [128, 89], F32)
            bq = P.sbuf("b_bq", [128, 4], F32)
            btok = P.sbuf("b_btok", [128, 3664], F32)
            gb = P.sbuf("b_gb", [128, 16], F32)
            rs = P.sbuf("b_rs", [128, 4, 2], F32)
            gl = P.sbuf("b_gl", [128, 80], F32)
            e1 = P.sbuf("b_e1", [128, 8], F32)
            ur = P.sbuf("b_ur", [128, 4, 8, 128], BF16)
            gfm = P.sbuf("b_gfm", [128, 8], F32)
            bgs = P.sbuf("b_bgs", [128, 8], F32)
            P.dma("sp", gfm[:, 0:4], self.qa_g.ap()[l].rearrange("(c p) -> p c", p=128), reads=[self.qa_g], writes=[gfm], allow_slow_non_contiguous=True)
            P.dma("sp", gfm[:, 4:8], self.kva_g.ap()[l].rearrange("(c p) -> p c", p=128), reads=[self.kva_g], writes=[gfm], allow_slow_non_contiguous=True)
            bi = self.b_in.ap()[l]
            fm = [("q", OFF_Q, 512), ("k", OFF_K, 512), ("qa", OFF_QA, 512), ("kva", OFF_KVA, 512), ("u", OFF_U, 1024), ("bg", OFF_BG, 6144)]
            fmc = {}
            c0 = 0
            for name, off, n in fm:
                fmc[name] = c0
                P.dma("sp", binT[:, c0:c0 + n // 128], bi[off:off + n].rearrange("(c p) -> p c", p=128), reads=[self.b_in], writes=[binT], allow_slow_non_contiguous=True)
                c0 += n // 128
            self.ts("dve", bq[:], binT[:, 0:4], 128.0 ** -0.5, ALU.mult, [binT], [bq])
            self.tt("dve", bgs[:], binT[:, fmc["qa"]:fmc["qa"] + 8], gfm[:], ALU.mult, [binT, gfm], [bgs])
            tmo = {"k": 0, "v": 512, "o": 1536, "u": 2560, "g": 3584, "kpe": 3600}
            for name, off, n in (("k", OFF_K, 512), ("v", OFF_V, 1024), ("o", OFF_O, 1024), ("u", OFF_U, 1024), ("g", OFF_G, 16), ("kpe", OFF_KPE, 64)):
                P.dma("sp", btok[:, tmo[name]:tmo[name] + n], bass.AP(self.b_in.h, l * DIN + off, [[0, 128], [1, n]]), reads=[self.b_in], writes=[btok])
            P.dma("sp", gb[:], bass.AP(self.ml_gate_b.h, l * 16, [[0, 128], [1, 16]]), reads=[self.ml_gate_b], writes=[gb])
            self.tt("dve", btok[:, 3584:3600], btok[:, 3584:3600], gb[:], ALU.add, [btok, gb], [btok])
            wv = self.w_in.ap()[l].rearrange("(kc p) n -> p kc n", p=128)
            P.dma("pool", wsm[:, :, 0:16], wv[:, :, OFF_G:OFF_G + 16], reads=[self.w_in], writes=[wsm])
            P.dma("pool", wsm[:, :, 16:80], wv[:, :, OFF_KPE:OFF_KPE + 64], reads=[self.w_in], writes=[wsm])

            for (s0, ng) in self.groups:
                col = 1 if s0 < self.TC else 0
                ntt = ng // 128
                P.dma("sp", xg[:, :, 0:ng], xTl.ap().rearrange("(c p) t -> p c t", p=128)[:, :, s0:s0 + ng], reads=[xTl], writes=[xg])
                self.norm_group(xg, hT, ng, 0, col, sqb, rbc, tmp, self.ps[7])

                def fm_tile(w, wc, name, ci, dst, dt_f32=False):
                    o = self.rot("b_of", of) if dt_f32 else self.rot("b_ob", ob)
                    for j in range(4):
                        ps = self.rot("b_ps", self.ps[0:5])
                        for kc in range(KC):
                            self.mm(ps[:, 0:ng], w[:, kc, wc + j * 128:wc + (j + 1) * 128], hT[:, kc, 0:ng], kc == 0, kc == KC - 1, [w, hT], [ps])
                        bcol = binT[:, fmc[name] + ci + j:fmc[name] + ci + j + 1]
                        if name == "q":
                            self.act(o[:, j, 0:ng], ps[:, 0:ng], AF.Identity, [ps, bq], [], pw=[o], scale=128.0 ** -0.5, bias=bq[:, ci + j:ci + j + 1])
                        elif name == "bg":
                            self.act(o[:, j, 0:ng], ps[:, 0:ng], AF.Sigmoid, [ps, binT], [], pw=[o], bias=bcol)
                        elif name in ("qa", "kva"):
                            which = 0 if name == "qa" else 1
                            self.act(o[:, j, 0:ng], ps[:, 0:ng], AF.Identity, [ps, gfm, bgs], [], pw=[o],
                                     scale=gfm[:, 4 * which + j:4 * which + j + 1], bias=bgs[:, 4 * which + j:4 * which + j + 1])
                            sq = self.rot("b_sqs", sqs)
                            self.act(sq[:, 0:ng], ps[:, 0:ng], AF.Square, [ps, binT], [sq], bias=bcol)
                            which = 0 if name == "qa" else 1
                            for t in range(ntt):
                                self.mm(self.ps[6][:, 8 * which + t:8 * which + t + 1], sq[:, t * 128:(t + 1) * 128], self.ones_b[:, 0:1],
                                        (which == 0 and j == 0 and t == 0), j == 3, [sq, self.cstb], [self.ps[6]])
                        else:
                            eng = "act" if j % 2 == 0 else "dve"
                            if eng == "act":
                                self.act(o[:, j, 0:ng], ps[:, 0:ng], AF.Identity, [ps, binT], [], pw=[o], bias=bcol)
                            else:
                                self.ts("dve", o[:, j, 0:ng], ps[:, 0:ng], bcol, ALU.add, [ps, binT], [], pw=[o])
                    P.dma("act", dst, o[:, :, 0:ng], reads=[o], writes=[dst_t[0]])

                def tm_tile(w, wc, n, name, bo):
                    for t in range(ntt):
                        ps = self.rot("b_ps", self.ps[0:5])
                        for kc in range(KC):
                            self.mm(ps[:, 0:n], hT[:, kc, t * 128:(t + 1) * 128], w[:, kc, wc:wc + n], kc == 0, kc == KC - 1, [w, hT], [ps])
                        r0 = s0 + t * 128
                        if name in ("k", "v"):
                            o = self.rot("b_ot", ot)
                            self.tt("dve", o[:, 0:n], ps[:, 0:n], btok[:, bo:bo + n], ALU.add, [ps, btok], [o])
                            dstT = self.ktok if name == "k" else self.vtok
                            dcol = 0 if name == "k" else tm_c[0]
                            P.dma("act", dstT.ap()[r0:r0 + 128, dcol:dcol + n], o[:, 0:n], reads=[o], writes=[dstT])
                        elif name == "o":
                            f = self.rot("b_otf", otf)
                            o = self.rot("b_ot", ot)
                            self.tt("dve", f[:, 0:n], ps[:, 0:n], btok[:, bo:bo + n], ALU.add, [ps, btok], [f])
                            self.act(o[:, 0:n], f[:, 0:n], AF.Sigmoid, [f], [o])
                            P.dma("act", self.otok.ap()[r0:r0 + 128, tm_c[0]:tm_c[0] + n], o[:, 0:n], reads=[o], writes=[self.otok])
                        elif name == "u":
                            o = self.rot("b_ot", ot)
                            self.tt("dve", o[:, 0:n], ps[:, 0:n], btok[:, bo:bo + n], ALU.add, [ps, btok], [o])
                            pst = self.ps[5]
                            psb = pst[:].bitcast(BF16)
                            for j in range(4):
                                self.tr(psb[:, j * 128:(j + 1) * 128], o[:, j * 128:(j + 1) * 128], self.J_b, [o, self.cstb], [pst])
                            ch0 = tm_c[0] // 128
                            self.copy("act", ur[:, t, ch0:ch0 + 4, :], psb[:, 0:512].rearrange("p (j t) -> p j t", j=4), [pst], [], pw=[ur])
                            if ch0 == 4:
                                i0 = self.mirror(r0)
                                P.dma("act", self.uTr.ap().rearrange("(c p) t -> p c t", p=128)[:, :, i0:i0 + 128], ur[:, t], reads=[ur], writes=[self.uTr])
                        elif name == "gk":
                            self.tt("dve", gl[:], ps[:, 0:80], btok[:, 3584:3664], ALU.add, [ps, btok], [gl])
                            g4 = gl[:, 0:16].rearrange("p (d t h) -> p d t h", d=2, t=2)
                            self.act(e1[:].rearrange("p (d h) -> p d h", d=2), g4[:, :, 1, :], AF.Exp, [gl], [e1], scale=-1.0)
                            self.act(e1[:], e1[:], AF.Ln, [e1], [e1], bias=1.0)
                            self.ts("dve", g4[:, :, 1, :], e1[:].rearrange("p (d h) -> p d h", d=2), -1.0, ALU.mult, [e1], [gl])
                            P.dma("act", self.gtok.ap()[r0:r0 + 128, :], gl[:, 0:16], reads=[gl], writes=[self.gtok])
                            P.dma("act", self.kpe.ap()[r0:r0 + 128, :], gl[:, 16:80], reads=[gl], writes=[self.kpe])

                def load_w(off):
                    w = self.rot("b_w", wt)
                    P.dma("pool", w[:], wv[:, :, off:off + 512], reads=[self.w_in], writes=[w])
                    return w

                def fmdst(Tt, row0, f32=False):
                    return Tt.ap().rearrange("(j p) t -> p j t", p=128)[:, row0 // 128:row0 // 128 + 4, s0:s0 + ng]

                dst_t = [None]
                tm_c = [0]
                w = load_w(OFF_Q)
                dst_t[0] = self.qT
                fm_tile(w, 0, "q", 0, self.qT.ap().rearrange("h p t -> p h t")[:, :, s0:s0 + ng])
                w = load_w(OFF_K)
                dst_t[0] = self.kT
                fm_tile(w, 0, "k", 0, self.kT.ap().rearrange("h p t -> p h t")[:, :, s0:s0 + ng])
                tm_tile(w, 0, 512, "k", tmo["k"])
                for h2 in range(2):
                    w = load_w(OFF_V + 512 * h2)
                    tm_c[0] = 512 * h2
                    tm_tile(w, 0, 512, "v", tmo["v"] + 512 * h2)
                for h2 in range(2):
                    w = load_w(OFF_O + 512 * h2)
                    tm_c[0] = 512 * h2
                    tm_tile(w, 0, 512, "o", tmo["o"] + 512 * h2)
                tm_tile(wsm, 0, 80, "gk", 0)
                for name, off, Tt in (("qa", OFF_QA, self.qaT), ("kva", OFF_KVA, self.kvaT)):
                    w = load_w(off)
                    dst_t[0] = Tt
                    fm_tile(w, 0, name, 0, fmdst(Tt, 0))
                for which in range(2):
                    self.act(rs[:, 0:ntt, which], self.ps[6][:, 8 * which:8 * which + ntt], AF.Sqrt, [self.ps[6], self.eps_t], [], pw=[rs], scale=1.0 / 512, bias=self.eps_t[:, 0:1])
                P.op("dve", lambda e: e.reciprocal(out=rs[:, 0:ntt, :], in_=rs[:, 0:ntt, :]), reads=[rs], writes=[rs])
                P.dma("act", self.rsq.ap()[s0:s0 + ng, :].rearrange("(t p) c -> p t c", p=128), rs[:, 0:ntt, :], reads=[rs], writes=[self.rsq])
                for h2 in range(2):
                    w = load_w(OFF_U + 512 * h2)
                    dst_t[0] = self.uT
                    fm_tile(w, 0, "u", 4 * h2, fmdst(self.uT, 512 * h2), dt_f32=True)
                    tm_c[0] = 512 * h2
                    tm_tile(w, 0, 512, "u", tmo["u"] + 512 * h2)
                for b12 in range(12):
                    w = load_w(OFF_BG + 512 * b12)
                    dst_t[0] = self.gT
                    fm_tile(w, 0, "bg", 4 * b12, fmdst(self.gT, 512 * b12))

    def mirror(self, r0, n=128):
        TC, TL = self.TC, self.TL
        if r0 < TC:
            return TC - n - r0
        return TC + (TL - n - (r0 - TC))


def make_consts(TL, TC):
    cst = np.zeros((128, 640), np.float32)
    cst[:, 0:128] = np.eye(128)
    cst[:, 128:256] = np.eye(128)[::-1]
    k = np.arange(64)
    cst[0:64, 256:320] = (k[:, None] <= k[None, :])
    cst[0:64, 320:384] = (k[:, None] >= k[None, :])
    cst[:, 384:512] = 1.0
    TT = TL + TC
    rope = np.zeros((TT, 128), np.float32)
    rope[:, 0:64] = 1.0
    t = np.arange(TL)
    row = (t // 64).astype(np.float32)
    col = (t % 64).astype(np.float32)
    inv = (np.float32(10000.0) ** (-np.arange(16, dtype=np.float32) / np.float32(16))).astype(np.float32)
    ar = (row[:, None] * inv).astype(np.float32)
    ac = (col[:, None] * inv).astype(np.float32)
    rope[TC:, 0:64] = np.concatenate([np.cos(ar), np.cos(ar), np.cos(ac), np.cos(ac)], axis=1)
    rope[TC:, 64:128] = np.concatenate([-np.sin(ar), np.sin(ar), -np.sin(ac), np.sin(ac)], axis=1)
    return cst, rope


def prep_shared(inp, L):
    f = lambda a: np.ascontiguousarray(np.asarray(a, dtype=np.float32))
    sh = {}
    for k in ("w_mod", "b_mod", "norm_g", "w_in", "b_in", "mla_qa_g", "mla_kva_g", "mla_w_uq", "mla_w_ukv", "mla_qn_g", "mla_kn_g",
              "s5_d", "s5_w_glu", "s5_b_glu", "w_branch", "w_out", "w_ff1", "w_ff2"):
        sh[k] = f(inp[k])[:L]
    sh["ml_gate_b"] = f(inp["ml_gate_b"])[:L].reshape(L, 16)
    sh["ml_norm_g"] = f(inp["ml_norm_g"])[:L].reshape(L, 1024)
    lam = np.zeros((L, 2, 3, 128, 32), np.float32)
    sb = np.zeros((L, 2, 2, 128, 8, 2, 128), np.float32)
    sc = np.zeros((L, 2, 2, 128, 32, 32), np.float32)
    a_re, a_im, ldt = f(inp["s5_a_re"]), f(inp["s5_a_im"]), f(inp["s5_log_dt"])
    bs = (f(inp["s5_b_re"]), f(inp["s5_b_im"]))
    cs = (f(inp["s5_c_re"]), f(inp["s5_c_im"]))

    def lay(a):
        return a.reshape(32, 2, 64).transpose(1, 2, 0).reshape(128, 32)

    for l in range(L):
        for d in range(2):
            lam[l, d, 0] = lay(a_re[l, d])
            lam[l, d, 1] = lay(a_im[l, d])
            lam[l, d, 2] = lay(np.repeat(ldt[l, d][:, None], 64, axis=1))
            for ri in range(2):
                b = bs[ri][l, d]
                c = cs[ri][l, d]
                for g in range(64):
                    q, pp, g2 = g // 8, (g % 8) // 2, g % 2
                    sb[l, d, ri, pp * 32 + g2 * 16:pp * 32 + g2 * 16 + 16, q, 0, g2 * 64:(g2 + 1) * 64] = b[g].T
                    if pp == 3:
                        sb[l, d, ri, pp * 32 + g2 * 16:pp * 32 + g2 * 16 + 16, q, 1, g2 * 64:(g2 + 1) * 64] = b[g].T
                    sc[l, d, ri, g2 * 64:(g2 + 1) * 64, g // 2, g2 * 16:(g2 + 1) * 16] = c[g].T
    sh["s5_lam"], sh["s5_b"], sh["s5_c"] = lam, sb, sc
    return sh


def prep_core(inp, b, TL, TC):
    f = lambda a: np.ascontiguousarray(np.asarray(a, dtype=np.float32))
    return {
        "x": f(inp["x"][b]),
        "ctx": f(inp["ctx"][b]),
        "cvec": f(np.stack([np.asarray(inp["c"][b]), np.asarray(inp["c_ctx"])])),
    }


PHASES = ("A", "B", "C1", "C2", "C3", "DE")


def build_program(TL, TC, depth=DEPTH, debug=(), stop_after=None):
    B = Builder(TL, TC, depth, debug, stop_after)
    B.declare()
    B.phase_T0()
    done = False
    for l in range(depth):
        for name in PHASES:
            getattr(B, "phase_" + name)(l)
            if stop_after == (name, l):
                done = True
                break
        if done:
            break
    B.P.finish()
    return B


def phase_C1(self, l):
    P, TC, TL, TT = self.P, self.TC, self.TL, self.TT
    with P.scope():
        QTg = [P.sbuf("m_q%d" % d, [128, 4, 512], BF16) for d in range(2)]
        KTg = [P.sbuf("m_k%d" % d, [128, 4, 512], BF16) for d in range(2)]
        Ktg = [P.sbuf("m_kt%d" % d, [64, 8, 512], BF16) for d in range(2)]
        Vg = [P.sbuf("m_v%d" % d, [64, 8, 1024], BF16) for d in range(2)]
        Gg = [P.sbuf("m_g%d" % d, [64, 8, 16], F32) for d in range(2)]
        Cf = [P.sbuf("m_cf%d" % d, [128, 4, 257], F32) for d in range(2)]
        Cb = [P.sbuf("m_cb%d" % d, [128, 4, 257], BF16) for d in range(2)]
        sm = [[P.sbuf("m_s%d_%d" % (d, i), [128, 40], F32) for i in range(2)] for d in range(2)]
        smb = [[P.sbuf("m_sb%d_%d" % (d, i), [64, 8], BF16) for i in range(2)] for d in range(2)]
        Vs = [[P.sbuf("m_vs%d_%d" % (d, i), [64, 4, 256], BF16) for i in range(2)] for d in range(2)]
        VF = [[P.sbuf("m_vf%d_%d" % (d, i), [64, 4, 256], BF16) for i in range(2)] for d in range(2)]
        Sm = [[P.sbuf("m_sm%d_%d" % (d, i), [64, 4, 64], BF16) for i in range(2)] for d in range(2)]
        hst = [[P.sbuf("m_h%d_%d" % (d, i), [64, 4, 256], F32) for i in range(2)] for d in range(2)]
        for d in range(2):
            P.op("dve", lambda e, d=d: e.memset(Cf[d][:], 0.0), writes=[Cf[d]])
            P.op("pool", lambda e, d=d: e.memset(Cb[d][:], 0.0), writes=[Cb[d]])
        order = []
        for d in range(2):
            o = []
            glist = self.groups if d == 0 else [self.groups[0]] + self.groups[1:][::-1]
            for gi, (s0, ng) in enumerate(glist):
                cis = list(range(ng // 64))
                if d == 1:
                    cis = cis[::-1]
                for ci in cis:
                    o.append((s0, ng, ci))
            order.append(o)
        cur = [None, None]
        psm = [self.ps[0], self.ps[5]]
        pacc = [(self.ps[1], self.ps[2]), (self.ps[6], self.ps[7])]
        pdc = (self.ps[3], self.ps[4])
        for step in range(len(order[0])):
            for d in range(2):
                s0, ng, ci = order[d][step]
                nch = ng // 64
                if cur[d] != s0:
                    cur[d] = s0
                    P.dma("sp", QTg[d][:, :, 0:ng], self.qT.ap().rearrange("h p t -> p h t")[:, :, s0:s0 + ng], reads=[self.qT], writes=[QTg[d]])
                    P.dma("sp", KTg[d][:, :, 0:ng], self.kT.ap().rearrange("h p t -> p h t")[:, :, s0:s0 + ng], reads=[self.kT], writes=[KTg[d]])
                    P.dma("sp", Ktg[d][:, 0:nch, :], self.ktok.ap()[s0:s0 + ng, :].rearrange("(c p) n -> p c n", p=64), reads=[self.ktok], writes=[Ktg[d]])
                    P.dma("sp", Vg[d][:, 0:nch, :], self.vtok.ap()[s0:s0 + ng, :].rearrange("(c p) n -> p c n", p=64), reads=[self.vtok], writes=[Vg[d]])
                    P.dma("sp", Gg[d][:, 0:nch, :], self.gtok.ap()[s0:s0 + ng, :].rearrange("(c p) n -> p c n", p=64), reads=[self.gtok], writes=[Gg[d]])
                c0 = ci * 64
                t0 = s0 + c0
                par = step % 2
                S_, SB_, Vs_, VF_, Sm_, H_ = sm[d][par], smb[d][par], Vs[d][par], VF[d][par], Sm[d][par], hst[d][par]
                pA, (pa0, pa1), (pd0, pd1) = psm[d], pacc[d], pdc
                tri = self.triL_f if d == 0 else self.triU_f
                li = Gg[d][:, ci, d * 8:d * 8 + 4]
                lf = Gg[d][:, ci, d * 8 + 4:d * 8 + 8]
                cst = self.cst_t
                self.mm(pA[0:64, 0:4], tri, lf, True, True, [cst, Gg[d]], [pA])
                self.mm(pA[:, 8:12], self.ones_f[0:64, :], lf, True, True, [cst, Gg[d]], [pA])
                dif, esc, ecn, eF, escF, absd, rden = (S_[0:64, 0:4], S_[0:64, 4:8], S_[0:64, 8:12], S_[:, 12:16], S_[0:64, 16:20], S_[0:64, 20:24], S_[0:64, 24:28])
                self.tt("dve", dif, li, pA[0:64, 0:4], ALU.subtract, [Gg[d], pA], [S_])
                self.act(esc, dif, AF.Exp, [S_], [S_])
                self.act(ecn, pA[0:64, 0:4], AF.Exp, [pA], [S_], scale=-1.0)
                self.act(eF, pA[:, 8:12], AF.Exp, [pA], [S_])
                self.tt("dve", escF, esc, eF[0:64, :], ALU.mult, [S_], [S_])
                self.copy("act", SB_[:, 0:4], esc, [S_], [SB_])
                self.copy("act", SB_[:, 4:8], escF, [S_], [SB_])
                vv = Vg[d][:, ci, :].rearrange("p (h v) -> p h v", h=4)
                self.tt("dve", Vs_[:], vv, bc(esc.unsqueeze(2), [64, 4, 256]), ALU.mult, [Vg[d], S_], [Vs_])
                self.tt("pool", VF_[:], vv, bc(escF.unsqueeze(2), [64, 4, 256]), ALU.mult, [Vg[d], S_], [VF_])
                for h in range(4):
                    self.mm(pA[0:64, 256 + h * 64:256 + (h + 1) * 64], KTg[d][:, h, c0:c0 + 64], QTg[d][:, h, c0:c0 + 64], True, True, [KTg[d], QTg[d]], [pA])
                self.tt("dve", Sm_[:], pA[0:64, 256:512].rearrange("p (h t) -> p h t", h=4), bc(tri.unsqueeze(1), [64, 4, 64]), ALU.mult, [pA, cst], [Sm_])
                for h in range(4):
                    bank = pa0 if h < 2 else pa1
                    off = (h % 2) * 256
                    self.mm(bank[0:64, off:off + 256], QTg[d][:, h, c0:c0 + 64], Cb[d][:, h, 0:256], True, False, [QTg[d], Cb[d]], [bank])
                    self.mm(bank[0:64, off:off + 256], Sm_[:, h, :], Vs_[:, h, :], False, True, [Sm_, Vs_], [bank])
                for h in range(4):
                    self.mm(pA[0:64, 16 + h:17 + h], QTg[d][:, h, c0:c0 + 64], Cb[d][:, h, 256:257], True, False, [QTg[d], Cb[d]], [pA])
                    self.mm(pA[0:64, 16 + h:17 + h], Sm_[:, h, :], SB_[:, h:h + 1], False, True, [Sm_, SB_], [pA])
                for h in range(4):
                    bank = pd0 if h < 2 else pd1
                    off = (h % 2) * 256
                    self.mm(bank[:, off:off + 256], Ktg[d][:, ci, h * 128:(h + 1) * 128], VF_[:, h, :], True, True, [Ktg[d], VF_], [bank])
                    self.mm(pA[:, 24 + h:25 + h], Ktg[d][:, ci, h * 128:(h + 1) * 128], SB_[:, 4 + h:5 + h], True, True, [Ktg[d], SB_], [pA])
                self.act(absd, pA[0:64, 16:20], AF.Abs, [pA], [S_])
                self.tt("dve", rden, absd, ecn, ALU.max, [S_], [S_])
                P.op("dve", lambda e, rden=rden: e.reciprocal(out=rden, in_=rden), reads=[S_], writes=[S_])
                for b2, bank in enumerate((pa0, pa1)):
                    self.tt("dve", H_[:, 2 * b2:2 * b2 + 2, :], bank[0:64, :].rearrange("p (h v) -> p h v", h=2),
                            bc(rden[:, 2 * b2:2 * b2 + 2].unsqueeze(2), [64, 2, 256]), ALU.mult, [bank, S_], [], pw=[H_])
                P.dma("act", self.hml.ap()[d, t0:t0 + 64, :], H_[:].rearrange("p h v -> p (h v)"), reads=[H_], writes=[self.hml])
                for h in range(4):
                    bank = pd0 if h < 2 else pd1
                    off = (h % 2) * 256
                    self.stt(Cf[d][:, h, 0:256], Cf[d][:, h, 0:256], eF[:, h:h + 1], bank[:, off:off + 256], ALU.mult, ALU.add, [Cf[d], S_, bank], [Cf[d]])
                    self.stt(Cf[d][:, h, 256:257], Cf[d][:, h, 256:257], eF[:, h:h + 1], pA[:, 24 + h:25 + h], ALU.mult, ALU.add, [Cf[d], S_, pA], [Cf[d]])
                self.copy("act", Cb[d][:], Cf[d][:], [Cf[d]], [Cb[d]])

    with P.scope():
        gnb = P.sbuf("mo_gn", [128, 1024], F32)
        P.dma("sp", gnb[:], bass.AP(self.ml_norm_g.h, l * 1024, [[0, 128], [1, 1024]]), reads=[self.ml_norm_g], writes=[gnb])
        h0 = [P.sbuf("mo_h0_%d" % i, [128, 1024], F32) for i in range(2)]
        h1 = [P.sbuf("mo_h1_%d" % i, [128, 1024], F32) for i in range(2)]
        og = [P.sbuf("mo_og_%d" % i, [128, 1024], BF16) for i in range(2)]
        sq = P.sbuf("mo_sq", [128, 1024], F32)
        ss = [P.sbuf("mo_ss%d" % i, [128, 4], F32) for i in range(2)]
        ab = [P.sbuf("mo_ab%d" % i, [128, 1024], BF16) for i in range(2)]
        aT = [P.sbuf("mo_aT%d" % i, [128, 8, 512], BF16) for i in range(2)]
        for gi, (s0, ng) in enumerate(self.groups):
            A_ = aT[gi % 2]
            for t in range(ng // 128):
                r0 = s0 + t * 128
                i = self.rr.get("m_o", 0)
                self.rr["m_o"] = i + 1
                a0, a1, o_, s_, b_ = h0[i % 2], h1[i % 2], og[i % 2], ss[i % 2], ab[i % 2]
                P.dma("sp", a0[:], self.hml.ap()[0, r0:r0 + 128, :], reads=[self.hml], writes=[a0])
                P.dma("sp", a1[:], self.hml.ap()[1, r0:r0 + 128, :], reads=[self.hml], writes=[a1])
                P.dma("sp", o_[:], self.otok.ap()[r0:r0 + 128, :], reads=[self.otok], writes=[o_])
                self.tt("dve", a0[:], a0[:], a1[:], ALU.add, [a0, a1], [a0])
                self.act(sq[:], a0[:], AF.Square, [a0], [sq])
                P.op("dve", lambda e, s_=s_: e.tensor_reduce(out=s_[:], in_=sq[:].rearrange("p (h v) -> p h v", h=4), axis=AX.X, op=ALU.add), reads=[sq], writes=[s_])
                self.act(s_[:], s_[:], AF.Sqrt, [s_, self.eps_t], [s_], scale=1.0 / 256, bias=self.eps_t[:, 0:1])
                P.op("dve", lambda e, s_=s_: e.reciprocal(out=s_[:], in_=s_[:]), reads=[s_], writes=[s_])
                a3 = a0[:].rearrange("p (h v) -> p h v", h=4)
                self.tt("dve", a3, a3, bc(s_[:].unsqueeze(2), [128, 4, 256]), ALU.mult, [a0, s_], [a0])
                self.tt("pool", a0[:], a0[:], gnb[:], ALU.mult, [a0, gnb], [a0])
                self.tt("dve", b_[:], a0[:], o_[:], ALU.mult, [a0, o_], [b_])
                pst = self.rot("m_pst", self.ps[0:2])
                psb = pst[:].bitcast(BF16)
                for j in range(8):
                    self.tr(psb[:, j * 128:(j + 1) * 128], b_[:, j * 128:(j + 1) * 128], self.ident_b, [b_, self.cstb], [pst])
                self.copy("act", A_[:, :, t * 128:(t + 1) * 128], psb[:, 0:1024].rearrange("p (j t) -> p j t", j=8), [pst], [], pw=[A_])
            P.dma("act", self.brT.ap()[0].rearrange("(c p) t -> p c t", p=128)[:, :, s0:s0 + ng], A_[:, :, 0:ng], reads=[A_], writes=[self.brT])


Builder.phase_C1 = phase_C1


def rope_ops(self, eng, src, dst, tmp1, tmp2, rt, nh, reads, writes):
    C = bc(rt[:, 0:64].unsqueeze(1), [128, nh, 64])
    S5 = rt[:, 64:128].rearrange("p (b f i) -> p b f i", b=2, f=2)
    s5 = src.rearrange("p h (b f i) -> p h b f i", b=2, f=2)
    t5 = tmp2.rearrange("p h (b f i) -> p h b f i", b=2, f=2)
    self.tt(eng, tmp1, src, C, ALU.mult, reads, writes)
    for f in range(2):
        self.tt(eng, t5[:, :, :, f, :], s5[:, :, :, 1 - f, :], bc(S5[:, :, f, :].unsqueeze(1), [128, nh, 2, 16]), ALU.mult, reads, writes)
    self.tt(eng, dst, tmp1, tmp2, ALU.add, reads, writes)


def phase_C2(self, l):
    P, TC, TL, TT, NT = self.P, self.TC, self.TL, self.TT, self.NT
    with P.scope():
        wq = P.sbuf("c_wq", [128, 4, 1536], BF16)
        wkv = P.sbuf("c_wkv", [128, 4, 2048], BF16)
        gq = P.sbuf("c_gq", [128, 192], F32)
        gk = P.sbuf("c_gk", [128, 192], F32)
        P.dma("pool", wq[:], self.w_uq.ap()[l].rearrange("(kc p) n -> p kc n", p=128), reads=[self.w_uq], writes=[wq])
        P.dma("pool", wkv[:], self.w_ukv.ap()[l].rearrange("(kc p) n -> p kc n", p=128), reads=[self.w_ukv], writes=[wkv])
        P.dma("sp", gq[:], bass.AP(self.qn_g.h, l * 192, [[0, 128], [1, 192]]), reads=[self.qn_g], writes=[gq])
        P.dma("sp", gk[:], bass.AP(self.kn_g.h, l * 192, [[0, 128], [1, 192]]), reads=[self.kn_g], writes=[gk])
        qaTg = P.sbuf("c_qa", [128, 4, 512], BF16)
        kvaTg = P.sbuf("c_kva", [128, 4, 512], BF16)
        rsg = P.sbuf("c_rs", [128, 4, 2], F32)
        rpt = [P.sbuf("c_rp%d" % i, [128, 128], F32) for i in range(2)]
        kpt = [P.sbuf("c_kp%d" % i, [128, 64], F32) for i in range(2)]
        qf = P.sbuf("c_qf", [128, 2048], F32)
        sqv = P.sbuf("c_sq", [128, 2048], F32)
        fin = [P.sbuf("c_fin%d" % i, [128, 8, 192], BF16) for i in range(2)]
        vfin = [P.sbuf("c_vf%d" % i, [128, 8, 128], BF16) for i in range(2)]
        rp = P.sbuf("c_rpp", [128, 8, 64], F32)
        t1 = P.sbuf("c_t1", [128, 8, 64], F32)
        t2 = P.sbuf("c_t2", [128, 8, 64], F32)
        st = [P.sbuf("c_st%d" % i, [128, 12], F32) for i in range(2)]
        kk = P.sbuf("c_kk", [128, 4, 64], F32)
        stn = [P.sbuf("c_stn%d" % i, [128, 8, 512], BF16) for i in range(2)]
        str_ = [P.sbuf("c_str%d" % i, [64, 8, 512], BF16) for i in range(2)]
        for (s0, ng) in self.groups:
            ntt = ng // 128
            P.dma("sp", qaTg[:, :, 0:ng], self.qaT.ap().rearrange("(c p) t -> p c t", p=128)[:, :, s0:s0 + ng], reads=[self.qaT], writes=[qaTg])
            P.dma("sp", kvaTg[:, :, 0:ng], self.kvaT.ap().rearrange("(c p) t -> p c t", p=128)[:, :, s0:s0 + ng], reads=[self.kvaT], writes=[kvaTg])
            P.dma("sp", rsg[:, 0:ntt, :], self.rsq.ap()[s0:s0 + ng, :].rearrange("(t p) c -> p t c", p=128), reads=[self.rsq], writes=[rsg])
            qn_, qr_ = stn[0], str_[0]
            kn_, kr_ = stn[1], str_[1]
            for t in range(ntt):
                r0 = s0 + t * 128
                i = self.rr.get("c_i", 0)
                self.rr["c_i"] = i + 1
                rt, kp, S_ = rpt[i % 2], kpt[i % 2], st[i % 2]
                P.dma("sp", rt[:], self.rope.ap()[r0:r0 + 128, :], reads=[self.rope], writes=[rt])
                P.dma("sp", kp[:], self.kpe.ap()[r0:r0 + 128, :], reads=[self.kpe], writes=[kp])
                for cg in range(3):
                    ps = self.rot("c_ps", self.ps[0:6])
                    for kc in range(4):
                        self.mm(ps[:, 0:512], qaTg[:, kc, t * 128:(t + 1) * 128], wq[:, kc, cg * 512:(cg + 1) * 512], kc == 0, kc == 3, [qaTg, wq], [ps])
                    self.act(qf[:, cg * 512:(cg + 1) * 512], ps[:, 0:512], AF.Identity, [ps, rsg], [], pw=[qf], scale=rsg[:, t, 0:1])
                    self.act(sqv[:, cg * 512:(cg + 1) * 512], ps[:, 0:512], AF.Square, [ps, rsg], [], pw=[sqv], scale=rsg[:, t, 0:1])
                q3 = qf[:, 0:1536].rearrange("p (h e) -> p h e", h=8)
                s3 = sqv[:, 0:1536].rearrange("p (h e) -> p h e", h=8)
                rq = S_[:, 0:8]
                P.op("dve", lambda e, rq=rq, s3=s3: e.tensor_reduce(out=rq, in_=s3, axis=AX.X, op=ALU.add), reads=[sqv], writes=[S_])
                self.act(rq, rq, AF.Sqrt, [S_, self.eps_t], [S_], scale=1.0 / 192, bias=self.eps_t[:, 0:1])
                P.op("dve", lambda e, rq=rq: e.reciprocal(out=rq, in_=rq), reads=[S_], writes=[S_])
                self.tt("dve", s3, q3, bc(rq.unsqueeze(2), [128, 8, 192]), ALU.mult, [qf, S_], [sqv])
                F_ = fin[0]
                self.tt("dve", F_[:, :, 0:128], s3[:, :, 0:128], bc(gq[:, 0:128].unsqueeze(1), [128, 8, 128]), ALU.mult, [sqv, gq], [F_])
                self.tt("pool", rp[:], s3[:, :, 128:192], bc(gq[:, 128:192].unsqueeze(1), [128, 8, 64]), ALU.mult, [sqv, gq], [rp])
                rope_ops(self, "dve", rp[:], F_[:, :, 128:192], t1[:], t2[:], rt, 8, [rp, rt, t1, t2], [t1, t2, F_])
                pn, pr = self.ps[6], self.ps[7]
                pnb, prb = pn[:].bitcast(BF16), pr[:].bitcast(BF16)
                for h in range(8):
                    self.tr(pnb[:, h * 128:(h + 1) * 128], F_[:, h, 0:128], self.ident_b, [F_, self.cstb], [pn])
                    self.tr(prb[0:64, h * 128:(h + 1) * 128], F_[:, h, 128:192], self.ident_b, [F_, self.cstb], [pr])
                self.copy("act", qn_[:, :, t * 128:(t + 1) * 128], pnb[:, 0:1024].rearrange("p (h t) -> p h t", h=8), [pn], [], pw=[qn_])
                self.copy("act", qr_[:, :, t * 128:(t + 1) * 128], prb[0:64, 0:1024].rearrange("p (h t) -> p h t", h=8), [pr], [], pw=[qr_])
                for cg in range(4):
                    ps = self.rot("c_ps", self.ps[0:6])
                    for kc in range(4):
                        self.mm(ps[:, 0:512], kvaTg[:, kc, t * 128:(t + 1) * 128], wkv[:, kc, cg * 512:(cg + 1) * 512], kc == 0, kc == 3, [kvaTg, wkv], [ps])
                    self.act(qf[:, cg * 512:(cg + 1) * 512], ps[:, 0:512], AF.Identity, [ps, rsg], [], pw=[qf], scale=rsg[:, t, 1:2])
                k3 = qf[:].rearrange("p (h e) -> p h e", h=8)
                s3k = sqv[:, 0:1024].rearrange("p (h e) -> p h e", h=8)
                self.act(s3k, k3[:, :, 0:128], AF.Square, [qf], [sqv])
                rk = S_[:, 0:8]
                sp_ = S_[:, 8:9]
                P.op("dve", lambda e, rk=rk, s3k=s3k: e.tensor_reduce(out=rk, in_=s3k, axis=AX.X, op=ALU.add), reads=[sqv], writes=[S_])
                self.act(kk[:, 0, :], kp[:], AF.Square, [kp], [kk, S_], accum_out=sp_)
                self.ts("dve", rk, rk, sp_, ALU.add, [S_], [S_])
                self.act(rk, rk, AF.Sqrt, [S_, self.eps_t], [S_], scale=1.0 / 192, bias=self.eps_t[:, 0:1])
                P.op("dve", lambda e, rk=rk: e.reciprocal(out=rk, in_=rk), reads=[S_], writes=[S_])
                Fk = fin[1]
                self.tt("dve", s3k, k3[:, :, 0:128], bc(rk.unsqueeze(2), [128, 8, 128]), ALU.mult, [qf, S_], [sqv])
                self.tt("dve", Fk[:, :, 0:128], s3k, bc(gk[:, 0:128].unsqueeze(1), [128, 8, 128]), ALU.mult, [sqv, gk], [Fk])
                self.tt("pool", kk[:, 1, :], kp[:], gk[:, 128:192], ALU.mult, [kp, gk], [kk])
                rope_ops(self, "pool", kk[:, 1:2, :], kk[:, 0:1, :], kk[:, 2:3, :], kk[:, 3:4, :], rt, 1, [kk, rt], [kk])
                self.tt("dve", Fk[:, :, 128:192], bc(kk[:, 0:1, :], [128, 8, 64]), bc(rk.unsqueeze(2), [128, 8, 64]), ALU.mult, [kk, S_], [Fk])
                V_ = vfin[i % 2]
                self.copy("act", V_[:], k3[:, :, 128:256], [qf], [V_])
                P.dma("act", self.vh.ap()[r0:r0 + 128, :], V_[:].rearrange("p h e -> p (h e)"), reads=[V_], writes=[self.vh])
                pn, pr = self.ps[6], self.ps[7]
                for h in range(8):
                    self.tr(pnb[:, h * 128:(h + 1) * 128], Fk[:, h, 0:128], self.ident_b, [Fk, self.cstb], [pn])
                    self.tr(prb[0:64, h * 128:(h + 1) * 128], Fk[:, h, 128:192], self.ident_b, [Fk, self.cstb], [pr])
                self.copy("act", kn_[:, :, t * 128:(t + 1) * 128], pnb[:, 0:1024].rearrange("p (h t) -> p h t", h=8), [pn], [], pw=[kn_])
                self.copy("act", kr_[:, :, t * 128:(t + 1) * 128], prb[0:64, 0:1024].rearrange("p (h t) -> p h t", h=8), [pr], [], pw=[kr_])
            for Tt, n_, r_ in ((self.qhT, qn_, qr_), (self.khT, kn_, kr_)):
                P.dma("act", Tt.ap()[:, 0:128, s0:s0 + ng].rearrange("h d t -> d h t"), n_[:, :, 0:ng], reads=[n_], writes=[Tt])
                P.dma("act", Tt.ap()[:, 128:192, s0:s0 + ng].rearrange("h d t -> d h t"), r_[:, :, 0:ng], reads=[r_], writes=[Tt])
    if "qk_only" in self.debug:
        return
    with P.scope():
        Kn = [P.sbuf("c_Kn%d" % i, [128, TT], BF16) for i in range(2)]
        Kr = [P.sbuf("c_Kr%d" % i, [64, TT], BF16) for i in range(2)]
        Vh = [P.sbuf("c_Vh%d" % i, [128, NT, 132], BF16) for i in range(2)]
        Qn = [P.sbuf("c_Qn%d" % i, [128, 512], BF16) for i in range(2)]
        Qr = [P.sbuf("c_Qr%d" % i, [64, 512], BF16) for i in range(2)]
        pT = [P.sbuf("c_pT%d" % i, [128, 512], BF16) for i in range(3)]
        ob = [P.sbuf("c_ob%d" % i, [128, 4, 128], BF16) for i in range(2)]
        rd = [P.sbuf("c_rd%d" % i, [128, 4], F32) for i in range(2)]
        bst = [P.sbuf("c_bst%d" % i, [128, 512], BF16) for i in range(2)]
        for i in range(2):
            P.op("pool", lambda e, i=i: e.memset(Vh[i][:], 1.0), writes=[Vh[i]])
        sc = 192.0 ** -0.5
        for h in range(8):
            K1, K2, V_ = Kn[h % 2], Kr[h % 2], Vh[h % 2]
            P.dma("sp", K1[:], self.khT.ap()[h, 0:128, :], reads=[self.khT], writes=[K1])
            P.dma("sp", K2[:], self.khT.ap()[h, 128:192, :], reads=[self.khT], writes=[K2])
            P.dma("sp", V_[:, :, 0:128], self.vh.ap()[:, h * 128:(h + 1) * 128].rearrange("(kt p) e -> p kt e", p=128), reads=[self.vh], writes=[V_])
            for (q0, nq) in self.groups:
                i = self.rr.get("c_q", 0)
                self.rr["c_q"] = i + 1
                Q1, Q2, O_, R_, B_ = Qn[i % 2], Qr[i % 2], ob[i % 2], rd[i % 2], bst[i % 2]
                P.dma("sp", Q1[:, 0:nq], self.qhT.ap()[h, 0:128, q0:q0 + nq], reads=[self.qhT], writes=[Q1])
                P.dma("sp", Q2[:, 0:nq], self.qhT.ap()[h, 128:192, q0:q0 + nq], reads=[self.qhT], writes=[Q2])
                nqt = nq // 128
                pO = self.ps[0:4]
                nkt = TC // 128 if q0 < TC else NT
                def qk(kt):
                    pS = self.rot("c_pS", self.ps[4:8])
                    self.mm(pS[:, 0:nq], K1[:, kt * 128:(kt + 1) * 128], Q1[:, 0:nq], True, False, [K1, Q1], [pS])
                    self.mm(pS[:, 0:nq], K2[:, kt * 128:(kt + 1) * 128], Q2[:, 0:nq], False, True, [K2, Q2], [pS])
                    return pS

                pS_next = qk(0)
                for kt in range(nkt):
                    pS = pS_next
                    p_ = self.rot("c_pT", pT)
                    self.act(p_[:, 0:nq], pS[:, 0:nq], AF.Exp, [pS], [p_], scale=sc)
                    if kt + 1 < nkt:
                        pS_next = qk(kt + 1)
                    for j in range(nqt):
                        self.mm(pO[j][:, 0:129], p_[:, j * 128:(j + 1) * 128], V_[:, kt, 0:129], kt == 0, kt == nkt - 1, [p_, V_], [pO[j]])
                pS = self.rot("c_pS", self.ps[4:8])
                psb = pS[:].bitcast(BF16)
                for j in range(nqt):
                    P.op("dve", lambda e, j=j, R_=R_: e.reciprocal(out=R_[:, j:j + 1], in_=pO[j][:, 128:129]), reads=[pO[j]], writes=[R_])
                    self.act(O_[:, j, :], pO[j][:, 0:128], AF.Identity, [pO[j], R_], [], pw=[O_], scale=R_[:, j:j + 1])
                    self.tr(psb[:, j * 128:(j + 1) * 128], O_[:, j, :], self.ident_b, [O_, self.cstb], [pS])
                self.copy("dve", B_[:, 0:nq], psb[:, 0:nq], [pS], [B_])
                P.dma("act", self.brT.ap()[1, h * 128:(h + 1) * 128, q0:q0 + nq], B_[:, 0:nq], reads=[B_], writes=[self.brT])


Builder.phase_C2 = phase_C2


def phase_C3(self, l):
    P, TC, TL, TT = self.P, self.TC, self.TL, self.TT
    TWO_PI = 2.0 * math.pi
    for d in range(2):
        with P.scope():
            lam = P.sbuf("s_lam", [128, 3, 32], F32)
            sc = P.sbuf("s_sc", [128, 24, 32], F32)
            Er = P.sbuf("s_Er", [128, 32, SL], F32)
            Ei = P.sbuf("s_Ei", [128, 32, SL], F32)
            Tr = P.sbuf("s_Tr", [128, 32, SL], F32)
            Ti = P.sbuf("s_Ti", [128, 32, SL], F32)
            tmpE = P.sbuf("s_tmpE", [128, 32, SL], F32)
            Bw = P.sbuf("s_Bw", [128, 2, 8, 2, 128], BF16)
            Cw = P.sbuf("s_Cw", [128, 2, 32, 32], BF16)
            Cn = P.sbuf("s_Cn", [128, 2, 32, 32], BF16)
            xst = P.sbuf("s_xst", [128, 32, 2], F32)
            P.dma("sp", lam[:], self.s5_lam.ap()[l, d].rearrange("i p q -> p i q"), reads=[self.s5_lam], writes=[lam])
            P.dma("pool", Bw[:], self.s5_b.ap()[l, d].rearrange("r p q v n -> p r q v n"), reads=[self.s5_b], writes=[Bw])
            P.dma("pool", Cw[:], self.s5_c.ap()[l, d].rearrange("r p q n -> p r q n"), reads=[self.s5_c], writes=[Cw])
            self.ts("dve", Cn[:], Cw[:], -1.0, ALU.mult, [Cw], [Cn])
            P.op("dve", lambda e: e.memset(xst[:], 0.0), writes=[xst])
            S = lambda i: sc[:, i, :]
            rd = [sc]
            ar, aim, dt, r1, th, k_, c1, s1, lbr, lbi, fr, fi = (S(i) for i in range(12))
            u0, u1, u2, u3 = S(12), S(13), S(14), S(15)
            self.ts("dve", ar, lam[:, 0, :], -1e-4, ALU.min, [lam], [sc])
            self.copy("dve", aim, lam[:, 1, :], [lam], [sc])
            self.act(dt, lam[:, 2, :], AF.Exp, [lam], [sc])
            self.tt("dve", u0, ar, dt, ALU.mult, rd, rd)
            self.act(r1, u0, AF.Exp, rd, rd)
            self.tt("dve", th, aim, dt, ALU.mult, rd, rd)
            self.ts("dve", u0, th, 1.0 / TWO_PI, ALU.mult, rd, rd)
            self.ts("dve", u1, u0, 12582912.0, ALU.add, rd, rd)
            self.ts("dve", k_, u1, -12582912.0, ALU.add, rd, rd)
            self.stt(u2, k_, -TWO_PI, th, ALU.mult, ALU.add, rd, rd)
            self.ts("dve", u2, u2, math.pi, ALU.min, rd, rd, s2=-math.pi, op1=ALU.max)
            self.act(s1, u2, AF.Sin, rd, rd)
            self.act(u3, u2, AF.Abs, rd, rd)
            self.ts("dve", u3, u3, -1.0, ALU.mult, rd, rd, s2=math.pi / 2, op1=ALU.add)
            self.act(c1, u3, AF.Sin, rd, rd)
            self.tt("dve", lbr, r1, c1, ALU.mult, rd, rd)
            self.tt("dve", lbi, r1, s1, ALU.mult, rd, rd)
            x_, y_ = S(16), S(17)
            self.ts("dve", x_, lbr, -1.0, ALU.add, rd, rd)
            self.copy("dve", y_, lbi, rd, rd)
            a_, b_ = ar, aim
            n1, n2, den = S(18), S(19), S(20)
            self.tt("dve", n1, x_, a_, ALU.mult, rd, rd)
            self.tt("dve", u0, y_, b_, ALU.mult, rd, rd)
            self.tt("dve", n1, n1, u0, ALU.add, rd, rd)
            self.tt("dve", n2, y_, a_, ALU.mult, rd, rd)
            self.tt("dve", u0, x_, b_, ALU.mult, rd, rd)
            self.tt("dve", n2, n2, u0, ALU.subtract, rd, rd)
            self.tt("dve", den, a_, a_, ALU.mult, rd, rd)
            self.tt("dve", u0, b_, b_, ALU.mult, rd, rd)
            self.tt("dve", den, den, u0, ALU.add, rd, rd)
            P.op("dve", lambda e, den=den: e.reciprocal(out=den, in_=den), reads=rd, writes=rd)
            self.tt("dve", fr, n1, den, ALU.mult, rd, rd)
            self.tt("dve", fi, n2, den, ALU.mult, rd, rd)
            self.copy("dve", Er[:, :, 0], c1, rd, [Er])
            self.copy("dve", Ei[:, :, 0], s1, rd, [Ei])
            n = 1
            while n < SL:
                pr_ = bc(Er[:, :, n - 1:n], [128, 32, n])
                pi_ = bc(Ei[:, :, n - 1:n], [128, 32, n])
                sr, si = Er[:, :, 0:n], Ei[:, :, 0:n]
                dr, di = Er[:, :, n:2 * n], Ei[:, :, n:2 * n]
                tm = tmpE[:, :, 0:n]
                self.tt("dve", dr, sr, pr_, ALU.mult, [Er], [Er])
                self.tt("dve", tm, si, pi_, ALU.mult, [Ei], [tmpE])
                self.tt("dve", dr, dr, tm, ALU.subtract, [Er, tmpE], [Er])
                self.tt("dve", di, sr, pi_, ALU.mult, [Er, Ei], [Ei])
                self.tt("dve", tm, si, pr_, ALU.mult, [Er, Ei], [tmpE])
                self.tt("dve", di, di, tm, ALU.add, [Ei, tmpE], [Ei])
                n *= 2
            frb = bc(fr.unsqueeze(2), [128, 32, SL])
            fib = bc(fi.unsqueeze(2), [128, 32, SL])
            self.tt("dve", Tr[:], Er[:], frb, ALU.mult, [Er, sc], [Tr])
            self.tt("dve", tmpE[:], Ei[:], fib, ALU.mult, [Ei, sc], [tmpE])
            self.tt("dve", Tr[:], Tr[:], tmpE[:], ALU.add, [Tr, tmpE], [Tr])
            self.tt("dve", Ti[:], Er[:], fib, ALU.mult, [Er, sc], [Ti])
            self.tt("dve", tmpE[:], Ei[:], frb, ALU.mult, [Ei, sc], [tmpE])
            self.tt("dve", Ti[:], Ti[:], tmpE[:], ALU.subtract, [Ti, tmpE], [Ti])

            ut = [P.sbuf("s_ut%d" % i, [128, 512], BF16) for i in range(2)]
            ut32 = [P.sbuf("s_ut32_%d" % i, [128, 512], F32) for i in range(2)]
            NB = 4
            mt = [[P.sbuf("s_m%d_%d" % (k, i), [128, 512], F32) for k in range(4)] for i in range(2)]
            bre = [P.sbuf("s_bre%d" % i, [128, 512], F32) for i in range(NB)]
            bim = [P.sbuf("s_bim%d" % i, [128, 512], F32) for i in range(NB)]
            wre = [P.sbuf("s_wre%d" % i, [128, 512], F32) for i in range(NB)]
            wim = [P.sbuf("s_wim%d" % i, [128, 512], F32) for i in range(NB)]
            pr4 = [[P.sbuf("s_p%d_%d" % (k, i), [128, 512], BF16) for k in range(4)] for i in range(NB)]
            tsm = [P.sbuf("s_ts%d" % i, [128, 2], F32) for i in range(4)]
            yev = [P.sbuf("s_yev%d" % i, [128, 4, 128], F32) for i in range(2)]
            first = True
            src = self.uT if d == 0 else self.uTr
            for (i0, ng) in self.groups:
                nchk = ng // SL
                ntt = ng // 128
                v3 = lambda ap: ap[:, 0:ng].rearrange("p (c j) -> p c j", j=SL)
                for q in range(8):
                    U_ = self.rot("s_ut", ut)
                    if d == 0:
                        U32 = self.rot("s_ut32", ut32)
                        P.dma("sp", U32[:, 0:ng], src.ap()[q * 128:(q + 1) * 128, i0:i0 + ng], reads=[src], writes=[U32])
                        self.copy("act", U_[:, 0:ng], U32[:, 0:ng], [U32], [U_])
                    else:
                        P.dma("sp", U_[:, 0:ng], src.ap()[q * 128:(q + 1) * 128, i0:i0 + ng], reads=[src], writes=[U_])
                    pY = self.rot("s_pY", self.ps[6:8])
                    Y_ = self.rot("s_yev", yev)
                    for hp in range(2):
                        prs = []
                        for k2 in range(2):
                            pp = 2 * hp + k2
                            p = 4 * q + pp
                            ii = self.rr.get("s_i", 0)
                            self.rr["s_i"] = ii + 1
                            pR = self.rot("s_pR", self.ps[0:6])
                            pI = self.rot("s_pR", self.ps[0:6])
                            rows = slice(32 * pp, 32 * pp + 32) if pp < 3 else slice(64, 128)
                            vv = 0 if pp < 3 else 1
                            self.mm(pR[:, 0:ng], Bw[rows, 0, q, vv, :], U_[rows, 0:ng], True, True, [Bw, U_], [pR])
                            self.mm(pI[:, 0:ng], Bw[rows, 1, q, vv, :], U_[rows, 0:ng], True, True, [Bw, U_], [pI])
                            prs.append(dict(pp=pp, p=p, pR=pR, pI=pI, m=mt[k2], BR=bre[ii % NB], BI=bim[ii % NB], WR=wre[ii % NB], WI=wim[ii % NB],
                                            PR=pr4[ii % NB], T=tsm[ii % 4]))
                        for z in prs:
                            p = z["p"]
                            Trb = bc(Tr[:, p, :].unsqueeze(1), [128, nchk, SL])
                            Tib = bc(Ti[:, p, :].unsqueeze(1), [128, nchk, SL])
                            m1, m2, m3, m4 = z["m"]
                            self.tt("dve", v3(m1), v3(z["pR"]), Trb, ALU.mult, [z["pR"], Tr], [m1])
                            self.tt("dve", v3(m2), v3(z["pI"]), Tib, ALU.mult, [z["pI"], Ti], [m2])
                            self.tt("dve", v3(m3), v3(z["pI"]), Trb, ALU.mult, [z["pI"], Tr], [m3])
                            self.tt("dve", v3(m4), v3(z["pR"]), Tib, ALU.mult, [z["pR"], Ti], [m4])
                            self.tt("pool", z["BR"][:, 0:ng], m1[:, 0:ng], m2[:, 0:ng], ALU.subtract, [m1, m2], [z["BR"]])
                            self.tt("pool", z["BI"][:, 0:ng], m3[:, 0:ng], m4[:, 0:ng], ALU.add, [m3, m4], [z["BI"]])
                        for c in range(nchk):
                            a0 = c * SL
                            la = a0 + SL - 1
                            steps = []
                            for z in prs:
                                p, BR, BI, WR, WI, T_ = z["p"], z["BR"], z["BI"], z["WR"], z["WI"], z["T"]
                                r1p = r1[:, p:p + 1]
                                ErL, EiL = Er[:, p, SL - 1:SL], Ei[:, p, SL - 1:SL]
                                st_ = []
                                if not (first and c == 0):
                                    st_.append(lambda BR=BR, p=p, r1p=r1p: self.stt(BR[:, a0:a0 + 1], xst[:, p, 0:1], r1p, BR[:, a0:a0 + 1], ALU.mult, ALU.add, [xst, sc, BR], [BR]))
                                    st_.append(lambda BI=BI, p=p, r1p=r1p: self.stt(BI[:, a0:a0 + 1], xst[:, p, 1:2], r1p, BI[:, a0:a0 + 1], ALU.mult, ALU.add, [xst, sc, BI], [BI]))
                                else:
                                    st_.append(None)
                                    st_.append(None)
                                st_.append(lambda WR=WR, BR=BR, o_=WR[:, a0:a0 + SL], d0=bc(r1p, [128, SL]), d1=BR[:, a0:a0 + SL]: P.op("dve", lambda e: e.tensor_tensor_scan(out=o_, data0=d0, data1=d1, initial=0.0, op0=ALU.mult, op1=ALU.add), reads=[BR, sc], writes=[WR]))
                                st_.append(lambda WI=WI, BI=BI, o_=WI[:, a0:a0 + SL], d0=bc(r1p, [128, SL]), d1=BI[:, a0:a0 + SL]: P.op("dve", lambda e: e.tensor_tensor_scan(out=o_, data0=d0, data1=d1, initial=0.0, op0=ALU.mult, op1=ALU.add), reads=[BI, sc], writes=[WI]))
                                st_.append(lambda WI=WI, T_=T_, EiL=EiL: self.ts("dve", T_[:, 0:1], WI[:, la:la + 1], EiL, ALU.mult, [WI, Ei], [T_]))
                                st_.append(lambda WR=WR, T_=T_, EiL=EiL: self.ts("dve", T_[:, 1:2], WR[:, la:la + 1], EiL, ALU.mult, [WR, Ei], [T_]))
                                st_.append(lambda WR=WR, T_=T_, ErL=ErL, p=p: self.stt(xst[:, p, 0:1], WR[:, la:la + 1], ErL, T_[:, 0:1], ALU.mult, ALU.subtract, [WR, Er, T_], [xst]))
                                st_.append(lambda WI=WI, T_=T_, ErL=ErL, p=p: self.stt(xst[:, p, 1:2], WI[:, la:la + 1], ErL, T_[:, 1:2], ALU.mult, ALU.add, [WI, Er, T_], [xst]))
                                steps.append(st_)
                            for k in range(len(steps[0])):
                                for st_ in steps:
                                    if st_[k] is not None:
                                        st_[k]()
                        for z in prs:
                            p, pp, PR, WR, WI = z["p"], z["pp"], z["PR"], z["WR"], z["WI"]
                            Erb = bc(Er[:, p, :].unsqueeze(1), [128, nchk, SL])
                            Eib = bc(Ei[:, p, :].unsqueeze(1), [128, nchk, SL])
                            self.tt("pool", v3(PR[0]), v3(WR), Erb, ALU.mult, [WR, Er], [PR[0]])
                            self.tt("pool", v3(PR[1]), v3(WI), Eib, ALU.mult, [WI, Ei], [PR[1]])
                            self.tt("pool", v3(PR[2]), v3(WR), Eib, ALU.mult, [WR, Ei], [PR[2]])
                            self.tt("pool", v3(PR[3]), v3(WI), Erb, ALU.mult, [WI, Er], [PR[3]])
                            for t in range(ntt):
                                o_ = pY[:, t * 128 + pp * 32:t * 128 + pp * 32 + 32]
                                tsl = slice(t * 128, (t + 1) * 128)
                                self.mm(o_, PR[0][:, tsl], Cw[:, 0, p, :], True, False, [PR[0], Cw], [pY])
                                self.mm(o_, PR[1][:, tsl], Cn[:, 0, p, :], False, False, [PR[1], Cn], [pY])
                                self.mm(o_, PR[2][:, tsl], Cn[:, 1, p, :], False, False, [PR[2], Cn], [pY])
                                self.mm(o_, PR[3][:, tsl], Cn[:, 1, p, :], False, True, [PR[3], Cn], [pY])
                    self.copy("act", Y_[:, 0:ntt, :], pY[:, 0:ng].rearrange("p (t c) -> p t c", c=128), [pY], [Y_])
                    P.dma("act", self.ytok.ap()[d, i0:i0 + ng, q * 128:(q + 1) * 128].rearrange("(t p) c -> p t c", p=128), Y_[:, 0:ntt, :], reads=[Y_], writes=[self.ytok])
                first = False
    with P.scope():
        wg = P.sbuf("g_wg", [128, 8, 1024], BF16)
        dsk = P.sbuf("g_dsk", [128, 8], F32)
        bgl = P.sbuf("g_bgl", [128, 8], F32)
        P.dma("pool", wg[:], self.w_glu.ap()[l].rearrange("(kc p) n -> p kc n", p=128), reads=[self.w_glu], writes=[wg])
        P.dma("sp", dsk[:], self.s5_d.ap()[l].rearrange("(c p) -> p c", p=128), reads=[self.s5_d], writes=[dsk], allow_slow_non_contiguous=True)
        P.dma("sp", bgl[:], self.b_glu.ap()[l].rearrange("(c p) -> p c", p=128), reads=[self.b_glu], writes=[bgl], allow_slow_non_contiguous=True)
        y0 = [P.sbuf("g_y0_%d" % i, [128, 1024], F32) for i in range(2)]
        y1 = [P.sbuf("g_y1_%d" % i, [128, 1024], F32) for i in range(2)]
        uTt = P.sbuf("g_uT", [128, 8, 512], F32)
        yT = P.sbuf("g_yT", [128, 8, 512], F32)
        t_a = P.sbuf("g_ta", [128, 8, 512], F32)
        gb = P.sbuf("g_gb", [128, 8, 512], BF16)
        sg = [P.sbuf("g_sg%d" % i, [128, 512], F32) for i in range(2)]
        cT = P.sbuf("g_cT", [128, 8, 512], BF16)
        for (s0, ng) in self.groups:
            P.dma("sp", uTt[:, :, 0:ng], self.uT.ap().rearrange("(c p) t -> p c t", p=128)[:, :, s0:s0 + ng], reads=[self.uT], writes=[uTt])
            for t in range(ng // 128):
                r0 = s0 + t * 128
                a0, a1 = self.rot("g_y0", y0), self.rot("g_y1", y1)
                P.dma("sp", a0[:], self.ytok.ap()[0, r0:r0 + 128, :], reads=[self.ytok], writes=[a0])
                m0 = self.mirror(r0)
                P.dma("sp", a1[:], self.ytok.ap()[1, m0:m0 + 128, :], reads=[self.ytok], writes=[a1])
                for half in range(2):
                    ps = self.rot("g_ps", self.ps[0:4])
                    for qq in range(4):
                        q = half * 4 + qq
                        o_ = ps[:, qq * 128:(qq + 1) * 128]
                        self.mm(o_, a0[:, q * 128:(q + 1) * 128], self.ident_f, True, False, [a0, self.cst_t], [ps])
                        self.mm(o_, a1[:, q * 128:(q + 1) * 128], self.J_f, False, True, [a1, self.cst_t], [ps])
                    for qq in range(4):
                        q = half * 4 + qq
                        self.stt(yT[:, q, t * 128:(t + 1) * 128], uTt[:, q, t * 128:(t + 1) * 128], dsk[:, q:q + 1], ps[:, qq * 128:(qq + 1) * 128], ALU.mult, ALU.add, [uTt, dsk, ps], [], pw=[yT])
            Y = yT[:, :, 0:ng]
            A = t_a[:, :, 0:ng]
            self.act(A, Y, AF.Square, [yT], [t_a])
            self.ts("dve", A, A, 0.044715, ALU.mult, [t_a], [t_a], s2=1.0, op1=ALU.add)
            self.tt("pool", A, A, Y, ALU.mult, [t_a, yT], [t_a])
            self.act(A, A, AF.Sigmoid, [t_a], [t_a], scale=2.0 * math.sqrt(2.0 / math.pi))
            self.tt("dve", Y, Y, A, ALU.mult, [yT, t_a], [yT])
            self.copy("act", gb[:, :, 0:ng], Y, [yT], [gb])
            for jc in range(8):
                ps = self.rot("g_ps", self.ps[0:4])
                for kc in range(8):
                    self.mm(ps[:, 0:ng], wg[:, kc, jc * 128:(jc + 1) * 128], gb[:, kc, 0:ng], kc == 0, kc == 7, [wg, gb], [ps])
                s_ = self.rot("g_sg", sg)
                self.act(s_[:, 0:ng], ps[:, 0:ng], AF.Sigmoid, [ps, bgl], [s_], bias=bgl[:, jc:jc + 1])
                self.tt("dve", cT[:, jc, 0:ng], yT[:, jc, 0:ng], s_[:, 0:ng], ALU.mult, [yT, s_], [], pw=[cT])
            P.dma("act", self.brT.ap()[2].rearrange("(c p) t -> p c t", p=128)[:, :, s0:s0 + ng], cT[:, :, 0:ng], reads=[cT], writes=[self.brT])


Builder.phase_C3 = phase_C3


def phase_DE(self, l):
    P, TC, TL, TT = self.P, self.TC, self.TL, self.TT
    last = (l == self.depth - 1)
    with P.scope():
        X = P.sbuf("e_X", [128, KC, 512], F32)
        M = P.sbuf("e_M", [128, KC, 512], BF16)
        H = P.sbuf("e_H", [128, 32, 512], BF16)
        wt = [P.sbuf("e_w%d" % i, [128, KC, 512], BF16) for i in range(3)]
        gt = [P.sbuf("e_g%d" % i, [128, 3, 512], BF16) for i in range(2)]
        ma = [P.sbuf("e_ma%d" % i, [128, 512], F32) for i in range(2)]
        mb = [P.sbuf("e_mb%d" % i, [128, 512], F32) for i in range(2)]
        sqb = [P.sbuf("e_sq%d" % i, [128, 512], BF16) for i in range(3)]
        tmp = [P.sbuf("e_tmp%d" % i, [128, 512], F32) for i in range(3)]
        rbc = P.sbuf("e_rbc", [128, 512], F32)
        rl = [P.sbuf("e_rl%d" % i, [128, 512], F32) for i in range(3)]
        yt = [P.sbuf("e_yt%d" % i, [128, D], F32) for i in range(2)] if last else None
        wbr = self.w_branch.ap()[l]
        wov = self.w_out.ap()[l].rearrange("(kc p) n -> p kc n", p=128)
        w1v = self.w_ff1.ap()[l].rearrange("(kc p) n -> p kc n", p=128)
        w2v = self.w_ff2.ap()[l].rearrange("(f p) n -> p f n", p=128)
        gate1, gate2 = self.modT[:, 2], self.modT[:, 5]
        xin = self.xT[l]
        groups = [g for g in self.groups if not (last and g[0] < TC)]
        for (s0, ng) in groups:
            col = 1 if s0 < TC else 0
            P.dma("sp", X[:, :, 0:ng], xin.ap().rearrange("(c p) t -> p c t", p=128)[:, :, s0:s0 + ng], reads=[xin], writes=[X])
            for r in range(3):
                P.dma("sp", H[:, r * 8:(r + 1) * 8, 0:ng], self.brT.ap()[r].rearrange("(c p) t -> p c t", p=128)[:, :, s0:s0 + ng], reads=[self.brT], writes=[H])
            for jb in range(4):
                ws = []
                for r in range(3):
                    w = self.rot("e_w", wt)
                    P.dma("pool", w[:, 0:8, :], wbr[r].rearrange("(kc p) n -> p kc n", p=128)[:, :, jb * 512:(jb + 1) * 512], reads=[self.w_branch], writes=[w])
                    ws.append(w)
                for jj in range(4):
                    j = jb * 4 + jj
                    G_ = self.rot("e_g", gt)
                    P.dma("sp", G_[:, :, 0:ng], self.gT.ap().rearrange("(r c p) t -> p r c t", r=3, p=128)[:, :, j, s0:s0 + ng], reads=[self.gT], writes=[G_])
                    A_, B_ = self.rot("e_ma", ma), self.rot("e_mb", mb)
                    for r in range(3):
                        ps = self.rot("e_ps", self.ps[0:6])
                        for kc in range(8):
                            self.mm(ps[:, 0:ng], ws[r][:, kc, jj * 128:(jj + 1) * 128], H[:, r * 8 + kc, 0:ng], kc == 0, kc == 7, [ws[r], H], [ps])
                        if r == 0:
                            self.tt("dve", A_[:, 0:ng], ps[:, 0:ng], G_[:, 0, 0:ng], ALU.mult, [ps, G_], [A_])
                        elif r == 1:
                            self.tt("dve", B_[:, 0:ng], ps[:, 0:ng], G_[:, 1, 0:ng], ALU.mult, [ps, G_], [B_])
                            self.tt("dve", A_[:, 0:ng], A_[:, 0:ng], B_[:, 0:ng], ALU.add, [A_, B_], [A_])
                        else:
                            self.tt("dve", B_[:, 0:ng], ps[:, 0:ng], G_[:, 2, 0:ng], ALU.mult, [ps, G_, A_], [B_])
                            self.tt("dve", M[:, j, 0:ng], A_[:, 0:ng], B_[:, 0:ng], ALU.add, [A_, B_], [], pw=[M])
            for jb in range(4):
                w = self.rot("e_w", wt)
                P.dma("pool", w[:], wov[:, :, jb * 512:(jb + 1) * 512], reads=[self.w_out], writes=[w])
                for jj in range(4):
                    j = jb * 4 + jj
                    ps = self.rot("e_ps", self.ps[0:6])
                    for kc in range(KC):
                        self.mm(ps[:, 0:ng], w[:, kc, jj * 128:(jj + 1) * 128], M[:, kc, 0:ng], kc == 0, kc == KC - 1, [w, M], [ps])
                    self.stt(X[:, j, 0:ng], ps[:, 0:ng], gate1[:, j, col:col + 1], X[:, j, 0:ng], ALU.mult, ALU.add, [ps, self.modT, X], [X])
            self.norm_group(X, M, ng, 1, col, sqb, rbc, tmp, self.ps[7])
            for half in range(2):
                for fb in range(8):
                    w = self.rot("e_w", wt)
                    c0 = half * 4096 + fb * 512
                    P.dma("pool", w[:], w1v[:, :, c0:c0 + 512], reads=[self.w_ff1], writes=[w])
                    for jj in range(4):
                        f = fb * 4 + jj
                        ps = self.rot("e_ps", self.ps[0:6])
                        for kc in range(KC):
                            self.mm(ps[:, 0:ng], w[:, kc, jj * 128:(jj + 1) * 128], M[:, kc, 0:ng], kc == 0, kc == KC - 1, [w, M], [ps])
                        R_ = self.rot("e_rl", rl)
                        self.act(R_[:, 0:ng], ps[:, 0:ng], AF.Relu, [ps], [R_])
                        self.tt("dve", H[:, f, 0:ng], R_[:, 0:ng], R_[:, 0:ng], ALU.mult, [R_], [], pw=[H])
                for jb in range(4):
                    wa = self.rot("e_w", wt)
                    wb_ = self.rot("e_w", wt)
                    f0 = half * 32
                    P.dma("pool", wa[:], w2v[:, f0:f0 + 16, jb * 512:(jb + 1) * 512], reads=[self.w_ff2], writes=[wa])
                    P.dma("pool", wb_[:], w2v[:, f0 + 16:f0 + 32, jb * 512:(jb + 1) * 512], reads=[self.w_ff2], writes=[wb_])
                    for jj in range(4):
                        j = jb * 4 + jj
                        ps = self.rot("e_ps", self.ps[0:6])
                        for f in range(32):
                            ww = wa if f < 16 else wb_
                            self.mm(ps[:, 0:ng], ww[:, f % 16, jj * 128:(jj + 1) * 128], H[:, f, 0:ng], f == 0, f == 31, [ww, H], [ps])
                        self.stt(X[:, j, 0:ng], ps[:, 0:ng], gate2[:, j, col:col + 1], X[:, j, 0:ng], ALU.mult, ALU.add, [ps, self.modT, X], [X])
            if not last:
                P.dma("act", self.xT[l + 1].ap().rearrange("(c p) t -> p c t", p=128)[:, :, s0:s0 + ng], X[:, :, 0:ng], reads=[X], writes=[self.xT[l + 1]])
            else:
                for t in range(ng // 128):
                    Y_ = self.rot("e_yt", yt)
                    for b4 in range(4):
                        ps = self.rot("e_ps", self.ps[0:6])
                        for jj in range(4):
                            c = b4 * 4 + jj
                            self.tr(ps[:, jj * 128:(jj + 1) * 128], X[:, c, t * 128:(t + 1) * 128], self.ident_f, [X, self.cst_t], [ps])
                        self.copy("act" if b4 % 2 == 0 else "dve", Y_[:, b4 * 512:(b4 + 1) * 512], ps[:, 0:512], [ps], [], pw=[Y_])
                    r0 = s0 - TC + t * 128
                    P.dma("act", self.y.ap()[r0:r0 + 128, :], Y_[:], reads=[Y_], writes=[self.y])


Builder.phase_DE = phase_DE


_CACHE = {}


def kernel(**inputs):
    TL, TC = 4096, 256
    if "prog" not in _CACHE:
        _CACHE["prog"] = build_program(TL, TC, depth=DEPTH)
    B = _CACHE["prog"]
    sh = prep_shared(inputs, DEPTH)
    cst, rope = make_consts(TL, TC)
    maps = []
    for core in range(8):
        b = core % 4
        m = dict(sh)
        m.update(prep_core(inputs, b, TL, TC))
        m["cst"] = cst
        m["rope"] = rope
        maps.append(m)
    res = run_bass_kernel_spmd(B.nc, maps, core_ids=list(range(8)))
    out = np.stack([np.asarray(res.results[b]["y"], dtype=np.float32) for b in range(4)], axis=0)
    return out
```

```python
import contextlib
import math
import numpy as np
import concourse.bass as bass
import concourse.mybir as mybir
from concourse.bass_utils import run_bass_kernel_spmd

F32 = mybir.dt.float32
BF16 = mybir.dt.bfloat16
AF = mybir.ActivationFunctionType
ALU = mybir.AluOpType
AX = mybir.AxisListType

ENGS = ("pe", "act", "dve", "pool", "sp")
EPOCH = 30000

D = 2048
KC = 16
DEPTH = 2
DIN = 11344
OFF_Q, OFF_K, OFF_V, OFF_O, OFF_G, OFF_QA, OFF_KVA, OFF_KPE, OFF_U, OFF_BG = 0, 512, 1024, 2048, 3072, 3088, 3600, 4112, 4176, 5200
EPS = 1e-6
SL = 128


class T:
    def __init__(self, h, name=""):
        self.h = h
        self.name = name
        self.w = {}
        self.r = {}

    def __getitem__(self, k):
        return self.h[k]

    def ap(self):
        return self.h.ap()


class Prog:
    def __init__(self, nc, n_dma_sems=8):
        self.nc = nc
        self.es = contextlib.ExitStack()
        self.q = {e: [] for e in ENGS}
        self.tick = {e: 0 for e in ENGS}
        self.epoch = {e: 0 for e in ENGS}
        self.sems = {}
        self.seen = {e: {} for e in ENGS}
        self.dma_slots = {}
        self.n_dma_sems = n_dma_sems
        self.dma_i = {e: 0 for e in ENGS}
        self.nsem = 0
        self.ninstr = 0
        self.scopes = []

    def sem(self, key):
        if key not in self.sems:
            self.sems[key] = self.es.enter_context(self.nc.semaphore("s%d" % self.nsem))
            self.nsem += 1
        return self.sems[key]

    def _stack(self):
        return self.scopes[-1] if self.scopes else self.es

    def sbuf(self, name, shape, dtype):
        self.uid = getattr(self, "uid", 0) + 1
        name = "%s_u%d" % (name, self.uid)
        return T(self._stack().enter_context(self.nc.sbuf_tensor(name, list(shape), dtype)), name)

    def psum(self, name, shape, dtype=F32):
        return T(self.es.enter_context(self.nc.psum_tensor(name, list(shape), dtype)), name)

    def dram(self, name, shape, dtype, kind=None):
        if kind is None:
            h = self.nc.dram_tensor(name, list(shape), dtype)
        else:
            h = self.nc.dram_tensor(name, list(shape), dtype, kind=kind)
        return T(h, name)

    @contextlib.contextmanager
    def scope(self):
        st = contextlib.ExitStack()
        self.scopes.append(st)
        try:
            yield
        finally:
            self.barrier()
            self.flush()
            self.scopes.pop()
            st.close()

    def _waits(self, e, reads, writes, pw=()):
        deps = {}
        for b in reads:
            for k, v in b.w.items():
                if deps.get(k, 0) < v:
                    deps[k] = v
        for b in writes:
            for d in (b.w, b.r):
                for k, v in d.items():
                    if deps.get(k, 0) < v:
                        deps[k] = v
        for b in pw:
            for k, v in b.r.items():
                if deps.get(k, 0) < v:
                    deps[k] = v
        out = []
        for k, v in deps.items():
            if k[0] == e and k[1] == "p" and e == "pe":
                continue
            if self.seen[e].get(k, 0) >= v:
                continue
            self.seen[e][k] = v
            out.append((k, v))
        return out

    def op(self, e, fn, reads=(), writes=(), pw=()):
        waits = self._waits(e, reads, writes, pw)
        if self.tick[e] >= EPOCH:
            self.epoch[e] += 1
            self.tick[e] = 0
        key = (e, "p", self.epoch[e])
        self.tick[e] += 1
        val = self.tick[e]
        s = self.sem(key)
        wl = [(self.sem(k), v) for k, v in waits]
        self.q[e].append((wl, fn, s, 1))
        for b in reads:
            b.r[key] = val
        for b in writes:
            b.w[key] = val
        for b in pw:
            b.w[key] = val
        self.ninstr += 1

    def dma(self, e, out_ap, in_ap, reads=(), writes=(), **kw):
        i = self.dma_i[e]
        self.dma_i[e] += 1
        slot = (e, "d", i % self.n_dma_sems)
        cnt = self.dma_slots.get(slot, 0)
        waits = self._waits(e, reads, writes)
        if cnt > 0 and self.seen[e].get(slot, 0) < cnt:
            self.seen[e][slot] = cnt
            waits.append((slot, cnt))
        cnt += 16
        self.dma_slots[slot] = cnt
        s = self.sem(slot)
        wl = [(self.sem(k), v) for k, v in waits]

        def fn(eng, out_ap=out_ap, in_ap=in_ap, kw=kw):
            return eng.dma_start(out=out_ap, in_=in_ap, **kw)

        self.q[e].append((wl, fn, s, 16))
        for b in reads:
            b.r[slot] = cnt
        for b in writes:
            b.w[slot] = cnt
        self.ninstr += 1

    def barrier(self):
        targets = {}
        for e in ENGS:
            if self.tick[e] > 0:
                targets[(e, "p", self.epoch[e])] = self.tick[e]
        for slot, cnt in self.dma_slots.items():
            targets[slot] = cnt
        for e in ENGS:
            wl = []
            for k, v in targets.items():
                if k[0] == e and k[1] == "p" and e == "pe":
                    continue
                if self.seen[e].get(k, 0) >= v:
                    continue
                self.seen[e][k] = v
                wl.append((self.sem(k), v))
            if wl:
                self.q[e].append((wl, None, None, 0))

    def flush(self):
        nc = self.nc
        q = self.q
        if not any(q[e] for e in ENGS):
            return
        with nc.Block() as block:

            def run(eng, lst):
                for wl, fn, s, inc in lst:
                    for ws, v in wl:
                        eng.wait_ge(ws, v)
                    if fn is not None:
                        fn(eng).then_inc(s, inc)

            @block.tensor
            def _(eng):
                run(eng, q["pe"])

            @block.scalar
            def _(eng):
                run(eng, q["act"])

            @block.vector
            def _(eng):
                run(eng, q["dve"])

            @block.gpsimd
            def _(eng):
                run(eng, q["pool"])

            @block.sync
            def _(eng):
                run(eng, q["sp"])

        self.q = {e: [] for e in ENGS}

    def finish(self):
        self.barrier()
        self.flush()
        self.es.close()


def bc(ap, shape):
    return ap.to_broadcast(list(shape))


class Builder:
    def __init__(self, TL, TC, depth=DEPTH, debug=(), stop_after=None):
        self.TL, self.TC, self.TT = TL, TC, TL + TC
        self.depth = depth
        self.debug = set(debug)
        self.stop_after = stop_after
        assert TC % 128 == 0 and TC <= 512 and TL % 512 == 0
        self.groups = [(0, TC)] + [(TC + 512 * i, 512) for i in range(TL // 512)]
        self.NT = self.TT // 128
        self.nc = bass.Bass("TRN2", target_bir_lowering=False)
        self.P = Prog(self.nc)
        self.rr = {}

    def ext_in(self, name, shape, dt=F32):
        return self.P.dram(name, shape, dt, kind="ExternalInput")

    def scratch(self, name, shape, dt):
        kind = "ExternalOutput" if name in self.debug else None
        return self.P.dram(name, shape, dt, kind=kind)

    def rot(self, key, lst):
        i = self.rr.get(key, 0)
        self.rr[key] = i + 1
        return lst[i % len(lst)]

    def act(self, out, in_, func, reads, writes, pw=(), **kw):
        self.P.op("act", lambda e: e.activation(out=out, in_=in_, func=func, **kw), reads=reads, writes=writes, pw=pw)

    def tt(self, eng, out, in0, in1, op, reads, writes, pw=()):
        self.P.op(eng, lambda e: e.tensor_tensor(out=out, in0=in0, in1=in1, op=op), reads=reads, writes=writes, pw=pw)

    def ts(self, eng, out, in0, s1, op0, reads, writes, s2=None, op1=None, pw=()):
        if op1 is None:
            self.P.op(eng, lambda e: e.tensor_scalar(out=out, in0=in0, scalar1=s1, scalar2=None, op0=op0), reads=reads, writes=writes, pw=pw)
        else:
            self.P.op(eng, lambda e: e.tensor_scalar(out=out, in0=in0, scalar1=s1, scalar2=s2, op0=op0, op1=op1), reads=reads, writes=writes, pw=pw)

    def stt(self, out, in0, scalar, in1, op0, op1, reads, writes, pw=()):
        self.P.op("dve", lambda e: e.scalar_tensor_tensor(out=out, in0=in0, scalar=scalar, in1=in1, op0=op0, op1=op1), reads=reads, writes=writes, pw=pw)

    def mm(self, out, lhsT, rhs, start, stop, reads, writes):
        self.P.op("pe", lambda e: e.matmul(out=out, lhsT=lhsT, rhs=rhs, start=start, stop=stop), reads=reads, writes=writes)

    def tr(self, out, in_, ident, reads, writes):
        self.P.op("pe", lambda e: e.transpose(out=out, in_=in_, identity=ident), reads=reads, writes=writes)

    def copy(self, eng, out, in_, reads, writes, pw=()):
        if eng == "act":
            self.act(out, in_, AF.Copy, reads, writes, pw=pw)
        else:
            self.P.op(eng, lambda e: e.tensor_copy(out=out, in_=in_), reads=reads, writes=writes, pw=pw)

    def rsqrt(self, out, in_, scale, reads, writes):
        self.act(out, in_, AF.Sqrt, reads, writes, scale=scale, bias=self.eps_t[0:out.shape[0], 0:1])
        self.P.op("dve", lambda e: e.reciprocal(out=out, in_=out), reads=writes, writes=writes)

    def declare(self):
        P, TT, TL, TC, L = self.P, self.TT, self.TL, self.TC, self.depth
        I = self.ext_in
        self.x_in = I("x", [TL, D])
        self.ctx_in = I("ctx", [TC, D])
        self.cvec = I("cvec", [2, D])
        self.w_mod = I("w_mod", [L, D, 6 * D])
        self.b_mod = I("b_mod", [L, 6 * D])
        self.norm_g = I("norm_g", [L, 2, D])
        self.w_in = I("w_in", [L, D, DIN])
        self.b_in = I("b_in", [L, DIN])
        self.ml_gate_b = I("ml_gate_b", [L, 16])
        self.ml_norm_g = I("ml_norm_g", [L, 1024])
        self.qa_g = I("mla_qa_g", [L, 512])
        self.kva_g = I("mla_kva_g", [L, 512])
        self.w_uq = I("mla_w_uq", [L, 512, 1536])
        self.w_ukv = I("mla_w_ukv", [L, 512, 2048])
        self.qn_g = I("mla_qn_g", [L, 192])
        self.kn_g = I("mla_kn_g", [L, 192])
        self.s5_lam = I("s5_lam", [L, 2, 3, 128, 32])
        self.s5_b = I("s5_b", [L, 2, 2, 128, 8, 2, 128])
        self.s5_c = I("s5_c", [L, 2, 2, 128, 32, 32])
        self.s5_d = I("s5_d", [L, 1024])
        self.w_glu = I("s5_w_glu", [L, 1024, 1024])
        self.b_glu = I("s5_b_glu", [L, 1024])
        self.w_branch = I("w_branch", [L, 3, 1024, D])
        self.w_out = I("w_out", [L, D, D])
        self.w_ff1 = I("w_ff1", [L, D, 4 * D])
        self.w_ff2 = I("w_ff2", [L, 4 * D, D])
        self.cst = I("cst", [128, 640])
        self.rope = I("rope", [TT, 128])
        self.y = P.dram("y", [TL, D], F32, kind="ExternalOutput")

        S = self.scratch
        self.xT = [S("xT0", [D, TT], F32), S("xT1", [D, TT], F32)]
        self.qT = S("qT", [4, 128, TT], BF16)
        self.kT = S("kT", [4, 128, TT], BF16)
        self.ktok = S("ktok", [TT, 512], BF16)
        self.vtok = S("vtok", [TT, 1024], BF16)
        self.otok = S("otok", [TT, 1024], BF16)
        self.gtok = S("gtok", [TT, 16], F32)
        self.kpe = S("kpe", [TT, 64], F32)
        self.qaT = S("qaT", [512, TT], BF16)
        self.kvaT = S("kvaT", [512, TT], BF16)
        self.rsq = S("rsq", [TT, 2], F32)
        self.uT = S("uT", [1024, TT], F32)
        self.uTr = S("uTr", [1024, TT], BF16)
        self.gT = S("gT", [6144, TT], BF16)
        self.hml = S("hml", [2, TT, 1024], F32)
        self.qhT = S("qhT", [8, 192, TT], BF16)
        self.khT = S("khT", [8, 192, TT], BF16)
        self.vh = S("vh", [TT, 1024], BF16)
        self.ytok = S("ytok", [2, TT, 1024], F32)
        self.brT = S("brT", [3, 1024, TT], BF16)

        self.cst_t = P.sbuf("cst_t", [128, 640], F32)
        self.cstb = P.sbuf("cstb", [128, 640], BF16)
        self.eps_t = P.sbuf("eps_t", [128, 1], F32)
        self.modT = P.sbuf("modT", [128, 6, 16, 2], F32)
        self.A1 = P.sbuf("A1", [128, 2, 16, 2], F32)
        self.ps = [P.psum("ps%d" % i, [128, 512], F32) for i in range(8)]
        P.dma("sp", self.cst_t[:], self.cst[:], reads=[self.cst], writes=[self.cst_t])
        P.op("dve", lambda e: e.tensor_copy(out=self.cstb[:], in_=self.cst_t[:]), reads=[self.cst_t], writes=[self.cstb])
        P.op("dve", lambda e: e.memset(self.eps_t[:], EPS), writes=[self.eps_t])
        c = self.cst_t
        self.ident_f, self.J_f = c[:, 0:128], c[:, 128:256]
        self.triL_f, self.triU_f = c[0:64, 256:320], c[0:64, 320:384]
        self.ones_f = c[:, 384:512]
        cb = self.cstb
        self.ident_b, self.J_b, self.ones_b = cb[:, 0:128], cb[:, 128:256], cb[:, 384:512]

    def seq_rows(self, s0, n):
        if s0 < self.TC:
            return self.ctx_in, self.ctx_in[s0:s0 + n, :]
        return self.x_in, self.x_in[s0 - self.TC:s0 - self.TC + n, :]

    def phase_T0(self):
        P = self.P
        with P.scope():
            xt = [P.sbuf("t0x%d" % i, [128, D], F32) for i in range(2)]
            xo = [P.sbuf("t0o%d" % i, [128, KC, 128], F32) for i in range(2)]
            dst = self.xT[0].ap().rearrange("(c p) t -> p c t", p=128)
            for i in range(self.NT):
                src_t, src = self.seq_rows(i * 128, 128)
                a, o = xt[i % 2], xo[i % 2]
                P.dma("sp", a[:], src, reads=[src_t], writes=[a])
                for b4 in range(4):
                    ps = self.ps[(i * 4 + b4) % 8]
                    for j in range(4):
                        c = b4 * 4 + j
                        self.tr(ps[:, j * 128:(j + 1) * 128], a[:, c * 128:(c + 1) * 128], self.ident_f, [a, self.cst_t], [ps])
                    self.copy("act" if b4 % 2 == 0 else "dve", o[:, b4 * 4:(b4 + 1) * 4, :], ps[:].rearrange("p (j t) -> p j t", j=4), [ps], [], pw=[o])
                P.dma("act", dst[:, :, i * 128:(i + 1) * 128], o[:], reads=[o], writes=[self.xT[0]])

    def phase_A(self, l):
        P = self.P
        with P.scope():
            cv = P.sbuf("a_cv", [128, 2, KC], F32)
            scT = P.sbuf("a_sc", [128, KC, 2], BF16)
            bm = P.sbuf("a_bm", [128, 6, KC], F32)
            ng = P.sbuf("a_ng", [128, 2, KC], F32)
            wt = [P.sbuf("a_w%d" % i, [128, KC, 512], BF16) for i in range(3)]
            P.dma("sp", cv[:], self.cvec.ap().rearrange("r (c p) -> p r c", p=128), reads=[self.cvec], writes=[cv], allow_slow_non_contiguous=True)
            P.dma("sp", bm[:], self.b_mod.ap()[l].rearrange("(i c p) -> p i c", p=128, c=KC), reads=[self.b_mod], writes=[bm], allow_slow_non_contiguous=True)
            P.dma("sp", ng[:], self.norm_g.ap()[l].rearrange("i (c p) -> p i c", p=128), reads=[self.norm_g], writes=[ng], allow_slow_non_contiguous=True)
            self.act(scT[:].rearrange("p c r -> p r c"), cv[:], AF.Silu, [cv], [scT])
            wv = self.w_mod.ap()[l].rearrange("(kc p) n -> p kc n", p=128)
            for idx in range(6):
                for jj in range(4):
                    w = self.rot("a_w", wt)
                    c0 = idx * D + jj * 512
                    P.dma("pool", w[:], wv[:, :, c0:c0 + 512], reads=[self.w_mod], writes=[w])
                    ps = self.rot("a_ps", self.ps[0:4])
                    for j in range(4):
                        for kc in range(KC):
                            self.mm(ps[:, 2 * j:2 * j + 2], w[:, kc, j * 128:(j + 1) * 128], scT[:, kc, :], kc == 0, kc == KC - 1, [w, scT], [ps])
                    self.tt("dve", self.modT[:, idx, 4 * jj:4 * jj + 4, :], ps[:, 0:8].rearrange("p (j r) -> p j r", r=2),
                            bc(bm[:, idx, 4 * jj:4 * jj + 4].unsqueeze(2), [128, 4, 2]), ALU.add, [ps, bm], [], pw=[self.modT])
            for n, idx in ((0, 1), (1, 4)):
                self.ts("dve", self.A1[:, n], self.modT[:, idx], 1.0, ALU.add, [self.modT], [], pw=[self.A1])
                self.tt("dve", self.A1[:, n], self.A1[:, n], bc(ng[:, n, :].unsqueeze(2), [128, KC, 2]), ALU.mult, [self.A1, ng], [self.A1])

    def norm_group(self, xg, hT, ng, n, col, sqb, rbc, tmp, ps):
        P = self.P
        shift = self.modT[:, 0 if n == 0 else 3]
        for c in range(KC):
            s = sqb[c % len(sqb)]
            self.act(s[:, 0:ng], xg[:, c, 0:ng], AF.Square, [xg], [s])
            self.mm(ps[:, 0:ng], self.ones_b, s[:, 0:ng], c == 0, c == KC - 1, [s, self.cstb], [ps])
        self.act(rbc[:, 0:ng], ps[:, 0:ng], AF.Sqrt, [ps, self.eps_t], [rbc], scale=1.0 / D, bias=self.eps_t[:, 0:1])
        P.op("dve", lambda e: e.reciprocal(out=rbc[:, 0:ng], in_=rbc[:, 0:ng]), reads=[rbc], writes=[rbc])
        for c in range(KC):
            t = tmp[c % len(tmp)]
            self.tt("dve", t[:, 0:ng], xg[:, c, 0:ng], rbc[:, 0:ng], ALU.mult, [xg, rbc], [t])
            self.act(hT[:, c, 0:ng], t[:, 0:ng], AF.Identity, [t, self.A1, self.modT], [], pw=[hT],
                     scale=self.A1[:, n, c, col:col + 1], bias=shift[:, c, col:col + 1])

    def phase_B(self, l):
        P, TT = self.P, self.TT
        xTl = self.xT[l]
        with P.scope():
            xg = P.sbuf("b_xg", [128, KC, 512], F32)
            hT = P.sbuf("b_hT", [128, KC, 512], BF16)
            sqb = [P.sbuf("b_sq%d" % i, [128, 512], BF16) for i in range(3)]
            tmp = [P.sbuf("b_tmp%d" % i, [128, 512], F32) for i in range(3)]
            rbc = P.sbuf("b_rbc", [128, 512], F32)
            wt = [P.sbuf("b_w%d" % i, [128, KC, 512], BF16) for i in range(3)]
            wsm = P.sbuf("b_wsm", [128, KC, 80], BF16)
            ob = [P.sbuf("b_ob%d" % i, [128, 4, 512], BF16) for i in range(2)]
            of = [P.sbuf("b_of%d" % i, [128, 4, 512], F32) for i in range(2)]
            ot = [P.sbuf("b_ot%d" % i, [128, 512], BF16) for i in range(3)]
            otf = [P.sbuf("b_otf%d" % i, [128, 512], F32) for i in range(2)]
            sqs = [P.sbuf("b_sqs%d" % i, [128, 512], BF16) for i in range(2)]
            binT = P.sbuf("b_binT", [128, 89], F32)
            bq = P.sbuf("b_bq", [128, 4], F32)
            btok = P.sbuf("b_btok", [128, 3664], F32)
            gb = P.sbuf("b_gb", [128, 16], F32)
            rs = P.sbuf("b_rs", [128, 4, 2], F32)
            gl = P.sbuf("b_gl", [128, 80], F32)
            e1 = P.sbuf("b_e1", [128, 8], F32)
            ur = P.sbuf("b_ur", [128, 4, 8, 128], BF16)
            gfm = P.sbuf("b_gfm", [128, 8], F32)
            bgs = P.sbuf("b_bgs", [128, 8], F32)
            P.dma("sp", gfm[:, 0:4], self.qa_g.ap()[l].rearrange("(c p) -> p c", p=128), reads=[self.qa_g], writes=[gfm], allow_slow_non_contiguous=True)
            P.dma("sp", gfm[:, 4:8], self.kva_g.ap()[l].rearrange("(c p) -> p c", p=128), reads=[self.kva_g], writes=[gfm], allow_slow_non_contiguous=True)
            bi = self.b_in.ap()[l]
            fm = [("q", OFF_Q, 512), ("k", OFF_K, 512), ("qa", OFF_QA, 512), ("kva", OFF_KVA, 512), ("u", OFF_U, 1024), ("bg", OFF_BG, 6144)]
            fmc = {}
            c0 = 0
            for name, off, n in fm:
                fmc[name] = c0
                P.dma("sp", binT[:, c0:c0 + n // 128], bi[off:off + n].rearrange("(c p) -> p c", p=128), reads=[self.b_in], writes=[binT], allow_slow_non_contiguous=True)
                c0 += n // 128
            self.ts("dve", bq[:], binT[:, 0:4], 128.0 ** -0.5, ALU.mult, [binT], [bq])
            self.tt("dve", bgs[:], binT[:, fmc["qa"]:fmc["qa"] + 8], gfm[:], ALU.mult, [binT, gfm], [bgs])
            tmo = {"k": 0, "v": 512, "o": 1536, "u": 2560, "g": 3584, "kpe": 3600}
            for name, off, n in (("k", OFF_K, 512), ("v", OFF_V, 1024), ("o", OFF_O, 1024), ("u", OFF_U, 1024), ("g", OFF_G, 16), ("kpe", OFF_KPE, 64)):
                P.dma("sp", btok[:, tmo[name]:tmo[name] + n], bass.AP(self.b_in.h, l * DIN + off, [[0, 128], [1, n]]), reads=[self.b_in], writes=[btok])
            P.dma("sp", gb[:], bass.AP(self.ml_gate_b.h, l * 16, [[0, 128], [1, 16]]), reads=[self.ml_gate_b], writes=[gb])
            self.tt("dve", btok[:, 3584:3600], btok[:, 3584:3600], gb[:], ALU.add, [btok, gb], [btok])
            wv = self.w_in.ap()[l].rearrange("(kc p) n -> p kc n", p=128)
            P.dma("pool", wsm[:, :, 0:16], wv[:, :, OFF_G:OFF_G + 16], reads=[self.w_in], writes=[wsm])
            P.dma("pool", wsm[:, :, 16:80], wv[:, :, OFF_KPE:OFF_KPE + 64], reads=[self.w_in], writes=[wsm])

            for (s0, ng) in self.groups:
                col = 1 if s0 < self.TC else 0
                ntt = ng // 128
                P.dma("sp", xg[:, :, 0:ng], xTl.ap().rearrange("(c p) t -> p c t", p=128)[:, :, s0:s0 + ng], reads=[xTl], writes=[xg])
                self.norm_group(xg, hT, ng, 0, col, sqb, rbc, tmp, self.ps[7])

                def fm_tile(w, wc, name, ci, dst, dt_f32=False):
                    o = self.rot("b_of", of) if dt_f32 else self.rot("b_ob", ob)
                    for j in range(4):
                        ps = self.rot("b_ps", self.ps[0:5])
                        for kc in range(KC):
                            self.mm(ps[:, 0:ng], w[:, kc, wc + j * 128:wc + (j + 1) * 128], hT[:, kc, 0:ng], kc == 0, kc == KC - 1, [w, hT], [ps])
                        bcol = binT[:, fmc[name] + ci + j:fmc[name] + ci + j + 1]
                        if name == "q":
                            self.act(o[:, j, 0:ng], ps[:, 0:ng], AF.Identity, [ps, bq], [], pw=[o], scale=128.0 ** -0.5, bias=bq[:, ci + j:ci + j + 1])
                        elif name == "bg":
                            self.act(o[:, j, 0:ng], ps[:, 0:ng], AF.Sigmoid, [ps, binT], [], pw=[o], bias=bcol)
                        elif name in ("qa", "kva"):
                            which = 0 if name == "qa" else 1
                            self.act(o[:, j, 0:ng], ps[:, 0:ng], AF.Identity, [ps, gfm, bgs], [], pw=[o],
                                     scale=gfm[:, 4 * which + j:4 * which + j + 1], bias=bgs[:, 4 * which + j:4 * which + j + 1])
                            sq = self.rot("b_sqs", sqs)
                            self.act(sq[:, 0:ng], ps[:, 0:ng], AF.Square, [ps, binT], [sq], bias=bcol)
                            which = 0 if name == "qa" else 1
                            for t in range(ntt):
                                self.mm(self.ps[6][:, 8 * which + t:8 * which + t + 1], sq[:, t * 128:(t + 1) * 128], self.ones_b[:, 0:1],
                                        (which == 0 and j == 0 and t == 0), j == 3, [sq, self.cstb], [self.ps[6]])
                        else:
                            eng = "act" if j % 2 == 0 else "dve"
                            if eng == "act":
                                self.act(o[:, j, 0:ng], ps[:, 0:ng], AF.Identity, [ps, binT], [], pw=[o], bias=bcol)
                            else:
                                self.ts("dve", o[:, j, 0:ng], ps[:, 0:ng], bcol, ALU.add, [ps, binT], [], pw=[o])
                    P.dma("act", dst, o[:, :, 0:ng], reads=[o], writes=[dst_t[0]])

                def tm_tile(w, wc, n, name, bo):
                    for t in range(ntt):
                        ps = self.rot("b_ps", self.ps[0:5])
                        for kc in range(KC):
                            self.mm(ps[:, 0:n], hT[:, kc, t * 128:(t + 1) * 128], w[:, kc, wc:wc + n], kc == 0, kc == KC - 1, [w, hT], [ps])
                        r0 = s0 + t * 128
                        if name in ("k", "v"):
                            o = self.rot("b_ot", ot)
                            self.tt("dve", o[:, 0:n], ps[:, 0:n], btok[:, bo:bo + n], ALU.add, [ps, btok], [o])
                            dstT = self.ktok if name == "k" else self.vtok
                            dcol = 0 if name == "k" else tm_c[0]
                            P.dma("act", dstT.ap()[r0:r0 + 128, dcol:dcol + n], o[:, 0:n], reads=[o], writes=[dstT])
                        elif name == "o":
                            f = self.rot("b_otf", otf)
                            o = self.rot("b_ot", ot)
                            self.tt("dve", f[:, 0:n], ps[:, 0:n], btok[:, bo:bo + n], ALU.add, [ps, btok], [f])
                            self.act(o[:, 0:n], f[:, 0:n], AF.Sigmoid, [f], [o])
                            P.dma("act", self.otok.ap()[r0:r0 + 128, tm_c[0]:tm_c[0] + n], o[:, 0:n], reads=[o], writes=[self.otok])
                        elif name == "u":
                            o = self.rot("b_ot", ot)
                            self.tt("dve", o[:, 0:n], ps[:, 0:n], btok[:, bo:bo + n], ALU.add, [ps, btok], [o])
                            pst = self.ps[5]
                            psb = pst[:].bitcast(BF16)
                            for j in range(4):
                                self.tr(psb[:, j * 128:(j + 1) * 128], o[:, j * 128:(j + 1) * 128], self.J_b, [o, self.cstb], [pst])
                            ch0 = tm_c[0] // 128
                            self.copy("act", ur[:, t, ch0:ch0 + 4, :], psb[:, 0:512].rearrange("p (j t) -> p j t", j=4), [pst], [], pw=[ur])
                            if ch0 == 4:
                                i0 = self.mirror(r0)
                                P.dma("act", self.uTr.ap().rearrange("(c p) t -> p c t", p=128)[:, :, i0:i0 + 128], ur[:, t], reads=[ur], writes=[self.uTr])
                        elif name == "gk":
                            self.tt("dve", gl[:], ps[:, 0:80], btok[:, 3584:3664], ALU.add, [ps, btok], [gl])
                            g4 = gl[:, 0:16].rearrange("p (d t h) -> p d t h", d=2, t=2)
                            self.act(e1[:].rearrange("p (d h) -> p d h", d=2), g4[:, :, 1, :], AF.Exp, [gl], [e1], scale=-1.0)
                            self.act(e1[:], e1[:], AF.Ln, [e1], [e1], bias=1.0)
                            self.ts("dve", g4[:, :, 1, :], e1[:].rearrange("p (d h) -> p d h", d=2), -1.0, ALU.mult, [e1], [gl])
                            P.dma("act", self.gtok.ap()[r0:r0 + 128, :], gl[:, 0:16], reads=[gl], writes=[self.gtok])
                            P.dma("act", self.kpe.ap()[r0:r0 + 128, :], gl[:, 16:80], reads=[gl], writes=[self.kpe])

                def load_w(off):
                    w = self.rot("b_w", wt)
                    P.dma("pool", w[:], wv[:, :, off:off + 512], reads=[self.w_in], writes=[w])
                    return w

                def fmdst(Tt, row0, f32=False):
                    return Tt.ap().rearrange("(j p) t -> p j t", p=128)[:, row0 // 128:row0 // 128 + 4, s0:s0 + ng]

                dst_t = [None]
                tm_c = [0]
                w = load_w(OFF_Q)
                dst_t[0] = self.qT
                fm_tile(w, 0, "q", 0, self.qT.ap().rearrange("h p t -> p h t")[:, :, s0:s0 + ng])
                w = load_w(OFF_K)
                dst_t[0] = self.kT
                fm_tile(w, 0, "k", 0, self.kT.ap().rearrange("h p t -> p h t")[:, :, s0:s0 + ng])
                tm_tile(w, 0, 512, "k", tmo["k"])
                for h2 in range(2):
                    w = load_w(OFF_V + 512 * h2)
                    tm_c[0] = 512 * h2
                    tm_tile(w, 0, 512, "v", tmo["v"] + 512 * h2)
                for h2 in range(2):
                    w = load_w(OFF_O + 512 * h2)
                    tm_c[0] = 512 * h2
                    tm_tile(w, 0, 512, "o", tmo["o"] + 512 * h2)
                tm_tile(wsm, 0, 80, "gk", 0)
                for name, off, Tt in (("qa", OFF_QA, self.qaT), ("kva", OFF_KVA, self.kvaT)):
                    w = load_w(off)
                    dst_t[0] = Tt
                    fm_tile(w, 0, name, 0, fmdst(Tt, 0))
                for which in range(2):
                    self.act(rs[:, 0:ntt, which], self.ps[6][:, 8 * which:8 * which + ntt], AF.Sqrt, [self.ps[6], self.eps_t], [], pw=[rs], scale=1.0 / 512, bias=self.eps_t[:, 0:1])
                P.op("dve", lambda e: e.reciprocal(out=rs[:, 0:ntt, :], in_=rs[:, 0:ntt, :]), reads=[rs], writes=[rs])
                P.dma("act", self.rsq.ap()[s0:s0 + ng, :].rearrange("(t p) c -> p t c", p=128), rs[:, 0:ntt, :], reads=[rs], writes=[self.rsq])
                for h2 in range(2):
                    w = load_w(OFF_U + 512 * h2)
                    dst_t[0] = self.uT
                    fm_tile(w, 0, "u", 4 * h2, fmdst(self.uT, 512 * h2), dt_f32=True)
                    tm_c[0] = 512 * h2
                    tm_tile(w, 0, 512, "u", tmo["u"] + 512 * h2)
                for b12 in range(12):
                    w = load_w(OFF_BG + 512 * b12)
                    dst_t[0] = self.gT
                    fm_tile(w, 0, "bg", 4 * b12, fmdst(self.gT, 512 * b12))

    def mirror(self, r0, n=128):
        TC, TL = self.TC, self.TL
        if r0 < TC:
            return TC - n - r0
        return TC + (TL - n - (r0 - TC))


def make_consts(TL, TC):
    cst = np.zeros((128, 640), np.float32)
    cst[:, 0:128] = np.eye(128)
    cst[:, 128:256] = np.eye(128)[::-1]
    k = np.arange(64)
    cst[0:64, 256:320] = (k[:, None] <= k[None, :])
    cst[0:64, 320:384] = (k[:, None] >= k[None, :])
    cst[:, 384:512] = 1.0
    TT = TL + TC
    rope = np.zeros((TT, 128), np.float32)
    rope[:, 0:64] = 1.0
    t = np.arange(TL)
    row = (t // 64).astype(np.float32)
    col = (t % 64).astype(np.float32)
    inv = (np.float32(10000.0) ** (-np.arange(16, dtype=np.float32) / np.float32(16))).astype(np.float32)
    ar = (row[:, None] * inv).astype(np.float32)
    ac = (col[:, None] * inv).astype(np.float32)
    rope[TC:, 0:64] = np.concatenate([np.cos(ar), np.cos(ar), np.cos(ac), np.cos(ac)], axis=1)
    rope[TC:, 64:128] = np.concatenate([-np.sin(ar), np.sin(ar), -np.sin(ac), np.sin(ac)], axis=1)
    return cst, rope


def prep_shared(inp, L):
    f = lambda a: np.ascontiguousarray(np.asarray(a, dtype=np.float32))
    sh = {}
    for k in ("w_mod", "b_mod", "norm_g", "w_in", "b_in", "mla_qa_g", "mla_kva_g", "mla_w_uq", "mla_w_ukv", "mla_qn_g", "mla_kn_g",
              "s5_d", "s5_w_glu", "s5_b_glu", "w_branch", "w_out", "w_ff1", "w_ff2"):
        sh[k] = f(inp[k])[:L]
    sh["ml_gate_b"] = f(inp["ml_gate_b"])[:L].reshape(L, 16)
    sh["ml_norm_g"] = f(inp["ml_norm_g"])[:L].reshape(L, 1024)
    lam = np.zeros((L, 2, 3, 128, 32), np.float32)
    sb = np.zeros((L, 2, 2, 128, 8, 2, 128), np.float32)
    sc = np.zeros((L, 2, 2, 128, 32, 32), np.float32)
    a_re, a_im, ldt = f(inp["s5_a_re"]), f(inp["s5_a_im"]), f(inp["s5_log_dt"])
    bs = (f(inp["s5_b_re"]), f(inp["s5_b_im"]))
    cs = (f(inp["s5_c_re"]), f(inp["s5_c_im"]))

    def lay(a):
        return a.reshape(32, 2, 64).transpose(1, 2, 0).reshape(128, 32)

    for l in range(L):
        for d in range(2):
            lam[l, d, 0] = lay(a_re[l, d])
            lam[l, d, 1] = lay(a_im[l, d])
            lam[l, d, 2] = lay(np.repeat(ldt[l, d][:, None], 64, axis=1))
            for ri in range(2):
                b = bs[ri][l, d]
                c = cs[ri][l, d]
                for g in range(64):
                    q, pp, g2 = g // 8, (g % 8) // 2, g % 2
                    sb[l, d, ri, pp * 32 + g2 * 16:pp * 32 + g2 * 16 + 16, q, 0, g2 * 64:(g2 + 1) * 64] = b[g].T
                    if pp == 3:
                        sb[l, d, ri, pp * 32 + g2 * 16:pp * 32 + g2 * 16 + 16, q, 1, g2 * 64:(g2 + 1) * 64] = b[g].T
                    sc[l, d, ri, g2 * 64:(g2 + 1) * 64, g // 2, g2 * 16:(g2 + 1) * 16] = c[g].T
    sh["s5_lam"], sh["s5_b"], sh["s5_c"] = lam, sb, sc
    return sh


def prep_core(inp, b, TL, TC):
    f = lambda a: np.ascontiguousarray(np.asarray(a, dtype=np.float32))
    return {
        "x": f(inp["x"][b]),
        "ctx": f(inp["ctx"][b]),
        "cvec": f(np.stack([np.asarray(inp["c"][b]), np.asarray(inp["c_ctx"])])),
    }


PHASES = ("A", "B", "C1", "C2", "C3", "DE")


def build_program(TL, TC, depth=DEPTH, debug=(), stop_after=None):
    B = Builder(TL, TC, depth, debug, stop_after)
    B.declare()
    B.phase_T0()
    done = False
    for l in range(depth):
        for name in PHASES:
            getattr(B, "phase_" + name)(l)
            if stop_after == (name, l):
                done = True
                break
        if done:
            break
    B.P.finish()
    return B


def phase_C1(self, l):
    P, TC, TL, TT = self.P, self.TC, self.TL, self.TT
    with P.scope():
        QTg = [P.sbuf("m_q%d" % d, [128, 4, 512], BF16) for d in range(2)]
        KTg = [P.sbuf("m_k%d" % d, [128, 4, 512], BF16) for d in range(2)]
        Ktg = [P.sbuf("m_kt%d" % d, [64, 8, 512], BF16) for d in range(2)]
        Vg = [P.sbuf("m_v%d" % d, [64, 8, 1024], BF16) for d in range(2)]
        Gg = [P.sbuf("m_g%d" % d, [64, 8, 16], F32) for d in range(2)]
        Cf = [P.sbuf("m_cf%d" % d, [128, 4, 257], F32) for d in range(2)]
        Cb = [P.sbuf("m_cb%d" % d, [128, 4, 257], BF16) for d in range(2)]
        sm = [[P.sbuf("m_s%d_%d" % (d, i), [128, 40], F32) for i in range(2)] for d in range(2)]
        smb = [[P.sbuf("m_sb%d_%d" % (d, i), [64, 8], BF16) for i in range(2)] for d in range(2)]
        Vs = [[P.sbuf("m_vs%d_%d" % (d, i), [64, 4, 256], BF16) for i in range(2)] for d in range(2)]
        VF = [[P.sbuf("m_vf%d_%d" % (d, i), [64, 4, 256], BF16) for i in range(2)] for d in range(2)]
        Sm = [[P.sbuf("m_sm%d_%d" % (d, i), [64, 4, 64], BF16) for i in range(2)] for d in range(2)]
        hst = [[P.sbuf("m_h%d_%d" % (d, i), [64, 4, 256], F32) for i in range(2)] for d in range(2)]
        for d in range(2):
            P.op("dve", lambda e, d=d: e.memset(Cf[d][:], 0.0), writes=[Cf[d]])
            P.op("pool", lambda e, d=d: e.memset(Cb[d][:], 0.0), writes=[Cb[d]])
        order = []
        for d in range(2):
            o = []
            glist = self.groups if d == 0 else [self.groups[0]] + self.groups[1:][::-1]
            for gi, (s0, ng) in enumerate(glist):
                cis = list(range(ng // 64))
                if d == 1:
                    cis = cis[::-1]
                for ci in cis:
                    o.append((s0, ng, ci))
            order.append(o)
        cur = [None, None]
        psm = [self.ps[0], self.ps[5]]
        pacc = [(self.ps[1], self.ps[2]), (self.ps[6], self.ps[7])]
        pdc = (self.ps[3], self.ps[4])
        for step in range(len(order[0])):
            for d in range(2):
                s0, ng, ci = order[d][step]
                nch = ng // 64
                if cur[d] != s0:
                    cur[d] = s0
                    P.dma("sp", QTg[d][:, :, 0:ng], self.qT.ap().rearrange("h p t -> p h t")[:, :, s0:s0 + ng], reads=[self.qT], writes=[QTg[d]])
                    P.dma("sp", KTg[d][:, :, 0:ng], self.kT.ap().rearrange("h p t -> p h t")[:, :, s0:s0 + ng], reads=[self.kT], writes=[KTg[d]])
                    P.dma("sp", Ktg[d][:, 0:nch, :], self.ktok.ap()[s0:s0 + ng, :].rearrange("(c p) n -> p c n", p=64), reads=[self.ktok], writes=[Ktg[d]])
                    P.dma("sp", Vg[d][:, 0:nch, :], self.vtok.ap()[s0:s0 + ng, :].rearrange("(c p) n -> p c n", p=64), reads=[self.vtok], writes=[Vg[d]])
                    P.dma("sp", Gg[d][:, 0:nch, :], self.gtok.ap()[s0:s0 + ng, :].rearrange("(c p) n -> p c n", p=64), reads=[self.gtok], writes=[Gg[d]])
                c0 = ci * 64
                t0 = s0 + c0
                par = step % 2
                S_, SB_, Vs_, VF_, Sm_, H_ = sm[d][par], smb[d][par], Vs[d][par], VF[d][par], Sm[d][par], hst[d][par]
                pA, (pa0, pa1), (pd0, pd1) = psm[d], pacc[d], pdc
                tri = self.triL_f if d == 0 else self.triU_f
                li = Gg[d][:, ci, d * 8:d * 8 + 4]
                lf = Gg[d][:, ci, d * 8 + 4:d * 8 + 8]
                cst = self.cst_t
                self.mm(pA[0:64, 0:4], tri, lf, True, True, [cst, Gg[d]], [pA])
                self.mm(pA[:, 8:12], self.ones_f[0:64, :], lf, True, True, [cst, Gg[d]], [pA])
                dif, esc, ecn, eF, escF, absd, rden = (S_[0:64, 0:4], S_[0:64, 4:8], S_[0:64, 8:12], S_[:, 12:16], S_[0:64, 16:20], S_[0:64, 20:24], S_[0:64, 24:28])
                self.tt("dve", dif, li, pA[0:64, 0:4], ALU.subtract, [Gg[d], pA], [S_])
                self.act(esc, dif, AF.Exp, [S_], [S_])
                self.act(ecn, pA[0:64, 0:4], AF.Exp, [pA], [S_], scale=-1.0)
                self.act(eF, pA[:, 8:12], AF.Exp, [pA], [S_])
                self.tt("dve", escF, esc, eF[0:64, :], ALU.mult, [S_], [S_])
                self.copy("act", SB_[:, 0:4], esc, [S_], [SB_])
                self.copy("act", SB_[:, 4:8], escF, [S_], [SB_])
                vv = Vg[d][:, ci, :].rearrange("p (h v) -> p h v", h=4)
                self.tt("dve", Vs_[:], vv, bc(esc.unsqueeze(2), [64, 4, 256]), ALU.mult, [Vg[d], S_], [Vs_])
                self.tt("pool", VF_[:], vv, bc(escF.unsqueeze(2), [64, 4, 256]), ALU.mult, [Vg[d], S_], [VF_])
                for h in range(4):
                    self.mm(pA[0:64, 256 + h * 64:256 + (h + 1) * 64], KTg[d][:, h, c0:c0 + 64], QTg[d][:, h, c0:c0 + 64], True, True, [KTg[d], QTg[d]], [pA])
                self.tt("dve", Sm_[:], pA[0:64, 256:512].rearrange("p (h t) -> p h t", h=4), bc(tri.unsqueeze(1), [64, 4, 64]), ALU.mult, [pA, cst], [Sm_])
                for h in range(4):
                    bank = pa0 if h < 2 else pa1
                    off = (h % 2) * 256
                    self.mm(bank[0:64, off:off + 256], QTg[d][:, h, c0:c0 + 64], Cb[d][:, h, 0:256], True, False, [QTg[d], Cb[d]], [bank])
                    self.mm(bank[0:64, off:off + 256], Sm_[:, h, :], Vs_[:, h, :], False, True, [Sm_, Vs_], [bank])
                for h in range(4):
                    self.mm(pA[0:64, 16 + h:17 + h], QTg[d][:, h, c0:c0 + 64], Cb[d][:, h, 256:257], True, False, [QTg[d], Cb[d]], [pA])
                    self.mm(pA[0:64, 16 + h:17 + h], Sm_[:, h, :], SB_[:, h:h + 1], False, True, [Sm_, SB_], [pA])
                for h in range(4):
                    bank = pd0 if h < 2 else pd1
                    off = (h % 2) * 256
                    self.mm(bank[:, off:off + 256], Ktg[d][:, ci, h * 128:(h + 1) * 128], VF_[:, h, :], True, True, [Ktg[d], VF_], [bank])
                    self.mm(pA[:, 24 + h:25 + h], Ktg[d][:, ci, h * 128:(h + 1) * 128], SB_[:, 4 + h:5 + h], True, True, [Ktg[d], SB_], [pA])
                self.act(absd, pA[0:64, 16:20], AF.Abs, [pA], [S_])
                self.tt("dve", rden, absd, ecn, ALU.max, [S_], [S_])
                P.op("dve", lambda e, rden=rden: e.reciprocal(out=rden, in_=rden), reads=[S_], writes=[S_])
                for b2, bank in enumerate((pa0, pa1)):
                    self.tt("dve", H_[:, 2 * b2:2 * b2 + 2, :], bank[0:64, :].rearrange("p (h v) -> p h v", h=2),
                            bc(rden[:, 2 * b2:2 * b2 + 2].unsqueeze(2), [64, 2, 256]), ALU.mult, [bank, S_], [], pw=[H_])
                P.dma("act", self.hml.ap()[d, t0:t0 + 64, :], H_[:].rearrange("p h v -> p (h v)"), reads=[H_], writes=[self.hml])
                for h in range(4):
                    bank = pd0 if h < 2 else pd1
                    off = (h % 2) * 256
                    self.stt(Cf[d][:, h, 0:256], Cf[d][:, h, 0:256], eF[:, h:h + 1], bank[:, off:off + 256], ALU.mult, ALU.add, [Cf[d], S_, bank], [Cf[d]])
                    self.stt(Cf[d][:, h, 256:257], Cf[d][:, h, 256:257], eF[:, h:h + 1], pA[:, 24 + h:25 + h], ALU.mult, ALU.add, [Cf[d], S_, pA], [Cf[d]])
                self.copy("act", Cb[d][:], Cf[d][:], [Cf[d]], [Cb[d]])

    with P.scope():
        gnb = P.sbuf("mo_gn", [128, 1024], F32)
        P.dma("sp", gnb[:], bass.AP(self.ml_norm_g.h, l * 1024, [[0, 128], [1, 1024]]), reads=[self.ml_norm_g], writes=[gnb])
        h0 = [P.sbuf("mo_h0_%d" % i, [128, 1024], F32) for i in range(2)]
        h1 = [P.sbuf("mo_h1_%d" % i, [128, 1024], F32) for i in range(2)]
        og = [P.sbuf("mo_og_%d" % i, [128, 1024], BF16) for i in range(2)]
        sq = P.sbuf("mo_sq", [128, 1024], F32)
        ss = [P.sbuf("mo_ss%d" % i, [128, 4], F32) for i in range(2)]
        ab = [P.sbuf("mo_ab%d" % i, [128, 1024], BF16) for i in range(2)]
        aT = [P.sbuf("mo_aT%d" % i, [128, 8, 512], BF16) for i in range(2)]
        for gi, (s0, ng) in enumerate(self.groups):
            A_ = aT[gi % 2]
            for t in range(ng // 128):
                r0 = s0 + t * 128
                i = self.rr.get("m_o", 0)
                self.rr["m_o"] = i + 1
                a0, a1, o_, s_, b_ = h0[i % 2], h1[i % 2], og[i % 2], ss[i % 2], ab[i % 2]
                P.dma("sp", a0[:], self.hml.ap()[0, r0:r0 + 128, :], reads=[self.hml], writes=[a0])
                P.dma("sp", a1[:], self.hml.ap()[1, r0:r0 + 128, :], reads=[self.hml], writes=[a1])
                P.dma("sp", o_[:], self.otok.ap()[r0:r0 + 128, :], reads=[self.otok], writes=[o_])
                self.tt("dve", a0[:], a0[:], a1[:], ALU.add, [a0, a1], [a0])
                self.act(sq[:], a0[:], AF.Square, [a0], [sq])
                P.op("dve", lambda e, s_=s_: e.tensor_reduce(out=s_[:], in_=sq[:].rearrange("p (h v) -> p h v", h=4), axis=AX.X, op=ALU.add), reads=[sq], writes=[s_])
                self.act(s_[:], s_[:], AF.Sqrt, [s_, self.eps_t], [s_], scale=1.0 / 256, bias=self.eps_t[:, 0:1])
                P.op("dve", lambda e, s_=s_: e.reciprocal(out=s_[:], in_=s_[:]), reads=[s_], writes=[s_])
                a3 = a0[:].rearrange("p (h v) -> p h v", h=4)
                self.tt("dve", a3, a3, bc(s_[:].unsqueeze(2), [128, 4, 256]), ALU.mult, [a0, s_], [a0])
                self.tt("pool", a0[:], a0[:], gnb[:], ALU.mult, [a0, gnb], [a0])
                self.tt("dve", b_[:], a0[:], o_[:], ALU.mult, [a0, o_], [b_])
                pst = self.rot("m_pst", self.ps[0:2])
                psb = pst[:].bitcast(BF16)
                for j in range(8):
                    self.tr(psb[:, j * 128:(j + 1) * 128], b_[:, j * 128:(j + 1) * 128], self.ident_b, [b_, self.cstb], [pst])
                self.copy("act", A_[:, :, t * 128:(t + 1) * 128], psb[:, 0:1024].rearrange("p (j t) -> p j t", j=8), [pst], [], pw=[A_])
            P.dma("act", self.brT.ap()[0].rearrange("(c p) t -> p c t", p=128)[:, :, s0:s0 + ng], A_[:, :, 0:ng], reads=[A_], writes=[self.brT])


Builder.phase_C1 = phase_C1


def rope_ops(self, eng, src, dst, tmp1, tmp2, rt, nh, reads, writes):
    C = bc(rt[:, 0:64].unsqueeze(1), [128, nh, 64])
    S5 = rt[:, 64:128].rearrange("p (b f i) -> p b f i", b=2, f=2)
    s5 = src.rearrange("p h (b f i) -> p h b f i", b=2, f=2)
    t5 = tmp2.rearrange("p h (b f i) -> p h b f i", b=2, f=2)
    self.tt(eng, tmp1, src, C, ALU.mult, reads, writes)
    for f in range(2):
        self.tt(eng, t5[:, :, :, f, :], s5[:, :, :, 1 - f, :], bc(S5[:, :, f, :].unsqueeze(1), [128, nh, 2, 16]), ALU.mult, reads, writes)
    self.tt(eng, dst, tmp1, tmp2, ALU.add, reads, writes)


def phase_C2(self, l):
    P, TC, TL, TT, NT = self.P, self.TC, self.TL, self.TT, self.NT
    with P.scope():
        wq = P.sbuf("c_wq", [128, 4, 1536], BF16)
        wkv = P.sbuf("c_wkv", [128, 4, 2048], BF16)
        gq = P.sbuf("c_gq", [128, 192], F32)
        gk = P.sbuf("c_gk", [128, 192], F32)
        P.dma("pool", wq[:], self.w_uq.ap()[l].rearrange("(kc p) n -> p kc n", p=128), reads=[self.w_uq], writes=[wq])
        P.dma("pool", wkv[:], self.w_ukv.ap()[l].rearrange("(kc p) n -> p kc n", p=128), reads=[self.w_ukv], writes=[wkv])
        P.dma("sp", gq[:], bass.AP(self.qn_g.h, l * 192, [[0, 128], [1, 192]]), reads=[self.qn_g], writes=[gq])
        P.dma("sp", gk[:], bass.AP(self.kn_g.h, l * 192, [[0, 128], [1, 192]]), reads=[self.kn_g], writes=[gk])
        qaTg = P.sbuf("c_qa", [128, 4, 512], BF16)
        kvaTg = P.sbuf("c_kva", [128, 4, 512], BF16)
        rsg = P.sbuf("c_rs", [128, 4, 2], F32)
        rpt = [P.sbuf("c_rp%d" % i, [128, 128], F32) for i in range(2)]
        kpt = [P.sbuf("c_kp%d" % i, [128, 64], F32) for i in range(2)]
        qf = P.sbuf("c_qf", [128, 2048], F32)
        sqv = P.sbuf("c_sq", [128, 2048], F32)
        fin = [P.sbuf("c_fin%d" % i, [128, 8, 192], BF16) for i in range(2)]
        vfin = [P.sbuf("c_vf%d" % i, [128, 8, 128], BF16) for i in range(2)]
        rp = P.sbuf("c_rpp", [128, 8, 64], F32)
        t1 = P.sbuf("c_t1", [128, 8, 64], F32)
        t2 = P.sbuf("c_t2", [128, 8, 64], F32)
        st = [P.sbuf("c_st%d" % i, [128, 12], F32) for i in range(2)]
        kk = P.sbuf("c_kk", [128, 4, 64], F32)
        stn = [P.sbuf("c_stn%d" % i, [128, 8, 512], BF16) for i in range(2)]
        str_ = [P.sbuf("c_str%d" % i, [64, 8, 512], BF16) for i in range(2)]
        for (s0, ng) in self.groups:
            ntt = ng // 128
            P.dma("sp", qaTg[:, :, 0:ng], self.qaT.ap().rearrange("(c p) t -> p c t", p=128)[:, :, s0:s0 + ng], reads=[self.qaT], writes=[qaTg])
            P.dma("sp", kvaTg[:, :, 0:ng], self.kvaT.ap().rearrange("(c p) t -> p c t", p=128)[:, :, s0:s0 + ng], reads=[self.kvaT], writes=[kvaTg])
            P.dma("sp", rsg[:, 0:ntt, :], self.rsq.ap()[s0:s0 + ng, :].rearrange("(t p) c -> p t c", p=128), reads=[self.rsq], writes=[rsg])
            qn_, qr_ = stn[0], str_[0]
            kn_, kr_ = stn[1], str_[1]
            for t in range(ntt):
                r0 = s0 + t * 128
                i = self.rr.get("c_i", 0)
                self.rr["c_i"] = i + 1
                rt, kp, S_ = rpt[i % 2], kpt[i % 2], st[i % 2]
                P.dma("sp", rt[:], self.rope.ap()[r0:r0 + 128, :], reads=[self.rope], writes=[rt])
                P.dma("sp", kp[:], self.kpe.ap()[r0:r0 + 128, :], reads=[self.kpe], writes=[kp])
                for cg in range(3):
                    ps = self.rot("c_ps", self.ps[0:6])
                    for kc in range(4):
                        self.mm(ps[:, 0:512], qaTg[:, kc, t * 128:(t + 1) * 128], wq[:, kc, cg * 512:(cg + 1) * 512], kc == 0, kc == 3, [qaTg, wq], [ps])
                    self.act(qf[:, cg * 512:(cg + 1) * 512], ps[:, 0:512], AF.Identity, [ps, rsg], [], pw=[qf], scale=rsg[:, t, 0:1])
                    self.act(sqv[:, cg * 512:(cg + 1) * 512], ps[:, 0:512], AF.Square, [ps, rsg], [], pw=[sqv], scale=rsg[:, t, 0:1])
                q3 = qf[:, 0:1536].rearrange("p (h e) -> p h e", h=8)
                s3 = sqv[:, 0:1536].rearrange("p (h e) -> p h e", h=8)
                rq = S_[:, 0:8]
                P.op("dve", lambda e, rq=rq, s3=s3: e.tensor_reduce(out=rq, in_=s3, axis=AX.X, op=ALU.add), reads=[sqv], writes=[S_])
                self.act(rq, rq, AF.Sqrt, [S_, self.eps_t], [S_], scale=1.0 / 192, bias=self.eps_t[:, 0:1])
                P.op("dve", lambda e, rq=rq: e.reciprocal(out=rq, in_=rq), reads=[S_], writes=[S_])
                self.tt("dve", s3, q3, bc(rq.unsqueeze(2), [128, 8, 192]), ALU.mult, [qf, S_], [sqv])
                F_ = fin[0]
                self.tt("dve", F_[:, :, 0:128], s3[:, :, 0:128], bc(gq[:, 0:128].unsqueeze(1), [128, 8, 128]), ALU.mult, [sqv, gq], [F_])
                self.tt("pool", rp[:], s3[:, :, 128:192], bc(gq[:, 128:192].unsqueeze(1), [128, 8, 64]), ALU.mult, [sqv, gq], [rp])
                rope_ops(self, "dve", rp[:], F_[:, :, 128:192], t1[:], t2[:], rt, 8, [rp, rt, t1, t2], [t1, t2, F_])
                pn, pr = self.ps[6], self.ps[7]
                pnb, prb = pn[:].bitcast(BF16), pr[:].bitcast(BF16)
                for h in range(8):
                    self.tr(pnb[:, h * 128:(h + 1) * 128], F_[:, h, 0:128], self.ident_b, [F_, self.cstb], [pn])
                    self.tr(prb[0:64, h * 128:(h + 1) * 128], F_[:, h, 128:192], self.ident_b, [F_, self.cstb], [pr])
                self.copy("act", qn_[:, :, t * 128:(t + 1) * 128], pnb[:, 0:1024].rearrange("p (h t) -> p h t", h=8), [pn], [], pw=[qn_])
                self.copy("act", qr_[:, :, t * 128:(t + 1) * 128], prb[0:64, 0:1024].rearrange("p (h t) -> p h t", h=8), [pr], [], pw=[qr_])
                for cg in range(4):
                    ps = self.rot("c_ps", self.ps[0:6])
                    for kc in range(4):
                        self.mm(ps[:, 0:512], kvaTg[:, kc, t * 128:(t + 1) * 128], wkv[:, kc, cg * 512:(cg + 1) * 512], kc == 0, kc == 3, [kvaTg, wkv], [ps])
                    self.act(qf[:, cg * 512:(cg + 1) * 512], ps[:, 0:512], AF.Identity, [ps, rsg], [], pw=[qf], scale=rsg[:, t, 1:2])
                k3 = qf[:].rearrange("p (h e) -> p h e", h=8)
                s3k = sqv[:, 0:1024].rearrange("p (h e) -> p h e", h=8)
                self.act(s3k, k3[:, :, 0:128], AF.Square, [qf], [sqv])
                rk = S_[:, 0:8]
                sp_ = S_[:, 8:9]
                P.op("dve", lambda e, rk=rk, s3k=s3k: e.tensor_reduce(out=rk, in_=s3k, axis=AX.X, op=ALU.add), reads=[sqv], writes=[S_])
                self.act(kk[:, 0, :], kp[:], AF.Square, [kp], [kk, S_], accum_out=sp_)
                self.ts("dve", rk, rk, sp_, ALU.add, [S_], [S_])
                self.act(rk, rk, AF.Sqrt, [S_, self.eps_t], [S_], scale=1.0 / 192, bias=self.eps_t[:, 0:1])
                P.op("dve", lambda e, rk=rk: e.reciprocal(out=rk, in_=rk), reads=[S_], writes=[S_])
                Fk = fin[1]
                self.tt("dve", s3k, k3[:, :, 0:128], bc(rk.unsqueeze(2), [128, 8, 128]), ALU.mult, [qf, S_], [sqv])
                self.tt("dve", Fk[:, :, 0:128], s3k, bc(gk[:, 0:128].unsqueeze(1), [128, 8, 128]), ALU.mult, [sqv, gk], [Fk])
                self.tt("pool", kk[:, 1, :], kp[:], gk[:, 128:192], ALU.mult, [kp, gk], [kk])
                rope_ops(self, "pool", kk[:, 1:2, :], kk[:, 0:1, :], kk[:, 2:3, :], kk[:, 3:4, :], rt, 1, [kk, rt], [kk])
                self.tt("dve", Fk[:, :, 128:192], bc(kk[:, 0:1, :], [128, 8, 64]), bc(rk.unsqueeze(2), [128, 8, 64]), ALU.mult, [kk, S_], [Fk])
                V_ = vfin[i % 2]
                self.copy("act", V_[:], k3[:, :, 128:256], [qf], [V_])
                P.dma("act", self.vh.ap()[r0:r0 + 128, :], V_[:].rearrange("p h e -> p (h e)"), reads=[V_], writes=[self.vh])
                pn, pr = self.ps[6], self.ps[7]
                for h in range(8):
                    self.tr(pnb[:, h * 128:(h + 1) * 128], Fk[:, h, 0:128], self.ident_b, [Fk, self.cstb], [pn])
                    self.tr(prb[0:64, h * 128:(h + 1) * 128], Fk[:, h, 128:192], self.ident_b, [Fk, self.cstb], [pr])
                self.copy("act", kn_[:, :, t * 128:(t + 1) * 128], pnb[:, 0:1024].rearrange("p (h t) -> p h t", h=8), [pn], [], pw=[kn_])
                self.copy("act", kr_[:, :, t * 128:(t + 1) * 128], prb[0:64, 0:1024].rearrange("p (h t) -> p h t", h=8), [pr], [], pw=[kr_])
            for Tt, n_, r_ in ((self.qhT, qn_, qr_), (self.khT, kn_, kr_)):
                P.dma("act", Tt.ap()[:, 0:128, s0:s0 + ng].rearrange("h d t -> d h t"), n_[:, :, 0:ng], reads=[n_], writes=[Tt])
                P.dma("act", Tt.ap()[:, 128:192, s0:s0 + ng].rearrange("h d t -> d h t"), r_[:, :, 0:ng], reads=[r_], writes=[Tt])


def attn_gen(self, l, AK=8):
    P, TC, TL, TT, NT = self.P, self.TC, self.TL, self.TT, self.NT
    Kn = P.sbuf("c_Kn", [128, TT], BF16)
    Kr = P.sbuf("c_Kr", [64, TT], BF16)
    Vh = P.sbuf("c_Vh", [128, NT, 132], BF16)
    Qn = [P.sbuf("c_Qn%d" % i, [128, 256], BF16) for i in range(2)]
    Qr = [P.sbuf("c_Qr%d" % i, [64, 256], BF16) for i in range(2)]
    pT = [P.sbuf("c_pT%d" % i, [128, 256], BF16) for i in range(3)]
    ob = [P.sbuf("c_ob%d" % i, [128, 2, 128], BF16) for i in range(2)]
    rd = [P.sbuf("c_rd%d" % i, [128, 2], F32) for i in range(2)]
    bst = [P.sbuf("c_bst%d" % i, [128, 256], BF16) for i in range(2)]
    P.op("pool", lambda e: e.memset(Vh[:], 1.0), writes=[Vh])
    sc = 192.0 ** -0.5
    qgroups = []
    for (g0, ng) in self.groups:
        for o in range(0, ng, 256):
            qgroups.append((g0 + o, 256))
    cnt = 0
    pO = self.ps[0:2]
    pSb = self.ps[2:5]
    for h in range(8):
        K1, K2, V_ = Kn, Kr, Vh
        P.dma("sp", K1[:], self.khT.ap()[h, 0:128, :], reads=[self.khT], writes=[K1])
        P.dma("sp", K2[:], self.khT.ap()[h, 128:192, :], reads=[self.khT], writes=[K2])
        P.dma("sp", V_[:, :, 0:128], self.vh.ap()[:, h * 128:(h + 1) * 128].rearrange("(kt p) e -> p kt e", p=128), reads=[self.vh], writes=[V_])
        for (q0, nq) in qgroups:
            i = self.rr.get("c_q", 0)
            self.rr["c_q"] = i + 1
            Q1, Q2, O_, R_, B_ = Qn[i % 2], Qr[i % 2], ob[i % 2], rd[i % 2], bst[i % 2]
            P.dma("sp", Q1[:, 0:nq], self.qhT.ap()[h, 0:128, q0:q0 + nq], reads=[self.qhT], writes=[Q1])
            P.dma("sp", Q2[:, 0:nq], self.qhT.ap()[h, 128:192, q0:q0 + nq], reads=[self.qhT], writes=[Q2])
            nqt = nq // 128
            nkt = TC // 128 if q0 < TC else NT

            def qk(kt):
                pS = self.rot("c_pS", pSb)
                self.mm(pS[:, 0:nq], K1[:, kt * 128:(kt + 1) * 128], Q1[:, 0:nq], True, False, [K1, Q1], [pS])
                self.mm(pS[:, 0:nq], K2[:, kt * 128:(kt + 1) * 128], Q2[:, 0:nq], False, True, [K2, Q2], [pS])
                return pS

            pS_next = qk(0)
            for kt in range(nkt):
                pS = pS_next
                p_ = self.rot("c_pT", pT)
                self.act(p_[:, 0:nq], pS[:, 0:nq], AF.Exp, [pS], [p_], scale=sc)
                if kt + 1 < nkt:
                    pS_next = qk(kt + 1)
                for j in range(nqt):
                    self.mm(pO[j][:, 0:129], p_[:, j * 128:(j + 1) * 128], V_[:, kt, 0:129], kt == 0, kt == nkt - 1, [p_, V_], [pO[j]])
                cnt += 1
                if cnt % AK == 0:
                    yield
            pS = self.rot("c_pS", pSb)
            psb = pS[:].bitcast(BF16)
            for j in range(nqt):
                P.op("dve", lambda e, j=j, R_=R_: e.reciprocal(out=R_[:, j:j + 1], in_=pO[j][:, 128:129]), reads=[pO[j]], writes=[R_])
                self.act(O_[:, j, :], pO[j][:, 0:128], AF.Identity, [pO[j], R_], [], pw=[O_], scale=R_[:, j:j + 1])
                self.tr(psb[:, j * 128:(j + 1) * 128], O_[:, j, :], self.ident_b, [O_, self.cstb], [pS])
            self.copy("act", B_[:, 0:nq], psb[:, 0:nq], [pS], [B_])
            P.dma("act", self.brT.ap()[1, h * 128:(h + 1) * 128, q0:q0 + nq], B_[:, 0:nq], reads=[B_], writes=[self.brT])
            yield


Builder.phase_C2 = phase_C2


def phase_C3(self, l):
    P, TC, TL, TT = self.P, self.TC, self.TL, self.TT
    TWO_PI = 2.0 * math.pi
    with P.scope():
      attn = attn_gen(self, l)
      attn_live = [True]

      def attn_step():
          if attn_live[0]:
              try:
                  next(attn)
              except StopIteration:
                  attn_live[0] = False

      cache = {}

      def sb(name, shape, dt):
          if name not in cache:
              cache[name] = P.sbuf(name, shape, dt)
          return cache[name]

      for d in range(2):
        if True:
            lam = sb("s_lam", [128, 3, 32], F32)
            sc = sb("s_sc", [128, 24, 32], F32)
            Er = sb("s_Er", [128, 32, SL], F32)
            Ei = sb("s_Ei", [128, 32, SL], F32)
            Tr = sb("s_Tr", [128, 32, SL], F32)
            Ti = sb("s_Ti", [128, 32, SL], F32)
            tmpE = sb("s_tmpE", [128, 32, SL // 2], F32)
            Bw = sb("s_Bw", [128, 2, 8, 2, 128], BF16)
            Cw = sb("s_Cw", [128, 2, 32, 32], BF16)
            Cn = sb("s_Cn", [128, 2, 32, 32], BF16)
            xst = sb("s_xst", [128, 32, 2], F32)
            P.dma("sp", lam[:], self.s5_lam.ap()[l, d].rearrange("i p q -> p i q"), reads=[self.s5_lam], writes=[lam])
            P.dma("pool", Bw[:], self.s5_b.ap()[l, d].rearrange("r p q v n -> p r q v n"), reads=[self.s5_b], writes=[Bw])
            P.dma("pool", Cw[:], self.s5_c.ap()[l, d].rearrange("r p q n -> p r q n"), reads=[self.s5_c], writes=[Cw])
            self.ts("dve", Cn[:], Cw[:], -1.0, ALU.mult, [Cw], [Cn])
            P.op("dve", lambda e: e.memset(xst[:], 0.0), writes=[xst])
            S = lambda i: sc[:, i, :]
            rd = [sc]
            ar, aim, dt, r1, th, k_, c1, s1, lbr, lbi, fr, fi = (S(i) for i in range(12))
            u0, u1, u2, u3 = S(12), S(13), S(14), S(15)
            self.ts("dve", ar, lam[:, 0, :], -1e-4, ALU.min, [lam], [sc])
            self.copy("dve", aim, lam[:, 1, :], [lam], [sc])
            self.act(dt, lam[:, 2, :], AF.Exp, [lam], [sc])
            self.tt("dve", u0, ar, dt, ALU.mult, rd, rd)
            self.act(r1, u0, AF.Exp, rd, rd)
            self.tt("dve", th, aim, dt, ALU.mult, rd, rd)
            self.ts("dve", u0, th, 1.0 / TWO_PI, ALU.mult, rd, rd)
            self.ts("dve", u1, u0, 12582912.0, ALU.add, rd, rd)
            self.ts("dve", k_, u1, -12582912.0, ALU.add, rd, rd)
            self.stt(u2, k_, -TWO_PI, th, ALU.mult, ALU.add, rd, rd)
            self.ts("dve", u2, u2, math.pi, ALU.min, rd, rd, s2=-math.pi, op1=ALU.max)
            self.act(s1, u2, AF.Sin, rd, rd)
            self.act(u3, u2, AF.Abs, rd, rd)
            self.ts("dve", u3, u3, -1.0, ALU.mult, rd, rd, s2=math.pi / 2, op1=ALU.add)
            self.act(c1, u3, AF.Sin, rd, rd)
            self.tt("dve", lbr, r1, c1, ALU.mult, rd, rd)
            self.tt("dve", lbi, r1, s1, ALU.mult, rd, rd)
            x_, y_ = S(16), S(17)
            self.ts("dve", x_, lbr, -1.0, ALU.add, rd, rd)
            self.copy("dve", y_, lbi, rd, rd)
            a_, b_ = ar, aim
            n1, n2, den = S(18), S(19), S(20)
            self.tt("dve", n1, x_, a_, ALU.mult, rd, rd)
            self.tt("dve", u0, y_, b_, ALU.mult, rd, rd)
            self.tt("dve", n1, n1, u0, ALU.add, rd, rd)
            self.tt("dve", n2, y_, a_, ALU.mult, rd, rd)
            self.tt("dve", u0, x_, b_, ALU.mult, rd, rd)
            self.tt("dve", n2, n2, u0, ALU.subtract, rd, rd)
            self.tt("dve", den, a_, a_, ALU.mult, rd, rd)
            self.tt("dve", u0, b_, b_, ALU.mult, rd, rd)
            self.tt("dve", den, den, u0, ALU.add, rd, rd)
            P.op("dve", lambda e, den=den: e.reciprocal(out=den, in_=den), reads=rd, writes=rd)
            self.tt("dve", fr, n1, den, ALU.mult, rd, rd)
            self.tt("dve", fi, n2, den, ALU.mult, rd, rd)
            self.copy("dve", Er[:, :, 0], c1, rd, [Er])
            self.copy("dve", Ei[:, :, 0], s1, rd, [Ei])
            n = 1
            while n < SL:
                pr_ = bc(Er[:, :, n - 1:n], [128, 32, n])
                pi_ = bc(Ei[:, :, n - 1:n], [128, 32, n])
                sr, si = Er[:, :, 0:n], Ei[:, :, 0:n]
                dr, di = Er[:, :, n:2 * n], Ei[:, :, n:2 * n]
                tm = tmpE[:, :, 0:n]
                self.tt("dve", dr, sr, pr_, ALU.mult, [Er], [Er])
                self.tt("dve", tm, si, pi_, ALU.mult, [Ei], [tmpE])
                self.tt("dve", dr, dr, tm, ALU.subtract, [Er, tmpE], [Er])
                self.tt("dve", di, sr, pi_, ALU.mult, [Er, Ei], [Ei])
                self.tt("dve", tm, si, pr_, ALU.mult, [Er, Ei], [tmpE])
                self.tt("dve", di, di, tm, ALU.add, [Ei, tmpE], [Ei])
                n *= 2
            frb = bc(fr.unsqueeze(2), [128, 32, SL])
            fib = bc(fi.unsqueeze(2), [128, 32, SL])
            for hf in range(2):
                js = slice(hf * (SL // 2), (hf + 1) * (SL // 2))
                frh = bc(fr.unsqueeze(2), [128, 32, SL // 2])
                fih = bc(fi.unsqueeze(2), [128, 32, SL // 2])
                self.tt("dve", Tr[:, :, js], Er[:, :, js], frh, ALU.mult, [Er, sc], [Tr])
                self.tt("dve", tmpE[:], Ei[:, :, js], fih, ALU.mult, [Ei, sc], [tmpE])
                self.tt("dve", Tr[:, :, js], Tr[:, :, js], tmpE[:], ALU.add, [Tr, tmpE], [Tr])
                self.tt("dve", Ti[:, :, js], Er[:, :, js], fih, ALU.mult, [Er, sc], [Ti])
                self.tt("dve", tmpE[:], Ei[:, :, js], frh, ALU.mult, [Ei, sc], [tmpE])
                self.tt("dve", Ti[:, :, js], Ti[:, :, js], tmpE[:], ALU.subtract, [Ti, tmpE], [Ti])

            ut = [sb("s_ut%d" % i, [128, 256], BF16) for i in range(2)]
            ut32 = [sb("s_ut32_%d" % i, [128, 256], F32) for i in range(2)]
            NB = 4
            mt = [[sb("s_m%d_%d" % (k, i), [128, 256], F32) for k in range(4)] for i in range(2)]
            bre = [sb("s_bre%d" % i, [128, 256], F32) for i in range(NB)]
            bim = [sb("s_bim%d" % i, [128, 256], F32) for i in range(NB)]
            wre = [sb("s_wre%d" % i, [128, 256], F32) for i in range(NB)]
            wim = [sb("s_wim%d" % i, [128, 256], F32) for i in range(NB)]
            pr4 = [[sb("s_p%d_%d" % (k, i), [128, 256], BF16) for k in range(4)] for i in range(NB)]
            tsm = [sb("s_ts%d" % i, [128, 2], F32) for i in range(4)]
            yev = [sb("s_yev%d" % i, [128, 2, 128], F32) for i in range(2)]
            first = True
            src = self.uT if d == 0 else self.uTr
            s5groups = [(g0 + o, 256) for (g0, gn) in self.groups for o in range(0, gn, 256)]
            for (i0, ng) in s5groups:
                nchk = ng // SL
                ntt = ng // 128
                v3 = lambda ap, o=0: ap[:, o:o + ng].rearrange("p (c j) -> p c j", j=SL)
                for q in range(8):
                    U_ = self.rot("s_ut", ut)
                    if d == 0:
                        U32 = self.rot("s_ut32", ut32)
                        P.dma("sp", U32[:, 0:ng], src.ap()[q * 128:(q + 1) * 128, i0:i0 + ng], reads=[src], writes=[U32])
                        self.copy("act", U_[:, 0:ng], U32[:, 0:ng], [U32], [U_])
                    else:
                        P.dma("sp", U_[:, 0:ng], src.ap()[q * 128:(q + 1) * 128, i0:i0 + ng], reads=[src], writes=[U_])
                    pY = self.ps[7]
                    yo = 256 * (q % 2)
                    Y_ = self.rot("s_yev", yev)
                    for hp in range(2):
                        prs = []
                        for k2 in range(2):
                            pp = 2 * hp + k2
                            p = 4 * q + pp
                            ii = self.rr.get("s_i", 0)
                            self.rr["s_i"] = ii + 1
                            pB = self.ps[5 + k2]
                            pR = pB
                            pI = pB
                            rows = slice(32 * pp, 32 * pp + 32) if pp < 3 else slice(64, 128)
                            vv = 0 if pp < 3 else 1
                            self.mm(pB[:, 0:ng], Bw[rows, 0, q, vv, :], U_[rows, 0:ng], True, True, [Bw, U_], [pB])
                            self.mm(pB[:, 256:256 + ng], Bw[rows, 1, q, vv, :], U_[rows, 0:ng], True, True, [Bw, U_], [pB])
                            prs.append(dict(pp=pp, p=p, pR=pR, pI=pI, m=mt[k2], BR=bre[ii % NB], BI=bim[ii % NB], WR=wre[ii % NB], WI=wim[ii % NB],
                                            PR=pr4[ii % NB], T=tsm[ii % 4]))
                        for z in prs:
                            p = z["p"]
                            Trb = bc(Tr[:, p, :].unsqueeze(1), [128, nchk, SL])
                            Tib = bc(Ti[:, p, :].unsqueeze(1), [128, nchk, SL])
                            m1, m2, m3, m4 = z["m"]
                            self.tt("dve", v3(m1), v3(z["pR"]), Trb, ALU.mult, [z["pR"], Tr], [m1])
                            self.tt("dve", v3(m2), v3(z["pI"], 256), Tib, ALU.mult, [z["pI"], Ti], [m2])
                            self.tt("dve", v3(m3), v3(z["pI"], 256), Trb, ALU.mult, [z["pI"], Tr], [m3])
                            self.tt("dve", v3(m4), v3(z["pR"]), Tib, ALU.mult, [z["pR"], Ti], [m4])
                            self.tt("pool", z["BR"][:, 0:ng], m1[:, 0:ng], m2[:, 0:ng], ALU.subtract, [m1, m2], [z["BR"]])
                            self.tt("pool", z["BI"][:, 0:ng], m3[:, 0:ng], m4[:, 0:ng], ALU.add, [m3, m4], [z["BI"]])
                        for c in range(nchk):
                            a0 = c * SL
                            la = a0 + SL - 1
                            steps = []
                            for z in prs:
                                p, BR, BI, WR, WI, T_ = z["p"], z["BR"], z["BI"], z["WR"], z["WI"], z["T"]
                                r1p = r1[:, p:p + 1]
                                ErL, EiL = Er[:, p, SL - 1:SL], Ei[:, p, SL - 1:SL]
                                st_ = []
                                if not (first and c == 0):
                                    st_.append(lambda BR=BR, p=p, r1p=r1p: self.stt(BR[:, a0:a0 + 1], xst[:, p, 0:1], r1p, BR[:, a0:a0 + 1], ALU.mult, ALU.add, [xst, sc, BR], [BR]))
                                    st_.append(lambda BI=BI, p=p, r1p=r1p: self.stt(BI[:, a0:a0 + 1], xst[:, p, 1:2], r1p, BI[:, a0:a0 + 1], ALU.mult, ALU.add, [xst, sc, BI], [BI]))
                                else:
                                    st_.append(None)
                                    st_.append(None)
                                st_.append(lambda WR=WR, BR=BR, o_=WR[:, a0:a0 + SL], d0=bc(r1p, [128, SL]), d1=BR[:, a0:a0 + SL]: P.op("dve", lambda e: e.tensor_tensor_scan(out=o_, data0=d0, data1=d1, initial=0.0, op0=ALU.mult, op1=ALU.add), reads=[BR, sc], writes=[WR]))
                                st_.append(lambda WI=WI, BI=BI, o_=WI[:, a0:a0 + SL], d0=bc(r1p, [128, SL]), d1=BI[:, a0:a0 + SL]: P.op("dve", lambda e: e.tensor_tensor_scan(out=o_, data0=d0, data1=d1, initial=0.0, op0=ALU.mult, op1=ALU.add), reads=[BI, sc], writes=[WI]))
                                st_.append(lambda WI=WI, T_=T_, EiL=EiL: self.ts("dve", T_[:, 0:1], WI[:, la:la + 1], EiL, ALU.mult, [WI, Ei], [T_]))
                                st_.append(lambda WR=WR, T_=T_, EiL=EiL: self.ts("dve", T_[:, 1:2], WR[:, la:la + 1], EiL, ALU.mult, [WR, Ei], [T_]))
                                st_.append(lambda WR=WR, T_=T_, ErL=ErL, p=p: self.stt(xst[:, p, 0:1], WR[:, la:la + 1], ErL, T_[:, 0:1], ALU.mult, ALU.subtract, [WR, Er, T_], [xst]))
                                st_.append(lambda WI=WI, T_=T_, ErL=ErL, p=p: self.stt(xst[:, p, 1:2], WI[:, la:la + 1], ErL, T_[:, 1:2], ALU.mult, ALU.add, [WI, Er, T_], [xst]))
                                steps.append(st_)
                            for k in range(len(steps[0])):
                                for st_ in steps:
                                    if st_[k] is not None:
                                        st_[k]()
                        attn_step()
                        for z in prs:
                            p, pp, PR, WR, WI = z["p"], z["pp"], z["PR"], z["WR"], z["WI"]
                            Erb = bc(Er[:, p, :].unsqueeze(1), [128, nchk, SL])
                            Eib = bc(Ei[:, p, :].unsqueeze(1), [128, nchk, SL])
                            self.tt("pool", v3(PR[0]), v3(WR), Erb, ALU.mult, [WR, Er], [PR[0]])
                            self.tt("pool", v3(PR[1]), v3(WI), Eib, ALU.mult, [WI, Ei], [PR[1]])
                            self.tt("pool", v3(PR[2]), v3(WR), Eib, ALU.mult, [WR, Ei], [PR[2]])
                            self.tt("pool", v3(PR[3]), v3(WI), Erb, ALU.mult, [WI, Er], [PR[3]])
                            for t in range(ntt):
                                o_ = pY[:, yo + t * 128 + pp * 32:yo + t * 128 + pp * 32 + 32]
                                tsl = slice(t * 128, (t + 1) * 128)
                                self.mm(o_, PR[0][:, tsl], Cw[:, 0, p, :], True, False, [PR[0], Cw], [pY])
                                self.mm(o_, PR[1][:, tsl], Cn[:, 0, p, :], False, False, [PR[1], Cn], [pY])
                                self.mm(o_, PR[2][:, tsl], Cn[:, 1, p, :], False, False, [PR[2], Cn], [pY])
                                self.mm(o_, PR[3][:, tsl], Cn[:, 1, p, :], False, True, [PR[3], Cn], [pY])
                    self.copy("act", Y_[:, 0:ntt, :], pY[:, yo:yo + ng].rearrange("p (t c) -> p t c", c=128), [pY], [Y_])
                    P.dma("act", self.ytok.ap()[d, i0:i0 + ng, q * 128:(q + 1) * 128].rearrange("(t p) c -> p t c", p=128), Y_[:, 0:ntt, :], reads=[Y_], writes=[self.ytok])
                first = False
      while attn_live[0]:
          attn_step()
    with P.scope():
        wg = P.sbuf("g_wg", [128, 8, 1024], BF16)
        dsk = P.sbuf("g_dsk", [128, 8], F32)
        bgl = P.sbuf("g_bgl", [128, 8], F32)
        P.dma("pool", wg[:], self.w_glu.ap()[l].rearrange("(kc p) n -> p kc n", p=128), reads=[self.w_glu], writes=[wg])
        P.dma("sp", dsk[:], self.s5_d.ap()[l].rearrange("(c p) -> p c", p=128), reads=[self.s5_d], writes=[dsk], allow_slow_non_contiguous=True)
        P.dma("sp", bgl[:], self.b_glu.ap()[l].rearrange("(c p) -> p c", p=128), reads=[self.b_glu], writes=[bgl], allow_slow_non_contiguous=True)
        y0 = [P.sbuf("g_y0_%d" % i, [128, 1024], F32) for i in range(2)]
        y1 = [P.sbuf("g_y1_%d" % i, [128, 1024], F32) for i in range(2)]
        uTt = P.sbuf("g_uT", [128, 8, 512], F32)
        yT = P.sbuf("g_yT", [128, 8, 512], F32)
        t_a = P.sbuf("g_ta", [128, 8, 512], F32)
        gb = P.sbuf("g_gb", [128, 8, 512], BF16)
        sg = [P.sbuf("g_sg%d" % i, [128, 512], F32) for i in range(2)]
        cT = P.sbuf("g_cT", [128, 8, 512], BF16)
        for (s0, ng) in self.groups:
            P.dma("sp", uTt[:, :, 0:ng], self.uT.ap().rearrange("(c p) t -> p c t", p=128)[:, :, s0:s0 + ng], reads=[self.uT], writes=[uTt])
            for t in range(ng // 128):
                r0 = s0 + t * 128
                a0, a1 = self.rot("g_y0", y0), self.rot("g_y1", y1)
                P.dma("sp", a0[:], self.ytok.ap()[0, r0:r0 + 128, :], reads=[self.ytok], writes=[a0])
                m0 = self.mirror(r0)
                P.dma("sp", a1[:], self.ytok.ap()[1, m0:m0 + 128, :], reads=[self.ytok], writes=[a1])
                for half in range(2):
                    ps = self.rot("g_ps", self.ps[0:4])
                    for qq in range(4):
                        q = half * 4 + qq
                        o_ = ps[:, qq * 128:(qq + 1) * 128]
                        self.mm(o_, a0[:, q * 128:(q + 1) * 128], self.ident_f, True, False, [a0, self.cst_t], [ps])
                        self.mm(o_, a1[:, q * 128:(q + 1) * 128], self.J_f, False, True, [a1, self.cst_t], [ps])
                    for qq in range(4):
                        q = half * 4 + qq
                        self.stt(yT[:, q, t * 128:(t + 1) * 128], uTt[:, q, t * 128:(t + 1) * 128], dsk[:, q:q + 1], ps[:, qq * 128:(qq + 1) * 128], ALU.mult, ALU.add, [uTt, dsk, ps], [], pw=[yT])
            Y = yT[:, :, 0:ng]
            A = t_a[:, :, 0:ng]
            self.act(A, Y, AF.Square, [yT], [t_a])
            self.ts("dve", A, A, 0.044715, ALU.mult, [t_a], [t_a], s2=1.0, op1=ALU.add)
            self.tt("pool", A, A, Y, ALU.mult, [t_a, yT], [t_a])
            self.act(A, A, AF.Sigmoid, [t_a], [t_a], scale=2.0 * math.sqrt(2.0 / math.pi))
            self.tt("dve", Y, Y, A, ALU.mult, [yT, t_a], [yT])
            self.copy("act", gb[:, :, 0:ng], Y, [yT], [gb])
            for jc in range(8):
                ps = self.rot("g_ps", self.ps[0:4])
                for kc in range(8):
                    self.mm(ps[:, 0:ng], wg[:, kc, jc * 128:(jc + 1) * 128], gb[:, kc, 0:ng], kc == 0, kc == 7, [wg, gb], [ps])
                s_ = self.rot("g_sg", sg)
                self.act(s_[:, 0:ng], ps[:, 0:ng], AF.Sigmoid, [ps, bgl], [s_], bias=bgl[:, jc:jc + 1])
                self.tt("dve", cT[:, jc, 0:ng], yT[:, jc, 0:ng], s_[:, 0:ng], ALU.mult, [yT, s_], [], pw=[cT])
            P.dma("act", self.brT.ap()[2].rearrange("(c p) t -> p c t", p=128)[:, :, s0:s0 + ng], cT[:, :, 0:ng], reads=[cT], writes=[self.brT])


Builder.phase_C3 = phase_C3


def phase_DE(self, l):
    P, TC, TL, TT = self.P, self.TC, self.TL, self.TT
    last = (l == self.depth - 1)
    with P.scope():
        X = P.sbuf("e_X", [128, KC, 512], F32)
        M = P.sbuf("e_M", [128, KC, 512], BF16)
        H = P.sbuf("e_H", [128, 32, 512], BF16)
        wt = [P.sbuf("e_w%d" % i, [128, KC, 512], BF16) for i in range(3)]
        gt = [P.sbuf("e_g%d" % i, [128, 3, 512], BF16) for i in range(2)]
        ma = [P.sbuf("e_ma%d" % i, [128, 512], F32) for i in range(2)]
        mb = [P.sbuf("e_mb%d" % i, [128, 512], F32) for i in range(2)]
        sqb = [P.sbuf("e_sq%d" % i, [128, 512], BF16) for i in range(3)]
        tmp = [P.sbuf("e_tmp%d" % i, [128, 512], F32) for i in range(3)]
        rbc = P.sbuf("e_rbc", [128, 512], F32)
        rl = [P.sbuf("e_rl%d" % i, [128, 512], F32) for i in range(3)]
        yt = [P.sbuf("e_yt%d" % i, [128, D], F32) for i in range(2)] if last else None
        wbr = self.w_branch.ap()[l]
        wov = self.w_out.ap()[l].rearrange("(kc p) n -> p kc n", p=128)
        w1v = self.w_ff1.ap()[l].rearrange("(kc p) n -> p kc n", p=128)
        w2v = self.w_ff2.ap()[l].rearrange("(f p) n -> p f n", p=128)
        gate1, gate2 = self.modT[:, 2], self.modT[:, 5]
        xin = self.xT[l]
        groups = [g for g in self.groups if not (last and g[0] < TC)]
        for (s0, ng) in groups:
            col = 1 if s0 < TC else 0
            P.dma("sp", X[:, :, 0:ng], xin.ap().rearrange("(c p) t -> p c t", p=128)[:, :, s0:s0 + ng], reads=[xin], writes=[X])
            for r in range(3):
                P.dma("sp", H[:, r * 8:(r + 1) * 8, 0:ng], self.brT.ap()[r].rearrange("(c p) t -> p c t", p=128)[:, :, s0:s0 + ng], reads=[self.brT], writes=[H])
            for jb in range(4):
                ws = []
                for r in range(3):
                    w = self.rot("e_w", wt)
                    P.dma("pool", w[:, 0:8, :], wbr[r].rearrange("(kc p) n -> p kc n", p=128)[:, :, jb * 512:(jb + 1) * 512], reads=[self.w_branch], writes=[w])
                    ws.append(w)
                for jj in range(4):
                    j = jb * 4 + jj
                    G_ = self.rot("e_g", gt)
                    P.dma("sp", G_[:, :, 0:ng], self.gT.ap().rearrange("(r c p) t -> p r c t", r=3, p=128)[:, :, j, s0:s0 + ng], reads=[self.gT], writes=[G_])
                    A_, B_ = self.rot("e_ma", ma), self.rot("e_mb", mb)
                    for r in range(3):
                        ps = self.rot("e_ps", self.ps[0:6])
                        for kc in range(8):
                            self.mm(ps[:, 0:ng], ws[r][:, kc, jj * 128:(jj + 1) * 128], H[:, r * 8 + kc, 0:ng], kc == 0, kc == 7, [ws[r], H], [ps])
                        if r == 0:
                            self.tt("dve", A_[:, 0:ng], ps[:, 0:ng], G_[:, 0, 0:ng], ALU.mult, [ps, G_], [A_])
                        elif r == 1:
                            self.tt("dve", B_[:, 0:ng], ps[:, 0:ng], G_[:, 1, 0:ng], ALU.mult, [ps, G_], [B_])
                            self.tt("dve", A_[:, 0:ng], A_[:, 0:ng], B_[:, 0:ng], ALU.add, [A_, B_], [A_])
                        else:
                            self.tt("dve", B_[:, 0:ng], ps[:, 0:ng], G_[:, 2, 0:ng], ALU.mult, [ps, G_, A_], [B_])
                            self.tt("dve", M[:, j, 0:ng], A_[:, 0:ng], B_[:, 0:ng], ALU.add, [A_, B_], [], pw=[M])
            for jb in range(4):
                w = self.rot("e_w", wt)
                P.dma("pool", w[:], wov[:, :, jb * 512:(jb + 1) * 512], reads=[self.w_out], writes=[w])
                for jj in range(4):
                    j = jb * 4 + jj
                    ps = self.rot("e_ps", self.ps[0:6])
                    for kc in range(KC):
                        self.mm(ps[:, 0:ng], w[:, kc, jj * 128:(jj + 1) * 128], M[:, kc, 0:ng], kc == 0, kc == KC - 1, [w, M], [ps])
                    self.stt(X[:, j, 0:ng], ps[:, 0:ng], gate1[:, j, col:col + 1], X[:, j, 0:ng], ALU.mult, ALU.add, [ps, self.modT, X], [X])
            self.norm_group(X, M, ng, 1, col, sqb, rbc, tmp, self.ps[7])
            for half in range(2):
                for fb in range(8):
                    w = self.rot("e_w", wt)
                    c0 = half * 4096 + fb * 512
                    P.dma("pool", w[:], w1v[:, :, c0:c0 + 512], reads=[self.w_ff1], writes=[w])
                    for jj in range(4):
                        f = fb * 4 + jj
                        ps = self.rot("e_ps", self.ps[0:6])
                        for kc in range(KC):
                            self.mm(ps[:, 0:ng], w[:, kc, jj * 128:(jj + 1) * 128], M[:, kc, 0:ng], kc == 0, kc == KC - 1, [w, M], [ps])
                        R_ = self.rot("e_rl", rl)
                        self.act(R_[:, 0:ng], ps[:, 0:ng], AF.Relu, [ps], [R_])
                        self.tt("dve", H[:, f, 0:ng], R_[:, 0:ng], R_[:, 0:ng], ALU.mult, [R_], [], pw=[H])
                for jb in range(4):
                    wa = self.rot("e_w", wt)
                    wb_ = self.rot("e_w", wt)
                    f0 = half * 32
                    P.dma("pool", wa[:], w2v[:, f0:f0 + 16, jb * 512:(jb + 1) * 512], reads=[self.w_ff2], writes=[wa])
                    P.dma("pool", wb_[:], w2v[:, f0 + 16:f0 + 32, jb * 512:(jb + 1) * 512], reads=[self.w_ff2], writes=[wb_])
                    for jj in range(4):
                        j = jb * 4 + jj
                        ps = self.rot("e_ps", self.ps[0:6])
                        for f in range(32):
                            ww = wa if f < 16 else wb_
                            self.mm(ps[:, 0:ng], ww[:, f % 16, jj * 128:(jj + 1) * 128], H[:, f, 0:ng], f == 0, f == 31, [ww, H], [ps])
                        self.stt(X[:, j, 0:ng], ps[:, 0:ng], gate2[:, j, col:col + 1], X[:, j, 0:ng], ALU.mult, ALU.add, [ps, self.modT, X], [X])
            if not last:
                P.dma("act", self.xT[l + 1].ap().rearrange("(c p) t -> p c t", p=128)[:, :, s0:s0 + ng], X[:, :, 0:ng], reads=[X], writes=[self.xT[l + 1]])
            else:
                for t in range(ng // 128):
                    Y_ = self.rot("e_yt", yt)
                    for b4 in range(4):
                        ps = self.rot("e_ps", self.ps[0:6])
                        for jj in range(4):
                            c = b4 * 4 + jj
                            self.tr(ps[:, jj * 128:(jj + 1) * 128], X[:, c, t * 128:(t + 1) * 128], self.ident_f, [X, self.cst_t], [ps])
                        self.copy("act" if b4 % 2 == 0 else "dve", Y_[:, b4 * 512:(b4 + 1) * 512], ps[:, 0:512], [ps], [], pw=[Y_])
                    r0 = s0 - TC + t * 128
                    P.dma("act", self.y.ap()[r0:r0 + 128, :], Y_[:], reads=[Y_], writes=[self.y])


Builder.phase_DE = phase_DE


_CACHE = {}


def kernel(**inputs):
    TL, TC = 4096, 256
    if "prog" not in _CACHE:
        _CACHE["prog"] = build_program(TL, TC, depth=DEPTH)
    B = _CACHE["prog"]
    sh = prep_shared(inputs, DEPTH)
    cst, rope = make_consts(TL, TC)
    maps = []
    for core in range(8):
        b = core % 4
        m = dict(sh)
        m.update(prep_core(inputs, b, TL, TC))
        m["cst"] = cst
        m["rope"] = rope
        maps.append(m)
    res = run_bass_kernel_spmd(B.nc, maps, core_ids=list(range(8)))
    out = np.stack([np.asarray(res.results[b]["y"], dtype=np.float32) for b in range(4)], axis=0)
    return out
```

```python
import contextlib
import math
import numpy as np
import concourse.bass as bass
import concourse.mybir as mybir
from concourse.bass_utils import run_bass_kernel_spmd

F32 = mybir.dt.float32
BF16 = mybir.dt.bfloat16
AF = mybir.ActivationFunctionType
ALU = mybir.AluOpType
AX = mybir.AxisListType

ENGS = ("pe", "act", "dve", "pool", "sp")
EPOCH = 30000

D = 2048
KC = 16
DEPTH = 2
DIN = 11344
OFF_Q, OFF_K, OFF_V, OFF_O, OFF_G, OFF_QA, OFF_KVA, OFF_KPE, OFF_U, OFF_BG = 0, 512, 1024, 2048, 3072, 3088, 3600, 4112, 4176, 5200
EPS = 1e-6
SL = 128


class T:
    def __init__(self, h, name=""):
        self.h = h
        self.name = name
        self.w = {}
        self.r = {}

    def __getitem__(self, k):
        return self.h[k]

    def ap(self):
        return self.h.ap()


class Prog:
    def __init__(self, nc, n_dma_sems=8):
        self.nc = nc
        self.es = contextlib.ExitStack()
        self.q = {e: [] for e in ENGS}
        self.tick = {e: 0 for e in ENGS}
        self.epoch = {e: 0 for e in ENGS}
        self.sems = {}
        self.seen = {e: {} for e in ENGS}
        self.dma_slots = {}
        self.n_dma_sems = n_dma_sems
        self.dma_i = {e: 0 for e in ENGS}
        self.nsem = 0
        self.ninstr = 0
        self.scopes = []

    def sem(self, key):
        if key not in self.sems:
            self.sems[key] = self.es.enter_context(self.nc.semaphore("s%d" % self.nsem))
            self.nsem += 1
        return self.sems[key]

    def _stack(self):
        return self.scopes[-1] if self.scopes else self.es

    def sbuf(self, name, shape, dtype):
        self.uid = getattr(self, "uid", 0) + 1
        name = "%s_u%d" % (name, self.uid)
        return T(self._stack().enter_context(self.nc.sbuf_tensor(name, list(shape), dtype)), name)

    def psum(self, name, shape, dtype=F32):
        return T(self.es.enter_context(self.nc.psum_tensor(name, list(shape), dtype)), name)

    def dram(self, name, shape, dtype, kind=None):
        if kind is None:
            h = self.nc.dram_tensor(name, list(shape), dtype)
        else:
            h = self.nc.dram_tensor(name, list(shape), dtype, kind=kind)
        return T(h, name)

    @contextlib.contextmanager
    def scope(self):
        st = contextlib.ExitStack()
        self.scopes.append(st)
        try:
            yield
        finally:
            self.barrier()
            self.flush()
            self.scopes.pop()
            st.close()

    def _waits(self, e, reads, writes, pw=()):
        deps = {}
        for b in reads:
            for k, v in b.w.items():
                if deps.get(k, 0) < v:
                    deps[k] = v
        for b in writes:
            for d in (b.w, b.r):
                for k, v in d.items():
                    if deps.get(k, 0) < v:
                        deps[k] = v
        for b in pw:
            for k, v in b.r.items():
                if deps.get(k, 0) < v:
                    deps[k] = v
        out = []
        for k, v in deps.items():
            if k[0] == e and k[1] == "p" and e == "pe":
                continue
            if self.seen[e].get(k, 0) >= v:
                continue
            self.seen[e][k] = v
            out.append((k, v))
        return out

    def op(self, e, fn, reads=(), writes=(), pw=()):
        waits = self._waits(e, reads, writes, pw)
        if self.tick[e] >= EPOCH:
            self.epoch[e] += 1
            self.tick[e] = 0
        key = (e, "p", self.epoch[e])
        self.tick[e] += 1
        val = self.tick[e]
        s = self.sem(key)
        wl = [(self.sem(k), v) for k, v in waits]
        self.q[e].append((wl, fn, s, 1))
        for b in reads:
            b.r[key] = val
        for b in writes:
            b.w[key] = val
        for b in pw:
            b.w[key] = val
        self.ninstr += 1

    def dma(self, e, out_ap, in_ap, reads=(), writes=(), **kw):
        i = self.dma_i[e]
        self.dma_i[e] += 1
        slot = (e, "d", i % self.n_dma_sems)
        cnt = self.dma_slots.get(slot, 0)
        waits = self._waits(e, reads, writes)
        if cnt > 0 and self.seen[e].get(slot, 0) < cnt:
            self.seen[e][slot] = cnt
            waits.append((slot, cnt))
        cnt += 16
        self.dma_slots[slot] = cnt
        s = self.sem(slot)
        wl = [(self.sem(k), v) for k, v in waits]

        def fn(eng, out_ap=out_ap, in_ap=in_ap, kw=kw):
            return eng.dma_start(out=out_ap, in_=in_ap, **kw)

        self.q[e].append((wl, fn, s, 16))
        for b in reads:
            b.r[slot] = cnt
        for b in writes:
            b.w[slot] = cnt
        self.ninstr += 1

    def barrier(self):
        targets = {}
        for e in ENGS:
            if self.tick[e] > 0:
                targets[(e, "p", self.epoch[e])] = self.tick[e]
        for slot, cnt in self.dma_slots.items():
            targets[slot] = cnt
        for e in ENGS:
            wl = []
            for k, v in targets.items():
                if k[0] == e and k[1] == "p" and e == "pe":
                    continue
                if self.seen[e].get(k, 0) >= v:
                    continue
                self.seen[e][k] = v
                wl.append((self.sem(k), v))
            if wl:
                self.q[e].append((wl, None, None, 0))

    def flush(self):
        nc = self.nc
        q = self.q
        if not any(q[e] for e in ENGS):
            return
        with nc.Block() as block:

            def run(eng, lst):
                for wl, fn, s, inc in lst:
                    for ws, v in wl:
                        eng.wait_ge(ws, v)
                    if fn is not None:
                        fn(eng).then_inc(s, inc)

            @block.tensor
            def _(eng):
                run(eng, q["pe"])

            @block.scalar
            def _(eng):
                run(eng, q["act"])

            @block.vector
            def _(eng):
                run(eng, q["dve"])

            @block.gpsimd
            def _(eng):
                run(eng, q["pool"])

            @block.sync
            def _(eng):
                run(eng, q["sp"])

        self.q = {e: [] for e in ENGS}

    def finish(self):
        self.barrier()
        self.flush()
        self.es.close()


def bc(ap, shape):
    return ap.to_broadcast(list(shape))


class Builder:
    def __init__(self, TL, TC, depth=DEPTH, debug=(), stop_after=None):
        self.TL, self.TC, self.TT = TL, TC, TL + TC
        self.depth = depth
        self.debug = set(debug)
        self.stop_after = stop_after
        assert TC % 128 == 0 and TC <= 512 and TL % 512 == 0
        self.groups = [(0, TC)] + [(TC + 512 * i, 512) for i in range(TL // 512)]
        self.NT = self.TT // 128
        self.nc = bass.Bass("TRN2", target_bir_lowering=False)
        self.P = Prog(self.nc)
        self.rr = {}

    def ext_in(self, name, shape, dt=F32):
        return self.P.dram(name, shape, dt, kind="ExternalInput")

    def scratch(self, name, shape, dt):
        kind = "ExternalOutput" if name in self.debug else None
        return self.P.dram(name, shape, dt, kind=kind)

    def rot(self, key, lst):
        i = self.rr.get(key, 0)
        self.rr[key] = i + 1
        return lst[i % len(lst)]

    def act(self, out, in_, func, reads, writes, pw=(), **kw):
        self.P.op("act", lambda e: e.activation(out=out, in_=in_, func=func, **kw), reads=reads, writes=writes, pw=pw)

    def tt(self, eng, out, in0, in1, op, reads, writes, pw=()):
        self.P.op(eng, lambda e: e.tensor_tensor(out=out, in0=in0, in1=in1, op=op), reads=reads, writes=writes, pw=pw)

    def ts(self, eng, out, in0, s1, op0, reads, writes, s2=None, op1=None, pw=()):
        if op1 is None:
            self.P.op(eng, lambda e: e.tensor_scalar(out=out, in0=in0, scalar1=s1, scalar2=None, op0=op0), reads=reads, writes=writes, pw=pw)
        else:
            self.P.op(eng, lambda e: e.tensor_scalar(out=out, in0=in0, scalar1=s1, scalar2=s2, op0=op0, op1=op1), reads=reads, writes=writes, pw=pw)

    def stt(self, out, in0, scalar, in1, op0, op1, reads, writes, pw=()):
        self.P.op("dve", lambda e: e.scalar_tensor_tensor(out=out, in0=in0, scalar=scalar, in1=in1, op0=op0, op1=op1), reads=reads, writes=writes, pw=pw)

    def mm(self, out, lhsT, rhs, start, stop, reads, writes):
        self.P.op("pe", lambda e: e.matmul(out=out, lhsT=lhsT, rhs=rhs, start=start, stop=stop), reads=reads, writes=writes)

    def tr(self, out, in_, ident, reads, writes):
        self.P.op("pe", lambda e: e.transpose(out=out, in_=in_, identity=ident), reads=reads, writes=writes)

    def copy(self, eng, out, in_, reads, writes, pw=()):
        if eng == "act":
            self.act(out, in_, AF.Copy, reads, writes, pw=pw)
        else:
            self.P.op(eng, lambda e: e.tensor_copy(out=out, in_=in_), reads=reads, writes=writes, pw=pw)

    def rsqrt(self, out, in_, scale, reads, writes):
        self.act(out, in_, AF.Sqrt, reads, writes, scale=scale, bias=self.eps_t[0:out.shape[0], 0:1])
        self.P.op("dve", lambda e: e.reciprocal(out=out, in_=out), reads=writes, writes=writes)

    def declare(self):
        P, TT, TL, TC, L = self.P, self.TT, self.TL, self.TC, self.depth
        I = self.ext_in
        self.x_in = I("x", [TL, D])
        self.ctx_in = I("ctx", [TC, D])
        self.cvec = I("cvec", [2, D])
        self.w_mod = I("w_mod", [L, D, 6 * D])
        self.b_mod = I("b_mod", [L, 6 * D])
        self.norm_g = I("norm_g", [L, 2, D])
        self.w_in = I("w_in", [L, D, DIN])
        self.b_in = I("b_in", [L, DIN])
        self.ml_gate_b = I("ml_gate_b", [L, 16])
        self.ml_norm_g = I("ml_norm_g", [L, 1024])
        self.qa_g = I("mla_qa_g", [L, 512])
        self.kva_g = I("mla_kva_g", [L, 512])
        self.w_uq = I("mla_w_uq", [L, 512, 1536])
        self.w_ukv = I("mla_w_ukv", [L, 512, 2048])
        self.qn_g = I("mla_qn_g", [L, 192])
        self.kn_g = I("mla_kn_g", [L, 192])
        self.s5_lam = I("s5_lam", [L, 2, 3, 128, 32])
        self.s5_b = I("s5_b", [L, 2, 2, 128, 8, 2, 128])
        self.s5_c = I("s5_c", [L, 2, 2, 128, 32, 32])
        self.s5_d = I("s5_d", [L, 1024])
        self.w_glu = I("s5_w_glu", [L, 1024, 1024])
        self.b_glu = I("s5_b_glu", [L, 1024])
        self.w_branch = I("w_branch", [L, 3, 1024, D])
        self.w_out = I("w_out", [L, D, D])
        self.w_ff1 = I("w_ff1", [L, D, 4 * D])
        self.w_ff2 = I("w_ff2", [L, 4 * D, D])
        self.cst = I("cst", [128, 640])
        self.rope = I("rope", [TT, 128])
        self.y = P.dram("y", [TL, D], F32, kind="ExternalOutput")

        S = self.scratch
        self.xT = [S("xT0", [D, TT], F32), S("xT1", [D, TT], F32)]
        self.qT = S("qT", [4, 128, TT], BF16)
        self.kT = S("kT", [4, 128, TT], BF16)
        self.ktok = S("ktok", [TT, 512], BF16)
        self.vtok = S("vtok", [TT, 1024], BF16)
        self.otok = S("otok", [TT, 1024], BF16)
        self.gtok = S("gtok", [TT, 16], F32)
        self.kpe = S("kpe", [TT, 64], F32)
        self.qaT = S("qaT", [512, TT], BF16)
        self.kvaT = S("kvaT", [512, TT], BF16)
        self.rsq = S("rsq", [TT, 2], F32)
        self.uT = S("uT", [1024, TT], F32)
        self.uTr = S("uTr", [1024, TT], BF16)
        self.gT = S("gT", [6144, TT], BF16)
        self.hml = S("hml", [2, TT, 1024], F32)
        self.qhT = S("qhT", [8, 192, TT], BF16)
        self.khT = S("khT", [8, 192, TT], BF16)
        self.vh = S("vh", [TT, 1024], BF16)
        self.ytok = S("ytok", [2, TT, 1024], F32)
        self.brT = S("brT", [3, 1024, TT], BF16)
        self.wb_in = [P.dram("wb_in%d" % i, [D, DIN], BF16) for i in range(L)]
        self.wb_br = [P.dram("wb_br%d" % i, [3, 1024, D], BF16) for i in range(L)]
        self.wb_out = [P.dram("wb_out%d" % i, [D, D], BF16) for i in range(L)]
        self.wb_ff1 = [P.dram("wb_ff1%d" % i, [D, 4 * D], BF16) for i in range(L)]
        self.wb_ff2 = [P.dram("wb_ff2%d" % i, [4 * D, D], BF16) for i in range(L)]

        self.cst_t = P.sbuf("cst_t", [128, 640], F32)
        self.cstb = P.sbuf("cstb", [128, 640], BF16)
        self.eps_t = P.sbuf("eps_t", [128, 1], F32)
        self.modT = P.sbuf("modT", [128, 6, 16, 2], F32)
        self.A1 = P.sbuf("A1", [128, 2, 16, 2], F32)
        self.ps = [P.psum("ps%d" % i, [128, 512], F32) for i in range(8)]
        P.dma("sp", self.cst_t[:], self.cst[:], reads=[self.cst], writes=[self.cst_t])
        P.op("dve", lambda e: e.tensor_copy(out=self.cstb[:], in_=self.cst_t[:]), reads=[self.cst_t], writes=[self.cstb])
        P.op("dve", lambda e: e.memset(self.eps_t[:], EPS), writes=[self.eps_t])
        c = self.cst_t
        self.ident_f, self.J_f = c[:, 0:128], c[:, 128:256]
        self.triL_f, self.triU_f = c[0:64, 256:320], c[0:64, 320:384]
        self.ones_f = c[:, 384:512]
        cb = self.cstb
        self.ident_b, self.J_b, self.ones_b = cb[:, 0:128], cb[:, 128:256], cb[:, 384:512]

    def convert_weights(self, l, which):
        P = self.P
        if "in" in which:
            for r in range(KC):
                P.dma("pool", self.wb_in[l].ap()[r * 128:(r + 1) * 128, :], self.w_in.ap()[l, r * 128:(r + 1) * 128, :], reads=[self.w_in], writes=[self.wb_in[l]])
        if "de" in which:
            for b3 in range(3):
                for r in range(4):
                    P.dma("pool", self.wb_br[l].ap()[b3, r * 256:(r + 1) * 256, :], self.w_branch.ap()[l, b3, r * 256:(r + 1) * 256, :], reads=[self.w_branch], writes=[self.wb_br[l]])
            for r in range(8):
                P.dma("pool", self.wb_out[l].ap()[r * 256:(r + 1) * 256, :], self.w_out.ap()[l, r * 256:(r + 1) * 256, :], reads=[self.w_out], writes=[self.wb_out[l]])
            for r in range(KC):
                P.dma("pool", self.wb_ff1[l].ap()[r * 128:(r + 1) * 128, :], self.w_ff1.ap()[l, r * 128:(r + 1) * 128, :], reads=[self.w_ff1], writes=[self.wb_ff1[l]])
            for r in range(32):
                P.dma("pool", self.wb_ff2[l].ap()[r * 256:(r + 1) * 256, :], self.w_ff2.ap()[l, r * 256:(r + 1) * 256, :], reads=[self.w_ff2], writes=[self.wb_ff2[l]])

    def seq_rows(self, s0, n):
        if s0 < self.TC:
            return self.ctx_in, self.ctx_in[s0:s0 + n, :]
        return self.x_in, self.x_in[s0 - self.TC:s0 - self.TC + n, :]

    def phase_T0(self):
        P = self.P
        with P.scope():
            xt = [P.sbuf("t0x%d" % i, [128, D], F32) for i in range(2)]
            xo = [P.sbuf("t0o%d" % i, [128, KC, 128], F32) for i in range(2)]
            dst = self.xT[0].ap().rearrange("(c p) t -> p c t", p=128)
            for i in range(self.NT):
                src_t, src = self.seq_rows(i * 128, 128)
                a, o = xt[i % 2], xo[i % 2]
                P.dma("sp", a[:], src, reads=[src_t], writes=[a])
                for b4 in range(4):
                    ps = self.ps[(i * 4 + b4) % 8]
                    for j in range(4):
                        c = b4 * 4 + j
                        self.tr(ps[:, j * 128:(j + 1) * 128], a[:, c * 128:(c + 1) * 128], self.ident_f, [a, self.cst_t], [ps])
                    self.copy("act" if b4 % 2 == 0 else "dve", o[:, b4 * 4:(b4 + 1) * 4, :], ps[:].rearrange("p (j t) -> p j t", j=4), [ps], [], pw=[o])
                P.dma("act", dst[:, :, i * 128:(i + 1) * 128], o[:], reads=[o], writes=[self.xT[0]])

    def phase_A(self, l):
        P = self.P
        with P.scope():
            cv = P.sbuf("a_cv", [128, 2, KC], F32)
            scT = P.sbuf("a_sc", [128, KC, 2], BF16)
            bm = P.sbuf("a_bm", [128, 6, KC], F32)
            ng = P.sbuf("a_ng", [128, 2, KC], F32)
            wt = [P.sbuf("a_w%d" % i, [128, KC, 512], BF16) for i in range(3)]
            P.dma("sp", cv[:], self.cvec.ap().rearrange("r (c p) -> p r c", p=128), reads=[self.cvec], writes=[cv], allow_slow_non_contiguous=True)
            P.dma("sp", bm[:], self.b_mod.ap()[l].rearrange("(i c p) -> p i c", p=128, c=KC), reads=[self.b_mod], writes=[bm], allow_slow_non_contiguous=True)
            P.dma("sp", ng[:], self.norm_g.ap()[l].rearrange("i (c p) -> p i c", p=128), reads=[self.norm_g], writes=[ng], allow_slow_non_contiguous=True)
            self.act(scT[:].rearrange("p c r -> p r c"), cv[:], AF.Silu, [cv], [scT])
            wv = self.w_mod.ap()[l].rearrange("(kc p) n -> p kc n", p=128)
            for idx in range(6):
                for jj in range(4):
                    w = self.rot("a_w", wt)
                    c0 = idx * D + jj * 512
                    P.dma("pool", w[:], wv[:, :, c0:c0 + 512], reads=[self.w_mod], writes=[w])
                    ps = self.rot("a_ps", self.ps[0:4])
                    for j in range(4):
                        for kc in range(KC):
                            self.mm(ps[:, 2 * j:2 * j + 2], w[:, kc, j * 128:(j + 1) * 128], scT[:, kc, :], kc == 0, kc == KC - 1, [w, scT], [ps])
                    self.tt("dve", self.modT[:, idx, 4 * jj:4 * jj + 4, :], ps[:, 0:8].rearrange("p (j r) -> p j r", r=2),
                            bc(bm[:, idx, 4 * jj:4 * jj + 4].unsqueeze(2), [128, 4, 2]), ALU.add, [ps, bm], [], pw=[self.modT])
            for n, idx in ((0, 1), (1, 4)):
                self.ts("dve", self.A1[:, n], self.modT[:, idx], 1.0, ALU.add, [self.modT], [], pw=[self.A1])
                self.tt("dve", self.A1[:, n], self.A1[:, n], bc(ng[:, n, :].unsqueeze(2), [128, KC, 2]), ALU.mult, [self.A1, ng], [self.A1])

    def norm_group(self, xg, hT, ng, n, col, sqb, rbc, tmp, ps):
        P = self.P
        shift = self.modT[:, 0 if n == 0 else 3]
        for c in range(KC):
            s = sqb[c % len(sqb)]
            self.act(s[:, 0:ng], xg[:, c, 0:ng], AF.Square, [xg], [s])
            self.mm(ps[:, 0:ng], self.ones_b, s[:, 0:ng], c == 0, c == KC - 1, [s, self.cstb], [ps])
        self.act(rbc[:, 0:ng], ps[:, 0:ng], AF.Sqrt, [ps, self.eps_t], [rbc], scale=1.0 / D, bias=self.eps_t[:, 0:1])
        P.op("dve", lambda e: e.reciprocal(out=rbc[:, 0:ng], in_=rbc[:, 0:ng]), reads=[rbc], writes=[rbc])
        for c in range(KC):
            t = tmp[c % len(tmp)]
            self.tt("dve", t[:, 0:ng], xg[:, c, 0:ng], rbc[:, 0:ng], ALU.mult, [xg, rbc], [t])
            self.act(hT[:, c, 0:ng], t[:, 0:ng], AF.Identity, [t, self.A1, self.modT], [], pw=[hT],
                     scale=self.A1[:, n, c, col:col + 1], bias=shift[:, c, col:col + 1])

    def phase_B(self, l):
        P, TT = self.P, self.TT
        xTl = self.xT[l]
        with P.scope():
            xg = P.sbuf("b_xg", [128, KC, 512], F32)
            hT = P.sbuf("b_hT", [128, KC, 512], BF16)
            sqb = [P.sbuf("b_sq%d" % i, [128, 512], BF16) for i in range(3)]
            tmp = [P.sbuf("b_tmp%d" % i, [128, 512], F32) for i in range(3)]
            rbc = P.sbuf("b_rbc", [128, 512], F32)
            wt = [P.sbuf("b_w%d" % i, [128, KC, 512], BF16) for i in range(3)]
            wsm = P.sbuf("b_wsm", [128, KC, 80], BF16)
            ob = [P.sbuf("b_ob%d" % i, [128, 4, 512], BF16) for i in range(2)]
            of = [P.sbuf("b_of%d" % i, [128, 4, 512], F32) for i in range(2)]
            ot = [P.sbuf("b_ot%d" % i, [128, 512], BF16) for i in range(3)]
            otf = [P.sbuf("b_otf%d" % i, [128, 512], F32) for i in range(2)]
            sqs = [P.sbuf("b_sqs%d" % i, [128, 512], BF16) for i in range(2)]
            binT = P.sbuf("b_binT", [128, 89], F32)
            bq = P.sbuf("b_bq", [128, 4], F32)
            btok = P.sbuf("b_btok", [128, 3664], F32)
            gb = P.sbuf("b_gb", [128, 16], F32)
            rs = P.sbuf("b_rs", [128, 4, 2], F32)
            gl = P.sbuf("b_gl", [128, 80], F32)
            e1 = P.sbuf("b_e1", [128, 8], F32)
            ur = P.sbuf("b_ur", [128, 4, 8, 128], BF16)
            gfm = P.sbuf("b_gfm", [128, 8], F32)
            bgs = P.sbuf("b_bgs", [128, 8], F32)
            P.dma("sp", gfm[:, 0:4], self.qa_g.ap()[l].rearrange("(c p) -> p c", p=128), reads=[self.qa_g], writes=[gfm], allow_slow_non_contiguous=True)
            P.dma("sp", gfm[:, 4:8], self.kva_g.ap()[l].rearrange("(c p) -> p c", p=128), reads=[self.kva_g], writes=[gfm], allow_slow_non_contiguous=True)
            bi = self.b_in.ap()[l]
            fm = [("q", OFF_Q, 512), ("k", OFF_K, 512), ("qa", OFF_QA, 512), ("kva", OFF_KVA, 512), ("u", OFF_U, 1024), ("bg", OFF_BG, 6144)]
            fmc = {}
            c0 = 0
            for name, off, n in fm:
                fmc[name] = c0
                P.dma("sp", binT[:, c0:c0 + n // 128], bi[off:off + n].rearrange("(c p) -> p c", p=128), reads=[self.b_in], writes=[binT], allow_slow_non_contiguous=True)
                c0 += n // 128
            self.ts("dve", bq[:], binT[:, 0:4], 128.0 ** -0.5, ALU.mult, [binT], [bq])
            self.tt("dve", bgs[:], binT[:, fmc["qa"]:fmc["qa"] + 8], gfm[:], ALU.mult, [binT, gfm], [bgs])
            tmo = {"k": 0, "v": 512, "o": 1536, "u": 2560, "g": 3584, "kpe": 3600}
            for name, off, n in (("k", OFF_K, 512), ("v", OFF_V, 1024), ("o", OFF_O, 1024), ("u", OFF_U, 1024), ("g", OFF_G, 16), ("kpe", OFF_KPE, 64)):
                P.dma("sp", btok[:, tmo[name]:tmo[name] + n], bass.AP(self.b_in.h, l * DIN + off, [[0, 128], [1, n]]), reads=[self.b_in], writes=[btok])
            P.dma("sp", gb[:], bass.AP(self.ml_gate_b.h, l * 16, [[0, 128], [1, 16]]), reads=[self.ml_gate_b], writes=[gb])
            self.tt("dve", btok[:, 3584:3600], btok[:, 3584:3600], gb[:], ALU.add, [btok, gb], [btok])
            wv = self.wb_in[l].ap().rearrange("(kc p) n -> p kc n", p=128)
            P.dma("pool", wsm[:, :, 0:16], wv[:, :, OFF_G:OFF_G + 16], reads=[self.wb_in[l]], writes=[wsm])
            P.dma("pool", wsm[:, :, 16:80], wv[:, :, OFF_KPE:OFF_KPE + 64], reads=[self.wb_in[l]], writes=[wsm])

            for (s0, ng) in self.groups:
                col = 1 if s0 < self.TC else 0
                ntt = ng // 128
                P.dma("sp", xg[:, :, 0:ng], xTl.ap().rearrange("(c p) t -> p c t", p=128)[:, :, s0:s0 + ng], reads=[xTl], writes=[xg])
                self.norm_group(xg, hT, ng, 0, col, sqb, rbc, tmp, self.ps[7])

                def fm_tile(w, wc, name, ci, dst, dt_f32=False):
                    o = self.rot("b_of", of) if dt_f32 else self.rot("b_ob", ob)
                    for j in range(4):
                        ps = self.rot("b_ps", self.ps[0:5])
                        for kc in range(KC):
                            self.mm(ps[:, 0:ng], w[:, kc, wc + j * 128:wc + (j + 1) * 128], hT[:, kc, 0:ng], kc == 0, kc == KC - 1, [w, hT], [ps])
                        bcol = binT[:, fmc[name] + ci + j:fmc[name] + ci + j + 1]
                        if name == "q":
                            self.act(o[:, j, 0:ng], ps[:, 0:ng], AF.Identity, [ps, bq], [], pw=[o], scale=128.0 ** -0.5, bias=bq[:, ci + j:ci + j + 1])
                        elif name == "bg":
                            self.act(o[:, j, 0:ng], ps[:, 0:ng], AF.Sigmoid, [ps, binT], [], pw=[o], bias=bcol)
                        elif name in ("qa", "kva"):
                            which = 0 if name == "qa" else 1
                            self.act(o[:, j, 0:ng], ps[:, 0:ng], AF.Identity, [ps, gfm, bgs], [], pw=[o],
                                     scale=gfm[:, 4 * which + j:4 * which + j + 1], bias=bgs[:, 4 * which + j:4 * which + j + 1])
                            sq = self.rot("b_sqs", sqs)
                            self.act(sq[:, 0:ng], ps[:, 0:ng], AF.Square, [ps, binT], [sq], bias=bcol)
                            which = 0 if name == "qa" else 1
                            for t in range(ntt):
                                self.mm(self.ps[6][:, 8 * which + t:8 * which + t + 1], sq[:, t * 128:(t + 1) * 128], self.ones_b[:, 0:1],
                                        (which == 0 and j == 0 and t == 0), j == 3, [sq, self.cstb], [self.ps[6]])
                        else:
                            eng = "act" if j % 2 == 0 else "dve"
                            if eng == "act":
                                self.act(o[:, j, 0:ng], ps[:, 0:ng], AF.Identity, [ps, binT], [], pw=[o], bias=bcol)
                            else:
                                self.ts("dve", o[:, j, 0:ng], ps[:, 0:ng], bcol, ALU.add, [ps, binT], [], pw=[o])
                    P.dma("act", dst, o[:, :, 0:ng], reads=[o], writes=[dst_t[0]])

                def tm_tile(w, wc, n, name, bo):
                    for t in range(ntt):
                        ps = self.rot("b_ps", self.ps[0:5])
                        for kc in range(KC):
                            self.mm(ps[:, 0:n], hT[:, kc, t * 128:(t + 1) * 128], w[:, kc, wc:wc + n], kc == 0, kc == KC - 1, [w, hT], [ps])
                        r0 = s0 + t * 128
                        if name in ("k", "v"):
                            o = self.rot("b_ot", ot)
                            self.tt("dve", o[:, 0:n], ps[:, 0:n], btok[:, bo:bo + n], ALU.add, [ps, btok], [o])
                            dstT = self.ktok if name == "k" else self.vtok
                            dcol = 0 if name == "k" else tm_c[0]
                            P.dma("act", dstT.ap()[r0:r0 + 128, dcol:dcol + n], o[:, 0:n], reads=[o], writes=[dstT])
                        elif name == "o":
                            f = self.rot("b_otf", otf)
                            o = self.rot("b_ot", ot)
                            self.tt("dve", f[:, 0:n], ps[:, 0:n], btok[:, bo:bo + n], ALU.add, [ps, btok], [f])
                            self.act(o[:, 0:n], f[:, 0:n], AF.Sigmoid, [f], [o])
                            P.dma("act", self.otok.ap()[r0:r0 + 128, tm_c[0]:tm_c[0] + n], o[:, 0:n], reads=[o], writes=[self.otok])
                        elif name == "u":
                            o = self.rot("b_ot", ot)
                            self.tt("dve", o[:, 0:n], ps[:, 0:n], btok[:, bo:bo + n], ALU.add, [ps, btok], [o])
                            pst = self.ps[5]
                            psb = pst[:].bitcast(BF16)
                            for j in range(4):
                                self.tr(psb[:, j * 128:(j + 1) * 128], o[:, j * 128:(j + 1) * 128], self.J_b, [o, self.cstb], [pst])
                            ch0 = tm_c[0] // 128
                            self.copy("act", ur[:, t, ch0:ch0 + 4, :], psb[:, 0:512].rearrange("p (j t) -> p j t", j=4), [pst], [], pw=[ur])
                            if ch0 == 4:
                                i0 = self.mirror(r0)
                                P.dma("act", self.uTr.ap().rearrange("(c p) t -> p c t", p=128)[:, :, i0:i0 + 128], ur[:, t], reads=[ur], writes=[self.uTr])
                        elif name == "gk":
                            self.tt("dve", gl[:], ps[:, 0:80], btok[:, 3584:3664], ALU.add, [ps, btok], [gl])
                            g4 = gl[:, 0:16].rearrange("p (d t h) -> p d t h", d=2, t=2)
                            self.act(e1[:].rearrange("p (d h) -> p d h", d=2), g4[:, :, 1, :], AF.Exp, [gl], [e1], scale=-1.0)
                            self.act(e1[:], e1[:], AF.Ln, [e1], [e1], bias=1.0)
                            self.ts("dve", g4[:, :, 1, :], e1[:].rearrange("p (d h) -> p d h", d=2), -1.0, ALU.mult, [e1], [gl])
                            P.dma("act", self.gtok.ap()[r0:r0 + 128, :], gl[:, 0:16], reads=[gl], writes=[self.gtok])
                            P.dma("act", self.kpe.ap()[r0:r0 + 128, :], gl[:, 16:80], reads=[gl], writes=[self.kpe])

                def load_w(off):
                    w = self.rot("b_w", wt)
                    P.dma("pool", w[:], wv[:, :, off:off + 512], reads=[self.wb_in[l]], writes=[w])
                    return w

                def fmdst(Tt, row0, f32=False):
                    return Tt.ap().rearrange("(j p) t -> p j t", p=128)[:, row0 // 128:row0 // 128 + 4, s0:s0 + ng]

                dst_t = [None]
                tm_c = [0]
                w = load_w(OFF_Q)
                dst_t[0] = self.qT
                fm_tile(w, 0, "q", 0, self.qT.ap().rearrange("h p t -> p h t")[:, :, s0:s0 + ng])
                w = load_w(OFF_K)
                dst_t[0] = self.kT
                fm_tile(w, 0, "k", 0, self.kT.ap().rearrange("h p t -> p h t")[:, :, s0:s0 + ng])
                tm_tile(w, 0, 512, "k", tmo["k"])
                for h2 in range(2):
                    w = load_w(OFF_V + 512 * h2)
                    tm_c[0] = 512 * h2
                    tm_tile(w, 0, 512, "v", tmo["v"] + 512 * h2)
                for h2 in range(2):
                    w = load_w(OFF_O + 512 * h2)
                    tm_c[0] = 512 * h2
                    tm_tile(w, 0, 512, "o", tmo["o"] + 512 * h2)
                tm_tile(wsm, 0, 80, "gk", 0)
                for name, off, Tt in (("qa", OFF_QA, self.qaT), ("kva", OFF_KVA, self.kvaT)):
                    w = load_w(off)
                    dst_t[0] = Tt
                    fm_tile(w, 0, name, 0, fmdst(Tt, 0))
                for which in range(2):
                    self.act(rs[:, 0:ntt, which], self.ps[6][:, 8 * which:8 * which + ntt], AF.Sqrt, [self.ps[6], self.eps_t], [], pw=[rs], scale=1.0 / 512, bias=self.eps_t[:, 0:1])
                P.op("dve", lambda e: e.reciprocal(out=rs[:, 0:ntt, :], in_=rs[:, 0:ntt, :]), reads=[rs], writes=[rs])
                P.dma("act", self.rsq.ap()[s0:s0 + ng, :].rearrange("(t p) c -> p t c", p=128), rs[:, 0:ntt, :], reads=[rs], writes=[self.rsq])
                for h2 in range(2):
                    w = load_w(OFF_U + 512 * h2)
                    dst_t[0] = self.uT
                    fm_tile(w, 0, "u", 4 * h2, fmdst(self.uT, 512 * h2), dt_f32=True)
                    tm_c[0] = 512 * h2
                    tm_tile(w, 0, 512, "u", tmo["u"] + 512 * h2)
                for b12 in range(12):
                    w = load_w(OFF_BG + 512 * b12)
                    dst_t[0] = self.gT
                    fm_tile(w, 0, "bg", 4 * b12, fmdst(self.gT, 512 * b12))

    def mirror(self, r0, n=128):
        TC, TL = self.TC, self.TL
        if r0 < TC:
            return TC - n - r0
        return TC + (TL - n - (r0 - TC))


def make_consts(TL, TC):
    cst = np.zeros((128, 640), np.float32)
    cst[:, 0:128] = np.eye(128)
    cst[:, 128:256] = np.eye(128)[::-1]
    k = np.arange(64)
    cst[0:64, 256:320] = (k[:, None] <= k[None, :])
    cst[0:64, 320:384] = (k[:, None] >= k[None, :])
    cst[:, 384:512] = 1.0
    TT = TL + TC
    rope = np.zeros((TT, 128), np.float32)
    rope[:, 0:64] = 1.0
    t = np.arange(TL)
    row = (t // 64).astype(np.float32)
    col = (t % 64).astype(np.float32)
    inv = (np.float32(10000.0) ** (-np.arange(16, dtype=np.float32) / np.float32(16))).astype(np.float32)
    ar = (row[:, None] * inv).astype(np.float32)
    ac = (col[:, None] * inv).astype(np.float32)
    rope[TC:, 0:64] = np.concatenate([np.cos(ar), np.cos(ar), np.cos(ac), np.cos(ac)], axis=1)
    rope[TC:, 64:128] = np.concatenate([-np.sin(ar), np.sin(ar), -np.sin(ac), np.sin(ac)], axis=1)
    return cst, rope


def prep_shared(inp, L):
    f = lambda a: np.ascontiguousarray(np.asarray(a, dtype=np.float32))
    sh = {}
    for k in ("w_mod", "b_mod", "norm_g", "w_in", "b_in", "mla_qa_g", "mla_kva_g", "mla_w_uq", "mla_w_ukv", "mla_qn_g", "mla_kn_g",
              "s5_d", "s5_w_glu", "s5_b_glu", "w_branch", "w_out", "w_ff1", "w_ff2"):
        sh[k] = f(inp[k])[:L]
    sh["ml_gate_b"] = f(inp["ml_gate_b"])[:L].reshape(L, 16)
    sh["ml_norm_g"] = f(inp["ml_norm_g"])[:L].reshape(L, 1024)
    lam = np.zeros((L, 2, 3, 128, 32), np.float32)
    sb = np.zeros((L, 2, 2, 128, 8, 2, 128), np.float32)
    sc = np.zeros((L, 2, 2, 128, 32, 32), np.float32)
    a_re, a_im, ldt = f(inp["s5_a_re"]), f(inp["s5_a_im"]), f(inp["s5_log_dt"])
    bs = (f(inp["s5_b_re"]), f(inp["s5_b_im"]))
    cs = (f(inp["s5_c_re"]), f(inp["s5_c_im"]))

    def lay(a):
        return a.reshape(32, 2, 64).transpose(1, 2, 0).reshape(128, 32)

    for l in range(L):
        for d in range(2):
            lam[l, d, 0] = lay(a_re[l, d])
            lam[l, d, 1] = lay(a_im[l, d])
            lam[l, d, 2] = lay(np.repeat(ldt[l, d][:, None], 64, axis=1))
            for ri in range(2):
                b = bs[ri][l, d]
                c = cs[ri][l, d]
                for g in range(64):
                    q, pp, g2 = g // 8, (g % 8) // 2, g % 2
                    sb[l, d, ri, pp * 32 + g2 * 16:pp * 32 + g2 * 16 + 16, q, 0, g2 * 64:(g2 + 1) * 64] = b[g].T
                    if pp == 3:
                        sb[l, d, ri, pp * 32 + g2 * 16:pp * 32 + g2 * 16 + 16, q, 1, g2 * 64:(g2 + 1) * 64] = b[g].T
                    sc[l, d, ri, g2 * 64:(g2 + 1) * 64, g // 2, g2 * 16:(g2 + 1) * 16] = c[g].T
    sh["s5_lam"], sh["s5_b"], sh["s5_c"] = lam, sb, sc
    return sh


def prep_core(inp, b, TL, TC):
    f = lambda a: np.ascontiguousarray(np.asarray(a, dtype=np.float32))
    return {
        "x": f(inp["x"][b]),
        "ctx": f(inp["ctx"][b]),
        "cvec": f(np.stack([np.asarray(inp["c"][b]), np.asarray(inp["c_ctx"])])),
    }


PHASES = ("A", "B", "C1", "C2", "C3", "DE")


def build_program(TL, TC, depth=DEPTH, debug=(), stop_after=None):
    B = Builder(TL, TC, depth, debug, stop_after)
    B.declare()
    B.convert_weights(0, ("in",))
    B.phase_T0()
    done = False
    for l in range(depth):
        for name in PHASES:
            if name == "C1":
                B.convert_weights(l, ("de",))
                if l + 1 < depth:
                    B.convert_weights(l + 1, ("in",))
            getattr(B, "phase_" + name)(l)
            if stop_after == (name, l):
                done = True
                break
        if done:
            break
    B.P.finish()
    return B


def phase_C1(self, l):
    P, TC, TL, TT = self.P, self.TC, self.TL, self.TT
    with P.scope():
        QTg = [P.sbuf("m_q%d" % d, [128, 4, 512], BF16) for d in range(2)]
        KTg = [P.sbuf("m_k%d" % d, [128, 4, 512], BF16) for d in range(2)]
        Ktg = [P.sbuf("m_kt%d" % d, [64, 8, 512], BF16) for d in range(2)]
        Vg = [P.sbuf("m_v%d" % d, [64, 8, 1024], BF16) for d in range(2)]
        Gg = [P.sbuf("m_g%d" % d, [64, 8, 16], F32) for d in range(2)]
        Cf = [P.sbuf("m_cf%d" % d, [128, 4, 257], F32) for d in range(2)]
        Cb = [P.sbuf("m_cb%d" % d, [128, 4, 257], BF16) for d in range(2)]
        sm = [[P.sbuf("m_s%d_%d" % (d, i), [128, 40], F32) for i in range(2)] for d in range(2)]
        smb = [[P.sbuf("m_sb%d_%d" % (d, i), [64, 8], BF16) for i in range(2)] for d in range(2)]
        Vs = [[P.sbuf("m_vs%d_%d" % (d, i), [64, 4, 256], BF16) for i in range(2)] for d in range(2)]
        VF = [[P.sbuf("m_vf%d_%d" % (d, i), [64, 4, 256], BF16) for i in range(2)] for d in range(2)]
        Sm = [[P.sbuf("m_sm%d_%d" % (d, i), [64, 4, 64], BF16) for i in range(2)] for d in range(2)]
        hst = [[P.sbuf("m_h%d_%d" % (d, i), [64, 4, 256], F32) for i in range(2)] for d in range(2)]
        for d in range(2):
            P.op("dve", lambda e, d=d: e.memset(Cf[d][:], 0.0), writes=[Cf[d]])
            P.op("dve", lambda e, d=d: e.memset(Cb[d][:], 0.0), writes=[Cb[d]])
        order = []
        for d in range(2):
            o = []
            glist = self.groups if d == 0 else [self.groups[0]] + self.groups[1:][::-1]
            for gi, (s0, ng) in enumerate(glist):
                cis = list(range(ng // 64))
                if d == 1:
                    cis = cis[::-1]
                for ci in cis:
                    o.append((s0, ng, ci))
            order.append(o)
        cur = [None, None]
        psm = [self.ps[0], self.ps[5]]
        pacc = [(self.ps[1], self.ps[2]), (self.ps[6], self.ps[7])]
        pdc = (self.ps[3], self.ps[4])
        for step in range(len(order[0])):
            for d in range(2):
                s0, ng, ci = order[d][step]
                nch = ng // 64
                if cur[d] != s0:
                    cur[d] = s0
                    P.dma("sp", QTg[d][:, :, 0:ng], self.qT.ap().rearrange("h p t -> p h t")[:, :, s0:s0 + ng], reads=[self.qT], writes=[QTg[d]])
                    P.dma("sp", KTg[d][:, :, 0:ng], self.kT.ap().rearrange("h p t -> p h t")[:, :, s0:s0 + ng], reads=[self.kT], writes=[KTg[d]])
                    P.dma("sp", Ktg[d][:, 0:nch, :], self.ktok.ap()[s0:s0 + ng, :].rearrange("(c p) n -> p c n", p=64), reads=[self.ktok], writes=[Ktg[d]])
                    P.dma("sp", Vg[d][:, 0:nch, :], self.vtok.ap()[s0:s0 + ng, :].rearrange("(c p) n -> p c n", p=64), reads=[self.vtok], writes=[Vg[d]])
                    P.dma("sp", Gg[d][:, 0:nch, :], self.gtok.ap()[s0:s0 + ng, :].rearrange("(c p) n -> p c n", p=64), reads=[self.gtok], writes=[Gg[d]])
                c0 = ci * 64
                t0 = s0 + c0
                par = step % 2
                S_, SB_, Vs_, VF_, Sm_, H_ = sm[d][par], smb[d][par], Vs[d][par], VF[d][par], Sm[d][par], hst[d][par]
                pA, (pa0, pa1), (pd0, pd1) = psm[d], pacc[d], pdc
                tri = self.triL_f if d == 0 else self.triU_f
                li = Gg[d][:, ci, d * 8:d * 8 + 4]
                lf = Gg[d][:, ci, d * 8 + 4:d * 8 + 8]
                cst = self.cst_t
                self.mm(pA[0:64, 0:4], tri, lf, True, True, [cst, Gg[d]], [pA])
                self.mm(pA[:, 8:12], self.ones_f[0:64, :], lf, True, True, [cst, Gg[d]], [pA])
                dif, esc, ecn, eF, escF, absd, rden = (S_[0:64, 0:4], S_[0:64, 4:8], S_[0:64, 8:12], S_[:, 12:16], S_[0:64, 16:20], S_[0:64, 20:24], S_[0:64, 24:28])
                self.tt("dve", dif, li, pA[0:64, 0:4], ALU.subtract, [Gg[d], pA], [S_])
                self.act(esc, dif, AF.Exp, [S_], [S_])
                self.act(ecn, pA[0:64, 0:4], AF.Exp, [pA], [S_], scale=-1.0)
                self.act(eF, pA[:, 8:12], AF.Exp, [pA], [S_])
                self.tt("dve", escF, esc, eF[0:64, :], ALU.mult, [S_], [S_])
                self.copy("act", SB_[:, 0:4], esc, [S_], [SB_])
                self.copy("act", SB_[:, 4:8], escF, [S_], [SB_])
                vv = Vg[d][:, ci, :].rearrange("p (h v) -> p h v", h=4)
                self.tt("dve", Vs_[:], vv, bc(esc.unsqueeze(2), [64, 4, 256]), ALU.mult, [Vg[d], S_], [Vs_])
                self.tt("dve", VF_[:], vv, bc(escF.unsqueeze(2), [64, 4, 256]), ALU.mult, [Vg[d], S_], [VF_])
                for h in range(4):
                    self.mm(pA[0:64, 256 + h * 64:256 + (h + 1) * 64], KTg[d][:, h, c0:c0 + 64], QTg[d][:, h, c0:c0 + 64], True, True, [KTg[d], QTg[d]], [pA])
                self.tt("dve", Sm_[:], pA[0:64, 256:512].rearrange("p (h t) -> p h t", h=4), bc(tri.unsqueeze(1), [64, 4, 64]), ALU.mult, [pA, cst], [Sm_])
                for h in range(4):
                    bank = pa0 if h < 2 else pa1
                    off = (h % 2) * 256
                    self.mm(bank[0:64, off:off + 256], QTg[d][:, h, c0:c0 + 64], Cb[d][:, h, 0:256], True, False, [QTg[d], Cb[d]], [bank])
                    self.mm(bank[0:64, off:off + 256], Sm_[:, h, :], Vs_[:, h, :], False, True, [Sm_, Vs_], [bank])
                for h in range(4):
                    self.mm(pA[0:64, 16 + h:17 + h], QTg[d][:, h, c0:c0 + 64], Cb[d][:, h, 256:257], True, False, [QTg[d], Cb[d]], [pA])
                    self.mm(pA[0:64, 16 + h:17 + h], Sm_[:, h, :], SB_[:, h:h + 1], False, True, [Sm_, SB_], [pA])
                for h in range(4):
                    bank = pd0 if h < 2 else pd1
                    off = (h % 2) * 256
                    self.mm(bank[:, off:off + 256], Ktg[d][:, ci, h * 128:(h + 1) * 128], VF_[:, h, :], True, True, [Ktg[d], VF_], [bank])
                    self.mm(pA[:, 24 + h:25 + h], Ktg[d][:, ci, h * 128:(h + 1) * 128], SB_[:, 4 + h:5 + h], True, True, [Ktg[d], SB_], [pA])
                self.act(absd, pA[0:64, 16:20], AF.Abs, [pA], [S_])
                self.tt("dve", rden, absd, ecn, ALU.max, [S_], [S_])
                P.op("dve", lambda e, rden=rden: e.reciprocal(out=rden, in_=rden), reads=[S_], writes=[S_])
                for b2, bank in enumerate((pa0, pa1)):
                    self.tt("dve", H_[:, 2 * b2:2 * b2 + 2, :], bank[0:64, :].rearrange("p (h v) -> p h v", h=2),
                            bc(rden[:, 2 * b2:2 * b2 + 2].unsqueeze(2), [64, 2, 256]), ALU.mult, [bank, S_], [], pw=[H_])
                P.dma("act", self.hml.ap()[d, t0:t0 + 64, :], H_[:].rearrange("p h v -> p (h v)"), reads=[H_], writes=[self.hml])
                for h in range(4):
                    bank = pd0 if h < 2 else pd1
                    off = (h % 2) * 256
                    self.stt(Cf[d][:, h, 0:256], Cf[d][:, h, 0:256], eF[:, h:h + 1], bank[:, off:off + 256], ALU.mult, ALU.add, [Cf[d], S_, bank], [Cf[d]])
                    self.stt(Cf[d][:, h, 256:257], Cf[d][:, h, 256:257], eF[:, h:h + 1], pA[:, 24 + h:25 + h], ALU.mult, ALU.add, [Cf[d], S_, pA], [Cf[d]])
                self.copy("act", Cb[d][:], Cf[d][:], [Cf[d]], [Cb[d]])

    with P.scope():
        gnb = P.sbuf("mo_gn", [128, 1024], F32)
        P.dma("sp", gnb[:], bass.AP(self.ml_norm_g.h, l * 1024, [[0, 128], [1, 1024]]), reads=[self.ml_norm_g], writes=[gnb])
        h0 = [P.sbuf("mo_h0_%d" % i, [128, 1024], F32) for i in range(2)]
        h1 = [P.sbuf("mo_h1_%d" % i, [128, 1024], F32) for i in range(2)]
        og = [P.sbuf("mo_og_%d" % i, [128, 1024], BF16) for i in range(2)]
        sq = P.sbuf("mo_sq", [128, 1024], F32)
        ss = [P.sbuf("mo_ss%d" % i, [128, 4], F32) for i in range(2)]
        ab = [P.sbuf("mo_ab%d" % i, [128, 1024], BF16) for i in range(2)]
        aT = [P.sbuf("mo_aT%d" % i, [128, 8, 512], BF16) for i in range(2)]
        for gi, (s0, ng) in enumerate(self.groups):
            A_ = aT[gi % 2]
            for t in range(ng // 128):
                r0 = s0 + t * 128
                i = self.rr.get("m_o", 0)
                self.rr["m_o"] = i + 1
                a0, a1, o_, s_, b_ = h0[i % 2], h1[i % 2], og[i % 2], ss[i % 2], ab[i % 2]
                P.dma("sp", a0[:], self.hml.ap()[0, r0:r0 + 128, :], reads=[self.hml], writes=[a0])
                P.dma("sp", a1[:], self.hml.ap()[1, r0:r0 + 128, :], reads=[self.hml], writes=[a1])
                P.dma("sp", o_[:], self.otok.ap()[r0:r0 + 128, :], reads=[self.otok], writes=[o_])
                self.tt("dve", a0[:], a0[:], a1[:], ALU.add, [a0, a1], [a0])
                self.act(sq[:], a0[:], AF.Square, [a0], [sq])
                P.op("dve", lambda e, s_=s_: e.tensor_reduce(out=s_[:], in_=sq[:].rearrange("p (h v) -> p h v", h=4), axis=AX.X, op=ALU.add), reads=[sq], writes=[s_])
                self.act(s_[:], s_[:], AF.Sqrt, [s_, self.eps_t], [s_], scale=1.0 / 256, bias=self.eps_t[:, 0:1])
                P.op("dve", lambda e, s_=s_: e.reciprocal(out=s_[:], in_=s_[:]), reads=[s_], writes=[s_])
                a3 = a0[:].rearrange("p (h v) -> p h v", h=4)
                self.tt("dve", a3, a3, bc(s_[:].unsqueeze(2), [128, 4, 256]), ALU.mult, [a0, s_], [a0])
                self.tt("pool", a0[:], a0[:], gnb[:], ALU.mult, [a0, gnb], [a0])
                self.tt("dve", b_[:], a0[:], o_[:], ALU.mult, [a0, o_], [b_])
                pst = self.rot("m_pst", self.ps[0:2])
                psb = pst[:].bitcast(BF16)
                for j in range(8):
                    self.tr(psb[:, j * 128:(j + 1) * 128], b_[:, j * 128:(j + 1) * 128], self.ident_b, [b_, self.cstb], [pst])
                self.copy("act", A_[:, :, t * 128:(t + 1) * 128], psb[:, 0:1024].rearrange("p (j t) -> p j t", j=8), [pst], [], pw=[A_])
            P.dma("act", self.brT.ap()[0].rearrange("(c p) t -> p c t", p=128)[:, :, s0:s0 + ng], A_[:, :, 0:ng], reads=[A_], writes=[self.brT])


Builder.phase_C1 = phase_C1


def rope_ops(self, eng, src, dst, tmp1, tmp2, rt, nh, reads, writes):
    C = bc(rt[:, 0:64].unsqueeze(1), [128, nh, 64])
    S5 = rt[:, 64:128].rearrange("p (b f i) -> p b f i", b=2, f=2)
    s5 = src.rearrange("p h (b f i) -> p h b f i", b=2, f=2)
    t5 = tmp2.rearrange("p h (b f i) -> p h b f i", b=2, f=2)
    self.tt(eng, tmp1, src, C, ALU.mult, reads, writes)
    for f in range(2):
        self.tt(eng, t5[:, :, :, f, :], s5[:, :, :, 1 - f, :], bc(S5[:, :, f, :].unsqueeze(1), [128, nh, 2, 16]), ALU.mult, reads, writes)
    self.tt(eng, dst, tmp1, tmp2, ALU.add, reads, writes)


def phase_C2(self, l):
    P, TC, TL, TT, NT = self.P, self.TC, self.TL, self.TT, self.NT
    with P.scope():
        wq = P.sbuf("c_wq", [128, 4, 1536], BF16)
        wkv = P.sbuf("c_wkv", [128, 4, 2048], BF16)
        gq = P.sbuf("c_gq", [128, 192], F32)
        gk = P.sbuf("c_gk", [128, 192], F32)
        P.dma("pool", wq[:], self.w_uq.ap()[l].rearrange("(kc p) n -> p kc n", p=128), reads=[self.w_uq], writes=[wq])
        P.dma("pool", wkv[:], self.w_ukv.ap()[l].rearrange("(kc p) n -> p kc n", p=128), reads=[self.w_ukv], writes=[wkv])
        P.dma("sp", gq[:], bass.AP(self.qn_g.h, l * 192, [[0, 128], [1, 192]]), reads=[self.qn_g], writes=[gq])
        P.dma("sp", gk[:], bass.AP(self.kn_g.h, l * 192, [[0, 128], [1, 192]]), reads=[self.kn_g], writes=[gk])
        qaTg = P.sbuf("c_qa", [128, 4, 512], BF16)
        kvaTg = P.sbuf("c_kva", [128, 4, 512], BF16)
        rsg = P.sbuf("c_rs", [128, 4, 2], F32)
        rpt = [P.sbuf("c_rp%d" % i, [128, 128], F32) for i in range(2)]
        kpt = [P.sbuf("c_kp%d" % i, [128, 64], F32) for i in range(2)]
        qf = P.sbuf("c_qf", [128, 2048], F32)
        sqv = P.sbuf("c_sq", [128, 2048], F32)
        fin = [P.sbuf("c_fin%d" % i, [128, 8, 192], BF16) for i in range(2)]
        vfin = [P.sbuf("c_vf%d" % i, [128, 8, 128], BF16) for i in range(2)]
        rp = P.sbuf("c_rpp", [128, 8, 64], F32)
        t1 = P.sbuf("c_t1", [128, 8, 64], F32)
        t2 = P.sbuf("c_t2", [128, 8, 64], F32)
        st = [P.sbuf("c_st%d" % i, [128, 12], F32) for i in range(2)]
        kk = P.sbuf("c_kk", [128, 4, 64], F32)
        stn = [P.sbuf("c_stn%d" % i, [128, 8, 512], BF16) for i in range(2)]
        str_ = [P.sbuf("c_str%d" % i, [64, 8, 512], BF16) for i in range(2)]
        for (s0, ng) in self.groups:
            ntt = ng // 128
            P.dma("sp", qaTg[:, :, 0:ng], self.qaT.ap().rearrange("(c p) t -> p c t", p=128)[:, :, s0:s0 + ng], reads=[self.qaT], writes=[qaTg])
            P.dma("sp", kvaTg[:, :, 0:ng], self.kvaT.ap().rearrange("(c p) t -> p c t", p=128)[:, :, s0:s0 + ng], reads=[self.kvaT], writes=[kvaTg])
            P.dma("sp", rsg[:, 0:ntt, :], self.rsq.ap()[s0:s0 + ng, :].rearrange("(t p) c -> p t c", p=128), reads=[self.rsq], writes=[rsg])
            qn_, qr_ = stn[0], str_[0]
            kn_, kr_ = stn[1], str_[1]
            for t in range(ntt):
                r0 = s0 + t * 128
                i = self.rr.get("c_i", 0)
                self.rr["c_i"] = i + 1
                rt, kp, S_ = rpt[i % 2], kpt[i % 2], st[i % 2]
                P.dma("sp", rt[:], self.rope.ap()[r0:r0 + 128, :], reads=[self.rope], writes=[rt])
                P.dma("sp", kp[:], self.kpe.ap()[r0:r0 + 128, :], reads=[self.kpe], writes=[kp])
                for cg in range(3):
                    ps = self.rot("c_ps", self.ps[0:6])
                    for kc in range(4):
                        self.mm(ps[:, 0:512], qaTg[:, kc, t * 128:(t + 1) * 128], wq[:, kc, cg * 512:(cg + 1) * 512], kc == 0, kc == 3, [qaTg, wq], [ps])
                    self.act(qf[:, cg * 512:(cg + 1) * 512], ps[:, 0:512], AF.Identity, [ps, rsg], [], pw=[qf], scale=rsg[:, t, 0:1])
                    self.act(sqv[:, cg * 512:(cg + 1) * 512], ps[:, 0:512], AF.Square, [ps, rsg], [], pw=[sqv], scale=rsg[:, t, 0:1])
                q3 = qf[:, 0:1536].rearrange("p (h e) -> p h e", h=8)
                s3 = sqv[:, 0:1536].rearrange("p (h e) -> p h e", h=8)
                rq = S_[:, 0:8]
                P.op("dve", lambda e, rq=rq, s3=s3: e.tensor_reduce(out=rq, in_=s3, axis=AX.X, op=ALU.add), reads=[sqv], writes=[S_])
                self.act(rq, rq, AF.Sqrt, [S_, self.eps_t], [S_], scale=1.0 / 192, bias=self.eps_t[:, 0:1])
                P.op("dve", lambda e, rq=rq: e.reciprocal(out=rq, in_=rq), reads=[S_], writes=[S_])
                self.tt("dve", s3, q3, bc(rq.unsqueeze(2), [128, 8, 192]), ALU.mult, [qf, S_], [sqv])
                F_ = fin[0]
                self.tt("dve", F_[:, :, 0:128], s3[:, :, 0:128], bc(gq[:, 0:128].unsqueeze(1), [128, 8, 128]), ALU.mult, [sqv, gq], [F_])
                self.tt("pool", rp[:], s3[:, :, 128:192], bc(gq[:, 128:192].unsqueeze(1), [128, 8, 64]), ALU.mult, [sqv, gq], [rp])
                rope_ops(self, "dve", rp[:], F_[:, :, 128:192], t1[:], t2[:], rt, 8, [rp, rt, t1, t2], [t1, t2, F_])
                pn, pr = self.ps[6], self.ps[7]
                pnb, prb = pn[:].bitcast(BF16), pr[:].bitcast(BF16)
                for h in range(8):
                    self.tr(pnb[:, h * 128:(h + 1) * 128], F_[:, h, 0:128], self.ident_b, [F_, self.cstb], [pn])
                    self.tr(prb[0:64, h * 128:(h + 1) * 128], F_[:, h, 128:192], self.ident_b, [F_, self.cstb], [pr])
                self.copy("act", qn_[:, :, t * 128:(t + 1) * 128], pnb[:, 0:1024].rearrange("p (h t) -> p h t", h=8), [pn], [], pw=[qn_])
                self.copy("act", qr_[:, :, t * 128:(t + 1) * 128], prb[0:64, 0:1024].rearrange("p (h t) -> p h t", h=8), [pr], [], pw=[qr_])
                for cg in range(4):
                    ps = self.rot("c_ps", self.ps[0:6])
                    for kc in range(4):
                        self.mm(ps[:, 0:512], kvaTg[:, kc, t * 128:(t + 1) * 128], wkv[:, kc, cg * 512:(cg + 1) * 512], kc == 0, kc == 3, [kvaTg, wkv], [ps])
                    self.act(qf[:, cg * 512:(cg + 1) * 512], ps[:, 0:512], AF.Identity, [ps, rsg], [], pw=[qf], scale=rsg[:, t, 1:2])
                k3 = qf[:].rearrange("p (h e) -> p h e", h=8)
                s3k = sqv[:, 0:1024].rearrange("p (h e) -> p h e", h=8)
                self.act(s3k, k3[:, :, 0:128], AF.Square, [qf], [sqv])
                rk = S_[:, 0:8]
                sp_ = S_[:, 8:9]
                P.op("dve", lambda e, rk=rk, s3k=s3k: e.tensor_reduce(out=rk, in_=s3k, axis=AX.X, op=ALU.add), reads=[sqv], writes=[S_])
                self.act(kk[:, 0, :], kp[:], AF.Square, [kp], [kk, S_], accum_out=sp_)
                self.ts("dve", rk, rk, sp_, ALU.add, [S_], [S_])
                self.act(rk, rk, AF.Sqrt, [S_, self.eps_t], [S_], scale=1.0 / 192, bias=self.eps_t[:, 0:1])
                P.op("dve", lambda e, rk=rk: e.reciprocal(out=rk, in_=rk), reads=[S_], writes=[S_])
                Fk = fin[1]
                self.tt("dve", s3k, k3[:, :, 0:128], bc(rk.unsqueeze(2), [128, 8, 128]), ALU.mult, [qf, S_], [sqv])
                self.tt("dve", Fk[:, :, 0:128], s3k, bc(gk[:, 0:128].unsqueeze(1), [128, 8, 128]), ALU.mult, [sqv, gk], [Fk])
                self.tt("pool", kk[:, 1, :], kp[:], gk[:, 128:192], ALU.mult, [kp, gk], [kk])
                rope_ops(self, "pool", kk[:, 1:2, :], kk[:, 0:1, :], kk[:, 2:3, :], kk[:, 3:4, :], rt, 1, [kk, rt], [kk])
                self.tt("dve", Fk[:, :, 128:192], bc(kk[:, 0:1, :], [128, 8, 64]), bc(rk.unsqueeze(2), [128, 8, 64]), ALU.mult, [kk, S_], [Fk])
                V_ = vfin[i % 2]
                self.copy("act", V_[:], k3[:, :, 128:256], [qf], [V_])
                P.dma("act", self.vh.ap()[r0:r0 + 128, :], V_[:].rearrange("p h e -> p (h e)"), reads=[V_], writes=[self.vh])
                pn, pr = self.ps[6], self.ps[7]
                for h in range(8):
                    self.tr(pnb[:, h * 128:(h + 1) * 128], Fk[:, h, 0:128], self.ident_b, [Fk, self.cstb], [pn])
                    self.tr(prb[0:64, h * 128:(h + 1) * 128], Fk[:, h, 128:192], self.ident_b, [Fk, self.cstb], [pr])
                self.copy("act", kn_[:, :, t * 128:(t + 1) * 128], pnb[:, 0:1024].rearrange("p (h t) -> p h t", h=8), [pn], [], pw=[kn_])
                self.copy("act", kr_[:, :, t * 128:(t + 1) * 128], prb[0:64, 0:1024].rearrange("p (h t) -> p h t", h=8), [pr], [], pw=[kr_])
            for Tt, n_, r_ in ((self.qhT, qn_, qr_), (self.khT, kn_, kr_)):
                P.dma("act", Tt.ap()[:, 0:128, s0:s0 + ng].rearrange("h d t -> d h t"), n_[:, :, 0:ng], reads=[n_], writes=[Tt])
                P.dma("act", Tt.ap()[:, 128:192, s0:s0 + ng].rearrange("h d t -> d h t"), r_[:, :, 0:ng], reads=[r_], writes=[Tt])


def attn_gen(self, l, AK=8):
    P, TC, TL, TT, NT = self.P, self.TC, self.TL, self.TT, self.NT
    Kn = P.sbuf("c_Kn", [128, TT], BF16)
    Kr = P.sbuf("c_Kr", [64, TT], BF16)
    Vh = P.sbuf("c_Vh", [128, NT, 132], BF16)
    Qn = [P.sbuf("c_Qn%d" % i, [128, 256], BF16) for i in range(2)]
    Qr = [P.sbuf("c_Qr%d" % i, [64, 256], BF16) for i in range(2)]
    pT = [P.sbuf("c_pT%d" % i, [128, 256], BF16) for i in range(3)]
    ob = [P.sbuf("c_ob%d" % i, [128, 2, 128], BF16) for i in range(2)]
    rd = [P.sbuf("c_rd%d" % i, [128, 2], F32) for i in range(2)]
    bst = [P.sbuf("c_bst%d" % i, [128, 256], BF16) for i in range(2)]
    P.op("pool", lambda e: e.memset(Vh[:], 1.0), writes=[Vh])
    sc = 192.0 ** -0.5
    qgroups = []
    for (g0, ng) in self.groups:
        for o in range(0, ng, 256):
            qgroups.append((g0 + o, 256))
    cnt = 0
    pO = self.ps[0:2]
    pSb = self.ps[2:5]
    for h in range(8):
        K1, K2, V_ = Kn, Kr, Vh
        P.dma("sp", K1[:], self.khT.ap()[h, 0:128, :], reads=[self.khT], writes=[K1])
        P.dma("sp", K2[:], self.khT.ap()[h, 128:192, :], reads=[self.khT], writes=[K2])
        P.dma("sp", V_[:, :, 0:128], self.vh.ap()[:, h * 128:(h + 1) * 128].rearrange("(kt p) e -> p kt e", p=128), reads=[self.vh], writes=[V_])
        for (q0, nq) in qgroups:
            i = self.rr.get("c_q", 0)
            self.rr["c_q"] = i + 1
            Q1, Q2, O_, R_, B_ = Qn[i % 2], Qr[i % 2], ob[i % 2], rd[i % 2], bst[i % 2]
            P.dma("sp", Q1[:, 0:nq], self.qhT.ap()[h, 0:128, q0:q0 + nq], reads=[self.qhT], writes=[Q1])
            P.dma("sp", Q2[:, 0:nq], self.qhT.ap()[h, 128:192, q0:q0 + nq], reads=[self.qhT], writes=[Q2])
            nqt = nq // 128
            nkt = TC // 128 if q0 < TC else NT

            def qk(kt):
                pS = self.rot("c_pS", pSb)
                self.mm(pS[:, 0:nq], K1[:, kt * 128:(kt + 1) * 128], Q1[:, 0:nq], True, False, [K1, Q1], [pS])
                self.mm(pS[:, 0:nq], K2[:, kt * 128:(kt + 1) * 128], Q2[:, 0:nq], False, True, [K2, Q2], [pS])
                return pS

            pS_next = qk(0)
            for kt in range(nkt):
                pS = pS_next
                p_ = self.rot("c_pT", pT)
                self.act(p_[:, 0:nq], pS[:, 0:nq], AF.Exp, [pS], [p_], scale=sc)
                if kt + 1 < nkt:
                    pS_next = qk(kt + 1)
                for j in range(nqt):
                    self.mm(pO[j][:, 0:129], p_[:, j * 128:(j + 1) * 128], V_[:, kt, 0:129], kt == 0, kt == nkt - 1, [p_, V_], [pO[j]])
                cnt += 1
                if cnt % AK == 0:
                    yield
            pS = self.rot("c_pS", pSb)
            psb = pS[:].bitcast(BF16)
            for j in range(nqt):
                P.op("dve", lambda e, j=j, R_=R_: e.reciprocal(out=R_[:, j:j + 1], in_=pO[j][:, 128:129]), reads=[pO[j]], writes=[R_])
                self.act(O_[:, j, :], pO[j][:, 0:128], AF.Identity, [pO[j], R_], [], pw=[O_], scale=R_[:, j:j + 1])
                self.tr(psb[:, j * 128:(j + 1) * 128], O_[:, j, :], self.ident_b, [O_, self.cstb], [pS])
            self.copy("act", B_[:, 0:nq], psb[:, 0:nq], [pS], [B_])
            P.dma("act", self.brT.ap()[1, h * 128:(h + 1) * 128, q0:q0 + nq], B_[:, 0:nq], reads=[B_], writes=[self.brT])
            yield


Builder.phase_C2 = phase_C2


def phase_C3(self, l):
    P, TC, TL, TT = self.P, self.TC, self.TL, self.TT
    TWO_PI = 2.0 * math.pi
    with P.scope():
      attn = attn_gen(self, l)
      attn_live = [True]

      def attn_step():
          if attn_live[0]:
              try:
                  next(attn)
              except StopIteration:
                  attn_live[0] = False

      cache = {}

      def sb(name, shape, dt):
          if name not in cache:
              cache[name] = P.sbuf(name, shape, dt)
          return cache[name]

      for d in range(2):
        if True:
            lam = sb("s_lam", [128, 3, 32], F32)
            sc = sb("s_sc", [128, 24, 32], F32)
            Er = sb("s_Er", [128, 32, SL], F32)
            Ei = sb("s_Ei", [128, 32, SL], F32)
            Tr = sb("s_Tr", [128, 32, SL], F32)
            Ti = sb("s_Ti", [128, 32, SL], F32)
            tmpE = sb("s_tmpE", [128, 32, SL // 2], F32)
            Bw = sb("s_Bw", [128, 2, 8, 2, 128], BF16)
            Cw = sb("s_Cw", [128, 2, 32, 32], BF16)
            Cn = sb("s_Cn", [128, 2, 32, 32], BF16)
            xst = sb("s_xst", [128, 32, 2], F32)
            P.dma("sp", lam[:], self.s5_lam.ap()[l, d].rearrange("i p q -> p i q"), reads=[self.s5_lam], writes=[lam])
            P.dma("pool", Bw[:], self.s5_b.ap()[l, d].rearrange("r p q v n -> p r q v n"), reads=[self.s5_b], writes=[Bw])
            P.dma("pool", Cw[:], self.s5_c.ap()[l, d].rearrange("r p q n -> p r q n"), reads=[self.s5_c], writes=[Cw])
            self.ts("dve", Cn[:], Cw[:], -1.0, ALU.mult, [Cw], [Cn])
            P.op("dve", lambda e: e.memset(xst[:], 0.0), writes=[xst])
            S = lambda i: sc[:, i, :]
            rd = [sc]
            ar, aim, dt, r1, th, k_, c1, s1, lbr, lbi, fr, fi = (S(i) for i in range(12))
            u0, u1, u2, u3 = S(12), S(13), S(14), S(15)
            self.ts("dve", ar, lam[:, 0, :], -1e-4, ALU.min, [lam], [sc])
            self.copy("dve", aim, lam[:, 1, :], [lam], [sc])
            self.act(dt, lam[:, 2, :], AF.Exp, [lam], [sc])
            self.tt("dve", u0, ar, dt, ALU.mult, rd, rd)
            self.act(r1, u0, AF.Exp, rd, rd)
            self.tt("dve", th, aim, dt, ALU.mult, rd, rd)
            self.ts("dve", u0, th, 1.0 / TWO_PI, ALU.mult, rd, rd)
            self.ts("dve", u1, u0, 12582912.0, ALU.add, rd, rd)
            self.ts("dve", k_, u1, -12582912.0, ALU.add, rd, rd)
            self.stt(u2, k_, -TWO_PI, th, ALU.mult, ALU.add, rd, rd)
            self.ts("dve", u2, u2, math.pi, ALU.min, rd, rd, s2=-math.pi, op1=ALU.max)
            self.act(s1, u2, AF.Sin, rd, rd)
            self.act(u3, u2, AF.Abs, rd, rd)
            self.ts("dve", u3, u3, -1.0, ALU.mult, rd, rd, s2=math.pi / 2, op1=ALU.add)
            self.act(c1, u3, AF.Sin, rd, rd)
            self.tt("dve", lbr, r1, c1, ALU.mult, rd, rd)
            self.tt("dve", lbi, r1, s1, ALU.mult, rd, rd)
            x_, y_ = S(16), S(17)
            self.ts("dve", x_, lbr, -1.0, ALU.add, rd, rd)
            self.copy("dve", y_, lbi, rd, rd)
            a_, b_ = ar, aim
            n1, n2, den = S(18), S(19), S(20)
            self.tt("dve", n1, x_, a_, ALU.mult, rd, rd)
            self.tt("dve", u0, y_, b_, ALU.mult, rd, rd)
            self.tt("dve", n1, n1, u0, ALU.add, rd, rd)
            self.tt("dve", n2, y_, a_, ALU.mult, rd, rd)
            self.tt("dve", u0, x_, b_, ALU.mult, rd, rd)
            self.tt("dve", n2, n2, u0, ALU.subtract, rd, rd)
            self.tt("dve", den, a_, a_, ALU.mult, rd, rd)
            self.tt("dve", u0, b_, b_, ALU.mult, rd, rd)
            self.tt("dve", den, den, u0, ALU.add, rd, rd)
            P.op("dve", lambda e, den=den: e.reciprocal(out=den, in_=den), reads=rd, writes=rd)
            self.tt("dve", fr, n1, den, ALU.mult, rd, rd)
            self.tt("dve", fi, n2, den, ALU.mult, rd, rd)
            self.copy("dve", Er[:, :, 0], c1, rd, [Er])
            self.copy("dve", Ei[:, :, 0], s1, rd, [Ei])
            n = 1
            while n < SL:
                pr_ = bc(Er[:, :, n - 1:n], [128, 32, n])
                pi_ = bc(Ei[:, :, n - 1:n], [128, 32, n])
                sr, si = Er[:, :, 0:n], Ei[:, :, 0:n]
                dr, di = Er[:, :, n:2 * n], Ei[:, :, n:2 * n]
                tm = tmpE[:, :, 0:n]
                self.tt("dve", dr, sr, pr_, ALU.mult, [Er], [Er])
                self.tt("dve", tm, si, pi_, ALU.mult, [Ei], [tmpE])
                self.tt("dve", dr, dr, tm, ALU.subtract, [Er, tmpE], [Er])
                self.tt("dve", di, sr, pi_, ALU.mult, [Er, Ei], [Ei])
                self.tt("dve", tm, si, pr_, ALU.mult, [Er, Ei], [tmpE])
                self.tt("dve", di, di, tm, ALU.add, [Ei, tmpE], [Ei])
                n *= 2
            frb = bc(fr.unsqueeze(2), [128, 32, SL])
            fib = bc(fi.unsqueeze(2), [128, 32, SL])
            for hf in range(2):
                js = slice(hf * (SL // 2), (hf + 1) * (SL // 2))
                frh = bc(fr.unsqueeze(2), [128, 32, SL // 2])
                fih = bc(fi.unsqueeze(2), [128, 32, SL // 2])
                self.tt("dve", Tr[:, :, js], Er[:, :, js], frh, ALU.mult, [Er, sc], [Tr])
                self.tt("dve", tmpE[:], Ei[:, :, js], fih, ALU.mult, [Ei, sc], [tmpE])
                self.tt("dve", Tr[:, :, js], Tr[:, :, js], tmpE[:], ALU.add, [Tr, tmpE], [Tr])
                self.tt("dve", Ti[:, :, js], Er[:, :, js], fih, ALU.mult, [Er, sc], [Ti])
                self.tt("dve", tmpE[:], Ei[:, :, js], frh, ALU.mult, [Ei, sc], [tmpE])
                self.tt("dve", Ti[:, :, js], Ti[:, :, js], tmpE[:], ALU.subtract, [Ti, tmpE], [Ti])

            ut = [sb("s_ut%d" % i, [128, 256], BF16) for i in range(2)]
            ut32 = [sb("s_ut32_%d" % i, [128, 256], F32) for i in range(2)]
            NB = 4
            mt = [[sb("s_m%d_%d" % (k, i), [128, 256], F32) for k in range(4)] for i in range(2)]
            bre = [sb("s_bre%d" % i, [128, 256], F32) for i in range(NB)]
            bim = [sb("s_bim%d" % i, [128, 256], F32) for i in range(NB)]
            wre = [sb("s_wre%d" % i, [128, 256], F32) for i in range(NB)]
            wim = [sb("s_wim%d" % i, [128, 256], F32) for i in range(NB)]
            pr4 = [[sb("s_p%d_%d" % (k, i), [128, 256], BF16) for k in range(4)] for i in range(NB)]
            tsm = [sb("s_ts%d" % i, [128, 2], F32) for i in range(4)]
            yev = [sb("s_yev%d" % i, [128, 2, 128], F32) for i in range(2)]
            first = True
            src = self.uT if d == 0 else self.uTr
            s5groups = [(g0 + o, 256) for (g0, gn) in self.groups for o in range(0, gn, 256)]
            for (i0, ng) in s5groups:
                nchk = ng // SL
                ntt = ng // 128
                v3 = lambda ap, o=0: ap[:, o:o + ng].rearrange("p (c j) -> p c j", j=SL)
                for q in range(8):
                    U_ = self.rot("s_ut", ut)
                    if d == 0:
                        U32 = self.rot("s_ut32", ut32)
                        P.dma("sp", U32[:, 0:ng], src.ap()[q * 128:(q + 1) * 128, i0:i0 + ng], reads=[src], writes=[U32])
                        self.copy("act", U_[:, 0:ng], U32[:, 0:ng], [U32], [U_])
                    else:
                        P.dma("sp", U_[:, 0:ng], src.ap()[q * 128:(q + 1) * 128, i0:i0 + ng], reads=[src], writes=[U_])
                    pY = self.ps[7]
                    yo = 256 * (q % 2)
                    Y_ = self.rot("s_yev", yev)
                    for hp in range(2):
                        prs = []
                        for k2 in range(2):
                            pp = 2 * hp + k2
                            p = 4 * q + pp
                            ii = self.rr.get("s_i", 0)
                            self.rr["s_i"] = ii + 1
                            pB = self.ps[5 + k2]
                            pR = pB
                            pI = pB
                            rows = slice(32 * pp, 32 * pp + 32) if pp < 3 else slice(64, 128)
                            vv = 0 if pp < 3 else 1
                            self.mm(pB[:, 0:ng], Bw[rows, 0, q, vv, :], U_[rows, 0:ng], True, True, [Bw, U_], [pB])
                            self.mm(pB[:, 256:256 + ng], Bw[rows, 1, q, vv, :], U_[rows, 0:ng], True, True, [Bw, U_], [pB])
                            prs.append(dict(pp=pp, p=p, pR=pR, pI=pI, m=mt[k2], BR=bre[ii % NB], BI=bim[ii % NB], WR=wre[ii % NB], WI=wim[ii % NB],
                                            PR=pr4[ii % NB], T=tsm[ii % 4]))
                        for z in prs:
                            p = z["p"]
                            Trb = bc(Tr[:, p, :].unsqueeze(1), [128, nchk, SL])
                            Tib = bc(Ti[:, p, :].unsqueeze(1), [128, nchk, SL])
                            m1, m2, m3, m4 = z["m"]
                            self.tt("dve", v3(m1), v3(z["pR"]), Trb, ALU.mult, [z["pR"], Tr], [m1])
                            self.tt("dve", v3(m2), v3(z["pI"], 256), Tib, ALU.mult, [z["pI"], Ti], [m2])
                            self.tt("dve", v3(m3), v3(z["pI"], 256), Trb, ALU.mult, [z["pI"], Tr], [m3])
                            self.tt("dve", v3(m4), v3(z["pR"]), Tib, ALU.mult, [z["pR"], Ti], [m4])
                            self.tt("pool", z["BR"][:, 0:ng], m1[:, 0:ng], m2[:, 0:ng], ALU.subtract, [m1, m2], [z["BR"]])
                            self.tt("pool", z["BI"][:, 0:ng], m3[:, 0:ng], m4[:, 0:ng], ALU.add, [m3, m4], [z["BI"]])
                        for c in range(nchk):
                            a0 = c * SL
                            la = a0 + SL - 1
                            steps = []
                            for z in prs:
                                p, BR, BI, WR, WI, T_ = z["p"], z["BR"], z["BI"], z["WR"], z["WI"], z["T"]
                                r1p = r1[:, p:p + 1]
                                ErL, EiL = Er[:, p, SL - 1:SL], Ei[:, p, SL - 1:SL]
                                st_ = []
                                if not (first and c == 0):
                                    st_.append(lambda BR=BR, p=p, r1p=r1p: self.stt(BR[:, a0:a0 + 1], xst[:, p, 0:1], r1p, BR[:, a0:a0 + 1], ALU.mult, ALU.add, [xst, sc, BR], [BR]))
                                    st_.append(lambda BI=BI, p=p, r1p=r1p: self.stt(BI[:, a0:a0 + 1], xst[:, p, 1:2], r1p, BI[:, a0:a0 + 1], ALU.mult, ALU.add, [xst, sc, BI], [BI]))
                                else:
                                    st_.append(None)
                                    st_.append(None)
                                st_.append(lambda WR=WR, BR=BR, o_=WR[:, a0:a0 + SL], d0=bc(r1p, [128, SL]), d1=BR[:, a0:a0 + SL]: P.op("dve", lambda e: e.tensor_tensor_scan(out=o_, data0=d0, data1=d1, initial=0.0, op0=ALU.mult, op1=ALU.add), reads=[BR, sc], writes=[WR]))
                                st_.append(lambda WI=WI, BI=BI, o_=WI[:, a0:a0 + SL], d0=bc(r1p, [128, SL]), d1=BI[:, a0:a0 + SL]: P.op("dve", lambda e: e.tensor_tensor_scan(out=o_, data0=d0, data1=d1, initial=0.0, op0=ALU.mult, op1=ALU.add), reads=[BI, sc], writes=[WI]))
                                st_.append(lambda WI=WI, T_=T_, EiL=EiL: self.ts("dve", T_[:, 0:1], WI[:, la:la + 1], EiL, ALU.mult, [WI, Ei], [T_]))
                                st_.append(lambda WR=WR, T_=T_, EiL=EiL: self.ts("dve", T_[:, 1:2], WR[:, la:la + 1], EiL, ALU.mult, [WR, Ei], [T_]))
                                st_.append(lambda WR=WR, T_=T_, ErL=ErL, p=p: self.stt(xst[:, p, 0:1], WR[:, la:la + 1], ErL, T_[:, 0:1], ALU.mult, ALU.subtract, [WR, Er, T_], [xst]))
                                st_.append(lambda WI=WI, T_=T_, ErL=ErL, p=p: self.stt(xst[:, p, 1:2], WI[:, la:la + 1], ErL, T_[:, 1:2], ALU.mult, ALU.add, [WI, Er, T_], [xst]))
                                steps.append(st_)
                            for k in range(len(steps[0])):
                                for st_ in steps:
                                    if st_[k] is not None:
                                        st_[k]()
                        attn_step()
                        for z in prs:
                            p, pp, PR, WR, WI = z["p"], z["pp"], z["PR"], z["WR"], z["WI"]
                            Erb = bc(Er[:, p, :].unsqueeze(1), [128, nchk, SL])
                            Eib = bc(Ei[:, p, :].unsqueeze(1), [128, nchk, SL])
                            self.tt("pool", v3(PR[0]), v3(WR), Erb, ALU.mult, [WR, Er], [PR[0]])
                            self.tt("pool", v3(PR[1]), v3(WI), Eib, ALU.mult, [WI, Ei], [PR[1]])
                            self.tt("pool", v3(PR[2]), v3(WR), Eib, ALU.mult, [WR, Ei], [PR[2]])
                            self.tt("pool", v3(PR[3]), v3(WI), Erb, ALU.mult, [WI, Er], [PR[3]])
                            for t in range(ntt):
                                o_ = pY[:, yo + t * 128 + pp * 32:yo + t * 128 + pp * 32 + 32]
                                tsl = slice(t * 128, (t + 1) * 128)
                                self.mm(o_, PR[0][:, tsl], Cw[:, 0, p, :], True, False, [PR[0], Cw], [pY])
                                self.mm(o_, PR[1][:, tsl], Cn[:, 0, p, :], False, False, [PR[1], Cn], [pY])
                                self.mm(o_, PR[2][:, tsl], Cn[:, 1, p, :], False, False, [PR[2], Cn], [pY])
                                self.mm(o_, PR[3][:, tsl], Cn[:, 1, p, :], False, True, [PR[3], Cn], [pY])
                    self.copy("act", Y_[:, 0:ntt, :], pY[:, yo:yo + ng].rearrange("p (t c) -> p t c", c=128), [pY], [Y_])
                    P.dma("act", self.ytok.ap()[d, i0:i0 + ng, q * 128:(q + 1) * 128].rearrange("(t p) c -> p t c", p=128), Y_[:, 0:ntt, :], reads=[Y_], writes=[self.ytok])
                first = False
      while attn_live[0]:
          attn_step()
    with P.scope():
        wg = P.sbuf("g_wg", [128, 8, 1024], BF16)
        dsk = P.sbuf("g_dsk", [128, 8], F32)
        bgl = P.sbuf("g_bgl", [128, 8], F32)
        P.dma("pool", wg[:], self.w_glu.ap()[l].rearrange("(kc p) n -> p kc n", p=128), reads=[self.w_glu], writes=[wg])
        P.dma("sp", dsk[:], self.s5_d.ap()[l].rearrange("(c p) -> p c", p=128), reads=[self.s5_d], writes=[dsk], allow_slow_non_contiguous=True)
        P.dma("sp", bgl[:], self.b_glu.ap()[l].rearrange("(c p) -> p c", p=128), reads=[self.b_glu], writes=[bgl], allow_slow_non_contiguous=True)
        y0 = [P.sbuf("g_y0_%d" % i, [128, 1024], F32) for i in range(2)]
        y1 = [P.sbuf("g_y1_%d" % i, [128, 1024], F32) for i in range(2)]
        uTt = P.sbuf("g_uT", [128, 8, 512], F32)
        yT = P.sbuf("g_yT", [128, 8, 512], F32)
        t_a = P.sbuf("g_ta", [128, 8, 512], F32)
        gb = P.sbuf("g_gb", [128, 8, 512], BF16)
        sg = [P.sbuf("g_sg%d" % i, [128, 512], F32) for i in range(2)]
        cT = P.sbuf("g_cT", [128, 8, 512], BF16)
        for (s0, ng) in self.groups:
            P.dma("sp", uTt[:, :, 0:ng], self.uT.ap().rearrange("(c p) t -> p c t", p=128)[:, :, s0:s0 + ng], reads=[self.uT], writes=[uTt])
            for t in range(ng // 128):
                r0 = s0 + t * 128
                a0, a1 = self.rot("g_y0", y0), self.rot("g_y1", y1)
                P.dma("sp", a0[:], self.ytok.ap()[0, r0:r0 + 128, :], reads=[self.ytok], writes=[a0])
                m0 = self.mirror(r0)
                P.dma("sp", a1[:], self.ytok.ap()[1, m0:m0 + 128, :], reads=[self.ytok], writes=[a1])
                for half in range(2):
                    ps = self.rot("g_ps", self.ps[0:4])
                    for qq in range(4):
                        q = half * 4 + qq
                        o_ = ps[:, qq * 128:(qq + 1) * 128]
                        self.mm(o_, a0[:, q * 128:(q + 1) * 128], self.ident_f, True, False, [a0, self.cst_t], [ps])
                        self.mm(o_, a1[:, q * 128:(q + 1) * 128], self.J_f, False, True, [a1, self.cst_t], [ps])
                    for qq in range(4):
                        q = half * 4 + qq
                        self.stt(yT[:, q, t * 128:(t + 1) * 128], uTt[:, q, t * 128:(t + 1) * 128], dsk[:, q:q + 1], ps[:, qq * 128:(qq + 1) * 128], ALU.mult, ALU.add, [uTt, dsk, ps], [], pw=[yT])
            Y = yT[:, :, 0:ng]
            A = t_a[:, :, 0:ng]
            self.act(A, Y, AF.Square, [yT], [t_a])
            self.ts("dve", A, A, 0.044715, ALU.mult, [t_a], [t_a], s2=1.0, op1=ALU.add)
            self.tt("pool", A, A, Y, ALU.mult, [t_a, yT], [t_a])
            self.act(A, A, AF.Sigmoid, [t_a], [t_a], scale=2.0 * math.sqrt(2.0 / math.pi))
            self.tt("dve", Y, Y, A, ALU.mult, [yT, t_a], [yT])
            self.copy("act", gb[:, :, 0:ng], Y, [yT], [gb])
            for jc in range(8):
                ps = self.rot("g_ps", self.ps[0:4])
                for kc in range(8):
                    self.mm(ps[:, 0:ng], wg[:, kc, jc * 128:(jc + 1) * 128], gb[:, kc, 0:ng], kc == 0, kc == 7, [wg, gb], [ps])
                s_ = self.rot("g_sg", sg)
                self.act(s_[:, 0:ng], ps[:, 0:ng], AF.Sigmoid, [ps, bgl], [s_], bias=bgl[:, jc:jc + 1])
                self.tt("dve", cT[:, jc, 0:ng], yT[:, jc, 0:ng], s_[:, 0:ng], ALU.mult, [yT, s_], [], pw=[cT])
            P.dma("act", self.brT.ap()[2].rearrange("(c p) t -> p c t", p=128)[:, :, s0:s0 + ng], cT[:, :, 0:ng], reads=[cT], writes=[self.brT])


Builder.phase_C3 = phase_C3


def phase_DE(self, l):
    P, TC, TL, TT = self.P, self.TC, self.TL, self.TT
    last = (l == self.depth - 1)
    with P.scope():
        X = P.sbuf("e_X", [128, KC, 512], F32)
        M = P.sbuf("e_M", [128, KC, 512], BF16)
        H = P.sbuf("e_H", [128, 32, 512], BF16)
        wt = [P.sbuf("e_w%d" % i, [128, KC, 512], BF16) for i in range(3)]
        gt = [P.sbuf("e_g%d" % i, [128, 3, 512], BF16) for i in range(2)]
        ma = [P.sbuf("e_ma%d" % i, [128, 512], F32) for i in range(2)]
        mb = [P.sbuf("e_mb%d" % i, [128, 512], F32) for i in range(2)]
        sqb = [P.sbuf("e_sq%d" % i, [128, 512], BF16) for i in range(3)]
        tmp = [P.sbuf("e_tmp%d" % i, [128, 512], F32) for i in range(3)]
        rbc = P.sbuf("e_rbc", [128, 512], F32)
        rl = [P.sbuf("e_rl%d" % i, [128, 512], F32) for i in range(3)]
        yt = [P.sbuf("e_yt%d" % i, [128, D], F32) for i in range(2)] if last else None
        wbr = self.wb_br[l].ap()
        wov = self.wb_out[l].ap().rearrange("(kc p) n -> p kc n", p=128)
        w1v = self.wb_ff1[l].ap().rearrange("(kc p) n -> p kc n", p=128)
        w2v = self.wb_ff2[l].ap().rearrange("(f p) n -> p f n", p=128)
        gate1, gate2 = self.modT[:, 2], self.modT[:, 5]
        xin = self.xT[l]
        groups = [g for g in self.groups if not (last and g[0] < TC)]
        for (s0, ng) in groups:
            col = 1 if s0 < TC else 0
            P.dma("sp", X[:, :, 0:ng], xin.ap().rearrange("(c p) t -> p c t", p=128)[:, :, s0:s0 + ng], reads=[xin], writes=[X])
            for r in range(3):
                P.dma("sp", H[:, r * 8:(r + 1) * 8, 0:ng], self.brT.ap()[r].rearrange("(c p) t -> p c t", p=128)[:, :, s0:s0 + ng], reads=[self.brT], writes=[H])
            for jb in range(4):
                ws = []
                for r in range(3):
                    w = self.rot("e_w", wt)
                    P.dma("pool", w[:, 0:8, :], wbr[r].rearrange("(kc p) n -> p kc n", p=128)[:, :, jb * 512:(jb + 1) * 512], reads=[self.wb_br[l]], writes=[w])
                    ws.append(w)
                for jj in range(4):
                    j = jb * 4 + jj
                    G_ = self.rot("e_g", gt)
                    P.dma("sp", G_[:, :, 0:ng], self.gT.ap().rearrange("(r c p) t -> p r c t", r=3, p=128)[:, :, j, s0:s0 + ng], reads=[self.gT], writes=[G_])
                    A_, B_ = self.rot("e_ma", ma), self.rot("e_mb", mb)
                    for r in range(3):
                        ps = self.rot("e_ps", self.ps[0:6])
                        for kc in range(8):
                            self.mm(ps[:, 0:ng], ws[r][:, kc, jj * 128:(jj + 1) * 128], H[:, r * 8 + kc, 0:ng], kc == 0, kc == 7, [ws[r], H], [ps])
                        if r == 0:
                            self.tt("dve", A_[:, 0:ng], ps[:, 0:ng], G_[:, 0, 0:ng], ALU.mult, [ps, G_], [A_])
                        elif r == 1:
                            self.tt("dve", B_[:, 0:ng], ps[:, 0:ng], G_[:, 1, 0:ng], ALU.mult, [ps, G_], [B_])
                            self.tt("dve", A_[:, 0:ng], A_[:, 0:ng], B_[:, 0:ng], ALU.add, [A_, B_], [A_])
                        else:
                            self.tt("dve", B_[:, 0:ng], ps[:, 0:ng], G_[:, 2, 0:ng], ALU.mult, [ps, G_, A_], [B_])
                            self.tt("dve", M[:, j, 0:ng], A_[:, 0:ng], B_[:, 0:ng], ALU.add, [A_, B_], [], pw=[M])
            for jb in range(4):
                w = self.rot("e_w", wt)
                P.dma("pool", w[:], wov[:, :, jb * 512:(jb + 1) * 512], reads=[self.wb_out[l]], writes=[w])
                for jj in range(4):
                    j = jb * 4 + jj
                    ps = self.rot("e_ps", self.ps[0:6])
                    for kc in range(KC):
                        self.mm(ps[:, 0:ng], w[:, kc, jj * 128:(jj + 1) * 128], M[:, kc, 0:ng], kc == 0, kc == KC - 1, [w, M], [ps])
                    self.stt(X[:, j, 0:ng], ps[:, 0:ng], gate1[:, j, col:col + 1], X[:, j, 0:ng], ALU.mult, ALU.add, [ps, self.modT, X], [X])
            self.norm_group(X, M, ng, 1, col, sqb, rbc, tmp, self.ps[7])
            for half in range(2):
                for fb in range(8):
                    w = self.rot("e_w", wt)
                    c0 = half * 4096 + fb * 512
                    P.dma("pool", w[:], w1v[:, :, c0:c0 + 512], reads=[self.wb_ff1[l]], writes=[w])
                    for jj in range(4):
                        f = fb * 4 + jj
                        ps = self.rot("e_ps", self.ps[0:6])
                        for kc in range(KC):
                            self.mm(ps[:, 0:ng], w[:, kc, jj * 128:(jj + 1) * 128], M[:, kc, 0:ng], kc == 0, kc == KC - 1, [w, M], [ps])
                        R_ = self.rot("e_rl", rl)
                        self.act(R_[:, 0:ng], ps[:, 0:ng], AF.Relu, [ps], [R_])
                        self.tt("dve", H[:, f, 0:ng], R_[:, 0:ng], R_[:, 0:ng], ALU.mult, [R_], [], pw=[H])
                for jb in range(4):
                    wa = self.rot("e_w", wt)
                    wb_ = self.rot("e_w", wt)
                    f0 = half * 32
                    P.dma("pool", wa[:], w2v[:, f0:f0 + 16, jb * 512:(jb + 1) * 512], reads=[self.wb_ff2[l]], writes=[wa])
                    P.dma("pool", wb_[:], w2v[:, f0 + 16:f0 + 32, jb * 512:(jb + 1) * 512], reads=[self.wb_ff2[l]], writes=[wb_])
                    for jj in range(4):
                        j = jb * 4 + jj
                        ps = self.rot("e_ps", self.ps[0:6])
                        for f in range(32):
                            ww = wa if f < 16 else wb_
                            self.mm(ps[:, 0:ng], ww[:, f % 16, jj * 128:(jj + 1) * 128], H[:, f, 0:ng], f == 0, f == 31, [ww, H], [ps])
                        self.stt(X[:, j, 0:ng], ps[:, 0:ng], gate2[:, j, col:col + 1], X[:, j, 0:ng], ALU.mult, ALU.add, [ps, self.modT, X], [X])
            if not last:
                P.dma("act", self.xT[l + 1].ap().rearrange("(c p) t -> p c t", p=128)[:, :, s0:s0 + ng], X[:, :, 0:ng], reads=[X], writes=[self.xT[l + 1]])
            else:
                for t in range(ng // 128):
                    Y_ = self.rot("e_yt", yt)
                    for b4 in range(4):
                        ps = self.rot("e_ps", self.ps[0:6])
                        for jj in range(4):
                            c = b4 * 4 + jj
                            self.tr(ps[:, jj * 128:(jj + 1) * 128], X[:, c, t * 128:(t + 1) * 128], self.ident_f, [X, self.cst_t], [ps])
                        self.copy("act" if b4 % 2 == 0 else "dve", Y_[:, b4 * 512:(b4 + 1) * 512], ps[:, 0:512], [ps], [], pw=[Y_])
                    r0 = s0 - TC + t * 128
                    P.dma("act", self.y.ap()[r0:r0 + 128, :], Y_[:], reads=[Y_], writes=[self.y])


Builder.phase_DE = phase_DE


_CACHE = {}


def kernel(**inputs):
    TL, TC = 4096, 256
    if "prog" not in _CACHE:
        _CACHE["prog"] = build_program(TL, TC, depth=DEPTH)
    B = _CACHE["prog"]
    sh = prep_shared(inputs, DEPTH)
    cst, rope = make_consts(TL, TC)
    maps = []
    for core in range(8):
        b = core % 4
        m = dict(sh)
        m.update(prep_core(inputs, b, TL, TC))
        m["cst"] = cst
        m["rope"] = rope
        maps.append(m)
    res = run_bass_kernel_spmd(B.nc, maps, core_ids=list(range(8)))
    out = np.stack([np.asarray(res.results[b]["y"], dtype=np.float32) for b in range(4)], axis=0)
    return out
```
